# Optimizing a Trainium2 kernel written in Bass

```python
import math
import jax, jax.numpy as jnp
from jax import lax
import numpy as np

D_MODEL = 4096
BATCH = 2
SEQ = 4096
DEPTH = 2

CHUNK = 64
EPS = 1e-6
N_EVEN = (DEPTH + 1) // 2
N_ODD = DEPTH // 2

POOL_WINDOWS = (2, 4, 8, 16)
POOL_WIDTH = D_MODEL // 2
POOL_GROUP = POOL_WIDTH // len(POOL_WINDOWS)
GDN_HEAD_DIM = 128
GDN_WIDTH = D_MODEL // 2
GDN_HEADS = GDN_WIDTH // GDN_HEAD_DIM
CONV_WIDTH = 4
AB_IN = POOL_WIDTH + 4 * GDN_WIDTH + 2 * GDN_HEADS
AB_MIX = POOL_WIDTH + GDN_WIDTH

SGU_WIDTH = D_MODEL
SGU_GROUPS = 16
SGU_GROUP = SGU_WIDTH // SGU_GROUPS
SGU_LEN = 128

D_FF = 256 * (-(-(8 * D_MODEL) // (3 * 256)))

kernel_name = 'hybrid_pool_gdn_sgu_streaming'


def rms_norm(x, g):
    xf = x.astype(jnp.float32)
    y = xf * lax.rsqrt(jnp.mean(xf * xf, axis=-1, keepdims=True) + EPS)
    return (y * g.astype(jnp.float32)).astype(x.dtype)


def layer_norm(x, g, b):
    xf = x.astype(jnp.float32)
    mu = jnp.mean(xf, axis=-1, keepdims=True)
    xc = xf - mu
    y = xc * lax.rsqrt(jnp.mean(xc * xc, axis=-1, keepdims=True) + EPS)
    return (y * g.astype(jnp.float32) + b.astype(jnp.float32)).astype(x.dtype)


def l2norm(x):
    return x * lax.rsqrt(jnp.sum(x * x, axis=-1, keepdims=True) + EPS)


def causal_dwconv(x, w):
    c = x.shape[-1]
    return lax.conv_general_dilated(
        x, w[:, None, :].astype(x.dtype), window_strides=(1,),
        padding=[(CONV_WIDTH - 1, 0)], dimension_numbers=('NWC', 'WIO', 'NWC'),
        feature_group_count=c)


def pool_mixer(x, w_grp, scale):
    b, s, _ = x.shape
    n_g = len(POOL_WINDOWS)
    xg = x.astype(jnp.float32).reshape(b, s, n_g, POOL_GROUP)
    cs = jnp.cumsum(xg, axis=1)
    cs = jnp.concatenate([jnp.zeros_like(cs[:, :1]), cs], axis=1)
    pos = jnp.arange(1, s + 1)
    outs = []
    for gi, win in enumerate(POOL_WINDOWS):
        c = cs[:, :, gi]
        lower = jnp.concatenate(
            [jnp.zeros((b, win - 1, POOL_GROUP), jnp.float32), c[:, :s - win + 1]], axis=1)
        cnt = jnp.minimum(pos, win).astype(jnp.float32)[None, :, None]
        outs.append((c[:, 1:] - lower) / cnt - xg[:, :, gi])
    y = jnp.stack(outs, axis=2).astype(x.dtype)
    y = jnp.einsum('bsgc,gcd->bsgd', y, w_grp)
    return (y.reshape(b, s, POOL_WIDTH) * scale).astype(x.dtype)


def gated_delta_rule(q, k, v, beta, g):
    b, s, h, d = q.shape
    n = s // CHUNK

    def chunks(t):
        return t.reshape(b, n, CHUNK, h, -1).transpose(0, 3, 1, 2, 4)

    q, k, v = chunks(q), chunks(k), chunks(v)
    beta = beta.reshape(b, n, CHUNK, h).transpose(0, 3, 1, 2)
    gc = jnp.cumsum(g.reshape(b, n, CHUNK, h).transpose(0, 3, 1, 2), axis=-1)
    causal = jnp.tril(jnp.ones((CHUNK, CHUNK), bool))
    strict = jnp.tril(jnp.ones((CHUNK, CHUNK), bool), -1)
    decay = jnp.exp(jnp.where(causal, gc[..., :, None] - gc[..., None, :], -jnp.inf))
    kk = jnp.einsum('bhnid,bhnjd->bhnij', k, k)
    lmat = jnp.where(strict, beta[..., :, None] * kk * decay, 0.0)
    eye = jnp.eye(CHUNK, dtype=jnp.float32)
    gamma = jnp.exp(gc)[..., None]
    rhs = jnp.concatenate([k * beta[..., None] * gamma, v * beta[..., None]], axis=-1)
    sol = lax.linalg.triangular_solve(eye + lmat, rhs, left_side=True, lower=True,
                                      unit_diagonal=True)
    w, u = sol[..., :d], sol[..., d:]
    attn = jnp.einsum('bhnid,bhnjd->bhnij', q, k) * decay
    q_dec = q * gamma
    k_dec = k * jnp.exp(gc[..., -1:] - gc)[..., None]
    g_last = jnp.exp(gc[..., -1])

    def step(state, inp):
        w_c, u_c, qd, kd, a, gl = inp
        v_new = u_c - jnp.einsum('bhcd,bhde->bhce', w_c, state)
        o = jnp.einsum('bhcd,bhde->bhce', qd, state) + jnp.einsum('bhcj,bhje->bhce', a, v_new)
        state = gl[..., None, None] * state + jnp.einsum('bhcd,bhce->bhde', kd, v_new)
        return state, o

    xs = tuple(jnp.moveaxis(t, 2, 0) for t in (w, u, q_dec, k_dec, attn, g_last))
    s0 = jnp.zeros((b, h, d, v.shape[-1]), jnp.float32)
    _, o = lax.scan(step, s0, xs)
    return o.transpose(1, 0, 3, 2, 4).reshape(b, s, h, -1)


def mix_pool_gdn(h, w_in, pool_w, pool_scale, conv_w, a_log, dt_bias, norm_w, w_out):
    b, s, _ = h.shape
    f32 = jnp.float32
    z = h @ w_in
    xa = z[..., :POOL_WIDTH]
    qkv = z[..., POOL_WIDTH:POOL_WIDTH + 3 * GDN_WIDTH]
    gate = z[..., POOL_WIDTH + 3 * GDN_WIDTH:POOL_WIDTH + 4 * GDN_WIDTH]
    beta_logit = z[..., AB_IN - 2 * GDN_HEADS:AB_IN - GDN_HEADS]
    dec_logit = z[..., AB_IN - GDN_HEADS:]
    ya = pool_mixer(xa, pool_w, pool_scale)
    qkv = jax.nn.silu(causal_dwconv(qkv, conv_w)).astype(f32)
    q, k, v = [t.reshape(b, s, GDN_HEADS, GDN_HEAD_DIM) for t in jnp.split(qkv, 3, axis=-1)]
    q = l2norm(q) * (GDN_HEAD_DIM ** -0.5)
    k = l2norm(k)
    beta = jax.nn.sigmoid(beta_logit.astype(f32))
    g = -jnp.exp(a_log.astype(f32)) * jax.nn.softplus(dec_logit.astype(f32) + dt_bias.astype(f32))
    o = gated_delta_rule(q, k, v, beta, g)
    o = rms_norm(o, norm_w) * jax.nn.silu(gate.astype(f32).reshape(b, s, GDN_HEADS, GDN_HEAD_DIM))
    yb = o.reshape(b, s, GDN_WIDTH).astype(h.dtype)
    return jnp.concatenate([ya, yb], axis=-1) @ w_out


def mix_sgu(h, w_in, ln_g, ln_b, w_s, b_s, w_out):
    b, s, _ = h.shape
    z = jax.nn.gelu(h @ w_in)
    u, v = z[..., :SGU_WIDTH], z[..., SGU_WIDTH:]
    v = layer_norm(v, ln_g, ln_b)
    n = s // SGU_LEN
    vg = v.reshape(b, n, SGU_LEN, SGU_GROUPS, SGU_GROUP)
    pos = jnp.arange(SGU_LEN)
    mask = (pos[None, :] // CHUNK) <= (pos[:, None] // CHUNK)
    ws = jnp.where(mask[None], w_s, jnp.zeros_like(w_s))
    sv = jnp.einsum('gij,bnjgc->bnigc', ws, vg) + b_s.T[:, :, None]
    return (u * sv.reshape(b, s, SGU_WIDTH)) @ w_out


def swiglu(h, w_gate, w_up, w_down):
    return (jax.nn.silu(h @ w_gate) * (h @ w_up)) @ w_down


def setup_inputs(seed: int = 0) -> dict:
    key = jax.random.key(seed)
    ks = jax.random.split(key, 24)
    f32 = jnp.float32

    def nrm(k, shape, scale):
        return jax.random.normal(k, shape, f32) * scale

    def gain(k, shape):
        return 1.0 + 0.02 * jax.random.normal(k, shape, f32)

    dt = jnp.exp(jax.random.uniform(ks[7], (N_EVEN, GDN_HEADS), f32,
                                    math.log(1e-3), math.log(1e-1)))
    return {
        'x': jax.random.normal(ks[0], (BATCH, SEQ, D_MODEL), f32),
        'norm_mix_pre': gain(ks[1], (DEPTH, D_MODEL)),
        'norm_mix_post': gain(ks[2], (DEPTH, D_MODEL)),
        'norm_ffn_pre': gain(ks[3], (DEPTH, D_MODEL)),
        'norm_ffn_post': gain(ks[4], (DEPTH, D_MODEL)),
        'ab_w_in': nrm(ks[5], (N_EVEN, D_MODEL, AB_IN), D_MODEL ** -0.5),
        'pool_w': nrm(ks[6], (N_EVEN, len(POOL_WINDOWS), POOL_GROUP, POOL_GROUP), POOL_GROUP ** -0.5),
        'pool_scale': gain(ks[8], (N_EVEN, POOL_WIDTH)),
        'gdn_conv': nrm(ks[9], (N_EVEN, CONV_WIDTH, 3 * GDN_WIDTH), CONV_WIDTH ** -0.5),
        'gdn_a_log': jnp.log(jax.random.uniform(ks[10], (N_EVEN, GDN_HEADS), f32, 1.0, 16.0)),
        'gdn_dt_bias': jnp.log(jnp.expm1(dt)),
        'gdn_norm': gain(ks[11], (N_EVEN, GDN_HEAD_DIM)),
        'ab_w_out': nrm(ks[12], (N_EVEN, AB_MIX, D_MODEL), AB_MIX ** -0.5),
        'sgu_w_in': nrm(ks[13], (N_ODD, D_MODEL, 2 * SGU_WIDTH), D_MODEL ** -0.5),
        'sgu_ln_g': gain(ks[14], (N_ODD, SGU_WIDTH)),
        'sgu_ln_b': nrm(ks[15], (N_ODD, SGU_WIDTH), 0.02),
        'sgu_w_s': nrm(ks[16], (N_ODD, SGU_GROUPS, SGU_LEN, SGU_LEN), SGU_LEN ** -0.5),
        'sgu_b_s': gain(ks[17], (N_ODD, SGU_GROUPS, SGU_LEN)),
        'sgu_w_out': nrm(ks[18], (N_ODD, SGU_WIDTH, D_MODEL), SGU_WIDTH ** -0.5),
        'ffn_w_gate': nrm(ks[19], (DEPTH, D_MODEL, D_FF), D_MODEL ** -0.5),
        'ffn_w_up': nrm(ks[20], (DEPTH, D_MODEL, D_FF), D_MODEL ** -0.5),
        'ffn_w_down': nrm(ks[21], (DEPTH, D_FF, D_MODEL), D_FF ** -0.5),
    }


def reference(x, norm_mix_pre, norm_mix_post, norm_ffn_pre, norm_ffn_post,
              ab_w_in, pool_w, pool_scale, gdn_conv, gdn_a_log, gdn_dt_bias, gdn_norm,
              ab_w_out, sgu_w_in, sgu_ln_g, sgu_ln_b, sgu_w_s, sgu_b_s, sgu_w_out,
              ffn_w_gate, ffn_w_up, ffn_w_down):
    for layer in range(DEPTH):
        i = layer // 2
        h = rms_norm(x, norm_mix_pre[layer])
        if layer % 2 == 0:
            y = mix_pool_gdn(h, ab_w_in[i], pool_w[i], pool_scale[i], gdn_conv[i],
                             gdn_a_log[i], gdn_dt_bias[i], gdn_norm[i], ab_w_out[i])
        else:
            y = mix_sgu(h, sgu_w_in[i], sgu_ln_g[i], sgu_ln_b[i], sgu_w_s[i],
                        sgu_b_s[i], sgu_w_out[i])
        x = x + rms_norm(y, norm_mix_post[layer])
        h = rms_norm(x, norm_ffn_pre[layer])
        y = swiglu(h, ffn_w_gate[layer], ffn_w_up[layer], ffn_w_down[layer])
        x = x + rms_norm(y, norm_ffn_post[layer])
    return x
```

```python
import contextlib
import numpy as np
import ml_dtypes
import concourse.bass as bass
import concourse.mybir as mybir
from concourse.bass_utils import run_bass_kernel_spmd

F32 = mybir.dt.float32
BF16 = mybir.dt.bfloat16
AF = mybir.ActivationFunctionType
ALU = mybir.AluOpType
AX = mybir.AxisListType

D = 4096
DFF = 11008
KC = D // 128
TP = 1024
SEQ = 4096
EPS = 1e-6


class Buf:
    __slots__ = ("name", "w", "r", "strict")

    def __init__(self, name="", strict=False):
        self.name = name
        self.w = None
        self.r = []
        self.strict = strict


def SBuf(name=""):
    return Buf(name, True)


class Sched:
    ENGS = ("pe", "act", "dve", "pool", "sp")

    def __init__(self, nc, n_dma_sems=48):
        self.nc = nc
        self.prog = {e: [] for e in self.ENGS}
        self.sems = {e: nc.alloc_semaphore(name="s_" + e) for e in self.ENGS}
        self.cnt = {e: 0 for e in self.ENGS}
        self.waited = {e: {} for e in self.ENGS}
        self.dsems = [nc.alloc_semaphore(name="d%d" % i) for i in range(n_dma_sems)]
        self.dval = [0] * n_dma_sems
        self.dnext = 0
        self.n_ins = 0
        self.nosame = 1
        self.sems["cc"] = nc.alloc_semaphore(name="s_cc")
        self.ccval = 0

    def _sem(self, key):
        return self.sems[key] if isinstance(key, str) else self.dsems[key]

    def _collect(self, eng, reads, writes):
        need = {}

        relax = self.nosame and eng in ("dve", "act")

        def add(tok, strict):
            if tok is None:
                return
            k, v = tok
            if k == eng and (eng == "pe" or (relax and not strict)):
                return
            if need.get(k, 0) < v:
                need[k] = v
        for b in reads:
            add(b.w, b.strict)
        for b in writes:
            add(b.w, b.strict)
            for t in b.r:
                add(t, b.strict)
        waits = []
        wd = self.waited[eng]
        for k, v in need.items():
            if wd.get(k, 0) >= v:
                continue
            wd[k] = v
            waits.append((self._sem(k), v))
        return waits

    def _commit(self, tok, reads, writes):
        for b in reads:
            b.r.append(tok)
        for b in writes:
            b.w = tok
            b.r = []

    def op(self, eng, fn, reads=(), writes=(), signal=True):
        waits = self._collect(eng, reads, writes)
        if signal:
            self.cnt[eng] += 1
            tok = (eng, self.cnt[eng])
        else:
            tok = (eng, self.cnt[eng] + 1)
        sem = self.sems[eng]

        def run(e, waits=waits, fn=fn, sem=sem, signal=signal):
            for s, v in waits:
                e.wait_ge(s, v)
            ins = fn(e)
            if signal:
                ins.then_inc(sem, 1)
        self.prog[eng].append(run)
        self._commit(tok, reads, writes)
        self.n_ins += 1

    def dma(self, eng, out_ap, in_ap, reads=(), writes=(), **kw):
        i = self.dnext
        self.dnext = (self.dnext + 1) % len(self.dsems)
        waits = self._collect(eng, reads, writes)
        wd = self.waited[eng]
        if self.dval[i] > 0 and wd.get(i, 0) < self.dval[i]:
            wd[i] = self.dval[i]
            waits.append((self.dsems[i], self.dval[i]))
        self.dval[i] += 16
        tok = (i, self.dval[i])
        sem = self.dsems[i]

        def run(e, waits=waits, sem=sem):
            for s, v in waits:
                e.wait_ge(s, v)
            e.dma_start(out=out_ap, in_=in_ap, **kw).then_inc(sem, 16)
        self.prog[eng].append(run)
        self._commit(tok, reads, writes)
        self.n_ins += 1

    def collective(self, kind, in_ap, out_ap, groups, reads=(), writes=()):
        waits = self._collect("pool", reads, writes)
        self.ccval += 1
        tok = ("cc", self.ccval)
        sem = self.sems["cc"]

        def run(e, waits=waits, sem=sem):
            for s_, v in waits:
                e.wait_ge(s_, v)
            e.collective_compute(kind, ALU.bypass, replica_groups=groups, ins=[in_ap], outs=[out_ap]).then_inc(sem, 1)
        self.prog["pool"].append(run)
        self._commit(tok, reads, writes)

    def barrier(self):
        for e in self.ENGS:
            waits = []
            wd = self.waited[e]
            for e2 in self.ENGS:
                if e2 != e and self.cnt[e2] > wd.get(e2, 0):
                    wd[e2] = self.cnt[e2]
                    waits.append((self.sems[e2], self.cnt[e2]))
            for i, v in enumerate(self.dval):
                if v > wd.get(i, 0):
                    wd[i] = v
                    waits.append((self.dsems[i], v))
            if self.ccval > wd.get("cc", 0):
                wd["cc"] = self.ccval
                waits.append((self.sems["cc"], self.ccval))

            def run(en, waits=waits):
                for s, v in waits:
                    en.wait_ge(s, v)
            self.prog[e].append(run)

    def finish(self):
        self.barrier()
        nc = self.nc
        with nc.Block() as block:
            @block.tensor
            def _(e):
                for f in self.prog["pe"]:
                    f(e)

            @block.scalar
            def _(e):
                for f in self.prog["act"]:
                    f(e)

            @block.vector
            def _(e):
                for f in self.prog["dve"]:
                    f(e)

            @block.gpsimd
            def _(e):
                for f in self.prog["pool"]:
                    f(e)

            @block.sync
            def _(e):
                for f in self.prog["sp"]:
                    f(e)


class Ctx:
    def __init__(self, nc, S, st, NW=8, pfx=""):
        self.nc, self.S, self.st, self.pfx = nc, S, st, pfx
        self.ps = [st.enter_context(nc.psum_tensor(pfx + "ps%d" % i, [128, 512], F32)) for i in range(8)]
        self.psb = [Buf("ps%d" % i) for i in range(8)]
        self.ones_bf = self.sb("ones_bf", [128, 128], BF16)
        self.ones_f = self.sb("ones_f", [128, 128], F32)
        self.cb = SBuf("consts")
        S.op("dve", lambda e: e.memset(self.ones_bf[:], 1.0), writes=[self.cb])
        S.op("dve", lambda e: e.memset(self.ones_f[:], 1.0), writes=[self.cb])
        self.NW = NW
        self.wt = [self.sb("wt%d" % i, [128, 4096], BF16) for i in range(self.NW)]
        self.wtb = [Buf("wt%d" % i) for i in range(self.NW)]
        self.wnext = 0
        self.dmaq = 0

    def sb(self, name, shape, dt):
        return self.st.enter_context(self.nc.sbuf_tensor(self.pfx + name, shape, dt))

    def wslot(self):
        i = self.wnext
        self.wnext = (i + 1) % self.NW
        return self.wt[i], self.wtb[i]

    def q(self):
        self.dmaq ^= 1
        return "sp" if self.dmaq else "act"


def gemm_cg(C, W, c0, CW, rhs, rhsb, KCr, T, banks, tokmajor=False):
    S = C.S
    nth = T // 512
    noc = CW // 128
    ukc = 4096 // CW
    nu = (KCr + ukc - 1) // ukc
    Wv = W.rearrange("(kc p) n -> p kc n", p=128)
    for u in range(nu):
        k0 = u * ukc
        nk = min(ukc, KCr - k0)
        wt, wb = C.wslot()
        wv = wt[:, 0:nk * CW].rearrange("p (k n) -> p k n", n=CW)
        S.dma("pool", wv, Wv[:, k0:k0 + nk, c0:c0 + CW], writes=[wb])
        for oc in range(noc):
            for th in range(nth):
                bi = banks[oc * nth + th]
                for j in range(nk):
                    kc = k0 + j
                    S.op("pe", lambda e, bi=bi, wv=wv, j=j, oc=oc, kc=kc, th=th: e.matmul(
                        C.ps[bi][:, :], wv[:, j, oc * 128:(oc + 1) * 128], rhs[:, kc, th * 512:(th + 1) * 512],
                        start=(kc == 0), stop=(kc == KCr - 1)),
                        reads=[wb, rhsb], writes=[C.psb[bi]], signal=(j == nk - 1))


def load_gain(C, sb, name, g_ap, ncol=KC):
    t = sb(name, [128, ncol], F32)
    b = Buf(name)
    C.S.dma("sp", t[:, :], g_ap.rearrange("(kc p) -> p kc", p=128), writes=[b], allow_slow_non_contiguous=True)
    return t, b


def colsum_rstd(C, src_dram, srcb, nkc, T, rstd, rstdb, xin, xinb, sq, sqb, scale, tmp, tmpb):
    S = C.S
    nth = T // 512
    for kc in range(nkc):
        r = kc % len(xin)
        S.dma(C.q(), xin[r][:, :], src_dram[kc * 128:(kc + 1) * 128, :], reads=[srcb[kc]], writes=[xinb[r]])
        r2 = kc % len(sq)
        S.op("act", lambda e, r=r, r2=r2: e.activation(sq[r2][:, :], xin[r][:, :], AF.Square), reads=[xinb[r]], writes=[sqb[r2]])
        for th in range(nth):
            S.op("pe", lambda e, th=th, r2=r2, kc=kc: e.matmul(C.ps[th][:, :], C.ones_bf[:, :], sq[r2][:, th * 512:(th + 1) * 512],
                                                             start=(kc == 0), stop=(kc == nkc - 1)),
                 reads=[sqb[r2], C.cb], writes=[C.psb[th]])
    for th in range(nth):
        sl = slice(th * 512, (th + 1) * 512)
        S.op("act", lambda e, th=th, sl=sl: e.activation(tmp[:, sl], C.ps[th][:, :], AF.Sqrt, bias=C.eps_t[:, 0:1], scale=scale),
             reads=[C.psb[th], C.cb], writes=[tmpb])
        S.op("dve", lambda e, sl=sl: e.reciprocal(rstd[:, sl], tmp[:, sl]), reads=[tmpb], writes=[rstdb])


def norm_stage(C, XT, XTb, gain_ap, HT, HTb, tag):
    S, nc = C.S, C.nc
    with contextlib.ExitStack() as st:
        sb = lambda n, s, d: st.enter_context(nc.sbuf_tensor(tag + n, s, d))
        xin = [sb("xin%d" % i, [128, TP], F32) for i in range(3)]
        xinb = [Buf() for _ in range(3)]
        sq = [sb("sq%d" % i, [128, TP], BF16) for i in range(2)]
        sqb = [Buf() for _ in range(2)]
        rstd = sb("rstd", [128, TP], F32)
        rstdb = Buf()
        tmp = sb("tmp", [128, TP], F32)
        tmpb = Buf()
        g = sb("g", [128, KC], F32)
        gb = Buf()
        S.dma("sp", g[:, :], gain_ap.rearrange("(kc p) -> p kc", p=128), writes=[gb], allow_slow_non_contiguous=True)
        colsum_rstd(C, XT, XTb, KC, TP, rstd, rstdb, xin, xinb, sq, sqb, 1.0 / D, tmp, tmpb)
        for kc in range(KC):
            r = kc % 3
            S.dma(C.q(), xin[r][:, :], XT[kc * 128:(kc + 1) * 128, :], reads=[XTb[kc]], writes=[xinb[r]])
            S.op("dve", lambda e, r=r, kc=kc: e.scalar_tensor_tensor(HT[:, kc, :], xin[r][:, :], g[:, kc:kc + 1], rstd[:, :],
                                                                    ALU.mult, ALU.mult),
                 reads=[xinb[r], gb, rstdb], writes=[HTb])
        S.barrier()


def postnorm_resid(C, YT, YTb, gain_ap, XT, XTb, tag):
    S, nc = C.S, C.nc
    with contextlib.ExitStack() as st:
        sb = lambda n, s, d: st.enter_context(nc.sbuf_tensor(tag + n, s, d))
        xin = [sb("xin%d" % i, [128, TP], F32) for i in range(3)]
        xinb = [Buf() for _ in range(3)]
        yin = [sb("yin%d" % i, [128, TP], F32) for i in range(3)]
        yinb = [Buf() for _ in range(3)]
        sq = [sb("sq%d" % i, [128, TP], BF16) for i in range(2)]
        sqb = [Buf() for _ in range(2)]
        rstd = sb("rstd", [128, TP], F32)
        rstdb = Buf()
        tmp = sb("tmp", [128, TP], F32)
        tmpb = Buf()
        g = sb("g", [128, KC], F32)
        gb = Buf()
        S.dma("sp", g[:, :], gain_ap.rearrange("(kc p) -> p kc", p=128), writes=[gb], allow_slow_non_contiguous=True)
        colsum_rstd(C, YT, YTb, KC, TP, rstd, rstdb, yin, yinb, sq, sqb, 1.0 / D, tmp, tmpb)
        for kc in range(KC):
            r = kc % 3
            rows = slice(kc * 128, (kc + 1) * 128)
            S.dma("sp", yin[r][:, :], YT[rows, :], reads=[YTb[kc]], writes=[yinb[r]])
            S.dma("act", xin[r][:, :], XT[rows, :], reads=[XTb[kc]], writes=[xinb[r]])
            S.op("dve", lambda e, r=r, kc=kc: e.scalar_tensor_tensor(yin[r][:, :], yin[r][:, :], g[:, kc:kc + 1], rstd[:, :],
                                                                    ALU.mult, ALU.mult),
                 reads=[yinb[r], gb, rstdb], writes=[yinb[r]])
            S.op("pool", lambda e, r=r: e.tensor_tensor(xin[r][:, :], xin[r][:, :], yin[r][:, :], ALU.add),
                 reads=[yinb[r], xinb[r]], writes=[xinb[r]])
            S.dma("sp", XT[rows, :], xin[r][:, :], reads=[xinb[r]], writes=[XTb[kc]])
        S.barrier()


def gemm_to_dram(C, W, N, rhs, rhsb, KCr, T, OUT, OUTb, tok0, func, odt, tag):
    S, nc = C.S, C.nc
    CW = 256 if T == 1024 else 512
    nth = T // 512
    noc = CW // 128
    with contextlib.ExitStack() as st:
        ot = [st.enter_context(nc.sbuf_tensor(tag + "ot%d" % i, [128, 512], odt)) for i in range(4)]
        otb = [Buf() for _ in range(4)]
        oi = 0
        for cg in range(N // CW):
            banks = [(cg % 2) * 4 + i for i in range(4)]
            gemm_cg(C, W, cg * CW, CW, rhs, rhsb, KCr, T, banks)
            for oc in range(noc):
                for th in range(nth):
                    bi = banks[oc * nth + th]
                    o = oi % 4
                    oi += 1
                    if func is None:
                        S.op("dve", lambda e, o=o, bi=bi: e.tensor_copy(ot[o][:, :], C.ps[bi][:, :]),
                             reads=[C.psb[bi]], writes=[otb[o]])
                    else:
                        S.op("act", lambda e, o=o, bi=bi: e.activation(ot[o][:, :], C.ps[bi][:, :], func),
                             reads=[C.psb[bi]], writes=[otb[o]])
                    row = cg * CW + oc * 128
                    S.dma(C.q(), OUT[row:row + 128, tok0 + th * 512: tok0 + (th + 1) * 512], ot[o][:, :],
                          reads=[otb[o]], writes=[OUTb[row // 128]])
        S.barrier()


def load_fm(C, SRC, SRCb, nkc, T, tok0, dst, dstb, per=8):
    v = SRC.rearrange("(kc p) t -> p kc t", p=128)
    for k0 in range(0, nkc, per):
        k1 = min(nkc, k0 + per)
        C.S.dma(C.q(), dst[:, k0:k1, 0:T], v[:, k0:k1, tok0:tok0 + T], reads=[SRCb[k] for k in range(k0, k1)], writes=[dstb])


def ffn_stage(C, XT, XTb, YT, YTb, HID, HIDb, g_pre, g_post, Wg, Wu, Wd, tag):
    S, nc = C.S, C.nc
    with contextlib.ExitStack() as st:
        HT = st.enter_context(nc.sbuf_tensor(tag + "HT", [128, KC, TP], BF16))
        HTb = Buf()
        norm_stage(C, XT, XTb, g_pre, HT, HTb, tag + "n")
        sl_t = [st.enter_context(nc.sbuf_tensor(tag + "sl%d" % i, [128, 512], F32)) for i in range(2)]
        slb = [Buf() for _ in range(2)]
        ot = [st.enter_context(nc.sbuf_tensor(tag + "ho%d" % i, [128, 512], BF16)) for i in range(4)]
        otb = [Buf() for _ in range(4)]
        oi = 0
        for cg in range(DFF // 256):
            bg = [0, 1, 2, 3]
            bu = [4, 5, 6, 7]
            gemm_cg(C, Wg, cg * 256, 256, HT, HTb, KC, TP, bg)
            gemm_cg(C, Wu, cg * 256, 256, HT, HTb, KC, TP, bu)
            for oc in range(2):
                for th in range(2):
                    o = oi % 4
                    s2 = oi % 2
                    oi += 1
                    b1, b2 = bg[oc * 2 + th], bu[oc * 2 + th]
                    S.op("act", lambda e, s2=s2, b1=b1: e.activation(sl_t[s2][:, :], C.ps[b1][:, :], AF.Silu),
                         reads=[C.psb[b1]], writes=[slb[s2]])
                    S.op("dve", lambda e, s2=s2, b2=b2, o=o: e.tensor_tensor(ot[o][:, :], sl_t[s2][:, :], C.ps[b2][:, :], ALU.mult),
                         reads=[slb[s2], C.psb[b2]], writes=[otb[o]])
                    row = cg * 256 + oc * 128
                    S.dma(C.q(), HID[row:row + 128, th * 512:(th + 1) * 512], ot[o][:, :], reads=[otb[o]], writes=[HIDb[row // 128]])
        S.barrier()
    KF = DFF // 128
    with contextlib.ExitStack() as st:
        RH = st.enter_context(nc.sbuf_tensor(tag + "RH", [128, KF, 512], BF16))
        RHb = Buf()
        for th2 in range(2):
            load_fm(C, HID, HIDb, KF, 512, th2 * 512, RH, RHb)
            gemm_to_dram(C, Wd, D, RH, RHb, KF, 512, YT, YTb, th2 * 512, None, F32, tag + "d%d" % th2)
    postnorm_resid(C, YT, YTb, g_post, XT, XTb, tag + "p")


def about_stage(C, XT, XTb, YT, YTb, YC, YCb, g_post, Wo, tag):
    S, nc = C.S, C.nc
    with contextlib.ExitStack() as st:
        R = st.enter_context(nc.sbuf_tensor(tag + "R", [128, KC, TP], BF16))
        Rb = Buf()
        load_fm(C, YC, YCb, KC, TP, 0, R, Rb)
        gemm_to_dram(C, Wo, D, R, Rb, KC, TP, YT, YTb, 0, None, F32, tag + "g")
    postnorm_resid(C, YT, YTb, g_post, XT, XTb, tag + "p")


def sgu_stage(C, XT, XTb, YT, YTb, UT, UTb, VTM, VTMb, g_pre, g_post, Win, ln_g, ln_b, wsT, bs, maskT, Wout, tag):
    S, nc = C.S, C.nc
    with contextlib.ExitStack() as st:
        HT = st.enter_context(nc.sbuf_tensor(tag + "HT", [128, KC, TP], BF16))
        HTb = Buf()
        norm_stage(C, XT, XTb, g_pre, HT, HTb, tag + "n")
        gemm_to_dram(C, Win[:, 0:D], D, HT, HTb, KC, TP, UT, UTb, 0, AF.Gelu, BF16, tag + "u")
        vo = [st.enter_context(nc.sbuf_tensor(tag + "vo%d" % i, [128, 512], F32)) for i in range(3)]
        vob = [Buf() for _ in range(3)]
        Wv = Win.rearrange("(kc p) n -> p kc n", p=128)
        oi = 0
        for cg in range(D // 512):
            slots = []
            for u in range(4):
                wt, wb = C.wslot()
                wv = wt[:, :].rearrange("p (k n) -> p k n", n=512)
                S.dma("pool", wv, Wv[:, u * 8:(u + 1) * 8, D + cg * 512: D + (cg + 1) * 512], writes=[wb])
                slots.append((wv, wb))
            for tb in range(TP // 128):
                bi = oi % 8
                for kc in range(KC):
                    wv, wb = slots[kc // 8]
                    S.op("pe", lambda e, bi=bi, wv=wv, kc=kc, tb=tb: e.matmul(
                        C.ps[bi][:, :], HT[:, kc, tb * 128:(tb + 1) * 128], wv[:, kc % 8, :], start=(kc == 0), stop=(kc == KC - 1)),
                        reads=[wb, HTb], writes=[C.psb[bi]], signal=(kc % 8 == 7))
                o = oi % 3
                oi += 1
                S.op("act", lambda e, o=o, bi=bi: e.activation(vo[o][:, :], C.ps[bi][:, :], AF.Gelu), reads=[C.psb[bi]], writes=[vob[o]])
                S.dma(C.q(), VTM[tb * 128:(tb + 1) * 128, cg * 512:(cg + 1) * 512], vo[o][:, :], reads=[vob[o]], writes=[VTMb[tb]])
        S.barrier()
    with contextlib.ExitStack() as st:
        sb = lambda n, s, d: st.enter_context(nc.sbuf_tensor(tag + n, s, d))
        PT = sb("PT", [128, KC, TP], BF16)
        PTb = Buf()
        load_fm(C, UT, UTb, KC, TP, 0, PT, PTb)
        mk = sb("mk", [128, 128], F32)
        mkb = Buf()
        S.dma("sp", mk[:, :], maskT, writes=[mkb])
        wsbf = sb("wsbf", [128, 16, 128], BF16)
        wsbfb = Buf()
        S.dma("pool", wsbf[:, :, :], wsT, writes=[wsbfb])
        for g in range(16):
            S.op("dve", lambda e, g=g: e.tensor_tensor(wsbf[:, g, :], wsbf[:, g, :], mk[:, :], ALU.mult), reads=[wsbfb, mkb], writes=[wsbfb])
        BS = sb("BS", [128, 16, 128], F32)
        BSb = Buf()
        S.dma("sp", BS[:, :, :], bs, writes=[BSb])
        RS = sb("RS", [128, 16, 128], F32)
        RSb = Buf()
        for q4 in range(4):
            S.op("pe", lambda e, q4=q4: e.matmul(C.ps[q4][:, :], C.ones_bf[:, :], wsbf[:, q4 * 4:(q4 + 1) * 4, :], start=True, stop=True),
                 reads=[wsbfb, C.cb], writes=[C.psb[q4]])
            S.op("dve", lambda e, q4=q4: e.tensor_copy(RS[:, q4 * 4:(q4 + 1) * 4, :], C.ps[q4][:, :]), reads=[C.psb[q4]], writes=[RSb])
        lg, lgb = load_gain(C, sb, "lg", ln_g)
        lb, lbb = load_gain(C, sb, "lb", ln_b)
        T2 = sb("T2", [128, KC, 128], F32)
        T2b = Buf()
        for kc in range(KC):
            S.op("dve", lambda e, kc=kc: e.scalar_tensor_tensor(T2[:, kc, :], RS[:, kc // 2, :], lb[:, kc:kc + 1], BS[:, kc // 2, :],
                                                               ALU.mult, ALU.add), reads=[RSb, BSb, lbb], writes=[T2b])
        vin = [sb("vin0", [128, D], F32)] * 2
        vinb = [Buf()] * 2
        vh = [sb("vh0", [128, D], BF16)] * 2
        vhb = [Buf()] * 2
        junk = vh[0]
        junkb = vhb[0]
        st4 = [sb("st%d" % i, [128, 8], F32) for i in range(2)]
        st4b = [SBuf() for _ in range(2)]
        sv = [sb("sv%d" % i, [128, 128], F32) for i in range(3)]
        svb = [Buf() for _ in range(3)]
        oi = 0
        for tb in range(TP // 128):
            r = tb % 2
            S.dma("sp", vin[r][:, 0:D // 2], VTM[tb * 128:(tb + 1) * 128, 0:D // 2], reads=[VTMb[tb]], writes=[vinb[r]])
            S.dma("act", vin[r][:, D // 2:D], VTM[tb * 128:(tb + 1) * 128, D // 2:D], reads=[VTMb[tb]], writes=[vinb[r]])
            s4 = st4[r]
            S.op("act", lambda e, r=r, s4=s4: e.activation(junk[:, :], vin[r][:, :], AF.Identity, accum_out=s4[:, 0:1]),
                 reads=[vinb[r]], writes=[junkb, st4b[r]])
            S.op("act", lambda e, r=r, s4=s4: e.activation(junk[:, :], vin[r][:, :], AF.Square, accum_out=s4[:, 1:2]),
                 reads=[vinb[r]], writes=[junkb, st4b[r]])
            S.op("dve", lambda e, s4=s4: e.tensor_scalar(s4[:, 2:3], s4[:, 0:1], 1.0 / D, None, ALU.mult), reads=[st4b[r]], writes=[st4b[r]])
            S.op("dve", lambda e, s4=s4: e.tensor_tensor(s4[:, 3:4], s4[:, 2:3], s4[:, 2:3], ALU.mult), reads=[st4b[r]], writes=[st4b[r]])
            S.op("dve", lambda e, s4=s4: e.scalar_tensor_tensor(s4[:, 4:5], s4[:, 1:2], 1.0 / D, s4[:, 3:4], ALU.mult, ALU.subtract),
                 reads=[st4b[r]], writes=[st4b[r]])
            S.op("act", lambda e, s4=s4: e.activation(s4[:, 5:6], s4[:, 4:5], AF.Sqrt, bias=C.eps_t[:, 0:1], scale=1.0),
                 reads=[st4b[r], C.cb], writes=[st4b[r]])
            S.op("dve", lambda e, s4=s4: e.reciprocal(s4[:, 6:7], s4[:, 5:6]), reads=[st4b[r]], writes=[st4b[r]])
            S.op("dve", lambda e, s4=s4: e.scalar_tensor_tensor(s4[:, 7:8], s4[:, 2:3], -1.0, s4[:, 6:7], ALU.mult, ALU.mult),
                 reads=[st4b[r]], writes=[st4b[r]])
            S.op("dve", lambda e, r=r, s4=s4: e.tensor_scalar(vh[r][:, :], vin[r][:, :], s4[:, 6:7], s4[:, 7:8], ALU.mult, ALU.add),
                 reads=[vinb[r], st4b[r]], writes=[vhb[r]])
            for k4 in range(KC // 4):
                bi = oi % 8
                oi += 1
                for j in range(4):
                    kc = k4 * 4 + j
                    S.op("pe", lambda e, bi=bi, j=j, kc=kc, r=r: e.matmul(C.ps[bi][:, j * 128:(j + 1) * 128], vh[r][:, kc * 128:(kc + 1) * 128],
                                                                         wsbf[:, kc // 2, :], start=True, stop=True),
                         reads=[vhb[r], wsbfb], writes=[C.psb[bi]], signal=(j == 3))
                for j in range(4):
                    kc = k4 * 4 + j
                    s3 = (k4 * 4 + j) % 3
                    S.op("dve", lambda e, bi=bi, j=j, kc=kc, s3=s3: e.scalar_tensor_tensor(
                        sv[s3][:, :], C.ps[bi][:, j * 128:(j + 1) * 128], lg[:, kc:kc + 1], T2[:, kc, :], ALU.mult, ALU.add),
                        reads=[C.psb[bi], lgb, T2b], writes=[svb[s3]])
                    S.op("pool", lambda e, kc=kc, s3=s3, tb=tb: e.tensor_tensor(
                        PT[:, kc, tb * 128:(tb + 1) * 128], PT[:, kc, tb * 128:(tb + 1) * 128], sv[s3][:, :], ALU.mult),
                        reads=[svb[s3], PTb], writes=[PTb])
        gemm_to_dram(C, Wout, D, PT, PTb, KC, TP, YT, YTb, 0, None, F32, tag + "o")
    postnorm_resid(C, YT, YTb, g_post, XT, XTb, tag + "p")


def xin_stage(C, x_own, XT, XTb, ident, identb):
    S, nc = C.S, C.nc
    with contextlib.ExitStack() as st:
        xr = [st.enter_context(nc.sbuf_tensor("xi_r%d" % i, [128, D], F32)) for i in range(2)]
        xrb = [Buf() for _ in range(2)]
        xo = [st.enter_context(nc.sbuf_tensor("xi_o%d" % i, [128, 4, 128], F32)) for i in range(3)]
        xob = [Buf() for _ in range(3)]
        inb = Buf()
        oi = 0
        for tb in range(TP // 128):
            r = tb % 2
            S.dma("sp", xr[r][:, 0:D // 2], x_own[tb * 128:(tb + 1) * 128, 0:D // 2], reads=[inb], writes=[xrb[r]])
            S.dma("act", xr[r][:, D // 2:D], x_own[tb * 128:(tb + 1) * 128, D // 2:D], reads=[inb], writes=[xrb[r]])
            for k4 in range(KC // 4):
                bi = oi % 8
                o = oi % 3
                oi += 1
                for j in range(4):
                    kc = k4 * 4 + j
                    S.op("pe", lambda e, bi=bi, j=j, kc=kc, r=r: e.transpose(C.ps[bi][:, j * 128:(j + 1) * 128],
                                                                            xr[r][:, kc * 128:(kc + 1) * 128], ident[:, :]),
                         reads=[xrb[r], identb], writes=[C.psb[bi]], signal=(j == 3))
                S.op("dve", lambda e, bi=bi, o=o: e.tensor_copy(xo[o][:, :, :], C.ps[bi][:, :]), reads=[C.psb[bi]], writes=[xob[o]])
                dst = XT[k4 * 512:(k4 + 1) * 512, tb * 128:(tb + 1) * 128].rearrange("(j p) t -> p j t", p=128)
                S.dma(C.q(), dst, xo[o][:, :, :], reads=[xob[o]], writes=[XTb[k4 * 4 + j] for j in range(4)])
        S.barrier()


def xout_stage(C, XT, XTb, out, outb, ident, identb):
    S, nc = C.S, C.nc
    with contextlib.ExitStack() as st:
        xr = [st.enter_context(nc.sbuf_tensor("xo_r%d" % i, [128, TP], F32)) for i in range(2)]
        xrb = [Buf() for _ in range(2)]
        xo = [st.enter_context(nc.sbuf_tensor("xo_o%d" % i, [128, 4, 128], F32)) for i in range(3)]
        xob = [Buf() for _ in range(3)]
        oi = 0
        for kc in range(KC):
            r = kc % 2
            S.dma(C.q(), xr[r][:, :], XT[kc * 128:(kc + 1) * 128, :], reads=[XTb[kc]], writes=[xrb[r]])
            for t4 in range(TP // 512):
                bi = oi % 8
                o = oi % 3
                oi += 1
                for j in range(4):
                    tb = t4 * 4 + j
                    S.op("pe", lambda e, bi=bi, j=j, tb=tb, r=r: e.transpose(C.ps[bi][:, j * 128:(j + 1) * 128],
                                                                            xr[r][:, tb * 128:(tb + 1) * 128], ident[:, :]),
                         reads=[xrb[r], identb], writes=[C.psb[bi]], signal=(j == 3))
                S.op("dve", lambda e, bi=bi, o=o: e.tensor_copy(xo[o][:, :, :], C.ps[bi][:, :]), reads=[C.psb[bi]], writes=[xob[o]])
                dst = out[t4 * 512:(t4 + 1) * 512, kc * 128:(kc + 1) * 128].rearrange("(j p) f -> p j f", p=128)
                S.dma(C.q(), dst, xo[o][:, :, :], reads=[xob[o]], writes=[outb])
        S.barrier()


def dram_in(nc, name, shape, dt=F32):
    return nc.dram_tensor(name, list(shape), dt, kind="ExternalInput").ap()


def dram_scratch(nc, name, shape, dt=F32):
    return nc.dram_tensor(name, list(shape), dt, kind="Internal").ap()


def phase2_decl(nc):
    A = {}
    A["x_own"] = dram_in(nc, "x_own", [TP, D])
    A["ident_d"] = dram_in(nc, "ident", [128, 128])
    A["nmpost"] = dram_in(nc, "norm_mix_post", [2, D])
    A["nmpre"] = dram_in(nc, "norm_mix_pre", [2, D])
    A["nfpre"] = dram_in(nc, "norm_ffn_pre", [2, D])
    A["nfpost"] = dram_in(nc, "norm_ffn_post", [2, D])
    A["Wabo"] = dram_in(nc, "ab_w_out", [D, D])
    A["Wg"] = dram_in(nc, "ffn_w_gate", [2, D, DFF])
    A["Wu"] = dram_in(nc, "ffn_w_up", [2, D, DFF])
    A["Wd"] = dram_in(nc, "ffn_w_down", [2, DFF, D])
    A["Wsi"] = dram_in(nc, "sgu_w_in", [D, 2 * D])
    A["Wso"] = dram_in(nc, "sgu_w_out", [D, D])
    A["lng"] = dram_in(nc, "sgu_ln_g", [D])
    A["lnb"] = dram_in(nc, "sgu_ln_b", [D])
    A["wsT"] = dram_in(nc, "sgu_wsT", [128, 16, 128])
    A["bsb"] = dram_in(nc, "sgu_bs_b", [128, 16, 128])
    A["maskT"] = dram_in(nc, "sgu_maskT", [128, 128])
    A["out"] = nc.dram_tensor("out", [TP, D], F32, kind="ExternalOutput").ap()
    A["XT"] = dram_scratch(nc, "XT", [D, TP])
    A["YT"] = dram_scratch(nc, "YT", [D, TP])
    A["HID"] = dram_scratch(nc, "HID", [DFF, TP], BF16)
    A["UT"] = dram_scratch(nc, "UT", [D, TP], BF16)
    A["VTM"] = dram_scratch(nc, "VTM", [TP, D])
    return A


def build_phase2(stages=("in", "ab", "ffn0", "sgu", "ffn1", "out")):
    nc = bass.Bass("TRN2", target_bir_lowering=False)
    A = phase2_decl(nc)
    A["YC"] = dram_in(nc, "yc", [D, TP], BF16)
    with nc.cleanup_on_exit():
        S = Sched(nc)
        phase2_body(nc, S, A, stages, None)
        S.finish()
    return nc


def about_stage_sel(C, XT, XTb, YT, YTb, G, Gb, sel_d, g_post, Wo, tag):
    S, nc = C.S, C.nc
    with contextlib.ExitStack() as st:
        R = st.enter_context(nc.sbuf_tensor(tag + "R", [128, KC, TP], BF16))
        Rb = Buf()
        sel = st.enter_context(nc.sbuf_tensor(tag + "sel", [128, 4], F32))
        selb = Buf()
        S.dma("sp", sel[:, :], sel_d, writes=[selb])
        c4 = [st.enter_context(nc.sbuf_tensor(tag + "c4%d" % i, [128, 4, TT], BF16)) for i in range(3)]
        c4b = [Buf() for _ in range(3)]
        Gv = G.rearrange("(j i) r t -> i r j t", i=2)
        n = 0
        for kc in range(KC):
            if kc < 16:
                r, lc = kc // 4, kc % 4
            else:
                r, lc = (kc - 16) // 4, 4 + (kc - 16) % 4
            row = r * 1024 + lc * 128
            for hf in range(2):
                i = n % 3
                n += 1
                ts2 = slice(hf * TT, (hf + 1) * TT)
                S.dma(C.q(), c4[i][:, :, :], Gv[hf, row:row + 128, :, :], reads=list(Gb), writes=[c4b[i]])
                S.op("dve", lambda e, i=i, kc=kc, ts2=ts2: e.tensor_scalar(R[:, kc, ts2], c4[i][:, 0, :], sel[:, 0:1], None, ALU.mult),
                     reads=[c4b[i], selb], writes=[Rb])
                for j in range(1, 4):
                    S.op("dve", lambda e, i=i, kc=kc, j=j, ts2=ts2: e.scalar_tensor_tensor(R[:, kc, ts2], c4[i][:, j, :], sel[:, j:j + 1], R[:, kc, ts2],
                                                                                      ALU.mult, ALU.add), reads=[c4b[i], selb, Rb], writes=[Rb])
        gemm_to_dram(C, Wo, D, R, Rb, KC, TP, YT, YTb, 0, None, F32, tag + "g")
    postnorm_resid(C, YT, YTb, g_post, XT, XTb, tag + "p")


def phase2_body(nc, S, A, stages, gathered):
    x_own, ident_d, nmpost, nmpre, nfpre, nfpost, Wabo, Wg, Wu, Wd, Wsi, Wso, lng, lnb, wsT, bsb, maskT, out, XT, YT, HID, UT, VTM = [A[k] for k in (
        "x_own", "ident_d", "nmpost", "nmpre", "nfpre", "nfpost", "Wabo", "Wg", "Wu", "Wd", "Wsi", "Wso", "lng", "lnb", "wsT", "bsb", "maskT",
        "out", "XT", "YT", "HID", "UT", "VTM")]
    XTb = [Buf() for _ in range(KC)]
    YTb = [Buf() for _ in range(KC)]
    HIDb = [Buf() for _ in range(DFF // 128)]
    UTb = [Buf() for _ in range(KC)]
    VTMb = [Buf() for _ in range(TP // 128)]
    YCb = [Buf() for _ in range(KC)]
    outb = Buf()
    if True:
        with contextlib.ExitStack() as st:
            C = Ctx(nc, S, st)
            ident = C.sb("ident_sb", [128, 128], F32)
            identb = Buf()
            S.dma("sp", ident[:, :], ident_d, writes=[identb])
            C.eps_t = C.sb("eps_t2", [128, 1], F32)
            S.op("dve", lambda e: e.memset(C.eps_t[:, :], EPS), writes=[C.cb])
            if "in" in stages:
                xin_stage(C, x_own, XT, XTb, ident, identb)
            if "ab" in stages:
                if gathered is None:
                    about_stage(C, XT, XTb, YT, YTb, A["YC"], YCb, nmpost[0], Wabo, "ab")
                else:
                    G, Gb, sel_d = gathered
                    about_stage_sel(C, XT, XTb, YT, YTb, G, Gb, sel_d, nmpost[0], Wabo, "ab")
            if "ffn0" in stages:
                ffn_stage(C, XT, XTb, YT, YTb, HID, HIDb, nfpre[0], nfpost[0], Wg[0], Wu[0], Wd[0], "f0")
            if "sgu" in stages:
                sgu_stage(C, XT, XTb, YT, YTb, UT, UTb, VTM, VTMb, nmpre[1], nmpost[1], Wsi, lng, lnb, wsT, bsb, maskT, Wso, "sg")
            if "ffn1" in stages:
                ffn_stage(C, XT, XTb, YT, YTb, HID, HIDb, nfpre[1], nfpost[1], Wg[1], Wu[1], Wd[1], "f1")
            if "out" in stages:
                xout_stage(C, XT, XTb, out, outb, ident, identb)
            S.barrier()


def build_fused():
    nc = bass.Bass("TRN2", target_bir_lowering=False)
    A1 = phase1_decl(nc)
    A2 = phase2_decl(nc)
    sel_d = dram_in(nc, "sel", [128, 4])
    NT = SEQ // TT
    YL = [nc.dram_tensor("YL%d" % t, [1024, TT], BF16) for t in range(NT)]
    GG = nc.dram_tensor("YG", [NT, 4 * 1024, TT], BF16)
    A1["YCT"] = lambda t: YL[t].ap()
    with nc.cleanup_on_exit():
        S = Sched(nc)
        ylb = [Buf() for _ in range(NT)]
        Gb = [Buf() for _ in range(NT)]
        A1["after_tile"] = lambda t: S.collective("AllGather", YL[t].ap().opt(), GG.ap()[t].opt(), [[0, 1, 2, 3], [4, 5, 6, 7]],
                                                  reads=[ylb[t]], writes=[Gb[t]])
        phase1_body(nc, S, A1, ylb)
        phase2_body(nc, S, A2, ("in", "ab", "ffn0", "sgu", "ffn1", "out"), (GG.ap(), Gb, sel_d))
        S.finish()
    return nc


def phase2_consts(inputs):
    ws = np.asarray(inputs["sgu_w_s"][0], np.float32)
    wsT = np.ascontiguousarray(ws.transpose(2, 0, 1))
    pos = np.arange(128)
    maskT = ((pos[:, None] // 64) <= (pos[None, :] // 64)).astype(np.float32)
    bs = np.asarray(inputs["sgu_b_s"][0], np.float32)
    bsb = np.ascontiguousarray(np.broadcast_to(bs[None], (128, 16, 128)))
    return {
        "ident": np.eye(128, dtype=np.float32),
        "norm_mix_post": np.asarray(inputs["norm_mix_post"], np.float32),
        "norm_mix_pre": np.asarray(inputs["norm_mix_pre"], np.float32),
        "norm_ffn_pre": np.asarray(inputs["norm_ffn_pre"], np.float32),
        "norm_ffn_post": np.asarray(inputs["norm_ffn_post"], np.float32),
        "ab_w_out": np.asarray(inputs["ab_w_out"][0], np.float32),
        "ffn_w_gate": np.asarray(inputs["ffn_w_gate"], np.float32),
        "ffn_w_up": np.asarray(inputs["ffn_w_up"], np.float32),
        "ffn_w_down": np.asarray(inputs["ffn_w_down"], np.float32),
        "sgu_w_in": np.asarray(inputs["sgu_w_in"][0], np.float32),
        "sgu_w_out": np.asarray(inputs["sgu_w_out"][0], np.float32),
        "sgu_ln_g": np.asarray(inputs["sgu_ln_g"][0], np.float32),
        "sgu_ln_b": np.asarray(inputs["sgu_ln_b"][0], np.float32),
        "sgu_wsT": wsT, "sgu_bs_b": bsb, "sgu_maskT": maskT,
    }


NH = 4
TT = 512
NCOL1 = 2568


def phase1_decl(nc):
    A = {}
    A["xb"] = dram_in(nc, "xb", [SEQ, D])
    A["gpre"] = dram_in(nc, "g_pre", [D])
    A["Wc"] = dram_in(nc, "w_in_c", [D, NCOL1])
    A["cw_d"] = dram_in(nc, "conv_w", [128, 12, 4])
    A["nega_d"] = dram_in(nc, "neg_a", [128, 4])
    A["dtb_d"] = dram_in(nc, "dt_b", [128, 4])
    A["gnw_d"] = dram_in(nc, "gn_w", [128, 128])
    A["pw_d"] = dram_in(nc, "pool_wg", [512, 512])
    A["psc_d"] = dram_in(nc, "pool_sc", [512])
    A["band_d"] = dram_in(nc, "bandm", [3, 128, 128])
    A["mask_d"] = dram_in(nc, "masks", [5, 128, 128])
    return A


def build_phase1(ntiles=SEQ // TT, stop_after=None, dbgk=99):
    nc = bass.Bass("TRN2", target_bir_lowering=False)
    A = phase1_decl(nc)
    yct = nc.dram_tensor("yct", [1024, SEQ], BF16, kind="ExternalOutput").ap()
    A["YCT"] = lambda t: yct[:, t * TT:(t + 1) * TT]
    with nc.cleanup_on_exit():
        S = Sched(nc)
        phase1_body(nc, S, A, [Buf() for _ in range(SEQ // TT)], ntiles, stop_after, dbgk)
        S.finish()
    return nc


def phase1_body(nc, S, A, outb, ntiles=SEQ // TT, stop_after=None, dbgk=99):
    xb, gpre, Wc, cw_d, nega_d, dtb_d, gnw_d, pw_d, psc_d, band_d, mask_d, YCT = [A[k] for k in (
        "xb", "gpre", "Wc", "cw_d", "nega_d", "dtb_d", "gnw_d", "pw_d", "psc_d", "band_d", "mask_d", "YCT")]
    if True:
        with contextlib.ExitStack() as st:
            C = Ctx(nc, S, st, NW=4, pfx="p1_")
            sb = C.sb
            C.eps_t = sb("eps_t", [128, 1], F32)
            S.op("dve", lambda e: e.memset(C.eps_t[:, :], EPS), writes=[C.cb])
            C.one_t = sb("one_t", [128, 1], F32)
            S.op("dve", lambda e: e.memset(C.one_t[:, :], 1.0), writes=[C.cb])
            mk = sb("mk", [128, 5, 128], F32)
            mkb = Buf()
            S.dma("sp", mk[:, :, :], mask_d.rearrange("m p f -> p m f"), writes=[mkb])
            triA, strictU, MS, MU, ident = [mk[:, i, :] for i in range(5)]
            band = sb("band", [128, 3, 128], F32)
            bandb = Buf()
            S.dma("act", band[:, :, :], band_d.rearrange("m p f -> p m f"), writes=[bandb])
            g = sb("g", [128, KC], F32)
            gb = Buf()
            S.dma("sp", g[:, :], gpre.rearrange("(kc p) -> p kc", p=128), writes=[gb], allow_slow_non_contiguous=True)
            cw = sb("cw", [128, 12, 4], F32)
            cwb = Buf()
            S.dma("sp", cw[:, :, :], cw_d, writes=[cwb])
            nega = sb("nega", [128, 4], F32)
            dtb = sb("dtb", [128, 4], F32)
            gnw = sb("gnw", [128, 128], F32)
            psc = sb("psc", [128, 4], F32)
            smb = SBuf()
            S.dma("sp", nega[:, :], nega_d, writes=[smb])
            S.dma("sp", dtb[:, :], dtb_d, writes=[smb])
            S.dma("sp", gnw[:, :], gnw_d, writes=[smb])
            S.dma("sp", psc[:, :], psc_d.rearrange("(c p) -> p c", p=128), writes=[smb], allow_slow_non_contiguous=True)
            S.op("act", lambda e: e.activation(nega[:, :], nega[:, :], AF.Exp), reads=[smb], writes=[smb])
            S.op("dve", lambda e: e.tensor_scalar(nega[:, :], nega[:, :], -1.0, None, ALU.mult), reads=[smb], writes=[smb])
            pw = sb("pw", [128, 4, 512], BF16)
            pwb = Buf()
            S.dma("pool", pw[:, :, :], pw_d.rearrange("(c p) n -> p c n", p=128), writes=[pwb])
            wbd = sb("wbd", [128, KC, 8], BF16)
            wbdb = Buf()
            S.dma("pool", wbd[:, :, :], Wc.rearrange("(kc p) n -> p kc n", p=128)[:, :, 2560:2568], writes=[wbdb],
                  allow_slow_non_contiguous=True)
            xr = sb("xr", [128, D], F32)
            xrb = Buf()
            HT = sb("HT", [128, KC, TT], BF16)
            HTb = Buf()
            st8 = sb("st8", [128, 12], F32)
            st8b = SBuf()
            halo = sb("halo", [128, 12, 4], F32)
            halob = Buf()
            S.op("dve", lambda e: e.memset(halo[:, :, :], 0.0), writes=[halob])
            Z = [sb("Z%d" % i, [128, 3 + TT], F32) for i in range(2)]
            Zb = [Buf() for _ in range(2)]
            acc = [sb("acc%d" % i, [128, TT], F32) for i in range(2)]
            accb = [Buf() for _ in range(2)]
            sl = [sb("sl%d" % i, [128, TT], F32) for i in range(2)]
            slb = [Buf() for _ in range(2)]
            QT = sb("QT", [128, NH, TT], BF16)
            KT = sb("KT", [128, NH, TT], BF16)
            KTf = sb("KTf", [128, NH, TT], F32)
            VT = sb("VT", [128, NH, TT], F32)
            QTb, KTb, KTfb, VTb = [[Buf() for _ in range(NH)] for _ in range(4)]
            sg = sb("sg", [128, 4, 512], F32)
            sgb = [Buf() for _ in range(4)]
            xa = sb("xa", [128, 5, 512], F32)
            xab = [Buf() for _ in range(5)]
            lg8 = sb("lg8", [128, 4, 8], F32)
            lg8b = [SBuf() for _ in range(4)]
            OT = sb("OT", [128, 8, TT], BF16)
            OTb = Buf()
            PLT = sb("PLT", [128, 4, TT], BF16)
            PLTb = Buf()
            Sst = sb("Sst", [128, NH, 128], F32)
            Sbf = sb("Sbf", [128, NH, 128], BF16)
            Sstb = [Buf() for _ in range(NH)]
            Sbfb = [Buf() for _ in range(NH)]
            for h in range(NH):
                S.op("dve", lambda e, h=h: e.memset(Sst[:, h, :], 0.0), writes=[Sstb[h]])
                S.op("dve", lambda e, h=h: e.memset(Sbf[:, h, :], 0.0), writes=[Sbfb[h]])

            def ht(name, n=128, dt=F32, strict=False):
                t = sb(name, [128, NH, n], dt)
                return t, [Buf(name, strict) for _ in range(NH)]
            b4, b4b = ht("b4", 8, strict=True)
            G2, G2b = ht("G2")
            gB, gBb = ht("gB")
            EX, EXb = ht("EX", 392, strict=True)
            Es, Esb = ht("Es")
            ETc, ETcb = ht("ETc")
            ngb, ngbb = ht("ngb", 2, strict=True)
            Lm, Lb = ht("L")
            AT, ATb = ht("AT")
            kd, kdb = ht("kd")
            vb, vbb = ht("vb")
            Xa, Xab_ = ht("Xa")
            Xat, Xatb = ht("Xat")
            Xb, Xbb = ht("Xb")
            Xbt, Xbtb = ht("Xbt")
            Pm, Pmb = ht("Pm")
            A2T, A2Tb = ht("A2T", 128, BF16)
            K2, K2b = ht("K2", 128, BF16)
            qdT, qdTb = ht("qdT", 128, BF16)
            Rbf, Rbfb = ht("Rbf", 128, BF16)
            gsg, gsgb = gB, gBb
            yo, yob = G2, G2b
            ps, psb = C.ps, C.psb
            Wv_ = Wc.rearrange("(kc p) n -> p kc n", p=128)

            for tile in range(ntiles):
                t0 = tile * TT
                for tb in range(4):
                    r0 = t0 + tb * 128
                    S.dma("sp", xr[:, 0:D // 2], xb[r0:r0 + 128, 0:D // 2], writes=[xrb])
                    S.dma("act", xr[:, D // 2:D], xb[r0:r0 + 128, D // 2:D], writes=[xrb])
                    S.op("act", lambda e: e.activation(HT[:, 0:8, :], xr[:, :], AF.Square, accum_out=st8[:, 0:1]),
                         reads=[xrb], writes=[HTb, st8b]) if False else None
                    S.op("dve", lambda e: e.tensor_tensor_reduce(out=sl[0][:, :], in0=xr[:, 0:TT], in1=xr[:, 0:TT], op0=ALU.mult, op1=ALU.add,
                                                               scale=1.0, scalar=0.0, accum_out=st8[:, 0:1]), reads=[xrb], writes=[slb[0], st8b]) if False else None
                    for q8 in range(8):
                        S.op("act", lambda e, q8=q8: e.activation(sl[0][:, :], xr[:, q8 * 512:(q8 + 1) * 512], AF.Square,
                                                                  accum_out=st8[:, q8:q8 + 1]), reads=[xrb], writes=[slb[0], st8b])
                    S.op("dve", lambda e: e.tensor_reduce(st8[:, 8:9], st8[:, 0:8], AX.X, ALU.add), reads=[st8b], writes=[st8b])
                    S.op("act", lambda e: e.activation(st8[:, 9:10], st8[:, 8:9], AF.Sqrt, bias=C.eps_t[:, 0:1], scale=1.0 / D),
                         reads=[st8b, C.cb], writes=[st8b])
                    S.op("dve", lambda e: e.reciprocal(st8[:, 10:11], st8[:, 9:10]), reads=[st8b], writes=[st8b])
                    S.op("act", lambda e: e.activation(xr[:, :], xr[:, :], AF.Copy, scale=st8[:, 10:11]), reads=[xrb, st8b], writes=[xrb])
                    for k4 in range(KC // 4):
                        bi = k4 % 8
                        for j in range(4):
                            kc = k4 * 4 + j
                            S.op("pe", lambda e, bi=bi, j=j, kc=kc: e.transpose(ps[bi][:, j * 128:(j + 1) * 128],
                                                                               xr[:, kc * 128:(kc + 1) * 128], ident),
                                 reads=[xrb, mkb], writes=[psb[bi]], signal=(j == 3))
                        for j in range(4):
                            kc = k4 * 4 + j
                            S.op("dve" if j % 2 == 0 else "act",
                                 (lambda e, bi=bi, j=j, kc=kc, tb=tb: e.tensor_scalar(HT[:, kc, tb * 128:(tb + 1) * 128], ps[bi][:, j * 128:(j + 1) * 128],
                                                                                     g[:, kc:kc + 1], None, ALU.mult)) if j % 2 == 0 else
                                 (lambda e, bi=bi, j=j, kc=kc, tb=tb: e.activation(HT[:, kc, tb * 128:(tb + 1) * 128], ps[bi][:, j * 128:(j + 1) * 128],
                                                                                  AF.Copy, scale=g[:, kc:kc + 1])),
                                 reads=[psb[bi], gb], writes=[HTb])
                if stop_after == 'A':
                    break
                for cgi in range(2):
                    slots = []
                    for u in range(4):
                        wt, wb = C.wslot()
                        wv = wt[:, :].rearrange("p (k n) -> p k n", n=512)
                        S.dma("pool", wv, Wv_[:, u * 8:(u + 1) * 8, cgi * 512:(cgi + 1) * 512], writes=[wb])
                        slots.append((wv, wb))
                    for tb in range(4):
                        bi = (cgi * 4 + tb) % 8
                        for kc in range(KC):
                            wv, wb = slots[kc // 8]
                            S.op("pe", lambda e, bi=bi, wv=wv, kc=kc, tb=tb: e.matmul(
                                ps[bi][:, :], HT[:, kc, tb * 128:(tb + 1) * 128], wv[:, kc % 8, :], start=(kc == 0), stop=(kc == KC - 1)),
                                reads=[wb, HTb], writes=[psb[bi]], signal=(kc % 8 == 7))
                        if cgi == 0:
                            S.op("act", lambda e, bi=bi, tb=tb: e.activation(xa[:, tb + 1, :], ps[bi][:, :], AF.Copy),
                                 reads=[psb[bi]], writes=[xab[tb + 1]])
                        else:
                            S.op("act", lambda e, bi=bi, tb=tb: e.activation(sg[:, tb, :], ps[bi][:, :], AF.Silu),
                                 reads=[psb[bi]], writes=[sgb[tb]])
                for tb in range(4):
                    bi = tb
                    for kc in range(KC):
                        S.op("pe", lambda e, bi=bi, kc=kc, tb=tb: e.matmul(ps[bi][:, 0:8], HT[:, kc, tb * 128:(tb + 1) * 128], wbd[:, kc, :],
                                                                          start=(kc == 0), stop=(kc == KC - 1)),
                             reads=[wbdb, HTb], writes=[psb[bi]], signal=(kc == KC - 1))
                    S.op("dve", lambda e, bi=bi, tb=tb: e.tensor_copy(lg8[:, tb, :], ps[bi][:, 0:8]), reads=[psb[bi]], writes=[lg8b[tb]])
                if stop_after == 'B1':
                    break
                for grp in range(3):
                    banks = [4, 5, 6, 7] if grp % 2 == 0 else [0, 1, 2, 3]
                    gemm_cg(C, Wc, 1024 + grp * 512, 512, HT, HTb, KC, TT, banks)
                    for h in range(NH):
                        c = grp * 4 + h
                        zi = c % 2
                        bi = banks[h]
                        S.op("dve", lambda e, zi=zi, c=c: e.tensor_copy(Z[zi][:, 0:3], halo[:, c, 0:3]), reads=[halob], writes=[Zb[zi]])
                        S.op("act", lambda e, zi=zi, bi=bi: e.activation(Z[zi][:, 3:3 + TT], ps[bi][:, :], AF.Copy),
                             reads=[psb[bi]], writes=[Zb[zi]])
                        S.op("dve", lambda e, zi=zi, c=c: e.tensor_copy(halo[:, c, 0:3], Z[zi][:, TT:TT + 3]), reads=[Zb[zi]], writes=[halob])
                        S.op("dve", lambda e, zi=zi, c=c: e.tensor_scalar(acc[zi][:, :], Z[zi][:, 3:3 + TT], cw[:, c, 3:4], None, ALU.mult),
                             reads=[Zb[zi], cwb], writes=[accb[zi]])
                        for j in range(3):
                            S.op("dve", lambda e, zi=zi, c=c, j=j: e.scalar_tensor_tensor(acc[zi][:, :], Z[zi][:, j:j + TT], cw[:, c, j:j + 1],
                                                                                         acc[zi][:, :], ALU.mult, ALU.add),
                                 reads=[Zb[zi], cwb, accb[zi]], writes=[accb[zi]])
                        if grp == 2:
                            S.op("act", lambda e, zi=zi, h=h: e.activation(VT[:, h, :], acc[zi][:, :], AF.Silu), reads=[accb[zi]], writes=[VTb[h]])
                            continue
                        S.op("act", lambda e, zi=zi: e.activation(sl[zi][:, :], acc[zi][:, :], AF.Silu), reads=[accb[zi]], writes=[slb[zi]])
                        S.op("pool", lambda e, zi=zi: e.tensor_tensor(acc[zi][:, :], sl[zi][:, :], sl[zi][:, :], ALU.mult),
                             reads=[slb[zi]], writes=[accb[zi]])
                        pb = banks[h]
                        S.op("pe", lambda e, pb=pb, zi=zi: e.matmul(ps[pb][:, :], C.ones_f[:, :], acc[zi][:, :], start=True, stop=True),
                             reads=[accb[zi], C.cb], writes=[psb[pb]])
                        S.op("act", lambda e, pb=pb, zi=zi: e.activation(acc[zi][:, :], ps[pb][:, :], AF.Sqrt, bias=C.eps_t[:, 0:1], scale=1.0),
                             reads=[psb[pb], C.cb], writes=[accb[zi]])
                        S.op("dve", lambda e, zi=zi: e.reciprocal(acc[zi][:, :], acc[zi][:, :]), reads=[accb[zi]], writes=[accb[zi]])
                        if grp == 0:
                            S.op("dve", lambda e, zi=zi, h=h: e.scalar_tensor_tensor(QT[:, h, :], sl[zi][:, :], 128.0 ** -0.5, acc[zi][:, :],
                                                                                    ALU.mult, ALU.mult), reads=[slb[zi], accb[zi]], writes=[QTb[h]])
                        else:
                            S.op("dve", lambda e, zi=zi, h=h: e.tensor_tensor(KTf[:, h, :], sl[zi][:, :], acc[zi][:, :], ALU.mult),
                                 reads=[slb[zi], accb[zi]], writes=[KTfb[h]])
                            S.op("pool", lambda e, h=h: e.tensor_copy(KT[:, h, :], KTf[:, h, :]), reads=[KTfb[h]], writes=[KTb[h]])
                if stop_after == 'B2':
                    break
                for tb in range(4):
                    first = (tile == 0 and tb == 0)
                    for c in range(4):
                        bi = 4 + c
                        if first:
                            S.op("pe", lambda e, bi=bi, c=c, tb=tb: e.matmul(ps[bi][:, 0:128], xa[:, tb + 1, c * 128:(c + 1) * 128], band[:, 0, :],
                                                                            start=True, stop=True), reads=[xab[tb + 1], bandb], writes=[psb[bi]])
                        else:
                            S.op("pe", lambda e, bi=bi, c=c, tb=tb: e.matmul(ps[bi][:, 0:128], xa[:, tb, c * 128:(c + 1) * 128], band[:, 2, :],
                                                                            start=True, stop=False), reads=[xab[tb], bandb], writes=[psb[bi]], signal=False)
                            S.op("pe", lambda e, bi=bi, c=c, tb=tb: e.matmul(ps[bi][:, 0:128], xa[:, tb + 1, c * 128:(c + 1) * 128], band[:, 1, :],
                                                                            start=False, stop=True), reads=[xab[tb + 1], bandb], writes=[psb[bi]])
                        S.op("act", lambda e, bi=bi, c=c, tb=tb: e.activation(PLT[:, c, tb * 128:(tb + 1) * 128], ps[bi][:, 0:128], AF.Copy),
                             reads=[psb[bi]], writes=[PLTb])
                S.op("pool", lambda e: e.tensor_copy(xa[:, 0, :], xa[:, 4, :]), reads=[xab[4]], writes=[xab[0]])
                for dc in range(4):
                    bi = dc
                    for c in range(4):
                        S.op("pe", lambda e, bi=bi, c=c, dc=dc: e.matmul(ps[bi][:, :], pw[:, c, dc * 128:(dc + 1) * 128], PLT[:, c, :],
                                                                        start=(c == 0), stop=(c == 3)), reads=[pwb, PLTb], writes=[psb[bi]], signal=(c == 3))
                    S.op("act", lambda e, bi=bi, dc=dc: e.activation(OT[:, dc, :], ps[bi][:, :], AF.Copy, scale=psc[:, dc:dc + 1]),
                         reads=[psb[bi], smb], writes=[OTb])
                if stop_after == 'E':
                    break
                H = range(NH)
                for tb in range(4):
                    ts_ = slice(tb * 128, (tb + 1) * 128)
                    S.op("act", lambda e, tb=tb: e.activation(b4[:, 0, 0:4], lg8[:, tb, 0:4], AF.Exp, scale=-1.0), reads=[lg8b[tb]], writes=[b4b[0]])
                    S.op("dve", lambda e: e.tensor_scalar(b4[:, 0, 0:4], b4[:, 0, 0:4], 1.0, None, ALU.add), reads=[b4b[0]], writes=[b4b[0]])
                    S.op("dve", lambda e: e.reciprocal(b4[:, 0, 0:4], b4[:, 0, 0:4]), reads=[b4b[0]], writes=[b4b[0]])
                    S.op("dve", lambda e, tb=tb: e.tensor_tensor(b4[:, 1, 0:4], lg8[:, tb, 4:8], dtb[:, :], ALU.add), reads=[lg8b[tb], smb], writes=[b4b[0]])
                    S.op("act", lambda e: e.activation(b4[:, 1, 0:4], b4[:, 1, 0:4], AF.Exp), reads=[b4b[0]], writes=[b4b[0]])
                    S.op("act", lambda e: e.activation(b4[:, 1, 0:4], b4[:, 1, 0:4], AF.Ln, bias=C.one_t[:, 0:1], scale=1.0), reads=[b4b[0], C.cb], writes=[b4b[0]])
                    S.op("dve", lambda e: e.tensor_tensor(b4[:, 0, 4:8], b4[:, 1, 0:4], nega[:, :], ALU.mult), reads=[b4b[0], smb], writes=[b4b[0]])
                    beta = lambda h: b4[:, 0, h:h + 1]
                    gcol = lambda h: b4[:, 0, 4 + h:5 + h]
                    gcol2 = lambda h: b4[:, 0, 4 + h:6 + h] if h < 3 else b4[:, 0, 6:8]
                    bb = b4b[0]
                    for h in H:
                        S.op("dve", lambda e, h=h: e.tensor_scalar(G2[:, h, :], strictU, gcol(h), None, ALU.mult), reads=[bb, mkb], writes=[G2b[h]])
                        S.op("pool", lambda e, h=h: e.tensor_scalar(gB[:, h, :], C.ones_f[:, :], gcol(h), None, ALU.mult), reads=[bb, C.cb], writes=[gBb[h]])
                    for h in H:
                        bi = h
                        S.op("pe", lambda e, h=h, bi=bi: e.matmul(ps[bi][:, 0:128], triA, G2[:, h, :], start=True, stop=True),
                             reads=[G2b[h], mkb], writes=[psb[bi]], signal=False)
                        S.op("pe", lambda e, h=h, bi=bi: e.matmul(ps[bi][:, 128:256], G2[:, h, :], triA, start=True, stop=True),
                             reads=[G2b[h], mkb], writes=[psb[bi]], signal=False)
                        S.op("pe", lambda e, h=h, bi=bi: e.matmul(ps[bi][:, 256:384], gB[:, h, :], triA, start=True, stop=True),
                             reads=[gBb[h], mkb], writes=[psb[bi]], signal=False)
                        gsrc = (lambda h: b4[:, 0, 4 + h:6 + h]) if True else None
                        hh = min(h, 2)
                        off = h - hh
                        S.op("pe", lambda e, hh=hh, bi=bi: e.matmul(ps[bi][:, 384:386], triA, b4[:, 0, 4 + hh:6 + hh], start=True, stop=True),
                             reads=[bb, mkb], writes=[psb[bi]], signal=False)
                        S.op("pe", lambda e, hh=hh, bi=bi: e.matmul(ps[bi][:, 386:388], strictU, b4[:, 0, 4 + hh:6 + hh], start=True, stop=True),
                             reads=[bb, mkb], writes=[psb[bi]], signal=False)
                        S.op("pe", lambda e, hh=hh, bi=bi: e.matmul(ps[bi][:, 388:390], C.ones_f[:, :], b4[:, 0, 4 + hh:6 + hh], start=True, stop=True),
                             reads=[bb, C.cb], writes=[psb[bi]])
                        S.op("act", lambda e, h=h, bi=bi: e.activation(EX[:, h, 0:390], ps[bi][:, 0:390], AF.Exp), reads=[psb[bi]], writes=[EXb[h]])
                    if stop_after == 'D1':
                        break
                    gam = lambda h: EX[:, h, 384 + (h - min(h, 2)):385 + (h - min(h, 2))]
                    kds = lambda h: EX[:, h, 386 + (h - min(h, 2)):387 + (h - min(h, 2))]
                    gl_ = lambda h: EX[:, h, 388 + (h - min(h, 2)):389 + (h - min(h, 2))]
                    for h in H:
                        S.op("dve", lambda e, h=h: e.tensor_tensor(Es[:, h, :], EX[:, h, 0:128], MS, ALU.mult), reads=[EXb[h], mkb], writes=[Esb[h]])
                        S.op("pool", lambda e, h=h: e.tensor_tensor(ETc[:, h, :], EX[:, h, 128:256], MU, ALU.mult), reads=[EXb[h], mkb], writes=[ETcb[h]])
                        S.op("dve", lambda e, h=h: e.scalar_tensor_tensor(ngb[:, h, 0:1], gam(h), -1.0, beta(h), ALU.mult, ALU.mult),
                             reads=[EXb[h], bb], writes=[ngbb[h]])
                    if stop_after == 'D1a':
                        break
                    for h in H:
                        bi = 4 + h
                        S.op("pe", lambda e, ts_=ts_, h=h, bi=bi: e.matmul(ps[bi][:, 0:128], KT[:, h, ts_], KT[:, h, ts_], start=True, stop=True),
                             reads=[KTb[h]], writes=[psb[bi]], signal=False)
                        S.op("pe", lambda e, ts_=ts_, h=h, bi=bi: e.matmul(ps[bi][:, 128:256], KT[:, h, ts_], QT[:, h, ts_], start=True, stop=True),
                             reads=[KTb[h], QTb[h]], writes=[psb[bi]], signal=False)
                        S.op("pe", lambda e, ts_=ts_, h=h, bi=bi: e.transpose(ps[bi][:, 256:384], KTf[:, h, ts_], ident), reads=[KTfb[h], mkb], writes=[psb[bi]], signal=False)
                        S.op("pe", lambda e, ts_=ts_, h=h, bi=bi: e.transpose(ps[bi][:, 384:512], VT[:, h, ts_], ident), reads=[VTb[h], mkb], writes=[psb[bi]])
                    if stop_after == 'D1b':
                        break
                    for h in H:
                        bi = 4 + h
                        if dbgk > 0:
                            S.op("dve", lambda e, h=h, bi=bi: e.scalar_tensor_tensor(Lm[:, h, :], ps[bi][:, 0:128], beta(h), Es[:, h, :], ALU.mult, ALU.mult),
                                 reads=[psb[bi], bb, Esb[h]], writes=[Lb[h]])
                        if dbgk > 1:
                            S.op("dve", lambda e, h=h, bi=bi: e.tensor_tensor(AT[:, h, :], ps[bi][:, 128:256], ETc[:, h, :], ALU.mult),
                                 reads=[psb[bi], ETcb[h]], writes=[ATb[h]])
                        if dbgk > 2:
                            S.op("dve", lambda e, h=h, bi=bi: e.tensor_scalar(kd[:, h, :], ps[bi][:, 256:384], kds(h), None, ALU.mult),
                                 reads=[psb[bi], EXb[h]], writes=[kdb[h]])
                        if dbgk > 3:
                            S.op("dve", lambda e, h=h, bi=bi: e.tensor_scalar(vb[:, h, :], ps[bi][:, 384:512], beta(h), None, ALU.mult),
                                 reads=[psb[bi], bb], writes=[vbb[h]])
                        if dbgk > 4:
                            S.op("dve", lambda e, ts_=ts_, h=h: e.tensor_tensor(qdT[:, h, :], QT[:, h, ts_], EX[:, h, 256:384], ALU.mult),
                                 reads=[QTb[h], EXb[h]], writes=[qdTb[h]])
                        if dbgk > 5:
                            S.op("dve", lambda e, h=h, tb=tb: e.tensor_tensor(gsg[:, h, :], sg[:, tb, h * 128:(h + 1) * 128], gnw[:, :], ALU.mult),
                                 reads=[sgb[tb], smb], writes=[gsgb[h]])
                    if stop_after == 'D2':
                        break
                    for h in H:
                        bi = h
                        S.op("pe", lambda e, h=h, bi=bi: e.transpose(ps[bi][:, 0:128], Lm[:, h, :], ident), reads=[Lb[h], mkb], writes=[psb[bi]])
                        S.op("dve", lambda e, h=h, bi=bi: e.tensor_copy(Xat[:, h, :], ps[bi][:, 0:128]), reads=[psb[bi]], writes=[Xatb[h]])
                        S.op("dve", lambda e, h=h: e.scalar_tensor_tensor(Pm[:, h, :], Lm[:, h, :], -1.0, ident, ALU.mult, ALU.add),
                             reads=[Lb[h], mkb], writes=[Pmb[h]])
                    cur = (Lm, Lb, Xat, Xatb)
                    nxt = [(Xb, Xbb, Xbt, Xbtb), (Xa, Xab_, Xat, Xatb)]
                    for lvl in range(6):
                        X, Xbuf, XT_, XTbuf = cur
                        N_, Nb, NT, NTb = nxt[lvl % 2]
                        last = (lvl == 5)
                        for h in H:
                            bi = (h if lvl % 2 else 4 + h)
                            if not last:
                                S.op("pe", lambda e, h=h, bi=bi, X=X, XT_=XT_: e.matmul(ps[bi][:, 0:128], XT_[:, h, :], X[:, h, :], start=True, stop=True),
                                     reads=[Xbuf[h], XTbuf[h]], writes=[psb[bi]], signal=False)
                            S.op("pe", lambda e, h=h, bi=bi, X=X, XT_=XT_: e.matmul(ps[bi][:, 128:256], X[:, h, :], XT_[:, h, :], start=True, stop=True),
                                 reads=[Xbuf[h], XTbuf[h]], writes=[psb[bi]])
                        for h in H:
                            bi = (h if lvl % 2 else 4 + h)
                            if not last:
                                S.op("dve", lambda e, h=h, bi=bi, N_=N_: e.tensor_copy(N_[:, h, :], ps[bi][:, 0:128]), reads=[psb[bi]], writes=[Nb[h]])
                            S.op("dve", lambda e, h=h, bi=bi, NT=NT: e.tensor_copy(NT[:, h, :], ps[bi][:, 128:256]), reads=[psb[bi]], writes=[NTb[h]])
                        for h in H:
                            bi = (h if lvl % 2 else 4 + h)
                            S.op("pe", lambda e, h=h, bi=bi, NT=NT: e.matmul(ps[bi][:, 256:384], NT[:, h, :], Pm[:, h, :], start=True, stop=True),
                                 reads=[NTb[h], Pmb[h]], writes=[psb[bi]])
                        for h in H:
                            bi = (h if lvl % 2 else 4 + h)
                            S.op("dve", lambda e, h=h, bi=bi: e.tensor_tensor(Pm[:, h, :], Pm[:, h, :], ps[bi][:, 256:384], ALU.add),
                                 reads=[psb[bi], Pmb[h]], writes=[Pmb[h]])
                        cur = (N_, Nb, NT, NTb)
                    if stop_after == 'D3':
                        break
                    for h in H:
                        bi = h
                        S.op("pe", lambda e, h=h, bi=bi: e.matmul(ps[bi][:, 0:128], Pm[:, h, :], AT[:, h, :], start=True, stop=True),
                             reads=[Pmb[h], ATb[h]], writes=[psb[bi]], signal=False)
                        S.op("pe", lambda e, h=h, bi=bi: e.matmul(ps[bi][:, 128:256], Pm[:, h, :], kd[:, h, :], start=True, stop=True),
                             reads=[Pmb[h], kdb[h]], writes=[psb[bi]])
                    for h in H:
                        bi = h
                        S.op("dve", lambda e, h=h, bi=bi: e.tensor_copy(A2T[:, h, :], ps[bi][:, 0:128]), reads=[psb[bi]], writes=[A2Tb[h]])
                        S.op("dve", lambda e, h=h, bi=bi: e.tensor_copy(K2[:, h, :], ps[bi][:, 128:256]), reads=[psb[bi]], writes=[K2b[h]])
                    if stop_after == 'D4':
                        break
                    for h in H:
                        bi = 4 + h
                        S.op("pe", lambda e, ts_=ts_, h=h, bi=bi: e.matmul(ps[bi][:, 0:128], KT[:, h, ts_], Sbf[:, h, :], start=True, stop=True),
                             reads=[KTb[h], Sbfb[h]], writes=[psb[bi]])
                    for h in H:
                        bi = 4 + h
                        S.op("dve", lambda e, h=h, bi=bi: e.scalar_tensor_tensor(Rbf[:, h, :], ps[bi][:, 0:128], ngb[:, h, 0:1], vb[:, h, :], ALU.mult, ALU.add),
                             reads=[psb[bi], ngbb[h], vbb[h]], writes=[Rbfb[h]])
                    for h in H:
                        bi = 4 + h
                        S.op("pe", lambda e, h=h, bi=bi: e.matmul(ps[bi][:, 128:256], qdT[:, h, :], Sbf[:, h, :], start=True, stop=False),
                             reads=[qdTb[h], Sbfb[h]], writes=[psb[bi]], signal=False)
                        S.op("pe", lambda e, h=h, bi=bi: e.matmul(ps[bi][:, 128:256], A2T[:, h, :], Rbf[:, h, :], start=False, stop=True),
                             reads=[A2Tb[h], Rbfb[h]], writes=[psb[bi]], signal=False)
                        S.op("pe", lambda e, h=h, bi=bi: e.matmul(ps[bi][:, 256:384], K2[:, h, :], Rbf[:, h, :], start=True, stop=True),
                             reads=[K2b[h], Rbfb[h]], writes=[psb[bi]])
                    for h in H:
                        bi = 4 + h
                        S.op("dve", lambda e, h=h, bi=bi: e.scalar_tensor_tensor(Sst[:, h, :], Sst[:, h, :], gl_(h), ps[bi][:, 256:384], ALU.mult, ALU.add),
                             reads=[psb[bi], EXb[h], Sstb[h]], writes=[Sstb[h]])
                        S.op("pool", lambda e, h=h: e.tensor_copy(Sbf[:, h, :], Sst[:, h, :]), reads=[Sstb[h]], writes=[Sbfb[h]])
                        S.op("dve", lambda e, h=h, bi=bi: e.tensor_copy(Xa[:, h, :], ps[bi][:, 128:256]), reads=[psb[bi]], writes=[Xab_[h]])
                        S.op("act", lambda e, h=h: e.activation(yo[:, h, :], Xa[:, h, :], AF.Square, accum_out=ngb[:, h, 1:2]),
                             reads=[Xab_[h]], writes=[yob[h], ngbb[h]])
                        S.op("act", lambda e, h=h: e.activation(ngb[:, h, 1:2], ngb[:, h, 1:2], AF.Sqrt, bias=C.eps_t[:, 0:1], scale=1.0 / 128),
                             reads=[ngbb[h], C.cb], writes=[ngbb[h]])
                        S.op("dve", lambda e, h=h: e.reciprocal(ngb[:, h, 1:2], ngb[:, h, 1:2]), reads=[ngbb[h]], writes=[ngbb[h]])
                        S.op("dve", lambda e, h=h, bi=bi: e.scalar_tensor_tensor(yo[:, h, :], Xa[:, h, :], ngb[:, h, 1:2], gsg[:, h, :], ALU.mult, ALU.mult),
                             reads=[Xab_[h], ngbb[h], gsgb[h]], writes=[yob[h]])
                    for h in H:
                        bi = h
                        S.op("pe", lambda e, h=h, bi=bi: e.transpose(ps[bi][:, 0:128], yo[:, h, :], ident), reads=[yob[h], mkb], writes=[psb[bi]])
                        S.op("dve", lambda e, ts_=ts_, h=h, bi=bi: e.tensor_copy(OT[:, 4 + h, ts_], ps[bi][:, 0:128]), reads=[psb[bi]], writes=[OTb])
                if stop_after is not None and stop_after.startswith('D'):
                    break
                S.dma("sp", YCT(tile).rearrange("(c p) t -> p c t", p=128), OT[:, :, :], reads=[OTb], writes=[outb[tile]])
                if A.get("after_tile") is not None:
                    A["after_tile"](tile)
            print('phase1 sbuf', nc.sbuf_base, nc.sbuf_top)
            S.barrier()


POOL_WINDOWS = (2, 4, 8, 16)


def phase1_inputs(inputs, b, hg):
    w = np.asarray(inputs["ab_w_in"][0], np.float32)
    PW, GW = 2048, 2048
    cols = np.concatenate([
        np.arange(hg * 512, (hg + 1) * 512),
        PW + 3 * GW + np.arange(hg * 512, (hg + 1) * 512),
        PW + np.arange(hg * 512, (hg + 1) * 512),
        PW + GW + np.arange(hg * 512, (hg + 1) * 512),
        PW + 2 * GW + np.arange(hg * 512, (hg + 1) * 512),
        PW + 4 * GW + np.arange(hg * 4, (hg + 1) * 4),
        PW + 4 * GW + 16 + np.arange(hg * 4, (hg + 1) * 4),
    ])
    wc = np.ascontiguousarray(w[:, cols])
    conv = np.asarray(inputs["gdn_conv"][0], np.float32)
    cwl = np.stack([conv[:, s * GW + hg * 512: s * GW + (hg + 1) * 512] for s in range(3)], 0)
    cwl = cwl.reshape(3, 4, 4, 128).transpose(3, 0, 2, 1).reshape(128, 12, 4)
    bc = lambda v: np.ascontiguousarray(np.broadcast_to(np.asarray(v, np.float32)[None, :], (128, len(v))))
    win = POOL_WINDOWS[hg]
    pos = np.arange(256)
    def band_full(first):
        B = np.zeros((256, 128), np.float32)
        for t in range(128):
            cnt = min(t + 1, win) if first else win
            for s in range(max(0, 128 + t - win + 1) if not first else 128 + max(0, t - win + 1), 128 + t + 1):
                B[s, t] = 1.0 / cnt
            B[128 + t, t] -= 1.0
        return B
    Bn = band_full(False)
    B0 = band_full(True)
    band = np.stack([B0[128:], Bn[128:], Bn[:128]], 0)
    k = np.arange(128)
    triA = (k[:, None] <= k[None, :]).astype(np.float32)
    strictU = (k[:, None] > k[None, :]).astype(np.float32)
    MS = (k[:, None] > k[None, :]).astype(np.float32)
    MU = (k[:, None] <= k[None, :]).astype(np.float32)
    masks = np.stack([triA, strictU, MS, MU, np.eye(128, dtype=np.float32)], 0)
    return {
        "xb": np.ascontiguousarray(np.asarray(inputs["x"][b], np.float32)),
        "g_pre": np.asarray(inputs["norm_mix_pre"][0], np.float32),
        "w_in_c": wc,
        "conv_w": np.ascontiguousarray(cwl),
        "neg_a": bc(inputs["gdn_a_log"][0][hg * 4:(hg + 1) * 4]),
        "dt_b": bc(inputs["gdn_dt_bias"][0][hg * 4:(hg + 1) * 4]),
        "gn_w": bc(inputs["gdn_norm"][0]),
        "pool_wg": np.ascontiguousarray(np.asarray(inputs["pool_w"][0][hg], np.float32)),
        "pool_sc": np.ascontiguousarray(np.asarray(inputs["pool_scale"][0][hg * 512:(hg + 1) * 512], np.float32)),
        "bandm": np.ascontiguousarray(band), "masks": np.ascontiguousarray(masks),
    }


def kernel(**inputs):
    n = 8
    nc = build_fused()
    consts = phase2_consts(inputs)
    x = np.asarray(inputs["x"], np.float32)
    maps = []
    for c in range(n):
        b, j = c // 4, c % 4
        m = dict(consts)
        m.update(phase1_inputs(inputs, b, j))
        m["x_own"] = np.ascontiguousarray(x[b, j * TP:(j + 1) * TP, :])
        sel = np.zeros((128, 4), np.float32)
        sel[:, j] = 1.0
        m["sel"] = sel
        maps.append(m)
    res = run_bass_kernel_spmd(nc, maps, core_ids=list(range(n)))
    out = np.empty((2, SEQ, D), np.float32)
    for c in range(n):
        b, j = c // 4, c % 4
        out[b, j * TP:(j + 1) * TP, :] = np.asarray(res.results[c]["out"], np.float32)
    return out
```

```python
import contextlib
import numpy as np
import ml_dtypes
import concourse.bass as bass
import concourse.mybir as mybir
from concourse.bass_utils import run_bass_kernel_spmd

F32 = mybir.dt.float32
BF16 = mybir.dt.bfloat16
AF = mybir.ActivationFunctionType
ALU = mybir.AluOpType
AX = mybir.AxisListType

D = 4096
DFF = 11008
KC = D // 128
TP = 1024
SEQ = 4096
EPS = 1e-6


class Buf:
    __slots__ = ("name", "w", "r", "strict")

    def __init__(self, name="", strict=False):
        self.name = name
        self.w = None
        self.r = []
        self.strict = strict


def SBuf(name=""):
    return Buf(name, True)


class Sched:
    ENGS = ("pe", "act", "dve", "pool", "sp")

    def __init__(self, nc, n_dma_sems=48):
        self.nc = nc
        self.prog = {e: [] for e in self.ENGS}
        self.sems = {e: nc.alloc_semaphore(name="s_" + e) for e in self.ENGS}
        self.cnt = {e: 0 for e in self.ENGS}
        self.waited = {e: {} for e in self.ENGS}
        self.dsems = [nc.alloc_semaphore(name="d%d" % i) for i in range(n_dma_sems)]
        self.dval = [0] * n_dma_sems
        self.dnext = 0
        self.n_ins = 0
        self._rec = None
        self.nosame = 1
        self.sems["cc"] = nc.alloc_semaphore(name="s_cc")
        self.ccval = 0

    def _sem(self, key):
        return self.sems[key] if isinstance(key, str) else self.dsems[key]

    def _collect(self, eng, reads, writes):
        need = {}

        relax = self.nosame and eng in ("dve", "act")

        def add(tok, strict):
            if tok is None:
                return
            k, v = tok
            if k == eng and (eng == "pe" or (relax and not strict)):
                return
            if need.get(k, 0) < v:
                need[k] = v
        for b in reads:
            add(b.w, b.strict)
        for b in writes:
            add(b.w, b.strict)
            for t in b.r:
                add(t, b.strict)
        waits = []
        wd = self.waited[eng]
        for k, v in need.items():
            if wd.get(k, 0) >= v:
                continue
            wd[k] = v
            waits.append((self._sem(k), v))
        return waits

    def _commit(self, tok, reads, writes):
        for b in reads:
            b.r.append(tok)
        for b in writes:
            b.w = tok
            b.r = []

    def record(self):
        self._rec = []

    def stop(self):
        r, self._rec = self._rec, None
        return r

    def replay(self, lists):
        pos = [0] * len(lists)
        tot = max(len(l) for l in lists)
        for step in range(1, tot + 1):
            for i, l in enumerate(lists):
                upto = (step * len(l)) // tot
                while pos[i] < upto:
                    kind, a, kw = l[pos[i]]
                    pos[i] += 1
                    {"op": self.op, "dma": self.dma, "cc": self.collective}[kind](*a, **kw)

    def op(self, eng, fn, reads=(), writes=(), signal=True):
        if self._rec is not None:
            self._rec.append(("op", (eng, fn), dict(reads=list(reads), writes=list(writes), signal=signal)))
            return
        waits = self._collect(eng, reads, writes)
        if signal:
            self.cnt[eng] += 1
            tok = (eng, self.cnt[eng])
        else:
            tok = (eng, self.cnt[eng] + 1)
        sem = self.sems[eng]

        def run(e, waits=waits, fn=fn, sem=sem, signal=signal):
            for s, v in waits:
                e.wait_ge(s, v)
            ins = fn(e)
            if signal:
                ins.then_inc(sem, 1)
        self.prog[eng].append(run)
        self._commit(tok, reads, writes)
        self.n_ins += 1

    def dma(self, eng, out_ap, in_ap, reads=(), writes=(), **kw):
        if self._rec is not None:
            self._rec.append(("dma", (eng, out_ap, in_ap), dict(reads=list(reads), writes=list(writes), **kw)))
            return
        i = self.dnext
        self.dnext = (self.dnext + 1) % len(self.dsems)
        waits = self._collect(eng, reads, writes)
        wd = self.waited[eng]
        if self.dval[i] > 0 and wd.get(i, 0) < self.dval[i]:
            wd[i] = self.dval[i]
            waits.append((self.dsems[i], self.dval[i]))
        self.dval[i] += 16
        tok = (i, self.dval[i])
        sem = self.dsems[i]

        def run(e, waits=waits, sem=sem):
            for s, v in waits:
                e.wait_ge(s, v)
            e.dma_start(out=out_ap, in_=in_ap, **kw).then_inc(sem, 16)
        self.prog[eng].append(run)
        self._commit(tok, reads, writes)
        self.n_ins += 1

    def collective(self, kind, in_ap, out_ap, groups, reads=(), writes=()):
        if self._rec is not None:
            self._rec.append(("cc", (kind, in_ap, out_ap, groups), dict(reads=list(reads), writes=list(writes))))
            return
        waits = self._collect("pool", reads, writes)
        self.ccval += 1
        tok = ("cc", self.ccval)
        sem = self.sems["cc"]

        def run(e, waits=waits, sem=sem):
            for s_, v in waits:
                e.wait_ge(s_, v)
            e.collective_compute(kind, ALU.bypass, replica_groups=groups, ins=[in_ap], outs=[out_ap]).then_inc(sem, 1)
        self.prog["pool"].append(run)
        self._commit(tok, reads, writes)

    def barrier(self):
        for e in self.ENGS:
            waits = []
            wd = self.waited[e]
            for e2 in self.ENGS:
                if e2 != e and self.cnt[e2] > wd.get(e2, 0):
                    wd[e2] = self.cnt[e2]
                    waits.append((self.sems[e2], self.cnt[e2]))
            for i, v in enumerate(self.dval):
                if v > wd.get(i, 0):
                    wd[i] = v
                    waits.append((self.dsems[i], v))
            if self.ccval > wd.get("cc", 0):
                wd["cc"] = self.ccval
                waits.append((self.sems["cc"], self.ccval))

            def run(en, waits=waits):
                for s, v in waits:
                    en.wait_ge(s, v)
            self.prog[e].append(run)

    def finish(self):
        self.barrier()
        nc = self.nc
        with nc.Block() as block:
            @block.tensor
            def _(e):
                for f in self.prog["pe"]:
                    f(e)

            @block.scalar
            def _(e):
                for f in self.prog["act"]:
                    f(e)

            @block.vector
            def _(e):
                for f in self.prog["dve"]:
                    f(e)

            @block.gpsimd
            def _(e):
                for f in self.prog["pool"]:
                    f(e)

            @block.sync
            def _(e):
                for f in self.prog["sp"]:
                    f(e)


class Ctx:
    def __init__(self, nc, S, st, NW=8, pfx=""):
        self.nc, self.S, self.st, self.pfx = nc, S, st, pfx
        self.ps = [st.enter_context(nc.psum_tensor(pfx + "ps%d" % i, [128, 512], F32)) for i in range(8)]
        self.psb = [Buf("ps%d" % i) for i in range(8)]
        self.ones_bf = self.sb("ones_bf", [128, 128], BF16)
        self.ones_f = self.sb("ones_f", [128, 128], F32)
        self.cb = SBuf("consts")
        S.op("dve", lambda e: e.memset(self.ones_bf[:], 1.0), writes=[self.cb])
        S.op("dve", lambda e: e.memset(self.ones_f[:], 1.0), writes=[self.cb])
        self.NW = NW
        self.wt = [self.sb("wt%d" % i, [128, 4096], BF16) for i in range(self.NW)]
        self.wtb = [Buf("wt%d" % i) for i in range(self.NW)]
        self.wnext = 0
        self.dmaq = 0

    def sb(self, name, shape, dt):
        return self.st.enter_context(self.nc.sbuf_tensor(self.pfx + name, shape, dt))

    def wslot(self):
        i = self.wnext
        self.wnext = (i + 1) % self.NW
        return self.wt[i], self.wtb[i]

    def q(self):
        self.dmaq ^= 1
        return "sp" if self.dmaq else "act"


def gemm_cg(C, W, c0, CW, rhs, rhsb, KCr, T, banks, tokmajor=False):
    S = C.S
    nth = T // 512
    noc = CW // 128
    ukc = 4096 // CW
    nu = (KCr + ukc - 1) // ukc
    Wv = W.rearrange("(kc p) n -> p kc n", p=128)
    for u in range(nu):
        k0 = u * ukc
        nk = min(ukc, KCr - k0)
        wt, wb = C.wslot()
        wv = wt[:, 0:nk * CW].rearrange("p (k n) -> p k n", n=CW)
        S.dma("pool", wv, Wv[:, k0:k0 + nk, c0:c0 + CW], writes=[wb])
        for oc in range(noc):
            for th in range(nth):
                bi = banks[oc * nth + th]
                for j in range(nk):
                    kc = k0 + j
                    S.op("pe", lambda e, bi=bi, wv=wv, j=j, oc=oc, kc=kc, th=th: e.matmul(
                        C.ps[bi][:, :], wv[:, j, oc * 128:(oc + 1) * 128], rhs[:, kc, th * 512:(th + 1) * 512],
                        start=(kc == 0), stop=(kc == KCr - 1)),
                        reads=[wb, rhsb], writes=[C.psb[bi]], signal=(j == nk - 1))


def load_gain(C, sb, name, g_ap, ncol=KC):
    t = sb(name, [128, ncol], F32)
    b = Buf(name)
    C.S.dma("sp", t[:, :], g_ap.rearrange("(kc p) -> p kc", p=128), writes=[b], allow_slow_non_contiguous=True)
    return t, b


def colsum_rstd(C, src_dram, srcb, nkc, T, rstd, rstdb, xin, xinb, sq, sqb, scale, tmp, tmpb):
    S = C.S
    nth = T // 512
    for kc in range(nkc):
        r = kc % len(xin)
        S.dma(C.q(), xin[r][:, :], src_dram[kc * 128:(kc + 1) * 128, :], reads=[srcb[kc]], writes=[xinb[r]])
        r2 = kc % len(sq)
        S.op("act", lambda e, r=r, r2=r2: e.activation(sq[r2][:, :], xin[r][:, :], AF.Square), reads=[xinb[r]], writes=[sqb[r2]])
        for th in range(nth):
            S.op("pe", lambda e, th=th, r2=r2, kc=kc: e.matmul(C.ps[th][:, :], C.ones_bf[:, :], sq[r2][:, th * 512:(th + 1) * 512],
                                                             start=(kc == 0), stop=(kc == nkc - 1)),
                 reads=[sqb[r2], C.cb], writes=[C.psb[th]])
    for th in range(nth):
        sl = slice(th * 512, (th + 1) * 512)
        S.op("act", lambda e, th=th, sl=sl: e.activation(tmp[:, sl], C.ps[th][:, :], AF.Sqrt, bias=C.eps_t[:, 0:1], scale=scale),
             reads=[C.psb[th], C.cb], writes=[tmpb])
        S.op("dve", lambda e, sl=sl: e.reciprocal(rstd[:, sl], tmp[:, sl]), reads=[tmpb], writes=[rstdb])


def norm_stage(C, XT, XTb, gain_ap, HT, HTb, tag):
    S, nc = C.S, C.nc
    with contextlib.ExitStack() as st:
        sb = lambda n, s, d: st.enter_context(nc.sbuf_tensor(tag + n, s, d))
        xin = [sb("xin%d" % i, [128, TP], F32) for i in range(3)]
        xinb = [Buf() for _ in range(3)]
        sq = [sb("sq%d" % i, [128, TP], BF16) for i in range(2)]
        sqb = [Buf() for _ in range(2)]
        rstd = sb("rstd", [128, TP], F32)
        rstdb = Buf()
        tmp = sb("tmp", [128, TP], F32)
        tmpb = Buf()
        g = sb("g", [128, KC], F32)
        gb = Buf()
        S.dma("sp", g[:, :], gain_ap.rearrange("(kc p) -> p kc", p=128), writes=[gb], allow_slow_non_contiguous=True)
        colsum_rstd(C, XT, XTb, KC, TP, rstd, rstdb, xin, xinb, sq, sqb, 1.0 / D, tmp, tmpb)
        for kc in range(KC):
            r = kc % 3
            S.dma(C.q(), xin[r][:, :], XT[kc * 128:(kc + 1) * 128, :], reads=[XTb[kc]], writes=[xinb[r]])
            S.op("dve", lambda e, r=r, kc=kc: e.scalar_tensor_tensor(HT[:, kc, :], xin[r][:, :], g[:, kc:kc + 1], rstd[:, :],
                                                                    ALU.mult, ALU.mult),
                 reads=[xinb[r], gb, rstdb], writes=[HTb])
        S.barrier()


def postnorm_resid(C, YT, YTb, gain_ap, XT, XTb, tag):
    S, nc = C.S, C.nc
    with contextlib.ExitStack() as st:
        sb = lambda n, s, d: st.enter_context(nc.sbuf_tensor(tag + n, s, d))
        xin = [sb("xin%d" % i, [128, TP], F32) for i in range(3)]
        xinb = [Buf() for _ in range(3)]
        yin = [sb("yin%d" % i, [128, TP], F32) for i in range(3)]
        yinb = [Buf() for _ in range(3)]
        sq = [sb("sq%d" % i, [128, TP], BF16) for i in range(2)]
        sqb = [Buf() for _ in range(2)]
        rstd = sb("rstd", [128, TP], F32)
        rstdb = Buf()
        tmp = sb("tmp", [128, TP], F32)
        tmpb = Buf()
        g = sb("g", [128, KC], F32)
        gb = Buf()
        S.dma("sp", g[:, :], gain_ap.rearrange("(kc p) -> p kc", p=128), writes=[gb], allow_slow_non_contiguous=True)
        colsum_rstd(C, YT, YTb, KC, TP, rstd, rstdb, yin, yinb, sq, sqb, 1.0 / D, tmp, tmpb)
        for kc in range(KC):
            r = kc % 3
            rows = slice(kc * 128, (kc + 1) * 128)
            S.dma("sp", yin[r][:, :], YT[rows, :], reads=[YTb[kc]], writes=[yinb[r]])
            S.dma("act", xin[r][:, :], XT[rows, :], reads=[XTb[kc]], writes=[xinb[r]])
            S.op("dve", lambda e, r=r, kc=kc: e.scalar_tensor_tensor(yin[r][:, :], yin[r][:, :], g[:, kc:kc + 1], rstd[:, :],
                                                                    ALU.mult, ALU.mult),
                 reads=[yinb[r], gb, rstdb], writes=[yinb[r]])
            S.op("pool", lambda e, r=r: e.tensor_tensor(xin[r][:, :], xin[r][:, :], yin[r][:, :], ALU.add),
                 reads=[yinb[r], xinb[r]], writes=[xinb[r]])
            S.dma("sp", XT[rows, :], xin[r][:, :], reads=[xinb[r]], writes=[XTb[kc]])
        S.barrier()


def gemm_to_dram(C, W, N, rhs, rhsb, KCr, T, OUT, OUTb, tok0, func, odt, tag):
    S, nc = C.S, C.nc
    CW = 256 if T == 1024 else 512
    nth = T // 512
    noc = CW // 128
    with contextlib.ExitStack() as st:
        ot = [st.enter_context(nc.sbuf_tensor(tag + "ot%d" % i, [128, 512], odt)) for i in range(4)]
        otb = [Buf() for _ in range(4)]
        oi = 0
        for cg in range(N // CW):
            banks = [(cg % 2) * 4 + i for i in range(4)]
            gemm_cg(C, W, cg * CW, CW, rhs, rhsb, KCr, T, banks)
            for oc in range(noc):
                for th in range(nth):
                    bi = banks[oc * nth + th]
                    o = oi % 4
                    oi += 1
                    if func is None:
                        S.op("dve", lambda e, o=o, bi=bi: e.tensor_copy(ot[o][:, :], C.ps[bi][:, :]),
                             reads=[C.psb[bi]], writes=[otb[o]])
                    else:
                        S.op("act", lambda e, o=o, bi=bi: e.activation(ot[o][:, :], C.ps[bi][:, :], func),
                             reads=[C.psb[bi]], writes=[otb[o]])
                    row = cg * CW + oc * 128
                    S.dma(C.q(), OUT[row:row + 128, tok0 + th * 512: tok0 + (th + 1) * 512], ot[o][:, :],
                          reads=[otb[o]], writes=[OUTb[row // 128]])
        S.barrier()


def load_fm(C, SRC, SRCb, nkc, T, tok0, dst, dstb, per=8):
    v = SRC.rearrange("(kc p) t -> p kc t", p=128)
    for k0 in range(0, nkc, per):
        k1 = min(nkc, k0 + per)
        C.S.dma(C.q(), dst[:, k0:k1, 0:T], v[:, k0:k1, tok0:tok0 + T], reads=[SRCb[k] for k in range(k0, k1)], writes=[dstb])


def ffn_stage(C, XT, XTb, YT, YTb, HID, HIDb, g_pre, g_post, Wg, Wu, Wd, tag):
    S, nc = C.S, C.nc
    with contextlib.ExitStack() as st:
        HT = st.enter_context(nc.sbuf_tensor(tag + "HT", [128, KC, TP], BF16))
        HTb = Buf()
        norm_stage(C, XT, XTb, g_pre, HT, HTb, tag + "n")
        sl_t = [st.enter_context(nc.sbuf_tensor(tag + "sl%d" % i, [128, 512], F32)) for i in range(2)]
        slb = [Buf() for _ in range(2)]
        ot = [st.enter_context(nc.sbuf_tensor(tag + "ho%d" % i, [128, 512], BF16)) for i in range(4)]
        otb = [Buf() for _ in range(4)]
        oi = 0
        for cg in range(DFF // 256):
            bg = [0, 1, 2, 3]
            bu = [4, 5, 6, 7]
            gemm_cg(C, Wg, cg * 256, 256, HT, HTb, KC, TP, bg)
            gemm_cg(C, Wu, cg * 256, 256, HT, HTb, KC, TP, bu)
            for oc in range(2):
                for th in range(2):
                    o = oi % 4
                    s2 = oi % 2
                    oi += 1
                    b1, b2 = bg[oc * 2 + th], bu[oc * 2 + th]
                    S.op("act", lambda e, s2=s2, b1=b1: e.activation(sl_t[s2][:, :], C.ps[b1][:, :], AF.Silu),
                         reads=[C.psb[b1]], writes=[slb[s2]])
                    S.op("dve", lambda e, s2=s2, b2=b2, o=o: e.tensor_tensor(ot[o][:, :], sl_t[s2][:, :], C.ps[b2][:, :], ALU.mult),
                         reads=[slb[s2], C.psb[b2]], writes=[otb[o]])
                    row = cg * 256 + oc * 128
                    S.dma(C.q(), HID[row:row + 128, th * 512:(th + 1) * 512], ot[o][:, :], reads=[otb[o]], writes=[HIDb[row // 128]])
        S.barrier()
    KF = DFF // 128
    with contextlib.ExitStack() as st:
        RH = st.enter_context(nc.sbuf_tensor(tag + "RH", [128, KF, 512], BF16))
        RHb = Buf()
        for th2 in range(2):
            load_fm(C, HID, HIDb, KF, 512, th2 * 512, RH, RHb)
            gemm_to_dram(C, Wd, D, RH, RHb, KF, 512, YT, YTb, th2 * 512, None, F32, tag + "d%d" % th2)
    postnorm_resid(C, YT, YTb, g_post, XT, XTb, tag + "p")


def about_stage(C, XT, XTb, YT, YTb, YC, YCb, g_post, Wo, tag):
    S, nc = C.S, C.nc
    with contextlib.ExitStack() as st:
        R = st.enter_context(nc.sbuf_tensor(tag + "R", [128, KC, TP], BF16))
        Rb = Buf()
        load_fm(C, YC, YCb, KC, TP, 0, R, Rb)
        gemm_to_dram(C, Wo, D, R, Rb, KC, TP, YT, YTb, 0, None, F32, tag + "g")
    postnorm_resid(C, YT, YTb, g_post, XT, XTb, tag + "p")


def sgu_stage(C, XT, XTb, YT, YTb, UT, UTb, VTM, VTMb, g_pre, g_post, Win, ln_g, ln_b, wsT, bs, maskT, Wout, tag):
    S, nc = C.S, C.nc
    with contextlib.ExitStack() as st:
        HT = st.enter_context(nc.sbuf_tensor(tag + "HT", [128, KC, TP], BF16))
        HTb = Buf()
        norm_stage(C, XT, XTb, g_pre, HT, HTb, tag + "n")
        gemm_to_dram(C, Win[:, 0:D], D, HT, HTb, KC, TP, UT, UTb, 0, AF.Gelu, BF16, tag + "u")
        vo = [st.enter_context(nc.sbuf_tensor(tag + "vo%d" % i, [128, 512], F32)) for i in range(3)]
        vob = [Buf() for _ in range(3)]
        Wv = Win.rearrange("(kc p) n -> p kc n", p=128)
        oi = 0
        for cg in range(D // 512):
            slots = []
            for u in range(4):
                wt, wb = C.wslot()
                wv = wt[:, :].rearrange("p (k n) -> p k n", n=512)
                S.dma("pool", wv, Wv[:, u * 8:(u + 1) * 8, D + cg * 512: D + (cg + 1) * 512], writes=[wb])
                slots.append((wv, wb))
            for tb in range(TP // 128):
                bi = oi % 8
                for kc in range(KC):
                    wv, wb = slots[kc // 8]
                    S.op("pe", lambda e, bi=bi, wv=wv, kc=kc, tb=tb: e.matmul(
                        C.ps[bi][:, :], HT[:, kc, tb * 128:(tb + 1) * 128], wv[:, kc % 8, :], start=(kc == 0), stop=(kc == KC - 1)),
                        reads=[wb, HTb], writes=[C.psb[bi]], signal=(kc % 8 == 7))
                o = oi % 3
                oi += 1
                S.op("act", lambda e, o=o, bi=bi: e.activation(vo[o][:, :], C.ps[bi][:, :], AF.Gelu), reads=[C.psb[bi]], writes=[vob[o]])
                S.dma(C.q(), VTM[tb * 128:(tb + 1) * 128, cg * 512:(cg + 1) * 512], vo[o][:, :], reads=[vob[o]], writes=[VTMb[tb]])
        S.barrier()
    with contextlib.ExitStack() as st:
        sb = lambda n, s, d: st.enter_context(nc.sbuf_tensor(tag + n, s, d))
        PT = sb("PT", [128, KC, TP], BF16)
        PTb = Buf()
        load_fm(C, UT, UTb, KC, TP, 0, PT, PTb)
        mk = sb("mk", [128, 128], F32)
        mkb = Buf()
        S.dma("sp", mk[:, :], maskT, writes=[mkb])
        wsbf = sb("wsbf", [128, 16, 128], BF16)
        wsbfb = Buf()
        S.dma("pool", wsbf[:, :, :], wsT, writes=[wsbfb])
        for g in range(16):
            S.op("dve", lambda e, g=g: e.tensor_tensor(wsbf[:, g, :], wsbf[:, g, :], mk[:, :], ALU.mult), reads=[wsbfb, mkb], writes=[wsbfb])
        BS = sb("BS", [128, 16, 128], F32)
        BSb = Buf()
        S.dma("sp", BS[:, :, :], bs, writes=[BSb])
        RS = sb("RS", [128, 16, 128], F32)
        RSb = Buf()
        for q4 in range(4):
            S.op("pe", lambda e, q4=q4: e.matmul(C.ps[q4][:, :], C.ones_bf[:, :], wsbf[:, q4 * 4:(q4 + 1) * 4, :], start=True, stop=True),
                 reads=[wsbfb, C.cb], writes=[C.psb[q4]])
            S.op("dve", lambda e, q4=q4: e.tensor_copy(RS[:, q4 * 4:(q4 + 1) * 4, :], C.ps[q4][:, :]), reads=[C.psb[q4]], writes=[RSb])
        lg, lgb = load_gain(C, sb, "lg", ln_g)
        lb, lbb = load_gain(C, sb, "lb", ln_b)
        T2 = sb("T2", [128, KC, 128], F32)
        T2b = Buf()
        for kc in range(KC):
            S.op("dve", lambda e, kc=kc: e.scalar_tensor_tensor(T2[:, kc, :], RS[:, kc // 2, :], lb[:, kc:kc + 1], BS[:, kc // 2, :],
                                                               ALU.mult, ALU.add), reads=[RSb, BSb, lbb], writes=[T2b])
        vin = [sb("vin0", [128, D], F32)] * 2
        vinb = [Buf()] * 2
        vh = [sb("vh0", [128, D], BF16)] * 2
        vhb = [Buf()] * 2
        junk = vh[0]
        junkb = vhb[0]
        st4 = [sb("st%d" % i, [128, 8], F32) for i in range(2)]
        st4b = [SBuf() for _ in range(2)]
        sv = [sb("sv%d" % i, [128, 128], F32) for i in range(3)]
        svb = [Buf() for _ in range(3)]
        oi = 0
        for tb in range(TP // 128):
            r = tb % 2
            S.dma("sp", vin[r][:, 0:D // 2], VTM[tb * 128:(tb + 1) * 128, 0:D // 2], reads=[VTMb[tb]], writes=[vinb[r]])
            S.dma("act", vin[r][:, D // 2:D], VTM[tb * 128:(tb + 1) * 128, D // 2:D], reads=[VTMb[tb]], writes=[vinb[r]])
            s4 = st4[r]
            S.op("act", lambda e, r=r, s4=s4: e.activation(junk[:, :], vin[r][:, :], AF.Identity, accum_out=s4[:, 0:1]),
                 reads=[vinb[r]], writes=[junkb, st4b[r]])
            S.op("act", lambda e, r=r, s4=s4: e.activation(junk[:, :], vin[r][:, :], AF.Square, accum_out=s4[:, 1:2]),
                 reads=[vinb[r]], writes=[junkb, st4b[r]])
            S.op("dve", lambda e, s4=s4: e.tensor_scalar(s4[:, 2:3], s4[:, 0:1], 1.0 / D, None, ALU.mult), reads=[st4b[r]], writes=[st4b[r]])
            S.op("dve", lambda e, s4=s4: e.tensor_tensor(s4[:, 3:4], s4[:, 2:3], s4[:, 2:3], ALU.mult), reads=[st4b[r]], writes=[st4b[r]])
            S.op("dve", lambda e, s4=s4: e.scalar_tensor_tensor(s4[:, 4:5], s4[:, 1:2], 1.0 / D, s4[:, 3:4], ALU.mult, ALU.subtract),
                 reads=[st4b[r]], writes=[st4b[r]])
            S.op("act", lambda e, s4=s4: e.activation(s4[:, 5:6], s4[:, 4:5], AF.Sqrt, bias=C.eps_t[:, 0:1], scale=1.0),
                 reads=[st4b[r], C.cb], writes=[st4b[r]])
            S.op("dve", lambda e, s4=s4: e.reciprocal(s4[:, 6:7], s4[:, 5:6]), reads=[st4b[r]], writes=[st4b[r]])
            S.op("dve", lambda e, s4=s4: e.scalar_tensor_tensor(s4[:, 7:8], s4[:, 2:3], -1.0, s4[:, 6:7], ALU.mult, ALU.mult),
                 reads=[st4b[r]], writes=[st4b[r]])
            S.op("dve", lambda e, r=r, s4=s4: e.tensor_scalar(vh[r][:, :], vin[r][:, :], s4[:, 6:7], s4[:, 7:8], ALU.mult, ALU.add),
                 reads=[vinb[r], st4b[r]], writes=[vhb[r]])
            for k4 in range(KC // 4):
                bi = oi % 8
                oi += 1
                for j in range(4):
                    kc = k4 * 4 + j
                    S.op("pe", lambda e, bi=bi, j=j, kc=kc, r=r: e.matmul(C.ps[bi][:, j * 128:(j + 1) * 128], vh[r][:, kc * 128:(kc + 1) * 128],
                                                                         wsbf[:, kc // 2, :], start=True, stop=True),
                         reads=[vhb[r], wsbfb], writes=[C.psb[bi]], signal=(j == 3))
                for j in range(4):
                    kc = k4 * 4 + j
                    s3 = (k4 * 4 + j) % 3
                    S.op("dve", lambda e, bi=bi, j=j, kc=kc, s3=s3: e.scalar_tensor_tensor(
                        sv[s3][:, :], C.ps[bi][:, j * 128:(j + 1) * 128], lg[:, kc:kc + 1], T2[:, kc, :], ALU.mult, ALU.add),
                        reads=[C.psb[bi], lgb, T2b], writes=[svb[s3]])
                    S.op("pool", lambda e, kc=kc, s3=s3, tb=tb: e.tensor_tensor(
                        PT[:, kc, tb * 128:(tb + 1) * 128], PT[:, kc, tb * 128:(tb + 1) * 128], sv[s3][:, :], ALU.mult),
                        reads=[svb[s3], PTb], writes=[PTb])
        gemm_to_dram(C, Wout, D, PT, PTb, KC, TP, YT, YTb, 0, None, F32, tag + "o")
    postnorm_resid(C, YT, YTb, g_post, XT, XTb, tag + "p")


def xin_stage(C, x_own, XT, XTb, ident, identb):
    S, nc = C.S, C.nc
    with contextlib.ExitStack() as st:
        xr = [st.enter_context(nc.sbuf_tensor("xi_r%d" % i, [128, D], F32)) for i in range(2)]
        xrb = [Buf() for _ in range(2)]
        xo = [st.enter_context(nc.sbuf_tensor("xi_o%d" % i, [128, 4, 128], F32)) for i in range(3)]
        xob = [Buf() for _ in range(3)]
        inb = Buf()
        oi = 0
        for tb in range(TP // 128):
            r = tb % 2
            S.dma("sp", xr[r][:, 0:D // 2], x_own[tb * 128:(tb + 1) * 128, 0:D // 2], reads=[inb], writes=[xrb[r]])
            S.dma("act", xr[r][:, D // 2:D], x_own[tb * 128:(tb + 1) * 128, D // 2:D], reads=[inb], writes=[xrb[r]])
            for k4 in range(KC // 4):
                bi = oi % 8
                o = oi % 3
                oi += 1
                for j in range(4):
                    kc = k4 * 4 + j
                    S.op("pe", lambda e, bi=bi, j=j, kc=kc, r=r: e.transpose(C.ps[bi][:, j * 128:(j + 1) * 128],
                                                                            xr[r][:, kc * 128:(kc + 1) * 128], ident[:, :]),
                         reads=[xrb[r], identb], writes=[C.psb[bi]], signal=(j == 3))
                S.op("dve", lambda e, bi=bi, o=o: e.tensor_copy(xo[o][:, :, :], C.ps[bi][:, :]), reads=[C.psb[bi]], writes=[xob[o]])
                dst = XT[k4 * 512:(k4 + 1) * 512, tb * 128:(tb + 1) * 128].rearrange("(j p) t -> p j t", p=128)
                S.dma(C.q(), dst, xo[o][:, :, :], reads=[xob[o]], writes=[XTb[k4 * 4 + j] for j in range(4)])
        S.barrier()


def xout_stage(C, XT, XTb, out, outb, ident, identb):
    S, nc = C.S, C.nc
    with contextlib.ExitStack() as st:
        xr = [st.enter_context(nc.sbuf_tensor("xo_r%d" % i, [128, TP], F32)) for i in range(2)]
        xrb = [Buf() for _ in range(2)]
        xo = [st.enter_context(nc.sbuf_tensor("xo_o%d" % i, [128, 4, 128], F32)) for i in range(3)]
        xob = [Buf() for _ in range(3)]
        oi = 0
        for kc in range(KC):
            r = kc % 2
            S.dma(C.q(), xr[r][:, :], XT[kc * 128:(kc + 1) * 128, :], reads=[XTb[kc]], writes=[xrb[r]])
            for t4 in range(TP // 512):
                bi = oi % 8
                o = oi % 3
                oi += 1
                for j in range(4):
                    tb = t4 * 4 + j
                    S.op("pe", lambda e, bi=bi, j=j, tb=tb, r=r: e.transpose(C.ps[bi][:, j * 128:(j + 1) * 128],
                                                                            xr[r][:, tb * 128:(tb + 1) * 128], ident[:, :]),
                         reads=[xrb[r], identb], writes=[C.psb[bi]], signal=(j == 3))
                S.op("dve", lambda e, bi=bi, o=o: e.tensor_copy(xo[o][:, :, :], C.ps[bi][:, :]), reads=[C.psb[bi]], writes=[xob[o]])
                dst = out[t4 * 512:(t4 + 1) * 512, kc * 128:(kc + 1) * 128].rearrange("(j p) f -> p j f", p=128)
                S.dma(C.q(), dst, xo[o][:, :, :], reads=[xob[o]], writes=[outb])
        S.barrier()


def dram_in(nc, name, shape, dt=F32):
    return nc.dram_tensor(name, list(shape), dt, kind="ExternalInput").ap()


def dram_scratch(nc, name, shape, dt=F32):
    return nc.dram_tensor(name, list(shape), dt, kind="Internal").ap()


def phase2_decl(nc):
    A = {}
    A["x_own"] = dram_in(nc, "x_own", [TP, D])
    A["ident_d"] = dram_in(nc, "ident", [128, 128])
    A["nmpost"] = dram_in(nc, "norm_mix_post", [2, D])
    A["nmpre"] = dram_in(nc, "norm_mix_pre", [2, D])
    A["nfpre"] = dram_in(nc, "norm_ffn_pre", [2, D])
    A["nfpost"] = dram_in(nc, "norm_ffn_post", [2, D])
    A["Wabo"] = dram_in(nc, "ab_w_out", [D, D])
    A["Wg"] = dram_in(nc, "ffn_w_gate", [2, D, DFF])
    A["Wu"] = dram_in(nc, "ffn_w_up", [2, D, DFF])
    A["Wd"] = dram_in(nc, "ffn_w_down", [2, DFF, D])
    A["Wsi"] = dram_in(nc, "sgu_w_in", [D, 2 * D])
    A["Wso"] = dram_in(nc, "sgu_w_out", [D, D])
    A["lng"] = dram_in(nc, "sgu_ln_g", [D])
    A["lnb"] = dram_in(nc, "sgu_ln_b", [D])
    A["wsT"] = dram_in(nc, "sgu_wsT", [128, 16, 128])
    A["bsb"] = dram_in(nc, "sgu_bs_b", [128, 16, 128])
    A["maskT"] = dram_in(nc, "sgu_maskT", [128, 128])
    A["out"] = nc.dram_tensor("out", [TP, D], F32, kind="ExternalOutput").ap()
    A["XT"] = dram_scratch(nc, "XT", [D, TP])
    A["YT"] = dram_scratch(nc, "YT", [D, TP])
    A["HID"] = dram_scratch(nc, "HID", [DFF, TP], BF16)
    A["UT"] = dram_scratch(nc, "UT", [D, TP], BF16)
    A["VTM"] = dram_scratch(nc, "VTM", [TP, D])
    return A


def build_phase2(stages=("in", "ab", "ffn0", "sgu", "ffn1", "out")):
    nc = bass.Bass("TRN2", target_bir_lowering=False)
    A = phase2_decl(nc)
    A["YC"] = dram_in(nc, "yc", [D, TP], BF16)
    with nc.cleanup_on_exit():
        S = Sched(nc)
        phase2_body(nc, S, A, stages, None)
        S.finish()
    return nc


def about_stage_sel(C, XT, XTb, YT, YTb, G, Gb, sel_d, g_post, Wo, tag):
    S, nc = C.S, C.nc
    with contextlib.ExitStack() as st:
        R = st.enter_context(nc.sbuf_tensor(tag + "R", [128, KC, TP], BF16))
        Rb = Buf()
        sel = st.enter_context(nc.sbuf_tensor(tag + "sel", [128, 4], F32))
        selb = Buf()
        S.dma("sp", sel[:, :], sel_d, writes=[selb])
        c4 = [st.enter_context(nc.sbuf_tensor(tag + "c4%d" % i, [128, 4, TT], BF16)) for i in range(3)]
        c4b = [Buf() for _ in range(3)]
        Gv = G.rearrange("(j i) r t -> i r j t", i=2)
        n = 0
        for kc in range(KC):
            if kc < 16:
                r, lc = kc // 4, kc % 4
            else:
                r, lc = (kc - 16) // 4, 4 + (kc - 16) % 4
            row = r * 1024 + lc * 128
            for hf in range(2):
                i = n % 3
                n += 1
                ts2 = slice(hf * TT, (hf + 1) * TT)
                S.dma(C.q(), c4[i][:, :, :], Gv[hf, row:row + 128, :, :], reads=list(Gb), writes=[c4b[i]])
                S.op("dve", lambda e, i=i, kc=kc, ts2=ts2: e.tensor_scalar(R[:, kc, ts2], c4[i][:, 0, :], sel[:, 0:1], None, ALU.mult),
                     reads=[c4b[i], selb], writes=[Rb])
                for j in range(1, 4):
                    S.op("dve", lambda e, i=i, kc=kc, j=j, ts2=ts2: e.scalar_tensor_tensor(R[:, kc, ts2], c4[i][:, j, :], sel[:, j:j + 1], R[:, kc, ts2],
                                                                                      ALU.mult, ALU.add), reads=[c4b[i], selb, Rb], writes=[Rb])
        gemm_to_dram(C, Wo, D, R, Rb, KC, TP, YT, YTb, 0, None, F32, tag + "g")
    postnorm_resid(C, YT, YTb, g_post, XT, XTb, tag + "p")


def phase2_body(nc, S, A, stages, gathered):
    x_own, ident_d, nmpost, nmpre, nfpre, nfpost, Wabo, Wg, Wu, Wd, Wsi, Wso, lng, lnb, wsT, bsb, maskT, out, XT, YT, HID, UT, VTM = [A[k] for k in (
        "x_own", "ident_d", "nmpost", "nmpre", "nfpre", "nfpost", "Wabo", "Wg", "Wu", "Wd", "Wsi", "Wso", "lng", "lnb", "wsT", "bsb", "maskT",
        "out", "XT", "YT", "HID", "UT", "VTM")]
    XTb = [Buf() for _ in range(KC)]
    YTb = [Buf() for _ in range(KC)]
    HIDb = [Buf() for _ in range(DFF // 128)]
    UTb = [Buf() for _ in range(KC)]
    VTMb = [Buf() for _ in range(TP // 128)]
    YCb = [Buf() for _ in range(KC)]
    outb = Buf()
    if True:
        with contextlib.ExitStack() as st:
            C = Ctx(nc, S, st)
            ident = C.sb("ident_sb", [128, 128], F32)
            identb = Buf()
            S.dma("sp", ident[:, :], ident_d, writes=[identb])
            C.eps_t = C.sb("eps_t2", [128, 1], F32)
            S.op("dve", lambda e: e.memset(C.eps_t[:, :], EPS), writes=[C.cb])
            if "in" in stages:
                xin_stage(C, x_own, XT, XTb, ident, identb)
            if "ab" in stages:
                if gathered is None:
                    about_stage(C, XT, XTb, YT, YTb, A["YC"], YCb, nmpost[0], Wabo, "ab")
                else:
                    G, Gb, sel_d = gathered
                    about_stage_sel(C, XT, XTb, YT, YTb, G, Gb, sel_d, nmpost[0], Wabo, "ab")
            if "ffn0" in stages:
                ffn_stage(C, XT, XTb, YT, YTb, HID, HIDb, nfpre[0], nfpost[0], Wg[0], Wu[0], Wd[0], "f0")
            if "sgu" in stages:
                sgu_stage(C, XT, XTb, YT, YTb, UT, UTb, VTM, VTMb, nmpre[1], nmpost[1], Wsi, lng, lnb, wsT, bsb, maskT, Wso, "sg")
            if "ffn1" in stages:
                ffn_stage(C, XT, XTb, YT, YTb, HID, HIDb, nfpre[1], nfpost[1], Wg[1], Wu[1], Wd[1], "f1")
            if "out" in stages:
                xout_stage(C, XT, XTb, out, outb, ident, identb)
            S.barrier()


def build_fused():
    nc = bass.Bass("TRN2", target_bir_lowering=False)
    A1 = phase1_decl(nc)
    A2 = phase2_decl(nc)
    sel_d = dram_in(nc, "sel", [128, 4])
    NT = SEQ // TT
    YL = [nc.dram_tensor("YL%d" % t, [1024, TT], BF16) for t in range(NT)]
    GG = nc.dram_tensor("YG", [NT, 4 * 1024, TT], BF16)
    A1["YCT"] = lambda t: YL[t].ap()
    with nc.cleanup_on_exit():
        S = Sched(nc)
        ylb = [Buf() for _ in range(NT)]
        Gb = [Buf() for _ in range(NT)]
        A1["after_tile"] = lambda t: S.collective("AllGather", YL[t].ap().opt(), GG.ap()[t].opt(), [[0, 1, 2, 3], [4, 5, 6, 7]],
                                                  reads=[ylb[t]], writes=[Gb[t]])
        phase1_body(nc, S, A1, ylb)
        phase2_body(nc, S, A2, ("in", "ab", "ffn0", "sgu", "ffn1", "out"), (GG.ap(), Gb, sel_d))
        S.finish()
    return nc


def phase2_consts(inputs):
    ws = np.asarray(inputs["sgu_w_s"][0], np.float32)
    wsT = np.ascontiguousarray(ws.transpose(2, 0, 1))
    pos = np.arange(128)
    maskT = ((pos[:, None] // 64) <= (pos[None, :] // 64)).astype(np.float32)
    bs = np.asarray(inputs["sgu_b_s"][0], np.float32)
    bsb = np.ascontiguousarray(np.broadcast_to(bs[None], (128, 16, 128)))
    return {
        "ident": np.eye(128, dtype=np.float32),
        "norm_mix_post": np.asarray(inputs["norm_mix_post"], np.float32),
        "norm_mix_pre": np.asarray(inputs["norm_mix_pre"], np.float32),
        "norm_ffn_pre": np.asarray(inputs["norm_ffn_pre"], np.float32),
        "norm_ffn_post": np.asarray(inputs["norm_ffn_post"], np.float32),
        "ab_w_out": np.asarray(inputs["ab_w_out"][0], np.float32),
        "ffn_w_gate": np.asarray(inputs["ffn_w_gate"], np.float32),
        "ffn_w_up": np.asarray(inputs["ffn_w_up"], np.float32),
        "ffn_w_down": np.asarray(inputs["ffn_w_down"], np.float32),
        "sgu_w_in": np.asarray(inputs["sgu_w_in"][0], np.float32),
        "sgu_w_out": np.asarray(inputs["sgu_w_out"][0], np.float32),
        "sgu_ln_g": np.asarray(inputs["sgu_ln_g"][0], np.float32),
        "sgu_ln_b": np.asarray(inputs["sgu_ln_b"][0], np.float32),
        "sgu_wsT": wsT, "sgu_bs_b": bsb, "sgu_maskT": maskT,
    }


NH = 4
TT = 512
NCOL1 = 2568


def phase1_decl(nc):
    A = {}
    A["xb"] = dram_in(nc, "xb", [SEQ, D])
    A["gpre"] = dram_in(nc, "g_pre", [D])
    A["Wc"] = dram_in(nc, "w_in_c", [D, NCOL1])
    A["cw_d"] = dram_in(nc, "conv_w", [128, 12, 4])
    A["nega_d"] = dram_in(nc, "neg_a", [128, 4])
    A["dtb_d"] = dram_in(nc, "dt_b", [128, 4])
    A["gnw_d"] = dram_in(nc, "gn_w", [128, 128])
    A["pw_d"] = dram_in(nc, "pool_wg", [512, 512])
    A["psc_d"] = dram_in(nc, "pool_sc", [512])
    A["band_d"] = dram_in(nc, "bandm", [3, 128, 128])
    A["mask_d"] = dram_in(nc, "masks", [5, 128, 128])
    return A


def build_phase1(ntiles=SEQ // TT, stop_after=None, dbgk=99):
    nc = bass.Bass("TRN2", target_bir_lowering=False)
    A = phase1_decl(nc)
    yct = nc.dram_tensor("yct", [1024, SEQ], BF16, kind="ExternalOutput").ap()
    A["YCT"] = lambda t: yct[:, t * TT:(t + 1) * TT]
    with nc.cleanup_on_exit():
        S = Sched(nc)
        phase1_body(nc, S, A, [Buf() for _ in range(SEQ // TT)], ntiles, stop_after, dbgk)
        S.finish()
    return nc


def phase1_body(nc, S, A, outb, ntiles=SEQ // TT, stop_after=None, dbgk=99):
    xb, gpre, Wc, cw_d, nega_d, dtb_d, gnw_d, pw_d, psc_d, band_d, mask_d, YCT = [A[k] for k in (
        "xb", "gpre", "Wc", "cw_d", "nega_d", "dtb_d", "gnw_d", "pw_d", "psc_d", "band_d", "mask_d", "YCT")]
    if True:
        with contextlib.ExitStack() as st:
            C = Ctx(nc, S, st, NW=4, pfx="p1_")
            sb = C.sb
            C.eps_t = sb("eps_t", [128, 1], F32)
            S.op("dve", lambda e: e.memset(C.eps_t[:, :], EPS), writes=[C.cb])
            C.one_t = sb("one_t", [128, 1], F32)
            S.op("dve", lambda e: e.memset(C.one_t[:, :], 1.0), writes=[C.cb])
            mk = sb("mk", [128, 5, 128], F32)
            mkb = Buf()
            S.dma("sp", mk[:, :, :], mask_d.rearrange("m p f -> p m f"), writes=[mkb])
            triA, strictU, MS, MU, ident = [mk[:, i, :] for i in range(5)]
            band = sb("band", [128, 3, 128], F32)
            bandb = Buf()
            S.dma("act", band[:, :, :], band_d.rearrange("m p f -> p m f"), writes=[bandb])
            g = sb("g", [128, KC], F32)
            gb = Buf()
            S.dma("sp", g[:, :], gpre.rearrange("(kc p) -> p kc", p=128), writes=[gb], allow_slow_non_contiguous=True)
            cw = sb("cw", [128, 12, 4], F32)
            cwb = Buf()
            S.dma("sp", cw[:, :, :], cw_d, writes=[cwb])
            nega = sb("nega", [128, 4], F32)
            dtb = sb("dtb", [128, 4], F32)
            gnw = sb("gnw", [128, 128], F32)
            psc = sb("psc", [128, 4], F32)
            smb = SBuf()
            S.dma("sp", nega[:, :], nega_d, writes=[smb])
            S.dma("sp", dtb[:, :], dtb_d, writes=[smb])
            S.dma("sp", gnw[:, :], gnw_d, writes=[smb])
            S.dma("sp", psc[:, :], psc_d.rearrange("(c p) -> p c", p=128), writes=[smb], allow_slow_non_contiguous=True)
            S.op("act", lambda e: e.activation(nega[:, :], nega[:, :], AF.Exp), reads=[smb], writes=[smb])
            S.op("dve", lambda e: e.tensor_scalar(nega[:, :], nega[:, :], -1.0, None, ALU.mult), reads=[smb], writes=[smb])
            pw = sb("pw", [128, 4, 512], BF16)
            pwb = Buf()
            S.dma("pool", pw[:, :, :], pw_d.rearrange("(c p) n -> p c n", p=128), writes=[pwb])
            wbd = sb("wbd", [128, KC, 8], BF16)
            wbdb = Buf()
            S.dma("pool", wbd[:, :, :], Wc.rearrange("(kc p) n -> p kc n", p=128)[:, :, 2560:2568], writes=[wbdb],
                  allow_slow_non_contiguous=True)
            xr = sb("xr", [128, D], F32)
            xrb = Buf()
            HT = sb("HT", [128, KC, TT], BF16)
            HTb = Buf()
            st8 = sb("st8", [128, 12], F32)
            st8b = SBuf()
            halo = sb("halo", [128, 12, 4], F32)
            halob = Buf()
            S.op("dve", lambda e: e.memset(halo[:, :, :], 0.0), writes=[halob])
            Z = [sb("Z%d" % i, [128, 3 + TT], F32) for i in range(2)]
            Zb = [Buf() for _ in range(2)]
            acc = [sb("acc%d" % i, [128, TT], F32) for i in range(2)]
            accb = [Buf() for _ in range(2)]
            sl = [sb("sl%d" % i, [128, TT], F32) for i in range(2)]
            slb = [Buf() for _ in range(2)]
            QT = sb("QT", [128, NH, TT], BF16)
            KT = sb("KT", [128, NH, TT], BF16)
            KTf = sb("KTf", [128, NH, TT], F32)
            VT = sb("VT", [128, NH, TT], F32)
            QTb, KTb, KTfb, VTb = [[Buf() for _ in range(NH)] for _ in range(4)]
            sg = sb("sg", [128, 4, 512], F32)
            sgb = [Buf() for _ in range(4)]
            xa = sb("xa", [128, 5, 512], F32)
            xab = [Buf() for _ in range(5)]
            lg8 = sb("lg8", [128, 4, 8], F32)
            lg8b = [SBuf() for _ in range(4)]
            OT = sb("OT", [128, 8, TT], BF16)
            OTb = Buf()
            PLT = sb("PLT", [128, 4, TT], BF16)
            PLTb = Buf()
            Sst = sb("Sst", [128, NH, 128], F32)
            Sbf = sb("Sbf", [128, NH, 128], BF16)
            Sstb = [Buf() for _ in range(NH)]
            Sbfb = [Buf() for _ in range(NH)]
            for h in range(NH):
                S.op("dve", lambda e, h=h: e.memset(Sst[:, h, :], 0.0), writes=[Sstb[h]])
                S.op("dve", lambda e, h=h: e.memset(Sbf[:, h, :], 0.0), writes=[Sbfb[h]])

            def ht(name, n=128, dt=F32, strict=False):
                t = sb(name, [128, NH, n], dt)
                return t, [Buf(name, strict) for _ in range(NH)]
            b4, b4b = ht("b4", 8, strict=True)
            G2, G2b = ht("G2")
            gB, gBb = ht("gB")
            EX, EXb = ht("EX", 392, strict=True)
            Es, Esb = ht("Es")
            ETc, ETcb = ht("ETc")
            ngb, ngbb = ht("ngb", 2, strict=True)
            Lm, Lb = ht("L")
            AT, ATb = ht("AT")
            kd, kdb = ht("kd")
            vb, vbb = ht("vb")
            Xa, Xab_ = ht("Xa")
            Xat, Xatb = ht("Xat")
            Xb, Xbb = ht("Xb")
            Xbt, Xbtb = ht("Xbt")
            Pm, Pmb = ht("Pm")
            A2T, A2Tb = ht("A2T", 128, BF16)
            K2, K2b = ht("K2", 128, BF16)
            qdT, qdTb = ht("qdT", 128, BF16)
            Rbf, Rbfb = ht("Rbf", 128, BF16)
            gsg, gsgb = gB, gBb
            yo, yob = G2, G2b
            ps, psb = C.ps, C.psb
            Wv_ = Wc.rearrange("(kc p) n -> p kc n", p=128)

            def emit_A(tile):
                t0 = tile * TT
                for tb in range(4):
                    r0 = t0 + tb * 128
                    S.dma("sp", xr[:, 0:D // 2], xb[r0:r0 + 128, 0:D // 2], writes=[xrb])
                    S.dma("act", xr[:, D // 2:D], xb[r0:r0 + 128, D // 2:D], writes=[xrb])
                    S.op("act", lambda e: e.activation(HT[:, 0:8, :], xr[:, :], AF.Square, accum_out=st8[:, 0:1]),
                         reads=[xrb], writes=[HTb, st8b]) if False else None
                    S.op("dve", lambda e: e.tensor_tensor_reduce(out=sl[0][:, :], in0=xr[:, 0:TT], in1=xr[:, 0:TT], op0=ALU.mult, op1=ALU.add,
                                                               scale=1.0, scalar=0.0, accum_out=st8[:, 0:1]), reads=[xrb], writes=[slb[0], st8b]) if False else None
                    for q8 in range(8):
                        S.op("act", lambda e, q8=q8: e.activation(sl[0][:, :], xr[:, q8 * 512:(q8 + 1) * 512], AF.Square,
                                                                  accum_out=st8[:, q8:q8 + 1]), reads=[xrb], writes=[slb[0], st8b])
                    S.op("dve", lambda e: e.tensor_reduce(st8[:, 8:9], st8[:, 0:8], AX.X, ALU.add), reads=[st8b], writes=[st8b])
                    S.op("act", lambda e: e.activation(st8[:, 9:10], st8[:, 8:9], AF.Sqrt, bias=C.eps_t[:, 0:1], scale=1.0 / D),
                         reads=[st8b, C.cb], writes=[st8b])
                    S.op("dve", lambda e: e.reciprocal(st8[:, 10:11], st8[:, 9:10]), reads=[st8b], writes=[st8b])
                    S.op("act", lambda e: e.activation(xr[:, :], xr[:, :], AF.Copy, scale=st8[:, 10:11]), reads=[xrb, st8b], writes=[xrb])
                    for k4 in range(KC // 4):
                        bi = 4 + k4 % 4
                        for j in range(4):
                            kc = k4 * 4 + j
                            S.op("pe", lambda e, bi=bi, j=j, kc=kc: e.transpose(ps[bi][:, j * 128:(j + 1) * 128],
                                                                               xr[:, kc * 128:(kc + 1) * 128], ident),
                                 reads=[xrb, mkb], writes=[psb[bi]], signal=(j == 3))
                        for j in range(4):
                            kc = k4 * 4 + j
                            S.op("dve" if j % 2 == 0 else "act",
                                 (lambda e, bi=bi, j=j, kc=kc, tb=tb: e.tensor_scalar(HT[:, kc, tb * 128:(tb + 1) * 128], ps[bi][:, j * 128:(j + 1) * 128],
                                                                                     g[:, kc:kc + 1], None, ALU.mult)) if j % 2 == 0 else
                                 (lambda e, bi=bi, j=j, kc=kc, tb=tb: e.activation(HT[:, kc, tb * 128:(tb + 1) * 128], ps[bi][:, j * 128:(j + 1) * 128],
                                                                                  AF.Copy, scale=g[:, kc:kc + 1])),
                                 reads=[psb[bi], gb], writes=[HTb])
            def emit_B(tile):
                t0 = tile * TT
                for cgi in range(2):
                    slots = []
                    for u in range(4):
                        wt, wb = C.wslot()
                        wv = wt[:, :].rearrange("p (k n) -> p k n", n=512)
                        S.dma("pool", wv, Wv_[:, u * 8:(u + 1) * 8, cgi * 512:(cgi + 1) * 512], writes=[wb])
                        slots.append((wv, wb))
                    for tb in range(4):
                        bi = (cgi * 4 + tb) % 8
                        for kc in range(KC):
                            wv, wb = slots[kc // 8]
                            S.op("pe", lambda e, bi=bi, wv=wv, kc=kc, tb=tb: e.matmul(
                                ps[bi][:, :], HT[:, kc, tb * 128:(tb + 1) * 128], wv[:, kc % 8, :], start=(kc == 0), stop=(kc == KC - 1)),
                                reads=[wb, HTb], writes=[psb[bi]], signal=(kc % 8 == 7))
                        if cgi == 0:
                            S.op("act", lambda e, bi=bi, tb=tb: e.activation(xa[:, tb + 1, :], ps[bi][:, :], AF.Copy),
                                 reads=[psb[bi]], writes=[xab[tb + 1]])
                        else:
                            S.op("act", lambda e, bi=bi, tb=tb: e.activation(sg[:, tb, :], ps[bi][:, :], AF.Silu),
                                 reads=[psb[bi]], writes=[sgb[tb]])
                for tb in range(4):
                    bi = tb
                    for kc in range(KC):
                        S.op("pe", lambda e, bi=bi, kc=kc, tb=tb: e.matmul(ps[bi][:, 0:8], HT[:, kc, tb * 128:(tb + 1) * 128], wbd[:, kc, :],
                                                                          start=(kc == 0), stop=(kc == KC - 1)),
                             reads=[wbdb, HTb], writes=[psb[bi]], signal=(kc == KC - 1))
                    S.op("dve", lambda e, bi=bi, tb=tb: e.tensor_copy(lg8[:, tb, :], ps[bi][:, 0:8]), reads=[psb[bi]], writes=[lg8b[tb]])
                for grp in range(3):
                    banks = [4, 5, 6, 7] if grp % 2 == 0 else [0, 1, 2, 3]
                    gemm_cg(C, Wc, 1024 + grp * 512, 512, HT, HTb, KC, TT, banks)
                    for h in range(NH):
                        c = grp * 4 + h
                        zi = c % 2
                        bi = banks[h]
                        S.op("dve", lambda e, zi=zi, c=c: e.tensor_copy(Z[zi][:, 0:3], halo[:, c, 0:3]), reads=[halob], writes=[Zb[zi]])
                        S.op("act", lambda e, zi=zi, bi=bi: e.activation(Z[zi][:, 3:3 + TT], ps[bi][:, :], AF.Copy),
                             reads=[psb[bi]], writes=[Zb[zi]])
                        S.op("dve", lambda e, zi=zi, c=c: e.tensor_copy(halo[:, c, 0:3], Z[zi][:, TT:TT + 3]), reads=[Zb[zi]], writes=[halob])
                        S.op("dve", lambda e, zi=zi, c=c: e.tensor_scalar(acc[zi][:, :], Z[zi][:, 3:3 + TT], cw[:, c, 3:4], None, ALU.mult),
                             reads=[Zb[zi], cwb], writes=[accb[zi]])
                        for j in range(3):
                            S.op("dve", lambda e, zi=zi, c=c, j=j: e.scalar_tensor_tensor(acc[zi][:, :], Z[zi][:, j:j + TT], cw[:, c, j:j + 1],
                                                                                         acc[zi][:, :], ALU.mult, ALU.add),
                                 reads=[Zb[zi], cwb, accb[zi]], writes=[accb[zi]])
                        if grp == 2:
                            S.op("act", lambda e, zi=zi, h=h: e.activation(VT[:, h, :], acc[zi][:, :], AF.Silu), reads=[accb[zi]], writes=[VTb[h]])
                            continue
                        S.op("act", lambda e, zi=zi: e.activation(sl[zi][:, :], acc[zi][:, :], AF.Silu), reads=[accb[zi]], writes=[slb[zi]])
                        S.op("pool", lambda e, zi=zi: e.tensor_tensor(acc[zi][:, :], sl[zi][:, :], sl[zi][:, :], ALU.mult),
                             reads=[slb[zi]], writes=[accb[zi]])
                        pb = banks[h]
                        S.op("pe", lambda e, pb=pb, zi=zi: e.matmul(ps[pb][:, :], C.ones_f[:, :], acc[zi][:, :], start=True, stop=True),
                             reads=[accb[zi], C.cb], writes=[psb[pb]])
                        S.op("act", lambda e, pb=pb, zi=zi: e.activation(acc[zi][:, :], ps[pb][:, :], AF.Sqrt, bias=C.eps_t[:, 0:1], scale=1.0),
                             reads=[psb[pb], C.cb], writes=[accb[zi]])
                        S.op("dve", lambda e, zi=zi: e.reciprocal(acc[zi][:, :], acc[zi][:, :]), reads=[accb[zi]], writes=[accb[zi]])
                        if grp == 0:
                            S.op("dve", lambda e, zi=zi, h=h: e.scalar_tensor_tensor(QT[:, h, :], sl[zi][:, :], 128.0 ** -0.5, acc[zi][:, :],
                                                                                    ALU.mult, ALU.mult), reads=[slb[zi], accb[zi]], writes=[QTb[h]])
                        else:
                            S.op("dve", lambda e, zi=zi, h=h: e.tensor_tensor(KTf[:, h, :], sl[zi][:, :], acc[zi][:, :], ALU.mult),
                                 reads=[slb[zi], accb[zi]], writes=[KTfb[h]])
                            S.op("pool", lambda e, h=h: e.tensor_copy(KT[:, h, :], KTf[:, h, :]), reads=[KTfb[h]], writes=[KTb[h]])
                for tb in range(4):
                    first = (tile == 0 and tb == 0)
                    for c in range(4):
                        bi = 4 + c
                        if first:
                            S.op("pe", lambda e, bi=bi, c=c, tb=tb: e.matmul(ps[bi][:, 0:128], xa[:, tb + 1, c * 128:(c + 1) * 128], band[:, 0, :],
                                                                            start=True, stop=True), reads=[xab[tb + 1], bandb], writes=[psb[bi]])
                        else:
                            S.op("pe", lambda e, bi=bi, c=c, tb=tb: e.matmul(ps[bi][:, 0:128], xa[:, tb, c * 128:(c + 1) * 128], band[:, 2, :],
                                                                            start=True, stop=False), reads=[xab[tb], bandb], writes=[psb[bi]], signal=False)
                            S.op("pe", lambda e, bi=bi, c=c, tb=tb: e.matmul(ps[bi][:, 0:128], xa[:, tb + 1, c * 128:(c + 1) * 128], band[:, 1, :],
                                                                            start=False, stop=True), reads=[xab[tb + 1], bandb], writes=[psb[bi]])
                        S.op("act", lambda e, bi=bi, c=c, tb=tb: e.activation(PLT[:, c, tb * 128:(tb + 1) * 128], ps[bi][:, 0:128], AF.Copy),
                             reads=[psb[bi]], writes=[PLTb])
                S.op("pool", lambda e: e.tensor_copy(xa[:, 0, :], xa[:, 4, :]), reads=[xab[4]], writes=[xab[0]])
                for dc in range(4):
                    bi = dc
                    for c in range(4):
                        S.op("pe", lambda e, bi=bi, c=c, dc=dc: e.matmul(ps[bi][:, :], pw[:, c, dc * 128:(dc + 1) * 128], PLT[:, c, :],
                                                                        start=(c == 0), stop=(c == 3)), reads=[pwb, PLTb], writes=[psb[bi]], signal=(c == 3))
                    S.op("act", lambda e, bi=bi, dc=dc: e.activation(OT[:, dc, :], ps[bi][:, :], AF.Copy, scale=psc[:, dc:dc + 1]),
                         reads=[psb[bi], smb], writes=[OTb])
            def emit_D(tile):
                t0 = tile * TT
                H = range(NH)
                for tb in range(4):
                    ts_ = slice(tb * 128, (tb + 1) * 128)
                    S.op("act", lambda e, tb=tb: e.activation(b4[:, 0, 0:4], lg8[:, tb, 0:4], AF.Exp, scale=-1.0), reads=[lg8b[tb]], writes=[b4b[0]])
                    S.op("dve", lambda e: e.tensor_scalar(b4[:, 0, 0:4], b4[:, 0, 0:4], 1.0, None, ALU.add), reads=[b4b[0]], writes=[b4b[0]])
                    S.op("dve", lambda e: e.reciprocal(b4[:, 0, 0:4], b4[:, 0, 0:4]), reads=[b4b[0]], writes=[b4b[0]])
                    S.op("dve", lambda e, tb=tb: e.tensor_tensor(b4[:, 1, 0:4], lg8[:, tb, 4:8], dtb[:, :], ALU.add), reads=[lg8b[tb], smb], writes=[b4b[0]])
                    S.op("act", lambda e: e.activation(b4[:, 1, 0:4], b4[:, 1, 0:4], AF.Exp), reads=[b4b[0]], writes=[b4b[0]])
                    S.op("act", lambda e: e.activation(b4[:, 1, 0:4], b4[:, 1, 0:4], AF.Ln, bias=C.one_t[:, 0:1], scale=1.0), reads=[b4b[0], C.cb], writes=[b4b[0]])
                    S.op("dve", lambda e: e.tensor_tensor(b4[:, 0, 4:8], b4[:, 1, 0:4], nega[:, :], ALU.mult), reads=[b4b[0], smb], writes=[b4b[0]])
                    beta = lambda h: b4[:, 0, h:h + 1]
                    gcol = lambda h: b4[:, 0, 4 + h:5 + h]
                    gcol2 = lambda h: b4[:, 0, 4 + h:6 + h] if h < 3 else b4[:, 0, 6:8]
                    bb = b4b[0]
                    for h in H:
                        S.op("dve", lambda e, h=h: e.tensor_scalar(G2[:, h, :], strictU, gcol(h), None, ALU.mult), reads=[bb, mkb], writes=[G2b[h]])
                        S.op("dve", lambda e, h=h: e.tensor_scalar(gB[:, h, :], C.ones_f[:, :], gcol(h), None, ALU.mult), reads=[bb, C.cb], writes=[gBb[h]])
                    for h in H:
                        bi = h
                        S.op("pe", lambda e, h=h, bi=bi: e.matmul(ps[bi][:, 0:128], triA, G2[:, h, :], start=True, stop=True),
                             reads=[G2b[h], mkb], writes=[psb[bi]], signal=False)
                        S.op("pe", lambda e, h=h, bi=bi: e.matmul(ps[bi][:, 128:256], G2[:, h, :], triA, start=True, stop=True),
                             reads=[G2b[h], mkb], writes=[psb[bi]], signal=False)
                        S.op("pe", lambda e, h=h, bi=bi: e.matmul(ps[bi][:, 256:384], gB[:, h, :], triA, start=True, stop=True),
                             reads=[gBb[h], mkb], writes=[psb[bi]], signal=False)
                        gsrc = (lambda h: b4[:, 0, 4 + h:6 + h]) if True else None
                        hh = min(h, 2)
                        off = h - hh
                        S.op("pe", lambda e, hh=hh, bi=bi: e.matmul(ps[bi][:, 384:386], triA, b4[:, 0, 4 + hh:6 + hh], start=True, stop=True),
                             reads=[bb, mkb], writes=[psb[bi]], signal=False)
                        S.op("pe", lambda e, hh=hh, bi=bi: e.matmul(ps[bi][:, 386:388], strictU, b4[:, 0, 4 + hh:6 + hh], start=True, stop=True),
                             reads=[bb, mkb], writes=[psb[bi]], signal=False)
                        S.op("pe", lambda e, hh=hh, bi=bi: e.matmul(ps[bi][:, 388:390], C.ones_f[:, :], b4[:, 0, 4 + hh:6 + hh], start=True, stop=True),
                             reads=[bb, C.cb], writes=[psb[bi]])
                        S.op("act", lambda e, h=h, bi=bi: e.activation(EX[:, h, 0:390], ps[bi][:, 0:390], AF.Exp), reads=[psb[bi]], writes=[EXb[h]])
                    gam = lambda h: EX[:, h, 384 + (h - min(h, 2)):385 + (h - min(h, 2))]
                    kds = lambda h: EX[:, h, 386 + (h - min(h, 2)):387 + (h - min(h, 2))]
                    gl_ = lambda h: EX[:, h, 388 + (h - min(h, 2)):389 + (h - min(h, 2))]
                    for h in H:
                        S.op("dve", lambda e, h=h: e.tensor_tensor(Es[:, h, :], EX[:, h, 0:128], MS, ALU.mult), reads=[EXb[h], mkb], writes=[Esb[h]])
                        S.op("dve", lambda e, h=h: e.tensor_tensor(ETc[:, h, :], EX[:, h, 128:256], MU, ALU.mult), reads=[EXb[h], mkb], writes=[ETcb[h]])
                        S.op("dve", lambda e, h=h: e.scalar_tensor_tensor(ngb[:, h, 0:1], gam(h), -1.0, beta(h), ALU.mult, ALU.mult),
                             reads=[EXb[h], bb], writes=[ngbb[h]])
                    for h in H:
                        bi = h
                        S.op("pe", lambda e, ts_=ts_, h=h, bi=bi: e.matmul(ps[bi][:, 0:128], KT[:, h, ts_], KT[:, h, ts_], start=True, stop=True),
                             reads=[KTb[h]], writes=[psb[bi]], signal=False)
                        S.op("pe", lambda e, ts_=ts_, h=h, bi=bi: e.matmul(ps[bi][:, 128:256], KT[:, h, ts_], QT[:, h, ts_], start=True, stop=True),
                             reads=[KTb[h], QTb[h]], writes=[psb[bi]], signal=False)
                        S.op("pe", lambda e, ts_=ts_, h=h, bi=bi: e.transpose(ps[bi][:, 256:384], KTf[:, h, ts_], ident), reads=[KTfb[h], mkb], writes=[psb[bi]], signal=False)
                        S.op("pe", lambda e, ts_=ts_, h=h, bi=bi: e.transpose(ps[bi][:, 384:512], VT[:, h, ts_], ident), reads=[VTb[h], mkb], writes=[psb[bi]])
                    for h in H:
                        bi = h
                        if dbgk > 0:
                            S.op("dve", lambda e, h=h, bi=bi: e.scalar_tensor_tensor(Lm[:, h, :], ps[bi][:, 0:128], beta(h), Es[:, h, :], ALU.mult, ALU.mult),
                                 reads=[psb[bi], bb, Esb[h]], writes=[Lb[h]])
                        if dbgk > 1:
                            S.op("dve", lambda e, h=h, bi=bi: e.tensor_tensor(AT[:, h, :], ps[bi][:, 128:256], ETc[:, h, :], ALU.mult),
                                 reads=[psb[bi], ETcb[h]], writes=[ATb[h]])
                        if dbgk > 2:
                            S.op("dve", lambda e, h=h, bi=bi: e.tensor_scalar(kd[:, h, :], ps[bi][:, 256:384], kds(h), None, ALU.mult),
                                 reads=[psb[bi], EXb[h]], writes=[kdb[h]])
                        if dbgk > 3:
                            S.op("dve", lambda e, h=h, bi=bi: e.tensor_scalar(vb[:, h, :], ps[bi][:, 384:512], beta(h), None, ALU.mult),
                                 reads=[psb[bi], bb], writes=[vbb[h]])
                        if dbgk > 4:
                            S.op("dve", lambda e, ts_=ts_, h=h: e.tensor_tensor(qdT[:, h, :], QT[:, h, ts_], EX[:, h, 256:384], ALU.mult),
                                 reads=[QTb[h], EXb[h]], writes=[qdTb[h]])
                        if dbgk > 5:
                            S.op("dve", lambda e, h=h, tb=tb: e.tensor_tensor(gsg[:, h, :], sg[:, tb, h * 128:(h + 1) * 128], gnw[:, :], ALU.mult),
                                 reads=[sgb[tb], smb], writes=[gsgb[h]])
                    for h in H:
                        bi = h
                        S.op("pe", lambda e, h=h, bi=bi: e.transpose(ps[bi][:, 0:128], Lm[:, h, :], ident), reads=[Lb[h], mkb], writes=[psb[bi]])
                        S.op("dve", lambda e, h=h, bi=bi: e.tensor_copy(Xat[:, h, :], ps[bi][:, 0:128]), reads=[psb[bi]], writes=[Xatb[h]])
                        S.op("dve", lambda e, h=h: e.scalar_tensor_tensor(Pm[:, h, :], Lm[:, h, :], -1.0, ident, ALU.mult, ALU.add),
                             reads=[Lb[h], mkb], writes=[Pmb[h]])
                    cur = (Lm, Lb, Xat, Xatb)
                    nxt = [(Xb, Xbb, Xbt, Xbtb), (Xa, Xab_, Xat, Xatb)]
                    for s_ in range(7):
                        X, Xbuf, XT_, XTbuf = cur
                        N_, Nb, NT, NTb = nxt[s_ % 2]
                        for h in H:
                            bi = h
                            if s_ < 5:
                                S.op("pe", lambda e, h=h, bi=bi, X=X, XT_=XT_: e.matmul(ps[bi][:, 0:128], XT_[:, h, :], X[:, h, :], start=True, stop=True),
                                     reads=[Xbuf[h], XTbuf[h]], writes=[psb[bi]], signal=False)
                            if s_ <= 5:
                                S.op("pe", lambda e, h=h, bi=bi, X=X, XT_=XT_: e.matmul(ps[bi][:, 128:256], X[:, h, :], XT_[:, h, :], start=True, stop=True),
                                     reads=[Xbuf[h], XTbuf[h]], writes=[psb[bi]], signal=(s_ == 0))
                            if s_ >= 1:
                                S.op("pe", lambda e, h=h, bi=bi, XT_=XT_: e.matmul(ps[bi][:, 256:384], XT_[:, h, :], Pm[:, h, :], start=True, stop=True),
                                     reads=[XTbuf[h], Pmb[h]], writes=[psb[bi]])
                        for h in H:
                            bi = h
                            if s_ < 5:
                                S.op("dve", lambda e, h=h, bi=bi, N_=N_: e.tensor_copy(N_[:, h, :], ps[bi][:, 0:128]), reads=[psb[bi]], writes=[Nb[h]])
                            if s_ <= 5:
                                S.op("dve", lambda e, h=h, bi=bi, NT=NT: e.tensor_copy(NT[:, h, :], ps[bi][:, 128:256]), reads=[psb[bi]], writes=[NTb[h]])
                            if s_ >= 1:
                                S.op("dve", lambda e, h=h, bi=bi: e.tensor_tensor(Pm[:, h, :], Pm[:, h, :], ps[bi][:, 256:384], ALU.add),
                                     reads=[psb[bi], Pmb[h]], writes=[Pmb[h]])
                        cur = (N_, Nb, NT, NTb)
                    for h in H:
                        bi = h
                        S.op("pe", lambda e, h=h, bi=bi: e.matmul(ps[bi][:, 0:128], Pm[:, h, :], AT[:, h, :], start=True, stop=True),
                             reads=[Pmb[h], ATb[h]], writes=[psb[bi]], signal=False)
                        S.op("pe", lambda e, h=h, bi=bi: e.matmul(ps[bi][:, 128:256], Pm[:, h, :], kd[:, h, :], start=True, stop=True),
                             reads=[Pmb[h], kdb[h]], writes=[psb[bi]])
                    for h in H:
                        bi = h
                        S.op("dve", lambda e, h=h, bi=bi: e.tensor_copy(A2T[:, h, :], ps[bi][:, 0:128]), reads=[psb[bi]], writes=[A2Tb[h]])
                        S.op("dve", lambda e, h=h, bi=bi: e.tensor_copy(K2[:, h, :], ps[bi][:, 128:256]), reads=[psb[bi]], writes=[K2b[h]])
                    for h in H:
                        bi = h
                        S.op("pe", lambda e, ts_=ts_, h=h, bi=bi: e.matmul(ps[bi][:, 0:128], KT[:, h, ts_], Sbf[:, h, :], start=True, stop=True),
                             reads=[KTb[h], Sbfb[h]], writes=[psb[bi]])
                    for h in H:
                        bi = h
                        S.op("dve", lambda e, h=h, bi=bi: e.scalar_tensor_tensor(Rbf[:, h, :], ps[bi][:, 0:128], ngb[:, h, 0:1], vb[:, h, :], ALU.mult, ALU.add),
                             reads=[psb[bi], ngbb[h], vbb[h]], writes=[Rbfb[h]])
                    for h in H:
                        bi = h
                        S.op("pe", lambda e, h=h, bi=bi: e.matmul(ps[bi][:, 128:256], qdT[:, h, :], Sbf[:, h, :], start=True, stop=False),
                             reads=[qdTb[h], Sbfb[h]], writes=[psb[bi]], signal=False)
                        S.op("pe", lambda e, h=h, bi=bi: e.matmul(ps[bi][:, 128:256], A2T[:, h, :], Rbf[:, h, :], start=False, stop=True),
                             reads=[A2Tb[h], Rbfb[h]], writes=[psb[bi]], signal=False)
                        S.op("pe", lambda e, h=h, bi=bi: e.matmul(ps[bi][:, 256:384], K2[:, h, :], Rbf[:, h, :], start=True, stop=True),
                             reads=[K2b[h], Rbfb[h]], writes=[psb[bi]])
                    for h in H:
                        bi = h
                        S.op("dve", lambda e, h=h, bi=bi: e.scalar_tensor_tensor(Sst[:, h, :], Sst[:, h, :], gl_(h), ps[bi][:, 256:384], ALU.mult, ALU.add),
                             reads=[psb[bi], EXb[h], Sstb[h]], writes=[Sstb[h]])
                        S.op("dve", lambda e, h=h: e.tensor_copy(Sbf[:, h, :], Sst[:, h, :]), reads=[Sstb[h]], writes=[Sbfb[h]])
                        S.op("dve", lambda e, h=h, bi=bi: e.tensor_copy(Xa[:, h, :], ps[bi][:, 128:256]), reads=[psb[bi]], writes=[Xab_[h]])
                        S.op("act", lambda e, h=h: e.activation(yo[:, h, :], Xa[:, h, :], AF.Square, accum_out=ngb[:, h, 1:2]),
                             reads=[Xab_[h]], writes=[yob[h], ngbb[h]])
                        S.op("act", lambda e, h=h: e.activation(ngb[:, h, 1:2], ngb[:, h, 1:2], AF.Sqrt, bias=C.eps_t[:, 0:1], scale=1.0 / 128),
                             reads=[ngbb[h], C.cb], writes=[ngbb[h]])
                        S.op("dve", lambda e, h=h: e.reciprocal(ngb[:, h, 1:2], ngb[:, h, 1:2]), reads=[ngbb[h]], writes=[ngbb[h]])
                        S.op("dve", lambda e, h=h, bi=bi: e.scalar_tensor_tensor(yo[:, h, :], Xa[:, h, :], ngb[:, h, 1:2], gsg[:, h, :], ALU.mult, ALU.mult),
                             reads=[Xab_[h], ngbb[h], gsgb[h]], writes=[yob[h]])
                    for h in H:
                        bi = h
                        S.op("pe", lambda e, h=h, bi=bi: e.transpose(ps[bi][:, 0:128], yo[:, h, :], ident), reads=[yob[h], mkb], writes=[psb[bi]])
                        S.op("dve", lambda e, ts_=ts_, h=h, bi=bi: e.tensor_copy(OT[:, 4 + h, ts_], ps[bi][:, 0:128]), reads=[psb[bi]], writes=[OTb])
                S.dma("sp", YCT(tile).rearrange("(c p) t -> p c t", p=128), OT[:, :, :], reads=[OTb], writes=[outb[tile]])
                if A.get("after_tile") is not None:
                    A["after_tile"](tile)

            emit_A(0)
            for tile in range(ntiles):
                emit_B(tile)
                if tile + 1 < ntiles:
                    S.record()
                    emit_D(tile)
                    lD = S.stop()
                    S.record()
                    emit_A(tile + 1)
                    lA = S.stop()
                    S.replay([lD, lA])
                else:
                    emit_D(tile)
            print('phase1 sbuf', nc.sbuf_base, nc.sbuf_top)
            S.barrier()


POOL_WINDOWS = (2, 4, 8, 16)


def phase1_inputs(inputs, b, hg):
    w = np.asarray(inputs["ab_w_in"][0], np.float32)
    PW, GW = 2048, 2048
    cols = np.concatenate([
        np.arange(hg * 512, (hg + 1) * 512),
        PW + 3 * GW + np.arange(hg * 512, (hg + 1) * 512),
        PW + np.arange(hg * 512, (hg + 1) * 512),
        PW + GW + np.arange(hg * 512, (hg + 1) * 512),
        PW + 2 * GW + np.arange(hg * 512, (hg + 1) * 512),
        PW + 4 * GW + np.arange(hg * 4, (hg + 1) * 4),
        PW + 4 * GW + 16 + np.arange(hg * 4, (hg + 1) * 4),
    ])
    wc = np.ascontiguousarray(w[:, cols])
    conv = np.asarray(inputs["gdn_conv"][0], np.float32)
    cwl = np.stack([conv[:, s * GW + hg * 512: s * GW + (hg + 1) * 512] for s in range(3)], 0)
    cwl = cwl.reshape(3, 4, 4, 128).transpose(3, 0, 2, 1).reshape(128, 12, 4)
    bc = lambda v: np.ascontiguousarray(np.broadcast_to(np.asarray(v, np.float32)[None, :], (128, len(v))))
    win = POOL_WINDOWS[hg]
    pos = np.arange(256)
    def band_full(first):
        B = np.zeros((256, 128), np.float32)
        for t in range(128):
            cnt = min(t + 1, win) if first else win
            for s in range(max(0, 128 + t - win + 1) if not first else 128 + max(0, t - win + 1), 128 + t + 1):
                B[s, t] = 1.0 / cnt
            B[128 + t, t] -= 1.0
        return B
    Bn = band_full(False)
    B0 = band_full(True)
    band = np.stack([B0[128:], Bn[128:], Bn[:128]], 0)
    k = np.arange(128)
    triA = (k[:, None] <= k[None, :]).astype(np.float32)
    strictU = (k[:, None] > k[None, :]).astype(np.float32)
    MS = (k[:, None] > k[None, :]).astype(np.float32)
    MU = (k[:, None] <= k[None, :]).astype(np.float32)
    masks = np.stack([triA, strictU, MS, MU, np.eye(128, dtype=np.float32)], 0)
    return {
        "xb": np.ascontiguousarray(np.asarray(inputs["x"][b], np.float32)),
        "g_pre": np.asarray(inputs["norm_mix_pre"][0], np.float32),
        "w_in_c": wc,
        "conv_w": np.ascontiguousarray(cwl),
        "neg_a": bc(inputs["gdn_a_log"][0][hg * 4:(hg + 1) * 4]),
        "dt_b": bc(inputs["gdn_dt_bias"][0][hg * 4:(hg + 1) * 4]),
        "gn_w": bc(inputs["gdn_norm"][0]),
        "pool_wg": np.ascontiguousarray(np.asarray(inputs["pool_w"][0][hg], np.float32)),
        "pool_sc": np.ascontiguousarray(np.asarray(inputs["pool_scale"][0][hg * 512:(hg + 1) * 512], np.float32)),
        "bandm": np.ascontiguousarray(band), "masks": np.ascontiguousarray(masks),
    }


def kernel(**inputs):
    n = 8
    nc = build_fused()
    consts = phase2_consts(inputs)
    x = np.asarray(inputs["x"], np.float32)
    maps = []
    for c in range(n):
        b, j = c // 4, c % 4
        m = dict(consts)
        m.update(phase1_inputs(inputs, b, j))
        m["x_own"] = np.ascontiguousarray(x[b, j * TP:(j + 1) * TP, :])
        sel = np.zeros((128, 4), np.float32)
        sel[:, j] = 1.0
        m["sel"] = sel
        maps.append(m)
    res = run_bass_kernel_spmd(nc, maps, core_ids=list(range(n)))
    out = np.empty((2, SEQ, D), np.float32)
    for c in range(n):
        b, j = c // 4, c % 4
        out[b, j * TP:(j + 1) * TP, :] = np.asarray(res.results[c]["out"], np.float32)
    return out
```

```python
import contextlib
import numpy as np
import ml_dtypes
import concourse.bass as bass
import concourse.mybir as mybir
from concourse.bass_utils import run_bass_kernel_spmd

F32 = mybir.dt.float32
BF16 = mybir.dt.bfloat16
AF = mybir.ActivationFunctionType
ALU = mybir.AluOpType
AX = mybir.AxisListType

D = 4096
DFF = 11008
KC = D // 128
TP = 1024
SEQ = 4096
EPS = 1e-6


class Buf:
    __slots__ = ("name", "w", "r", "strict")

    def __init__(self, name="", strict=False):
        self.name = name
        self.w = None
        self.r = []
        self.strict = strict


def SBuf(name=""):
    return Buf(name, True)


class Sched:
    ENGS = ("pe", "act", "dve", "pool", "sp")

    def __init__(self, nc, n_dma_sems=48):
        self.nc = nc
        self.prog = {e: [] for e in self.ENGS}
        self.sems = {e: nc.alloc_semaphore(name="s_" + e) for e in self.ENGS}
        self.cnt = {e: 0 for e in self.ENGS}
        self.waited = {e: {} for e in self.ENGS}
        self.dsems = [nc.alloc_semaphore(name="d%d" % i) for i in range(n_dma_sems)]
        self.dval = [0] * n_dma_sems
        self.dnext = 0
        self.n_ins = 0
        self._rec = None
        self.nosame = 1
        self.sems["cc"] = nc.alloc_semaphore(name="s_cc")
        self.ccval = 0

    def _sem(self, key):
        return self.sems[key] if isinstance(key, str) else self.dsems[key]

    def _collect(self, eng, reads, writes):
        need = {}

        relax = self.nosame and eng in ("dve", "act")

        def add(tok, strict):
            if tok is None:
                return
            k, v = tok
            if k == eng and (eng == "pe" or (relax and not strict)):
                return
            if need.get(k, 0) < v:
                need[k] = v
        for b in reads:
            add(b.w, b.strict)
        for b in writes:
            add(b.w, b.strict)
            for t in b.r:
                add(t, b.strict)
        waits = []
        wd = self.waited[eng]
        for k, v in need.items():
            if wd.get(k, 0) >= v:
                continue
            wd[k] = v
            waits.append((self._sem(k), v))
        return waits

    def _commit(self, tok, reads, writes):
        for b in reads:
            b.r.append(tok)
        for b in writes:
            b.w = tok
            b.r = []

    def record(self):
        self._rec = []

    def stop(self):
        r, self._rec = self._rec, None
        return r

    def replay(self, lists):
        pos = [0] * len(lists)
        tot = max(len(l) for l in lists)
        for step in range(1, tot + 1):
            for i, l in enumerate(lists):
                upto = (step * len(l)) // tot
                while pos[i] < upto:
                    kind, a, kw = l[pos[i]]
                    pos[i] += 1
                    {"op": self.op, "dma": self.dma, "cc": self.collective}[kind](*a, **kw)

    def op(self, eng, fn, reads=(), writes=(), signal=True):
        if self._rec is not None:
            self._rec.append(("op", (eng, fn), dict(reads=list(reads), writes=list(writes), signal=signal)))
            return
        waits = self._collect(eng, reads, writes)
        if signal:
            self.cnt[eng] += 1
            tok = (eng, self.cnt[eng])
        else:
            tok = (eng, self.cnt[eng] + 1)
        sem = self.sems[eng]

        def run(e, waits=waits, fn=fn, sem=sem, signal=signal):
            for s, v in waits:
                e.wait_ge(s, v)
            ins = fn(e)
            if signal:
                ins.then_inc(sem, 1)
        self.prog[eng].append(run)
        self._commit(tok, reads, writes)
        self.n_ins += 1

    def dma(self, eng, out_ap, in_ap, reads=(), writes=(), **kw):
        if self._rec is not None:
            self._rec.append(("dma", (eng, out_ap, in_ap), dict(reads=list(reads), writes=list(writes), **kw)))
            return
        i = self.dnext
        self.dnext = (self.dnext + 1) % len(self.dsems)
        waits = self._collect(eng, reads, writes)
        wd = self.waited[eng]
        if self.dval[i] > 0 and wd.get(i, 0) < self.dval[i]:
            wd[i] = self.dval[i]
            waits.append((self.dsems[i], self.dval[i]))
        self.dval[i] += 16
        tok = (i, self.dval[i])
        sem = self.dsems[i]

        def run(e, waits=waits, sem=sem):
            for s, v in waits:
                e.wait_ge(s, v)
            e.dma_start(out=out_ap, in_=in_ap, **kw).then_inc(sem, 16)
        self.prog[eng].append(run)
        self._commit(tok, reads, writes)
        self.n_ins += 1

    def collective(self, kind, in_ap, out_ap, groups, reads=(), writes=()):
        if self._rec is not None:
            self._rec.append(("cc", (kind, in_ap, out_ap, groups), dict(reads=list(reads), writes=list(writes))))
            return
        waits = self._collect("pool", reads, writes)
        self.ccval += 1
        tok = ("cc", self.ccval)
        sem = self.sems["cc"]

        def run(e, waits=waits, sem=sem):
            for s_, v in waits:
                e.wait_ge(s_, v)
            e.collective_compute(kind, ALU.bypass, replica_groups=groups, ins=[in_ap], outs=[out_ap]).then_inc(sem, 1)
        self.prog["pool"].append(run)
        self._commit(tok, reads, writes)

    def barrier(self):
        for e in self.ENGS:
            waits = []
            wd = self.waited[e]
            for e2 in self.ENGS:
                if e2 != e and self.cnt[e2] > wd.get(e2, 0):
                    wd[e2] = self.cnt[e2]
                    waits.append((self.sems[e2], self.cnt[e2]))
            for i, v in enumerate(self.dval):
                if v > wd.get(i, 0):
                    wd[i] = v
                    waits.append((self.dsems[i], v))
            if self.ccval > wd.get("cc", 0):
                wd["cc"] = self.ccval
                waits.append((self.sems["cc"], self.ccval))

            def run(en, waits=waits):
                for s, v in waits:
                    en.wait_ge(s, v)
            self.prog[e].append(run)

    def finish(self):
        self.barrier()
        nc = self.nc
        with nc.Block() as block:
            @block.tensor
            def _(e):
                for f in self.prog["pe"]:
                    f(e)

            @block.scalar
            def _(e):
                for f in self.prog["act"]:
                    f(e)

            @block.vector
            def _(e):
                for f in self.prog["dve"]:
                    f(e)

            @block.gpsimd
            def _(e):
                for f in self.prog["pool"]:
                    f(e)

            @block.sync
            def _(e):
                for f in self.prog["sp"]:
                    f(e)


class Ctx:
    def __init__(self, nc, S, st, NW=8, pfx=""):
        self.nc, self.S, self.st, self.pfx = nc, S, st, pfx
        self.ps = [st.enter_context(nc.psum_tensor(pfx + "ps%d" % i, [128, 512], F32)) for i in range(8)]
        self.psb = [Buf("ps%d" % i) for i in range(8)]
        self.ones_bf = self.sb("ones_bf", [128, 128], BF16)
        self.ones_f = self.sb("ones_f", [128, 128], F32)
        self.cb = SBuf("consts")
        S.op("dve", lambda e: e.memset(self.ones_bf[:], 1.0), writes=[self.cb])
        S.op("dve", lambda e: e.memset(self.ones_f[:], 1.0), writes=[self.cb])
        self.NW = NW
        self.wt = [self.sb("wt%d" % i, [128, 4096], BF16) for i in range(self.NW)]
        self.wtb = [Buf("wt%d" % i) for i in range(self.NW)]
        self.wnext = 0
        self.dmaq = 0

    def sb(self, name, shape, dt):
        return self.st.enter_context(self.nc.sbuf_tensor(self.pfx + name, shape, dt))

    def wslot(self):
        i = self.wnext
        self.wnext = (i + 1) % self.NW
        return self.wt[i], self.wtb[i]

    def q(self):
        self.dmaq ^= 1
        return "sp" if self.dmaq else "act"


def gemm_cg(C, W, c0, CW, rhs, rhsb, KCr, T, banks, tokmajor=False):
    S = C.S
    nth = T // 512
    noc = CW // 128
    ukc = 4096 // CW
    nu = (KCr + ukc - 1) // ukc
    Wv = W.rearrange("(kc p) n -> p kc n", p=128)
    for u in range(nu):
        k0 = u * ukc
        nk = min(ukc, KCr - k0)
        wt, wb = C.wslot()
        wv = wt[:, 0:nk * CW].rearrange("p (k n) -> p k n", n=CW)
        S.dma("pool", wv, Wv[:, k0:k0 + nk, c0:c0 + CW], writes=[wb])
        for oc in range(noc):
            for th in range(nth):
                bi = banks[oc * nth + th]
                for j in range(nk):
                    kc = k0 + j
                    S.op("pe", lambda e, bi=bi, wv=wv, j=j, oc=oc, kc=kc, th=th: e.matmul(
                        C.ps[bi][:, :], wv[:, j, oc * 128:(oc + 1) * 128], rhs[:, kc, th * 512:(th + 1) * 512],
                        start=(kc == 0), stop=(kc == KCr - 1)),
                        reads=[wb, rhsb], writes=[C.psb[bi]], signal=(j == nk - 1))


def load_gain(C, sb, name, g_ap, ncol=KC):
    t = sb(name, [128, ncol], F32)
    b = Buf(name)
    C.S.dma("sp", t[:, :], g_ap.rearrange("(kc p) -> p kc", p=128), writes=[b], allow_slow_non_contiguous=True)
    return t, b


def colsum_rstd(C, src_dram, srcb, nkc, T, rstd, rstdb, xin, xinb, sq, sqb, scale, tmp, tmpb):
    S = C.S
    nth = T // 512
    for kc in range(nkc):
        r = kc % len(xin)
        S.dma(C.q(), xin[r][:, :], src_dram[kc * 128:(kc + 1) * 128, :], reads=[srcb[kc]], writes=[xinb[r]])
        r2 = kc % len(sq)
        S.op("act", lambda e, r=r, r2=r2: e.activation(sq[r2][:, :], xin[r][:, :], AF.Square), reads=[xinb[r]], writes=[sqb[r2]])
        for th in range(nth):
            S.op("pe", lambda e, th=th, r2=r2, kc=kc: e.matmul(C.ps[th][:, :], C.ones_bf[:, :], sq[r2][:, th * 512:(th + 1) * 512],
                                                             start=(kc == 0), stop=(kc == nkc - 1)),
                 reads=[sqb[r2], C.cb], writes=[C.psb[th]])
    for th in range(nth):
        sl = slice(th * 512, (th + 1) * 512)
        S.op("act", lambda e, th=th, sl=sl: e.activation(tmp[:, sl], C.ps[th][:, :], AF.Sqrt, bias=C.eps_t[:, 0:1], scale=scale),
             reads=[C.psb[th], C.cb], writes=[tmpb])
        S.op("dve", lambda e, sl=sl: e.reciprocal(rstd[:, sl], tmp[:, sl]), reads=[tmpb], writes=[rstdb])


def norm_stage(C, XT, XTb, gain_ap, HT, HTb, tag):
    S, nc = C.S, C.nc
    with contextlib.ExitStack() as st:
        sb = lambda n, s, d: st.enter_context(nc.sbuf_tensor(tag + n, s, d))
        xin = [sb("xin%d" % i, [128, TP], F32) for i in range(3)]
        xinb = [Buf() for _ in range(3)]
        sq = [sb("sq%d" % i, [128, TP], BF16) for i in range(2)]
        sqb = [Buf() for _ in range(2)]
        rstd = sb("rstd", [128, TP], F32)
        rstdb = Buf()
        tmp = sb("tmp", [128, TP], F32)
        tmpb = Buf()
        g = sb("g", [128, KC], F32)
        gb = Buf()
        S.dma("sp", g[:, :], gain_ap.rearrange("(kc p) -> p kc", p=128), writes=[gb], allow_slow_non_contiguous=True)
        colsum_rstd(C, XT, XTb, KC, TP, rstd, rstdb, xin, xinb, sq, sqb, 1.0 / D, tmp, tmpb)
        for kc in range(KC):
            r = kc % 3
            S.dma(C.q(), xin[r][:, :], XT[kc * 128:(kc + 1) * 128, :], reads=[XTb[kc]], writes=[xinb[r]])
            S.op("dve", lambda e, r=r, kc=kc: e.scalar_tensor_tensor(HT[:, kc, :], xin[r][:, :], g[:, kc:kc + 1], rstd[:, :],
                                                                    ALU.mult, ALU.mult),
                 reads=[xinb[r], gb, rstdb], writes=[HTb])
        S.barrier()


def postnorm_resid(C, YT, YTb, gain_ap, XT, XTb, tag):
    S, nc = C.S, C.nc
    with contextlib.ExitStack() as st:
        sb = lambda n, s, d: st.enter_context(nc.sbuf_tensor(tag + n, s, d))
        xin = [sb("xin%d" % i, [128, TP], F32) for i in range(3)]
        xinb = [Buf() for _ in range(3)]
        yin = [sb("yin%d" % i, [128, TP], F32) for i in range(3)]
        yinb = [Buf() for _ in range(3)]
        sq = [sb("sq%d" % i, [128, TP], BF16) for i in range(2)]
        sqb = [Buf() for _ in range(2)]
        rstd = sb("rstd", [128, TP], F32)
        rstdb = Buf()
        tmp = sb("tmp", [128, TP], F32)
        tmpb = Buf()
        g = sb("g", [128, KC], F32)
        gb = Buf()
        S.dma("sp", g[:, :], gain_ap.rearrange("(kc p) -> p kc", p=128), writes=[gb], allow_slow_non_contiguous=True)
        colsum_rstd(C, YT, YTb, KC, TP, rstd, rstdb, yin, yinb, sq, sqb, 1.0 / D, tmp, tmpb)
        for kc in range(KC):
            r = kc % 3
            rows = slice(kc * 128, (kc + 1) * 128)
            S.dma("sp", yin[r][:, :], YT[rows, :], reads=[YTb[kc]], writes=[yinb[r]])
            S.dma("act", xin[r][:, :], XT[rows, :], reads=[XTb[kc]], writes=[xinb[r]])
            S.op("dve", lambda e, r=r, kc=kc: e.scalar_tensor_tensor(yin[r][:, :], yin[r][:, :], g[:, kc:kc + 1], rstd[:, :],
                                                                    ALU.mult, ALU.mult),
                 reads=[yinb[r], gb, rstdb], writes=[yinb[r]])
            S.op("pool", lambda e, r=r: e.tensor_tensor(xin[r][:, :], xin[r][:, :], yin[r][:, :], ALU.add),
                 reads=[yinb[r], xinb[r]], writes=[xinb[r]])
            S.dma("sp", XT[rows, :], xin[r][:, :], reads=[xinb[r]], writes=[XTb[kc]])
        S.barrier()


def gemm_to_dram(C, W, N, rhs, rhsb, KCr, T, OUT, OUTb, tok0, func, odt, tag):
    S, nc = C.S, C.nc
    CW = 256 if T == 1024 else 512
    nth = T // 512
    noc = CW // 128
    with contextlib.ExitStack() as st:
        ot = [st.enter_context(nc.sbuf_tensor(tag + "ot%d" % i, [128, 512], odt)) for i in range(4)]
        otb = [Buf() for _ in range(4)]
        oi = 0
        for cg in range(N // CW):
            banks = [(cg % 2) * 4 + i for i in range(4)]
            gemm_cg(C, W, cg * CW, CW, rhs, rhsb, KCr, T, banks)
            for oc in range(noc):
                for th in range(nth):
                    bi = banks[oc * nth + th]
                    o = oi % 4
                    oi += 1
                    if func is None:
                        S.op("dve", lambda e, o=o, bi=bi: e.tensor_copy(ot[o][:, :], C.ps[bi][:, :]),
                             reads=[C.psb[bi]], writes=[otb[o]])
                    else:
                        S.op("act", lambda e, o=o, bi=bi: e.activation(ot[o][:, :], C.ps[bi][:, :], func),
                             reads=[C.psb[bi]], writes=[otb[o]])
                    row = cg * CW + oc * 128
                    S.dma(C.q(), OUT[row:row + 128, tok0 + th * 512: tok0 + (th + 1) * 512], ot[o][:, :],
                          reads=[otb[o]], writes=[OUTb[row // 128]])
        S.barrier()


def load_fm(C, SRC, SRCb, nkc, T, tok0, dst, dstb, per=8):
    v = SRC.rearrange("(kc p) t -> p kc t", p=128)
    for k0 in range(0, nkc, per):
        k1 = min(nkc, k0 + per)
        C.S.dma(C.q(), dst[:, k0:k1, 0:T], v[:, k0:k1, tok0:tok0 + T], reads=[SRCb[k] for k in range(k0, k1)], writes=[dstb])


def ffn_stage(C, XT, XTb, YT, YTb, HID, HIDb, g_pre, g_post, Wg, Wu, Wd, tag):
    S, nc = C.S, C.nc
    with contextlib.ExitStack() as st:
        HT = st.enter_context(nc.sbuf_tensor(tag + "HT", [128, KC, TP], BF16))
        HTb = Buf()
        norm_stage(C, XT, XTb, g_pre, HT, HTb, tag + "n")
        sl_t = [st.enter_context(nc.sbuf_tensor(tag + "sl%d" % i, [128, 512], F32)) for i in range(2)]
        slb = [Buf() for _ in range(2)]
        ot = [st.enter_context(nc.sbuf_tensor(tag + "ho%d" % i, [128, 512], BF16)) for i in range(4)]
        otb = [Buf() for _ in range(4)]
        oi = 0
        for cg in range(DFF // 256):
            bg = [0, 1, 2, 3]
            bu = [4, 5, 6, 7]
            gemm_cg(C, Wg, cg * 256, 256, HT, HTb, KC, TP, bg)
            gemm_cg(C, Wu, cg * 256, 256, HT, HTb, KC, TP, bu)
            for oc in range(2):
                for th in range(2):
                    o = oi % 4
                    s2 = oi % 2
                    oi += 1
                    b1, b2 = bg[oc * 2 + th], bu[oc * 2 + th]
                    S.op("act", lambda e, s2=s2, b1=b1: e.activation(sl_t[s2][:, :], C.ps[b1][:, :], AF.Silu),
                         reads=[C.psb[b1]], writes=[slb[s2]])
                    S.op("dve", lambda e, s2=s2, b2=b2, o=o: e.tensor_tensor(ot[o][:, :], sl_t[s2][:, :], C.ps[b2][:, :], ALU.mult),
                         reads=[slb[s2], C.psb[b2]], writes=[otb[o]])
                    row = cg * 256 + oc * 128
                    S.dma(C.q(), HID[row:row + 128, th * 512:(th + 1) * 512], ot[o][:, :], reads=[otb[o]], writes=[HIDb[row // 128]])
        S.barrier()
    KF = DFF // 128
    with contextlib.ExitStack() as st:
        RH = st.enter_context(nc.sbuf_tensor(tag + "RH", [128, KF, 512], BF16))
        RHb = Buf()
        for th2 in range(2):
            load_fm(C, HID, HIDb, KF, 512, th2 * 512, RH, RHb)
            gemm_to_dram(C, Wd, D, RH, RHb, KF, 512, YT, YTb, th2 * 512, None, F32, tag + "d%d" % th2)
    postnorm_resid(C, YT, YTb, g_post, XT, XTb, tag + "p")


def about_stage(C, XT, XTb, YT, YTb, YC, YCb, g_post, Wo, tag):
    S, nc = C.S, C.nc
    with contextlib.ExitStack() as st:
        R = st.enter_context(nc.sbuf_tensor(tag + "R", [128, KC, TP], BF16))
        Rb = Buf()
        load_fm(C, YC, YCb, KC, TP, 0, R, Rb)
        gemm_to_dram(C, Wo, D, R, Rb, KC, TP, YT, YTb, 0, None, F32, tag + "g")
    postnorm_resid(C, YT, YTb, g_post, XT, XTb, tag + "p")


def sgu_stage(C, XT, XTb, YT, YTb, UT, UTb, VTM, VTMb, g_pre, g_post, Win, ln_g, ln_b, wsT, bs, maskT, Wout, tag):
    S, nc = C.S, C.nc
    with contextlib.ExitStack() as st:
        HT = st.enter_context(nc.sbuf_tensor(tag + "HT", [128, KC, TP], BF16))
        HTb = Buf()
        norm_stage(C, XT, XTb, g_pre, HT, HTb, tag + "n")
        gemm_to_dram(C, Win[:, 0:D], D, HT, HTb, KC, TP, UT, UTb, 0, AF.Gelu, BF16, tag + "u")
        vo = [st.enter_context(nc.sbuf_tensor(tag + "vo%d" % i, [128, 512], F32)) for i in range(3)]
        vob = [Buf() for _ in range(3)]
        Wv = Win.rearrange("(kc p) n -> p kc n", p=128)
        oi = 0
        for cg in range(D // 512):
            slots = []
            for u in range(4):
                wt, wb = C.wslot()
                wv = wt[:, :].rearrange("p (k n) -> p k n", n=512)
                S.dma("pool", wv, Wv[:, u * 8:(u + 1) * 8, D + cg * 512: D + (cg + 1) * 512], writes=[wb])
                slots.append((wv, wb))
            for tb in range(TP // 128):
                bi = oi % 8
                for kc in range(KC):
                    wv, wb = slots[kc // 8]
                    S.op("pe", lambda e, bi=bi, wv=wv, kc=kc, tb=tb: e.matmul(
                        C.ps[bi][:, :], HT[:, kc, tb * 128:(tb + 1) * 128], wv[:, kc % 8, :], start=(kc == 0), stop=(kc == KC - 1)),
                        reads=[wb, HTb], writes=[C.psb[bi]], signal=(kc % 8 == 7))
                o = oi % 3
                oi += 1
                S.op("act", lambda e, o=o, bi=bi: e.activation(vo[o][:, :], C.ps[bi][:, :], AF.Gelu), reads=[C.psb[bi]], writes=[vob[o]])
                S.dma(C.q(), VTM[tb * 128:(tb + 1) * 128, cg * 512:(cg + 1) * 512], vo[o][:, :], reads=[vob[o]], writes=[VTMb[tb]])
        S.barrier()
    with contextlib.ExitStack() as st:
        sb = lambda n, s, d: st.enter_context(nc.sbuf_tensor(tag + n, s, d))
        PT = sb("PT", [128, KC, TP], BF16)
        PTb = Buf()
        load_fm(C, UT, UTb, KC, TP, 0, PT, PTb)
        mk = sb("mk", [128, 128], F32)
        mkb = Buf()
        S.dma("sp", mk[:, :], maskT, writes=[mkb])
        wsbf = sb("wsbf", [128, 16, 128], BF16)
        wsbfb = Buf()
        S.dma("pool", wsbf[:, :, :], wsT, writes=[wsbfb])
        for g in range(16):
            S.op("dve", lambda e, g=g: e.tensor_tensor(wsbf[:, g, :], wsbf[:, g, :], mk[:, :], ALU.mult), reads=[wsbfb, mkb], writes=[wsbfb])
        BS = sb("BS", [128, 16, 128], F32)
        BSb = Buf()
        S.dma("sp", BS[:, :, :], bs, writes=[BSb])
        RS = sb("RS", [128, 16, 128], F32)
        RSb = Buf()
        for q4 in range(4):
            S.op("pe", lambda e, q4=q4: e.matmul(C.ps[q4][:, :], C.ones_bf[:, :], wsbf[:, q4 * 4:(q4 + 1) * 4, :], start=True, stop=True),
                 reads=[wsbfb, C.cb], writes=[C.psb[q4]])
            S.op("dve", lambda e, q4=q4: e.tensor_copy(RS[:, q4 * 4:(q4 + 1) * 4, :], C.ps[q4][:, :]), reads=[C.psb[q4]], writes=[RSb])
        lg, lgb = load_gain(C, sb, "lg", ln_g)
        lb, lbb = load_gain(C, sb, "lb", ln_b)
        T2 = sb("T2", [128, KC, 128], F32)
        T2b = Buf()
        for kc in range(KC):
            S.op("dve", lambda e, kc=kc: e.scalar_tensor_tensor(T2[:, kc, :], RS[:, kc // 2, :], lb[:, kc:kc + 1], BS[:, kc // 2, :],
                                                               ALU.mult, ALU.add), reads=[RSb, BSb, lbb], writes=[T2b])
        vin = [sb("vin0", [128, D], F32)] * 2
        vinb = [Buf()] * 2
        vh = [sb("vh0", [128, D], BF16)] * 2
        vhb = [Buf()] * 2
        junk = vh[0]
        junkb = vhb[0]
        st4 = [sb("st%d" % i, [128, 8], F32) for i in range(2)]
        st4b = [SBuf() for _ in range(2)]
        sv = [sb("sv%d" % i, [128, 128], F32) for i in range(3)]
        svb = [Buf() for _ in range(3)]
        oi = 0
        for tb in range(TP // 128):
            r = tb % 2
            S.dma("sp", vin[r][:, 0:D // 2], VTM[tb * 128:(tb + 1) * 128, 0:D // 2], reads=[VTMb[tb]], writes=[vinb[r]])
            S.dma("act", vin[r][:, D // 2:D], VTM[tb * 128:(tb + 1) * 128, D // 2:D], reads=[VTMb[tb]], writes=[vinb[r]])
            s4 = st4[r]
            S.op("act", lambda e, r=r, s4=s4: e.activation(junk[:, :], vin[r][:, :], AF.Identity, accum_out=s4[:, 0:1]),
                 reads=[vinb[r]], writes=[junkb, st4b[r]])
            S.op("act", lambda e, r=r, s4=s4: e.activation(junk[:, :], vin[r][:, :], AF.Square, accum_out=s4[:, 1:2]),
                 reads=[vinb[r]], writes=[junkb, st4b[r]])
            S.op("dve", lambda e, s4=s4: e.tensor_scalar(s4[:, 2:3], s4[:, 0:1], 1.0 / D, None, ALU.mult), reads=[st4b[r]], writes=[st4b[r]])
            S.op("dve", lambda e, s4=s4: e.tensor_tensor(s4[:, 3:4], s4[:, 2:3], s4[:, 2:3], ALU.mult), reads=[st4b[r]], writes=[st4b[r]])
            S.op("dve", lambda e, s4=s4: e.scalar_tensor_tensor(s4[:, 4:5], s4[:, 1:2], 1.0 / D, s4[:, 3:4], ALU.mult, ALU.subtract),
                 reads=[st4b[r]], writes=[st4b[r]])
            S.op("act", lambda e, s4=s4: e.activation(s4[:, 5:6], s4[:, 4:5], AF.Sqrt, bias=C.eps_t[:, 0:1], scale=1.0),
                 reads=[st4b[r], C.cb], writes=[st4b[r]])
            S.op("dve", lambda e, s4=s4: e.reciprocal(s4[:, 6:7], s4[:, 5:6]), reads=[st4b[r]], writes=[st4b[r]])
            S.op("dve", lambda e, s4=s4: e.scalar_tensor_tensor(s4[:, 7:8], s4[:, 2:3], -1.0, s4[:, 6:7], ALU.mult, ALU.mult),
                 reads=[st4b[r]], writes=[st4b[r]])
            S.op("dve", lambda e, r=r, s4=s4: e.tensor_scalar(vh[r][:, :], vin[r][:, :], s4[:, 6:7], s4[:, 7:8], ALU.mult, ALU.add),
                 reads=[vinb[r], st4b[r]], writes=[vhb[r]])
            for k4 in range(KC // 4):
                bi = oi % 8
                oi += 1
                for j in range(4):
                    kc = k4 * 4 + j
                    S.op("pe", lambda e, bi=bi, j=j, kc=kc, r=r: e.matmul(C.ps[bi][:, j * 128:(j + 1) * 128], vh[r][:, kc * 128:(kc + 1) * 128],
                                                                         wsbf[:, kc // 2, :], start=True, stop=True),
                         reads=[vhb[r], wsbfb], writes=[C.psb[bi]], signal=(j == 3))
                for j in range(4):
                    kc = k4 * 4 + j
                    s3 = (k4 * 4 + j) % 3
                    S.op("dve", lambda e, bi=bi, j=j, kc=kc, s3=s3: e.scalar_tensor_tensor(
                        sv[s3][:, :], C.ps[bi][:, j * 128:(j + 1) * 128], lg[:, kc:kc + 1], T2[:, kc, :], ALU.mult, ALU.add),
                        reads=[C.psb[bi], lgb, T2b], writes=[svb[s3]])
                    S.op("pool", lambda e, kc=kc, s3=s3, tb=tb: e.tensor_tensor(
                        PT[:, kc, tb * 128:(tb + 1) * 128], PT[:, kc, tb * 128:(tb + 1) * 128], sv[s3][:, :], ALU.mult),
                        reads=[svb[s3], PTb], writes=[PTb])
        gemm_to_dram(C, Wout, D, PT, PTb, KC, TP, YT, YTb, 0, None, F32, tag + "o")
    postnorm_resid(C, YT, YTb, g_post, XT, XTb, tag + "p")


def xin_stage(C, x_own, XT, XTb, ident, identb):
    S, nc = C.S, C.nc
    with contextlib.ExitStack() as st:
        xr = [st.enter_context(nc.sbuf_tensor("xi_r%d" % i, [128, D], F32)) for i in range(2)]
        xrb = [Buf() for _ in range(2)]
        xo = [st.enter_context(nc.sbuf_tensor("xi_o%d" % i, [128, 4, 128], F32)) for i in range(3)]
        xob = [Buf() for _ in range(3)]
        inb = Buf()
        oi = 0
        for tb in range(TP // 128):
            r = tb % 2
            S.dma("sp", xr[r][:, 0:D // 2], x_own[tb * 128:(tb + 1) * 128, 0:D // 2], reads=[inb], writes=[xrb[r]])
            S.dma("act", xr[r][:, D // 2:D], x_own[tb * 128:(tb + 1) * 128, D // 2:D], reads=[inb], writes=[xrb[r]])
            for k4 in range(KC // 4):
                bi = oi % 8
                o = oi % 3
                oi += 1
                for j in range(4):
                    kc = k4 * 4 + j
                    S.op("pe", lambda e, bi=bi, j=j, kc=kc, r=r: e.transpose(C.ps[bi][:, j * 128:(j + 1) * 128],
                                                                            xr[r][:, kc * 128:(kc + 1) * 128], ident[:, :]),
                         reads=[xrb[r], identb], writes=[C.psb[bi]], signal=(j == 3))
                S.op("dve", lambda e, bi=bi, o=o: e.tensor_copy(xo[o][:, :, :], C.ps[bi][:, :]), reads=[C.psb[bi]], writes=[xob[o]])
                dst = XT[k4 * 512:(k4 + 1) * 512, tb * 128:(tb + 1) * 128].rearrange("(j p) t -> p j t", p=128)
                S.dma(C.q(), dst, xo[o][:, :, :], reads=[xob[o]], writes=[XTb[k4 * 4 + j] for j in range(4)])
        S.barrier()


def xout_stage(C, XT, XTb, out, outb, ident, identb):
    S, nc = C.S, C.nc
    with contextlib.ExitStack() as st:
        xr = [st.enter_context(nc.sbuf_tensor("xo_r%d" % i, [128, TP], F32)) for i in range(2)]
        xrb = [Buf() for _ in range(2)]
        xo = [st.enter_context(nc.sbuf_tensor("xo_o%d" % i, [128, 4, 128], F32)) for i in range(3)]
        xob = [Buf() for _ in range(3)]
        oi = 0
        for kc in range(KC):
            r = kc % 2
            S.dma(C.q(), xr[r][:, :], XT[kc * 128:(kc + 1) * 128, :], reads=[XTb[kc]], writes=[xrb[r]])
            for t4 in range(TP // 512):
                bi = oi % 8
                o = oi % 3
                oi += 1
                for j in range(4):
                    tb = t4 * 4 + j
                    S.op("pe", lambda e, bi=bi, j=j, tb=tb, r=r: e.transpose(C.ps[bi][:, j * 128:(j + 1) * 128],
                                                                            xr[r][:, tb * 128:(tb + 1) * 128], ident[:, :]),
                         reads=[xrb[r], identb], writes=[C.psb[bi]], signal=(j == 3))
                S.op("dve", lambda e, bi=bi, o=o: e.tensor_copy(xo[o][:, :, :], C.ps[bi][:, :]), reads=[C.psb[bi]], writes=[xob[o]])
                dst = out[t4 * 512:(t4 + 1) * 512, kc * 128:(kc + 1) * 128].rearrange("(j p) f -> p j f", p=128)
                S.dma(C.q(), dst, xo[o][:, :, :], reads=[xob[o]], writes=[outb])
        S.barrier()


def dram_in(nc, name, shape, dt=F32):
    return nc.dram_tensor(name, list(shape), dt, kind="ExternalInput").ap()


def dram_scratch(nc, name, shape, dt=F32):
    return nc.dram_tensor(name, list(shape), dt, kind="Internal").ap()


def phase2_decl(nc):
    A = {}
    A["x_own"] = dram_in(nc, "x_own", [TP, D])
    A["ident_d"] = dram_in(nc, "ident", [128, 128])
    A["nmpost"] = dram_in(nc, "norm_mix_post", [2, D])
    A["nmpre"] = dram_in(nc, "norm_mix_pre", [2, D])
    A["nfpre"] = dram_in(nc, "norm_ffn_pre", [2, D])
    A["nfpost"] = dram_in(nc, "norm_ffn_post", [2, D])
    A["Wabo"] = dram_in(nc, "ab_w_out", [D, D])
    A["Wg"] = dram_in(nc, "ffn_w_gate", [2, D, DFF])
    A["Wu"] = dram_in(nc, "ffn_w_up", [2, D, DFF])
    A["Wd"] = dram_in(nc, "ffn_w_down", [2, DFF, D])
    A["Wsi"] = dram_in(nc, "sgu_w_in", [D, 2 * D])
    A["Wso"] = dram_in(nc, "sgu_w_out", [D, D])
    A["lng"] = dram_in(nc, "sgu_ln_g", [D])
    A["lnb"] = dram_in(nc, "sgu_ln_b", [D])
    A["wsT"] = dram_in(nc, "sgu_wsT", [128, 16, 128])
    A["bsb"] = dram_in(nc, "sgu_bs_b", [128, 16, 128])
    A["maskT"] = dram_in(nc, "sgu_maskT", [128, 128])
    A["out"] = nc.dram_tensor("out", [TP, D], F32, kind="ExternalOutput").ap()
    A["XT"] = dram_scratch(nc, "XT", [D, TP])
    A["YT"] = dram_scratch(nc, "YT", [D, TP])
    A["HID"] = dram_scratch(nc, "HID", [DFF, TP], BF16)
    A["UT"] = dram_scratch(nc, "UT", [D, TP], BF16)
    A["VTM"] = dram_scratch(nc, "VTM", [TP, D])
    return A


def build_phase2(stages=("in", "ab", "ffn0", "sgu", "ffn1", "out")):
    nc = bass.Bass("TRN2", target_bir_lowering=False)
    A = phase2_decl(nc)
    A["YC"] = dram_in(nc, "yc", [D, TP], BF16)
    with nc.cleanup_on_exit():
        S = Sched(nc)
        phase2_body(nc, S, A, stages, None)
        S.finish()
    return nc


def about_stage_sel(C, XT, XTb, YT, YTb, G, Gb, sel_d, g_post, Wo, tag):
    S, nc = C.S, C.nc
    with contextlib.ExitStack() as st:
        R = st.enter_context(nc.sbuf_tensor(tag + "R", [128, KC, TP], BF16))
        Rb = Buf()
        sel = st.enter_context(nc.sbuf_tensor(tag + "sel", [128, 4], F32))
        selb = Buf()
        S.dma("sp", sel[:, :], sel_d, writes=[selb])
        c4 = [st.enter_context(nc.sbuf_tensor(tag + "c4%d" % i, [128, 4, TT], BF16)) for i in range(3)]
        c4b = [Buf() for _ in range(3)]
        Gv = G.rearrange("(j i) r t -> i r j t", i=2)
        n = 0
        for kc in range(KC):
            if kc < 16:
                r, lc = kc // 4, kc % 4
            else:
                r, lc = (kc - 16) // 4, 4 + (kc - 16) % 4
            row = r * 1024 + lc * 128
            for hf in range(2):
                i = n % 3
                n += 1
                ts2 = slice(hf * TT, (hf + 1) * TT)
                S.dma(C.q(), c4[i][:, :, :], Gv[hf, row:row + 128, :, :], reads=list(Gb), writes=[c4b[i]])
                S.op("dve", lambda e, i=i, kc=kc, ts2=ts2: e.tensor_scalar(R[:, kc, ts2], c4[i][:, 0, :], sel[:, 0:1], None, ALU.mult),
                     reads=[c4b[i], selb], writes=[Rb])
                for j in range(1, 4):
                    S.op("dve", lambda e, i=i, kc=kc, j=j, ts2=ts2: e.scalar_tensor_tensor(R[:, kc, ts2], c4[i][:, j, :], sel[:, j:j + 1], R[:, kc, ts2],
                                                                                      ALU.mult, ALU.add), reads=[c4b[i], selb, Rb], writes=[Rb])
        gemm_to_dram(C, Wo, D, R, Rb, KC, TP, YT, YTb, 0, None, F32, tag + "g")
    postnorm_resid(C, YT, YTb, g_post, XT, XTb, tag + "p")


def phase2_body(nc, S, A, stages, gathered):
    x_own, ident_d, nmpost, nmpre, nfpre, nfpost, Wabo, Wg, Wu, Wd, Wsi, Wso, lng, lnb, wsT, bsb, maskT, out, XT, YT, HID, UT, VTM = [A[k] for k in (
        "x_own", "ident_d", "nmpost", "nmpre", "nfpre", "nfpost", "Wabo", "Wg", "Wu", "Wd", "Wsi", "Wso", "lng", "lnb", "wsT", "bsb", "maskT",
        "out", "XT", "YT", "HID", "UT", "VTM")]
    XTb = [Buf() for _ in range(KC)]
    YTb = [Buf() for _ in range(KC)]
    HIDb = [Buf() for _ in range(DFF // 128)]
    UTb = [Buf() for _ in range(KC)]
    VTMb = [Buf() for _ in range(TP // 128)]
    YCb = [Buf() for _ in range(KC)]
    outb = Buf()
    if True:
        with contextlib.ExitStack() as st:
            C = Ctx(nc, S, st)
            ident = C.sb("ident_sb", [128, 128], F32)
            identb = Buf()
            S.dma("sp", ident[:, :], ident_d, writes=[identb])
            C.eps_t = C.sb("eps_t2", [128, 1], F32)
            S.op("dve", lambda e: e.memset(C.eps_t[:, :], EPS), writes=[C.cb])
            if "in" in stages:
                xin_stage(C, x_own, XT, XTb, ident, identb)
            if "ab" in stages:
                if gathered is None:
                    about_stage(C, XT, XTb, YT, YTb, A["YC"], YCb, nmpost[0], Wabo, "ab")
                else:
                    G, Gb, sel_d = gathered
                    about_stage_sel(C, XT, XTb, YT, YTb, G, Gb, sel_d, nmpost[0], Wabo, "ab")
            if "ffn0" in stages:
                ffn_stage(C, XT, XTb, YT, YTb, HID, HIDb, nfpre[0], nfpost[0], Wg[0], Wu[0], Wd[0], "f0")
            if "sgu" in stages:
                sgu_stage(C, XT, XTb, YT, YTb, UT, UTb, VTM, VTMb, nmpre[1], nmpost[1], Wsi, lng, lnb, wsT, bsb, maskT, Wso, "sg")
            if "ffn1" in stages:
                ffn_stage(C, XT, XTb, YT, YTb, HID, HIDb, nfpre[1], nfpost[1], Wg[1], Wu[1], Wd[1], "f1")
            if "out" in stages:
                xout_stage(C, XT, XTb, out, outb, ident, identb)
            S.barrier()


def build_fused():
    nc = bass.Bass("TRN2", target_bir_lowering=False)
    A1 = phase1_decl(nc)
    A2 = phase2_decl(nc)
    sel_d = dram_in(nc, "sel", [128, 4])
    NT = SEQ // TT
    YL = [nc.dram_tensor("YL%d" % t, [1024, TT], BF16) for t in range(NT)]
    GG = nc.dram_tensor("YG", [NT, 4 * 1024, TT], BF16)
    A1["YCT"] = lambda t: YL[t].ap()
    with nc.cleanup_on_exit():
        S = Sched(nc)
        ylb = [Buf() for _ in range(NT)]
        Gb = [Buf() for _ in range(NT)]
        A1["after_tile"] = lambda t: S.collective("AllGather", YL[t].ap().opt(), GG.ap()[t].opt(), [[0, 1, 2, 3], [4, 5, 6, 7]],
                                                  reads=[ylb[t]], writes=[Gb[t]])
        phase1_body(nc, S, A1, ylb)
        phase2_body(nc, S, A2, ("in", "ab", "ffn0", "sgu", "ffn1", "out"), (GG.ap(), Gb, sel_d))
        S.finish()
    return nc


def phase2_consts(inputs):
    ws = np.asarray(inputs["sgu_w_s"][0], np.float32)
    wsT = np.ascontiguousarray(ws.transpose(2, 0, 1))
    pos = np.arange(128)
    maskT = ((pos[:, None] // 64) <= (pos[None, :] // 64)).astype(np.float32)
    bs = np.asarray(inputs["sgu_b_s"][0], np.float32)
    bsb = np.ascontiguousarray(np.broadcast_to(bs[None], (128, 16, 128)))
    return {
        "ident": np.eye(128, dtype=np.float32),
        "norm_mix_post": np.asarray(inputs["norm_mix_post"], np.float32),
        "norm_mix_pre": np.asarray(inputs["norm_mix_pre"], np.float32),
        "norm_ffn_pre": np.asarray(inputs["norm_ffn_pre"], np.float32),
        "norm_ffn_post": np.asarray(inputs["norm_ffn_post"], np.float32),
        "ab_w_out": np.asarray(inputs["ab_w_out"][0], np.float32),
        "ffn_w_gate": np.asarray(inputs["ffn_w_gate"], np.float32),
        "ffn_w_up": np.asarray(inputs["ffn_w_up"], np.float32),
        "ffn_w_down": np.asarray(inputs["ffn_w_down"], np.float32),
        "sgu_w_in": np.asarray(inputs["sgu_w_in"][0], np.float32),
        "sgu_w_out": np.asarray(inputs["sgu_w_out"][0], np.float32),
        "sgu_ln_g": np.asarray(inputs["sgu_ln_g"][0], np.float32),
        "sgu_ln_b": np.asarray(inputs["sgu_ln_b"][0], np.float32),
        "sgu_wsT": wsT, "sgu_bs_b": bsb, "sgu_maskT": maskT,
    }


NH = 4
TT = 512
NCOL1 = 2568


def phase1_decl(nc):
    A = {}
    A["xb"] = dram_in(nc, "xb", [SEQ, D])
    A["gpre"] = dram_in(nc, "g_pre", [D])
    A["Wc"] = dram_in(nc, "w_in_c", [D, NCOL1])
    A["cw_d"] = dram_in(nc, "conv_w", [128, 12, 4])
    A["nega_d"] = dram_in(nc, "neg_a", [128, 4])
    A["dtb_d"] = dram_in(nc, "dt_b", [128, 4])
    A["gnw_d"] = dram_in(nc, "gn_w", [128, 128])
    A["pw_d"] = dram_in(nc, "pool_wg", [512, 512])
    A["psc_d"] = dram_in(nc, "pool_sc", [512])
    A["band_d"] = dram_in(nc, "bandm", [3, 128, 128])
    A["mask_d"] = dram_in(nc, "masks", [5, 128, 128])
    return A


def build_phase1(ntiles=SEQ // TT, stop_after=None, dbgk=99):
    nc = bass.Bass("TRN2", target_bir_lowering=False)
    A = phase1_decl(nc)
    yct = nc.dram_tensor("yct", [1024, SEQ], BF16, kind="ExternalOutput").ap()
    A["YCT"] = lambda t: yct[:, t * TT:(t + 1) * TT]
    with nc.cleanup_on_exit():
        S = Sched(nc)
        phase1_body(nc, S, A, [Buf() for _ in range(SEQ // TT)], ntiles, stop_after, dbgk)
        S.finish()
    return nc


def phase1_body(nc, S, A, outb, ntiles=SEQ // TT, stop_after=None, dbgk=99):
    xb, gpre, Wc, cw_d, nega_d, dtb_d, gnw_d, pw_d, psc_d, band_d, mask_d, YCT = [A[k] for k in (
        "xb", "gpre", "Wc", "cw_d", "nega_d", "dtb_d", "gnw_d", "pw_d", "psc_d", "band_d", "mask_d", "YCT")]
    if True:
        with contextlib.ExitStack() as st:
            C = Ctx(nc, S, st, NW=4, pfx="p1_")
            sb = C.sb
            C.eps_t = sb("eps_t", [128, 1], F32)
            S.op("dve", lambda e: e.memset(C.eps_t[:, :], EPS), writes=[C.cb])
            C.one_t = sb("one_t", [128, 1], F32)
            S.op("dve", lambda e: e.memset(C.one_t[:, :], 1.0), writes=[C.cb])
            mk = sb("mk", [128, 5, 128], F32)
            mkb = Buf()
            S.dma("sp", mk[:, :, :], mask_d.rearrange("m p f -> p m f"), writes=[mkb])
            triA, strictU, MS, MU, ident = [mk[:, i, :] for i in range(5)]
            band = sb("band", [128, 3, 128], F32)
            bandb = Buf()
            S.dma("act", band[:, :, :], band_d.rearrange("m p f -> p m f"), writes=[bandb])
            g = sb("g", [128, KC], F32)
            gb = Buf()
            S.dma("sp", g[:, :], gpre.rearrange("(kc p) -> p kc", p=128), writes=[gb], allow_slow_non_contiguous=True)
            cw = sb("cw", [128, 12, 4], F32)
            cwb = Buf()
            S.dma("sp", cw[:, :, :], cw_d, writes=[cwb])
            nega = sb("nega", [128, 4], F32)
            dtb = sb("dtb", [128, 4], F32)
            gnw = sb("gnw", [128, 128], F32)
            psc = sb("psc", [128, 4], F32)
            smb = SBuf()
            S.dma("sp", nega[:, :], nega_d, writes=[smb])
            S.dma("sp", dtb[:, :], dtb_d, writes=[smb])
            S.dma("sp", gnw[:, :], gnw_d, writes=[smb])
            S.dma("sp", psc[:, :], psc_d.rearrange("(c p) -> p c", p=128), writes=[smb], allow_slow_non_contiguous=True)
            S.op("act", lambda e: e.activation(nega[:, :], nega[:, :], AF.Exp), reads=[smb], writes=[smb])
            S.op("dve", lambda e: e.tensor_scalar(nega[:, :], nega[:, :], -1.0, None, ALU.mult), reads=[smb], writes=[smb])
            pw = sb("pw", [128, 4, 512], BF16)
            pwb = Buf()
            S.dma("pool", pw[:, :, :], pw_d.rearrange("(c p) n -> p c n", p=128), writes=[pwb])
            wbd = sb("wbd", [128, KC, 8], BF16)
            wbdb = Buf()
            S.dma("pool", wbd[:, :, :], Wc.rearrange("(kc p) n -> p kc n", p=128)[:, :, 2560:2568], writes=[wbdb],
                  allow_slow_non_contiguous=True)
            xr = sb("xr", [128, D], F32)
            xrb = Buf()
            HT = sb("HT", [128, KC, TT], BF16)
            HTb = Buf()
            st8 = sb("st8", [128, 12], F32)
            st8b = SBuf()
            halo = sb("halo", [128, 12, 4], F32)
            halob = Buf()
            S.op("dve", lambda e: e.memset(halo[:, :, :], 0.0), writes=[halob])
            Z = [sb("Z%d" % i, [128, 3 + TT], F32) for i in range(4)]
            Zb = [Buf() for _ in range(4)]
            junkA = sb("junkA", [128, TT], BF16)
            junkAb = Buf()
            QT = sb("QT", [128, NH, TT], BF16)
            KT = sb("KT", [128, NH, TT], BF16)
            KTf = sb("KTf", [128, NH, TT], F32)
            VT = sb("VT", [128, NH, TT], F32)
            QTb, KTb, KTfb, VTb = [[Buf() for _ in range(NH)] for _ in range(4)]
            sg = sb("sg", [128, 4, 512], F32)
            sgb = [Buf() for _ in range(4)]
            xa = sb("xa", [128, 5, 512], F32)
            xab = [Buf() for _ in range(5)]
            lg8 = sb("lg8", [128, 4, 8], F32)
            lg8b = [SBuf() for _ in range(4)]
            OT = sb("OT", [128, 8, TT], BF16)
            OTb = Buf()
            PLT = sb("PLT", [128, 4, TT], BF16)
            PLTb = Buf()
            Sst = sb("Sst", [128, NH, 128], F32)
            Sbf = sb("Sbf", [128, NH, 128], BF16)
            Sstb = [Buf() for _ in range(NH)]
            Sbfb = [Buf() for _ in range(NH)]
            for h in range(NH):
                S.op("dve", lambda e, h=h: e.memset(Sst[:, h, :], 0.0), writes=[Sstb[h]])
                S.op("dve", lambda e, h=h: e.memset(Sbf[:, h, :], 0.0), writes=[Sbfb[h]])

            def ht(name, n=128, dt=F32, strict=False):
                t = sb(name, [128, NH, n], dt)
                return t, [Buf(name, strict) for _ in range(NH)]
            b4, b4b = ht("b4", 8, strict=True)
            G2, G2b = ht("G2")
            gB, gBb = ht("gB")
            EX, EXb = ht("EX", 392, strict=True)
            Es, Esb = ht("Es")
            ETc, ETcb = ht("ETc")
            ngb, ngbb = ht("ngb", 2, strict=True)
            Lm, Lb = ht("L")
            AT, ATb = ht("AT")
            kd, kdb = ht("kd")
            vb, vbb = ht("vb")
            Xa, Xab_ = ht("Xa")
            Xat, Xatb = ht("Xat")
            Xb, Xbb = ht("Xb")
            Xbt, Xbtb = ht("Xbt")
            Pm, Pmb = ht("Pm")
            A2T, A2Tb = ht("A2T", 128, BF16)
            K2, K2b = ht("K2", 128, BF16)
            qdT, qdTb = ht("qdT", 128, BF16)
            Rbf, Rbfb = ht("Rbf", 128, BF16)
            gsg, gsgb = gB, gBb
            yo, yob = G2, G2b
            flat = lambda t: t[:, :, :].rearrange("p h n -> p (h n)")
            accv = [flat(t) for t in (G2, gB, Es, ETc)]
            accB = [G2b, gBb, Esb, ETcb]
            slv = [flat(t) for t in (Lm, AT, kd, vb)]
            slB = [Lb, ATb, kdb, vbb]
            ps, psb = C.ps, C.psb
            Wv_ = Wc.rearrange("(kc p) n -> p kc n", p=128)

            def emit_A(tile):
                t0 = tile * TT
                for tb in range(4):
                    r0 = t0 + tb * 128
                    S.dma("sp", xr[:, 0:D // 2], xb[r0:r0 + 128, 0:D // 2], writes=[xrb])
                    S.dma("act", xr[:, D // 2:D], xb[r0:r0 + 128, D // 2:D], writes=[xrb])
                    for q8 in range(8):
                        S.op("act", lambda e, q8=q8: e.activation(junkA[:, :], xr[:, q8 * 512:(q8 + 1) * 512], AF.Square,
                                                                  accum_out=st8[:, q8:q8 + 1]), reads=[xrb], writes=[junkAb, st8b])
                    S.op("dve", lambda e: e.tensor_reduce(st8[:, 8:9], st8[:, 0:8], AX.X, ALU.add), reads=[st8b], writes=[st8b])
                    S.op("act", lambda e: e.activation(st8[:, 9:10], st8[:, 8:9], AF.Sqrt, bias=C.eps_t[:, 0:1], scale=1.0 / D),
                         reads=[st8b, C.cb], writes=[st8b])
                    S.op("dve", lambda e: e.reciprocal(st8[:, 10:11], st8[:, 9:10]), reads=[st8b], writes=[st8b])
                    S.op("act", lambda e: e.activation(xr[:, :], xr[:, :], AF.Copy, scale=st8[:, 10:11]), reads=[xrb, st8b], writes=[xrb])
                    for k4 in range(KC // 4):
                        bi = 4 + k4 % 4
                        for j in range(4):
                            kc = k4 * 4 + j
                            S.op("pe", lambda e, bi=bi, j=j, kc=kc: e.transpose(ps[bi][:, j * 128:(j + 1) * 128],
                                                                               xr[:, kc * 128:(kc + 1) * 128], ident),
                                 reads=[xrb, mkb], writes=[psb[bi]], signal=(j == 3))
                        for j in range(4):
                            kc = k4 * 4 + j
                            S.op("dve" if j % 2 == 0 else "act",
                                 (lambda e, bi=bi, j=j, kc=kc, tb=tb: e.tensor_scalar(HT[:, kc, tb * 128:(tb + 1) * 128], ps[bi][:, j * 128:(j + 1) * 128],
                                                                                     g[:, kc:kc + 1], None, ALU.mult)) if j % 2 == 0 else
                                 (lambda e, bi=bi, j=j, kc=kc, tb=tb: e.activation(HT[:, kc, tb * 128:(tb + 1) * 128], ps[bi][:, j * 128:(j + 1) * 128],
                                                                                  AF.Copy, scale=g[:, kc:kc + 1])),
                                 reads=[psb[bi], gb], writes=[HTb])
            def emit_B(tile):
                t0 = tile * TT
                for cgi in range(2):
                    slots = []
                    for u in range(4):
                        wt, wb = C.wslot()
                        wv = wt[:, :].rearrange("p (k n) -> p k n", n=512)
                        S.dma("pool", wv, Wv_[:, u * 8:(u + 1) * 8, cgi * 512:(cgi + 1) * 512], writes=[wb])
                        slots.append((wv, wb))
                    for tb in range(4):
                        bi = (cgi * 4 + tb) % 8
                        for kc in range(KC):
                            wv, wb = slots[kc // 8]
                            S.op("pe", lambda e, bi=bi, wv=wv, kc=kc, tb=tb: e.matmul(
                                ps[bi][:, :], HT[:, kc, tb * 128:(tb + 1) * 128], wv[:, kc % 8, :], start=(kc == 0), stop=(kc == KC - 1)),
                                reads=[wb, HTb], writes=[psb[bi]], signal=(kc % 8 == 7))
                        if cgi == 0:
                            S.op("act", lambda e, bi=bi, tb=tb: e.activation(xa[:, tb + 1, :], ps[bi][:, :], AF.Copy),
                                 reads=[psb[bi]], writes=[xab[tb + 1]])
                        else:
                            S.op("act", lambda e, bi=bi, tb=tb: e.activation(sg[:, tb, :], ps[bi][:, :], AF.Silu),
                                 reads=[psb[bi]], writes=[sgb[tb]])
                for tb in range(4):
                    bi = tb
                    for kc in range(KC):
                        S.op("pe", lambda e, bi=bi, kc=kc, tb=tb: e.matmul(ps[bi][:, 0:8], HT[:, kc, tb * 128:(tb + 1) * 128], wbd[:, kc, :],
                                                                          start=(kc == 0), stop=(kc == KC - 1)),
                             reads=[wbdb, HTb], writes=[psb[bi]], signal=(kc == KC - 1))
                    S.op("dve", lambda e, bi=bi, tb=tb: e.tensor_copy(lg8[:, tb, :], ps[bi][:, 0:8]), reads=[psb[bi]], writes=[lg8b[tb]])
                bank_of = lambda grp: [4, 5, 6, 7] if grp % 2 == 0 else [0, 1, 2, 3]
                gemm_cg(C, Wc, 1024, 512, HT, HTb, KC, TT, bank_of(0))
                for grp in range(3):
                    banks = bank_of(grp)
                    if grp + 1 < 3:
                        gemm_cg(C, Wc, 1024 + (grp + 1) * 512, 512, HT, HTb, KC, TT, bank_of(grp + 1))
                    lists = []
                    for h in range(NH):
                        S.record()
                        c = grp * 4 + h
                        zi = h
                        bi = banks[h]
                        S.op("dve", lambda e, zi=zi, c=c: e.tensor_copy(Z[zi][:, 0:3], halo[:, c, 0:3]), reads=[halob], writes=[Zb[zi]])
                        S.op("act", lambda e, zi=zi, bi=bi: e.activation(Z[zi][:, 3:3 + TT], ps[bi][:, :], AF.Copy),
                             reads=[psb[bi]], writes=[Zb[zi]])
                        S.op("dve", lambda e, zi=zi, c=c: e.tensor_copy(halo[:, c, 0:3], Z[zi][:, TT:TT + 3]), reads=[Zb[zi]], writes=[halob])
                        S.op("dve", lambda e, zi=zi, c=c: e.tensor_scalar(accv[zi], Z[zi][:, 3:3 + TT], cw[:, c, 3:4], None, ALU.mult),
                             reads=[Zb[zi], cwb], writes=[*accB[zi]])
                        for j in range(3):
                            S.op("dve", lambda e, zi=zi, c=c, j=j: e.scalar_tensor_tensor(accv[zi], Z[zi][:, j:j + TT], cw[:, c, j:j + 1],
                                                                                         accv[zi], ALU.mult, ALU.add),
                                 reads=[Zb[zi], cwb, *accB[zi]], writes=[*accB[zi]])
                        if grp == 2:
                            S.op("act", lambda e, zi=zi, h=h: e.activation(VT[:, h, :], accv[zi], AF.Silu), reads=[*accB[zi]], writes=[VTb[h]])
                            lists.append(S.stop())
                            continue
                        S.op("act", lambda e, zi=zi: e.activation(slv[zi], accv[zi], AF.Silu), reads=[*accB[zi]], writes=[*slB[zi]])
                        S.op("act", lambda e, zi=zi: e.activation(accv[zi], slv[zi], AF.Square),
                             reads=[*slB[zi]], writes=[*accB[zi]])
                        pb = banks[h]
                        S.op("pe", lambda e, pb=pb, zi=zi: e.matmul(ps[pb][:, :], C.ones_f[:, :], accv[zi], start=True, stop=True),
                             reads=[*accB[zi], C.cb], writes=[psb[pb]])
                        S.op("act", lambda e, pb=pb, zi=zi: e.activation(accv[zi], ps[pb][:, :], AF.Sqrt, bias=C.eps_t[:, 0:1], scale=1.0),
                             reads=[psb[pb], C.cb], writes=[*accB[zi]])
                        S.op("dve", lambda e, zi=zi: e.reciprocal(accv[zi], accv[zi]), reads=[*accB[zi]], writes=[*accB[zi]])
                        if grp == 0:
                            S.op("dve", lambda e, zi=zi, h=h: e.scalar_tensor_tensor(QT[:, h, :], slv[zi], 128.0 ** -0.5, accv[zi],
                                                                                    ALU.mult, ALU.mult), reads=[*slB[zi], *accB[zi]], writes=[QTb[h]])
                        else:
                            S.op("dve", lambda e, zi=zi, h=h: e.tensor_tensor(KTf[:, h, :], slv[zi], accv[zi], ALU.mult),
                                 reads=[*slB[zi], *accB[zi]], writes=[KTfb[h]])
                            S.op("pool", lambda e, h=h: e.tensor_copy(KT[:, h, :], KTf[:, h, :]), reads=[KTfb[h]], writes=[KTb[h]])
                        lists.append(S.stop())
                    S.replay(lists)
                for tb in range(4):
                    first = (tile == 0 and tb == 0)
                    for c in range(4):
                        bi = 4 + c
                        if first:
                            S.op("pe", lambda e, bi=bi, c=c, tb=tb: e.matmul(ps[bi][:, 0:128], xa[:, tb + 1, c * 128:(c + 1) * 128], band[:, 0, :],
                                                                            start=True, stop=True), reads=[xab[tb + 1], bandb], writes=[psb[bi]])
                        else:
                            S.op("pe", lambda e, bi=bi, c=c, tb=tb: e.matmul(ps[bi][:, 0:128], xa[:, tb, c * 128:(c + 1) * 128], band[:, 2, :],
                                                                            start=True, stop=False), reads=[xab[tb], bandb], writes=[psb[bi]], signal=False)
                            S.op("pe", lambda e, bi=bi, c=c, tb=tb: e.matmul(ps[bi][:, 0:128], xa[:, tb + 1, c * 128:(c + 1) * 128], band[:, 1, :],
                                                                            start=False, stop=True), reads=[xab[tb + 1], bandb], writes=[psb[bi]])
                        S.op("act", lambda e, bi=bi, c=c, tb=tb: e.activation(PLT[:, c, tb * 128:(tb + 1) * 128], ps[bi][:, 0:128], AF.Copy),
                             reads=[psb[bi]], writes=[PLTb])
                S.op("pool", lambda e: e.tensor_copy(xa[:, 0, :], xa[:, 4, :]), reads=[xab[4]], writes=[xab[0]])
                for dc in range(4):
                    bi = dc
                    for c in range(4):
                        S.op("pe", lambda e, bi=bi, c=c, dc=dc: e.matmul(ps[bi][:, :], pw[:, c, dc * 128:(dc + 1) * 128], PLT[:, c, :],
                                                                        start=(c == 0), stop=(c == 3)), reads=[pwb, PLTb], writes=[psb[bi]], signal=(c == 3))
                    S.op("act", lambda e, bi=bi, dc=dc: e.activation(OT[:, dc, :], ps[bi][:, :], AF.Copy, scale=psc[:, dc:dc + 1]),
                         reads=[psb[bi], smb], writes=[OTb])
            def emit_D(tile):
                t0 = tile * TT
                H = range(NH)
                for tb in range(4):
                    ts_ = slice(tb * 128, (tb + 1) * 128)
                    S.op("act", lambda e, tb=tb: e.activation(b4[:, 0, 0:4], lg8[:, tb, 0:4], AF.Exp, scale=-1.0), reads=[lg8b[tb]], writes=[b4b[0]])
                    S.op("dve", lambda e: e.tensor_scalar(b4[:, 0, 0:4], b4[:, 0, 0:4], 1.0, None, ALU.add), reads=[b4b[0]], writes=[b4b[0]])
                    S.op("dve", lambda e: e.reciprocal(b4[:, 0, 0:4], b4[:, 0, 0:4]), reads=[b4b[0]], writes=[b4b[0]])
                    S.op("dve", lambda e, tb=tb: e.tensor_tensor(b4[:, 1, 0:4], lg8[:, tb, 4:8], dtb[:, :], ALU.add), reads=[lg8b[tb], smb], writes=[b4b[0]])
                    S.op("act", lambda e: e.activation(b4[:, 1, 0:4], b4[:, 1, 0:4], AF.Exp), reads=[b4b[0]], writes=[b4b[0]])
                    S.op("act", lambda e: e.activation(b4[:, 1, 0:4], b4[:, 1, 0:4], AF.Ln, bias=C.one_t[:, 0:1], scale=1.0), reads=[b4b[0], C.cb], writes=[b4b[0]])
                    S.op("dve", lambda e: e.tensor_tensor(b4[:, 0, 4:8], b4[:, 1, 0:4], nega[:, :], ALU.mult), reads=[b4b[0], smb], writes=[b4b[0]])
                    beta = lambda h: b4[:, 0, h:h + 1]
                    gcol = lambda h: b4[:, 0, 4 + h:5 + h]
                    gcol2 = lambda h: b4[:, 0, 4 + h:6 + h] if h < 3 else b4[:, 0, 6:8]
                    bb = b4b[0]
                    for h in H:
                        S.op("dve", lambda e, h=h: e.tensor_scalar(G2[:, h, :], strictU, gcol(h), None, ALU.mult), reads=[bb, mkb], writes=[G2b[h]])
                        S.op("dve", lambda e, h=h: e.tensor_scalar(gB[:, h, :], C.ones_f[:, :], gcol(h), None, ALU.mult), reads=[bb, C.cb], writes=[gBb[h]])
                    for h in H:
                        bi = h
                        S.op("pe", lambda e, h=h, bi=bi: e.matmul(ps[bi][:, 0:128], triA, G2[:, h, :], start=True, stop=True),
                             reads=[G2b[h], mkb], writes=[psb[bi]], signal=False)
                        S.op("pe", lambda e, h=h, bi=bi: e.matmul(ps[bi][:, 128:256], G2[:, h, :], triA, start=True, stop=True),
                             reads=[G2b[h], mkb], writes=[psb[bi]], signal=False)
                        S.op("pe", lambda e, h=h, bi=bi: e.matmul(ps[bi][:, 256:384], gB[:, h, :], triA, start=True, stop=True),
                             reads=[gBb[h], mkb], writes=[psb[bi]], signal=False)
                        gsrc = (lambda h: b4[:, 0, 4 + h:6 + h]) if True else None
                        hh = min(h, 2)
                        off = h - hh
                        S.op("pe", lambda e, hh=hh, bi=bi: e.matmul(ps[bi][:, 384:386], triA, b4[:, 0, 4 + hh:6 + hh], start=True, stop=True),
                             reads=[bb, mkb], writes=[psb[bi]], signal=False)
                        S.op("pe", lambda e, hh=hh, bi=bi: e.matmul(ps[bi][:, 386:388], strictU, b4[:, 0, 4 + hh:6 + hh], start=True, stop=True),
                             reads=[bb, mkb], writes=[psb[bi]], signal=False)
                        S.op("pe", lambda e, hh=hh, bi=bi: e.matmul(ps[bi][:, 388:390], C.ones_f[:, :], b4[:, 0, 4 + hh:6 + hh], start=True, stop=True),
                             reads=[bb, C.cb], writes=[psb[bi]])
                        S.op("act", lambda e, h=h, bi=bi: e.activation(EX[:, h, 0:390], ps[bi][:, 0:390], AF.Exp), reads=[psb[bi]], writes=[EXb[h]])
                    gam = lambda h: EX[:, h, 384 + (h - min(h, 2)):385 + (h - min(h, 2))]
                    kds = lambda h: EX[:, h, 386 + (h - min(h, 2)):387 + (h - min(h, 2))]
                    gl_ = lambda h: EX[:, h, 388 + (h - min(h, 2)):389 + (h - min(h, 2))]
                    for h in H:
                        S.op("dve", lambda e, h=h: e.tensor_tensor(Es[:, h, :], EX[:, h, 0:128], MS, ALU.mult), reads=[EXb[h], mkb], writes=[Esb[h]])
                        S.op("dve", lambda e, h=h: e.tensor_tensor(ETc[:, h, :], EX[:, h, 128:256], MU, ALU.mult), reads=[EXb[h], mkb], writes=[ETcb[h]])
                        S.op("dve", lambda e, h=h: e.scalar_tensor_tensor(ngb[:, h, 0:1], gam(h), -1.0, beta(h), ALU.mult, ALU.mult),
                             reads=[EXb[h], bb], writes=[ngbb[h]])
                    for h in H:
                        bi = h
                        S.op("pe", lambda e, ts_=ts_, h=h, bi=bi: e.matmul(ps[bi][:, 0:128], KT[:, h, ts_], KT[:, h, ts_], start=True, stop=True),
                             reads=[KTb[h]], writes=[psb[bi]], signal=False)
                        S.op("pe", lambda e, ts_=ts_, h=h, bi=bi: e.matmul(ps[bi][:, 128:256], KT[:, h, ts_], QT[:, h, ts_], start=True, stop=True),
                             reads=[KTb[h], QTb[h]], writes=[psb[bi]], signal=False)
                        S.op("pe", lambda e, ts_=ts_, h=h, bi=bi: e.transpose(ps[bi][:, 256:384], KTf[:, h, ts_], ident), reads=[KTfb[h], mkb], writes=[psb[bi]], signal=False)
                        S.op("pe", lambda e, ts_=ts_, h=h, bi=bi: e.transpose(ps[bi][:, 384:512], VT[:, h, ts_], ident), reads=[VTb[h], mkb], writes=[psb[bi]])
                    for h in H:
                        bi = h
                        if dbgk > 0:
                            S.op("dve", lambda e, h=h, bi=bi: e.scalar_tensor_tensor(Lm[:, h, :], ps[bi][:, 0:128], beta(h), Es[:, h, :], ALU.mult, ALU.mult),
                                 reads=[psb[bi], bb, Esb[h]], writes=[Lb[h]])
                        if dbgk > 1:
                            S.op("dve", lambda e, h=h, bi=bi: e.tensor_tensor(AT[:, h, :], ps[bi][:, 128:256], ETc[:, h, :], ALU.mult),
                                 reads=[psb[bi], ETcb[h]], writes=[ATb[h]])
                        if dbgk > 2:
                            S.op("dve", lambda e, h=h, bi=bi: e.tensor_scalar(kd[:, h, :], ps[bi][:, 256:384], kds(h), None, ALU.mult),
                                 reads=[psb[bi], EXb[h]], writes=[kdb[h]])
                        if dbgk > 3:
                            S.op("dve", lambda e, h=h, bi=bi: e.tensor_scalar(vb[:, h, :], ps[bi][:, 384:512], beta(h), None, ALU.mult),
                                 reads=[psb[bi], bb], writes=[vbb[h]])
                        if dbgk > 4:
                            S.op("dve", lambda e, ts_=ts_, h=h: e.tensor_tensor(qdT[:, h, :], QT[:, h, ts_], EX[:, h, 256:384], ALU.mult),
                                 reads=[QTb[h], EXb[h]], writes=[qdTb[h]])
                        if dbgk > 5:
                            S.op("dve", lambda e, h=h, tb=tb: e.tensor_tensor(gsg[:, h, :], sg[:, tb, h * 128:(h + 1) * 128], gnw[:, :], ALU.mult),
                                 reads=[sgb[tb], smb], writes=[gsgb[h]])
                    for h in H:
                        bi = h
                        S.op("pe", lambda e, h=h, bi=bi: e.transpose(ps[bi][:, 0:128], Lm[:, h, :], ident), reads=[Lb[h], mkb], writes=[psb[bi]])
                        S.op("dve", lambda e, h=h, bi=bi: e.tensor_copy(Xat[:, h, :], ps[bi][:, 0:128]), reads=[psb[bi]], writes=[Xatb[h]])
                        S.op("dve", lambda e, h=h: e.scalar_tensor_tensor(Pm[:, h, :], Lm[:, h, :], -1.0, ident, ALU.mult, ALU.add),
                             reads=[Lb[h], mkb], writes=[Pmb[h]])
                    cur = (Lm, Lb, Xat, Xatb)
                    nxt = [(Xb, Xbb, Xbt, Xbtb), (Xa, Xab_, Xat, Xatb)]
                    for s_ in range(7):
                        X, Xbuf, XT_, XTbuf = cur
                        N_, Nb, NT, NTb = nxt[s_ % 2]
                        for h in H:
                            bi = h
                            if s_ < 5:
                                S.op("pe", lambda e, h=h, bi=bi, X=X, XT_=XT_: e.matmul(ps[bi][:, 0:128], XT_[:, h, :], X[:, h, :], start=True, stop=True),
                                     reads=[Xbuf[h], XTbuf[h]], writes=[psb[bi]], signal=False)
                            if s_ <= 5:
                                S.op("pe", lambda e, h=h, bi=bi, X=X, XT_=XT_: e.matmul(ps[bi][:, 128:256], X[:, h, :], XT_[:, h, :], start=True, stop=True),
                                     reads=[Xbuf[h], XTbuf[h]], writes=[psb[bi]], signal=(s_ == 0))
                            if s_ >= 1:
                                S.op("pe", lambda e, h=h, bi=bi, XT_=XT_: e.matmul(ps[bi][:, 256:384], XT_[:, h, :], Pm[:, h, :], start=True, stop=True),
                                     reads=[XTbuf[h], Pmb[h]], writes=[psb[bi]])
                        for h in H:
                            bi = h
                            if s_ < 5:
                                S.op("dve", lambda e, h=h, bi=bi, N_=N_: e.tensor_copy(N_[:, h, :], ps[bi][:, 0:128]), reads=[psb[bi]], writes=[Nb[h]])
                            if s_ <= 5:
                                S.op("dve", lambda e, h=h, bi=bi, NT=NT: e.tensor_copy(NT[:, h, :], ps[bi][:, 128:256]), reads=[psb[bi]], writes=[NTb[h]])
                            if s_ >= 1:
                                S.op("dve", lambda e, h=h, bi=bi: e.tensor_tensor(Pm[:, h, :], Pm[:, h, :], ps[bi][:, 256:384], ALU.add),
                                     reads=[psb[bi], Pmb[h]], writes=[Pmb[h]])
                        cur = (N_, Nb, NT, NTb)
                    for h in H:
                        bi = h
                        S.op("pe", lambda e, h=h, bi=bi: e.matmul(ps[bi][:, 0:128], Pm[:, h, :], AT[:, h, :], start=True, stop=True),
                             reads=[Pmb[h], ATb[h]], writes=[psb[bi]], signal=False)
                        S.op("pe", lambda e, h=h, bi=bi: e.matmul(ps[bi][:, 128:256], Pm[:, h, :], kd[:, h, :], start=True, stop=True),
                             reads=[Pmb[h], kdb[h]], writes=[psb[bi]])
                    for h in H:
                        bi = h
                        S.op("dve", lambda e, h=h, bi=bi: e.tensor_copy(A2T[:, h, :], ps[bi][:, 0:128]), reads=[psb[bi]], writes=[A2Tb[h]])
                        S.op("dve", lambda e, h=h, bi=bi: e.tensor_copy(K2[:, h, :], ps[bi][:, 128:256]), reads=[psb[bi]], writes=[K2b[h]])
                    for h in H:
                        bi = h
                        S.op("pe", lambda e, ts_=ts_, h=h, bi=bi: e.matmul(ps[bi][:, 0:128], KT[:, h, ts_], Sbf[:, h, :], start=True, stop=True),
                             reads=[KTb[h], Sbfb[h]], writes=[psb[bi]])
                    for h in H:
                        bi = h
                        S.op("dve", lambda e, h=h, bi=bi: e.scalar_tensor_tensor(Rbf[:, h, :], ps[bi][:, 0:128], ngb[:, h, 0:1], vb[:, h, :], ALU.mult, ALU.add),
                             reads=[psb[bi], ngbb[h], vbb[h]], writes=[Rbfb[h]])
                    for h in H:
                        bi = h
                        S.op("pe", lambda e, h=h, bi=bi: e.matmul(ps[bi][:, 128:256], qdT[:, h, :], Sbf[:, h, :], start=True, stop=False),
                             reads=[qdTb[h], Sbfb[h]], writes=[psb[bi]], signal=False)
                        S.op("pe", lambda e, h=h, bi=bi: e.matmul(ps[bi][:, 128:256], A2T[:, h, :], Rbf[:, h, :], start=False, stop=True),
                             reads=[A2Tb[h], Rbfb[h]], writes=[psb[bi]], signal=False)
                        S.op("pe", lambda e, h=h, bi=bi: e.matmul(ps[bi][:, 256:384], K2[:, h, :], Rbf[:, h, :], start=True, stop=True),
                             reads=[K2b[h], Rbfb[h]], writes=[psb[bi]])
                    for h in H:
                        bi = h
                        S.op("dve", lambda e, h=h, bi=bi: e.scalar_tensor_tensor(Sst[:, h, :], Sst[:, h, :], gl_(h), ps[bi][:, 256:384], ALU.mult, ALU.add),
                             reads=[psb[bi], EXb[h], Sstb[h]], writes=[Sstb[h]])
                        S.op("dve", lambda e, h=h: e.tensor_copy(Sbf[:, h, :], Sst[:, h, :]), reads=[Sstb[h]], writes=[Sbfb[h]])
                        S.op("dve", lambda e, h=h, bi=bi: e.tensor_copy(Xa[:, h, :], ps[bi][:, 128:256]), reads=[psb[bi]], writes=[Xab_[h]])
                        S.op("act", lambda e, h=h: e.activation(yo[:, h, :], Xa[:, h, :], AF.Square, accum_out=ngb[:, h, 1:2]),
                             reads=[Xab_[h]], writes=[yob[h], ngbb[h]])
                        S.op("act", lambda e, h=h: e.activation(ngb[:, h, 1:2], ngb[:, h, 1:2], AF.Sqrt, bias=C.eps_t[:, 0:1], scale=1.0 / 128),
                             reads=[ngbb[h], C.cb], writes=[ngbb[h]])
                        S.op("dve", lambda e, h=h: e.reciprocal(ngb[:, h, 1:2], ngb[:, h, 1:2]), reads=[ngbb[h]], writes=[ngbb[h]])
                        S.op("dve", lambda e, h=h, bi=bi: e.scalar_tensor_tensor(yo[:, h, :], Xa[:, h, :], ngb[:, h, 1:2], gsg[:, h, :], ALU.mult, ALU.mult),
                             reads=[Xab_[h], ngbb[h], gsgb[h]], writes=[yob[h]])
                    for h in H:
                        bi = h
                        S.op("pe", lambda e, h=h, bi=bi: e.transpose(ps[bi][:, 0:128], yo[:, h, :], ident), reads=[yob[h], mkb], writes=[psb[bi]])
                        S.op("dve", lambda e, ts_=ts_, h=h, bi=bi: e.tensor_copy(OT[:, 4 + h, ts_], ps[bi][:, 0:128]), reads=[psb[bi]], writes=[OTb])
                S.dma("sp", YCT(tile).rearrange("(c p) t -> p c t", p=128), OT[:, :, :], reads=[OTb], writes=[outb[tile]])
                if A.get("after_tile") is not None:
                    A["after_tile"](tile)

            emit_A(0)
            for tile in range(ntiles):
                emit_B(tile)
                if tile + 1 < ntiles:
                    S.record()
                    emit_D(tile)
                    lD = S.stop()
                    S.record()
                    emit_A(tile + 1)
                    lA = S.stop()
                    S.replay([lD, lA])
                else:
                    emit_D(tile)
            print('phase1 sbuf', nc.sbuf_base, nc.sbuf_top)
            S.barrier()


POOL_WINDOWS = (2, 4, 8, 16)


def phase1_inputs(inputs, b, hg):
    w = np.asarray(inputs["ab_w_in"][0], np.float32)
    PW, GW = 2048, 2048
    cols = np.concatenate([
        np.arange(hg * 512, (hg + 1) * 512),
        PW + 3 * GW + np.arange(hg * 512, (hg + 1) * 512),
        PW + np.arange(hg * 512, (hg + 1) * 512),
        PW + GW + np.arange(hg * 512, (hg + 1) * 512),
        PW + 2 * GW + np.arange(hg * 512, (hg + 1) * 512),
        PW + 4 * GW + np.arange(hg * 4, (hg + 1) * 4),
        PW + 4 * GW + 16 + np.arange(hg * 4, (hg + 1) * 4),
    ])
    wc = np.ascontiguousarray(w[:, cols])
    conv = np.asarray(inputs["gdn_conv"][0], np.float32)
    cwl = np.stack([conv[:, s * GW + hg * 512: s * GW + (hg + 1) * 512] for s in range(3)], 0)
    cwl = cwl.reshape(3, 4, 4, 128).transpose(3, 0, 2, 1).reshape(128, 12, 4)
    bc = lambda v: np.ascontiguousarray(np.broadcast_to(np.asarray(v, np.float32)[None, :], (128, len(v))))
    win = POOL_WINDOWS[hg]
    pos = np.arange(256)
    def band_full(first):
        B = np.zeros((256, 128), np.float32)
        for t in range(128):
            cnt = min(t + 1, win) if first else win
            for s in range(max(0, 128 + t - win + 1) if not first else 128 + max(0, t - win + 1), 128 + t + 1):
                B[s, t] = 1.0 / cnt
            B[128 + t, t] -= 1.0
        return B
    Bn = band_full(False)
    B0 = band_full(True)
    band = np.stack([B0[128:], Bn[128:], Bn[:128]], 0)
    k = np.arange(128)
    triA = (k[:, None] <= k[None, :]).astype(np.float32)
    strictU = (k[:, None] > k[None, :]).astype(np.float32)
    MS = (k[:, None] > k[None, :]).astype(np.float32)
    MU = (k[:, None] <= k[None, :]).astype(np.float32)
    masks = np.stack([triA, strictU, MS, MU, np.eye(128, dtype=np.float32)], 0)
    return {
        "xb": np.ascontiguousarray(np.asarray(inputs["x"][b], np.float32)),
        "g_pre": np.asarray(inputs["norm_mix_pre"][0], np.float32),
        "w_in_c": wc,
        "conv_w": np.ascontiguousarray(cwl),
        "neg_a": bc(inputs["gdn_a_log"][0][hg * 4:(hg + 1) * 4]),
        "dt_b": bc(inputs["gdn_dt_bias"][0][hg * 4:(hg + 1) * 4]),
        "gn_w": bc(inputs["gdn_norm"][0]),
        "pool_wg": np.ascontiguousarray(np.asarray(inputs["pool_w"][0][hg], np.float32)),
        "pool_sc": np.ascontiguousarray(np.asarray(inputs["pool_scale"][0][hg * 512:(hg + 1) * 512], np.float32)),
        "bandm": np.ascontiguousarray(band), "masks": np.ascontiguousarray(masks),
    }


def kernel(**inputs):
    n = 8
    nc = build_fused()
    consts = phase2_consts(inputs)
    x = np.asarray(inputs["x"], np.float32)
    maps = []
    for c in range(n):
        b, j = c // 4, c % 4
        m = dict(consts)
        m.update(phase1_inputs(inputs, b, j))
        m["x_own"] = np.ascontiguousarray(x[b, j * TP:(j + 1) * TP, :])
        sel = np.zeros((128, 4), np.float32)
        sel[:, j] = 1.0
        m["sel"] = sel
        maps.append(m)
    res = run_bass_kernel_spmd(nc, maps, core_ids=list(range(n)))
    out = np.empty((2, SEQ, D), np.float32)
    for c in range(n):
        b, j = c // 4, c % 4
        out[b, j * TP:(j + 1) * TP, :] = np.asarray(res.results[c]["out"], np.float32)
    return out
```

```python
import contextlib
import numpy as np
import ml_dtypes
import concourse.bass as bass
import concourse.mybir as mybir
from concourse.bass_utils import run_bass_kernel_spmd

F32 = mybir.dt.float32
BF16 = mybir.dt.bfloat16
AF = mybir.ActivationFunctionType
ALU = mybir.AluOpType
AX = mybir.AxisListType

D = 4096
DFF = 11008
KC = D // 128
TP = 1024
SEQ = 4096
EPS = 1e-6


class Buf:
    __slots__ = ("name", "w", "r", "strict")

    def __init__(self, name="", strict=False):
        self.name = name
        self.w = None
        self.r = []
        self.strict = strict


def SBuf(name=""):
    return Buf(name, True)


class Sched:
    ENGS = ("pe", "act", "dve", "pool", "sp")

    def __init__(self, nc, n_dma_sems=48):
        self.nc = nc
        self.prog = {e: [] for e in self.ENGS}
        self.sems = {e: nc.alloc_semaphore(name="s_" + e) for e in self.ENGS}
        self.cnt = {e: 0 for e in self.ENGS}
        self.waited = {e: {} for e in self.ENGS}
        self.dsems = [nc.alloc_semaphore(name="d%d" % i) for i in range(n_dma_sems)]
        self.dval = [0] * n_dma_sems
        self.dnext = 0
        self.n_ins = 0
        self._rec = None
        self.nosame = 1
        self.sems["cc"] = nc.alloc_semaphore(name="s_cc")
        self.ccval = 0

    def _sem(self, key):
        return self.sems[key] if isinstance(key, str) else self.dsems[key]

    def _collect(self, eng, reads, writes):
        need = {}

        relax = self.nosame and eng in ("dve", "act")

        def add(tok, strict):
            if tok is None:
                return
            k, v = tok
            if k == eng and (eng == "pe" or (relax and not strict)):
                return
            if need.get(k, 0) < v:
                need[k] = v
        for b in reads:
            add(b.w, b.strict)
        for b in writes:
            add(b.w, b.strict)
            for t in b.r:
                add(t, b.strict)
        waits = []
        wd = self.waited[eng]
        for k, v in need.items():
            if wd.get(k, 0) >= v:
                continue
            wd[k] = v
            waits.append((self._sem(k), v))
        return waits

    def _commit(self, tok, reads, writes):
        for b in reads:
            b.r.append(tok)
        for b in writes:
            b.w = tok
            b.r = []

    def record(self):
        self._rec = []

    def stop(self):
        r, self._rec = self._rec, None
        return r

    def replay(self, lists):
        pos = [0] * len(lists)
        tot = max(len(l) for l in lists)
        for step in range(1, tot + 1):
            for i, l in enumerate(lists):
                upto = (step * len(l)) // tot
                while pos[i] < upto:
                    kind, a, kw = l[pos[i]]
                    pos[i] += 1
                    {"op": self.op, "dma": self.dma, "cc": self.collective}[kind](*a, **kw)

    def op(self, eng, fn, reads=(), writes=(), signal=True):
        if self._rec is not None:
            self._rec.append(("op", (eng, fn), dict(reads=list(reads), writes=list(writes), signal=signal)))
            return
        waits = self._collect(eng, reads, writes)
        if signal:
            self.cnt[eng] += 1
            tok = (eng, self.cnt[eng])
        else:
            tok = (eng, self.cnt[eng] + 1)
        sem = self.sems[eng]

        def run(e, waits=waits, fn=fn, sem=sem, signal=signal):
            for s, v in waits:
                e.wait_ge(s, v)
            ins = fn(e)
            if signal:
                ins.then_inc(sem, 1)
        self.prog[eng].append(run)
        self._commit(tok, reads, writes)
        self.n_ins += 1

    def dma(self, eng, out_ap, in_ap, reads=(), writes=(), **kw):
        if self._rec is not None:
            self._rec.append(("dma", (eng, out_ap, in_ap), dict(reads=list(reads), writes=list(writes), **kw)))
            return
        i = self.dnext
        self.dnext = (self.dnext + 1) % len(self.dsems)
        waits = self._collect(eng, reads, writes)
        wd = self.waited[eng]
        if self.dval[i] > 0 and wd.get(i, 0) < self.dval[i]:
            wd[i] = self.dval[i]
            waits.append((self.dsems[i], self.dval[i]))
        self.dval[i] += 16
        tok = (i, self.dval[i])
        sem = self.dsems[i]

        def run(e, waits=waits, sem=sem):
            for s, v in waits:
                e.wait_ge(s, v)
            e.dma_start(out=out_ap, in_=in_ap, **kw).then_inc(sem, 16)
        self.prog[eng].append(run)
        self._commit(tok, reads, writes)
        self.n_ins += 1

    def collective(self, kind, in_ap, out_ap, groups, reads=(), writes=()):
        if self._rec is not None:
            self._rec.append(("cc", (kind, in_ap, out_ap, groups), dict(reads=list(reads), writes=list(writes))))
            return
        waits = self._collect("pool", reads, writes)
        self.ccval += 1
        tok = ("cc", self.ccval)
        sem = self.sems["cc"]

        def run(e, waits=waits, sem=sem):
            for s_, v in waits:
                e.wait_ge(s_, v)
            e.collective_compute(kind, ALU.bypass, replica_groups=groups, ins=[in_ap], outs=[out_ap]).then_inc(sem, 1)
        self.prog["pool"].append(run)
        self._commit(tok, reads, writes)

    def barrier(self):
        for e in self.ENGS:
            waits = []
            wd = self.waited[e]
            for e2 in self.ENGS:
                if e2 != e and self.cnt[e2] > wd.get(e2, 0):
                    wd[e2] = self.cnt[e2]
                    waits.append((self.sems[e2], self.cnt[e2]))
            for i, v in enumerate(self.dval):
                if v > wd.get(i, 0):
                    wd[i] = v
                    waits.append((self.dsems[i], v))
            if self.ccval > wd.get("cc", 0):
                wd["cc"] = self.ccval
                waits.append((self.sems["cc"], self.ccval))

            def run(en, waits=waits):
                for s, v in waits:
                    en.wait_ge(s, v)
            self.prog[e].append(run)

    def finish(self):
        self.barrier()
        nc = self.nc
        with nc.Block() as block:
            @block.tensor
            def _(e):
                for f in self.prog["pe"]:
                    f(e)

            @block.scalar
            def _(e):
                for f in self.prog["act"]:
                    f(e)

            @block.vector
            def _(e):
                for f in self.prog["dve"]:
                    f(e)

            @block.gpsimd
            def _(e):
                for f in self.prog["pool"]:
                    f(e)

            @block.sync
            def _(e):
                for f in self.prog["sp"]:
                    f(e)


class Ctx:
    def __init__(self, nc, S, st, NW=8, pfx=""):
        self.nc, self.S, self.st, self.pfx = nc, S, st, pfx
        self.ps = [st.enter_context(nc.psum_tensor(pfx + "ps%d" % i, [128, 512], F32)) for i in range(8)]
        self.psb = [Buf("ps%d" % i) for i in range(8)]
        self.ones_bf = self.sb("ones_bf", [128, 128], BF16)
        self.ones_f = self.sb("ones_f", [128, 128], F32)
        self.cb = SBuf("consts")
        S.op("dve", lambda e: e.memset(self.ones_bf[:], 1.0), writes=[self.cb])
        S.op("dve", lambda e: e.memset(self.ones_f[:], 1.0), writes=[self.cb])
        self.NW = NW
        self.wt = [self.sb("wt%d" % i, [128, 4096], BF16) for i in range(self.NW)]
        self.wtb = [Buf("wt%d" % i) for i in range(self.NW)]
        self.wnext = 0
        self.dmaq = 0

    def sb(self, name, shape, dt):
        return self.st.enter_context(self.nc.sbuf_tensor(self.pfx + name, shape, dt))

    def wslot(self):
        i = self.wnext
        self.wnext = (i + 1) % self.NW
        return self.wt[i], self.wtb[i]

    def q(self):
        self.dmaq ^= 1
        return "sp" if self.dmaq else "act"


def gemm_cg(C, W, c0, CW, rhs, rhsb, KCr, T, banks, tokmajor=False):
    S = C.S
    nth = T // 512
    noc = CW // 128
    ukc = 4096 // CW
    nu = (KCr + ukc - 1) // ukc
    Wv = W.rearrange("(kc p) n -> p kc n", p=128)
    for u in range(nu):
        k0 = u * ukc
        nk = min(ukc, KCr - k0)
        wt, wb = C.wslot()
        wv = wt[:, 0:nk * CW].rearrange("p (k n) -> p k n", n=CW)
        S.dma("pool", wv, Wv[:, k0:k0 + nk, c0:c0 + CW], writes=[wb])
        for oc in range(noc):
            for th in range(nth):
                bi = banks[oc * nth + th]
                for j in range(nk):
                    kc = k0 + j
                    S.op("pe", lambda e, bi=bi, wv=wv, j=j, oc=oc, kc=kc, th=th: e.matmul(
                        C.ps[bi][:, :], wv[:, j, oc * 128:(oc + 1) * 128], rhs[:, kc, th * 512:(th + 1) * 512],
                        start=(kc == 0), stop=(kc == KCr - 1)),
                        reads=[wb, rhsb], writes=[C.psb[bi]], signal=(j == nk - 1))


def load_gain(C, sb, name, g_ap, ncol=KC):
    t = sb(name, [128, ncol], F32)
    b = Buf(name)
    C.S.dma("sp", t[:, :], g_ap.rearrange("(kc p) -> p kc", p=128), writes=[b], allow_slow_non_contiguous=True)
    return t, b


def colsum_rstd(C, src_dram, srcb, nkc, T, rstd, rstdb, xin, xinb, sq, sqb, scale, tmp, tmpb):
    S = C.S
    nth = T // 512
    for kc in range(nkc):
        r = kc % len(xin)
        S.dma(C.q(), xin[r][:, :], src_dram[kc * 128:(kc + 1) * 128, :], reads=[srcb[kc]], writes=[xinb[r]])
        r2 = kc % len(sq)
        S.op("act", lambda e, r=r, r2=r2: e.activation(sq[r2][:, :], xin[r][:, :], AF.Square), reads=[xinb[r]], writes=[sqb[r2]])
        for th in range(nth):
            S.op("pe", lambda e, th=th, r2=r2, kc=kc: e.matmul(C.ps[th][:, :], C.ones_bf[:, :], sq[r2][:, th * 512:(th + 1) * 512],
                                                             start=(kc == 0), stop=(kc == nkc - 1)),
                 reads=[sqb[r2], C.cb], writes=[C.psb[th]])
    for th in range(nth):
        sl = slice(th * 512, (th + 1) * 512)
        S.op("act", lambda e, th=th, sl=sl: e.activation(tmp[:, sl], C.ps[th][:, :], AF.Sqrt, bias=C.eps_t[:, 0:1], scale=scale),
             reads=[C.psb[th], C.cb], writes=[tmpb])
        S.op("dve", lambda e, sl=sl: e.reciprocal(rstd[:, sl], tmp[:, sl]), reads=[tmpb], writes=[rstdb])


def norm_stage(C, XT, XTb, gain_ap, HT, HTb, tag):
    S, nc = C.S, C.nc
    with contextlib.ExitStack() as st:
        sb = lambda n, s, d: st.enter_context(nc.sbuf_tensor(tag + n, s, d))
        xin = [sb("xin%d" % i, [128, TP], F32) for i in range(3)]
        xinb = [Buf() for _ in range(3)]
        sq = [sb("sq%d" % i, [128, TP], BF16) for i in range(2)]
        sqb = [Buf() for _ in range(2)]
        rstd = sb("rstd", [128, TP], F32)
        rstdb = Buf()
        tmp = sb("tmp", [128, TP], F32)
        tmpb = Buf()
        g = sb("g", [128, KC], F32)
        gb = Buf()
        S.dma("sp", g[:, :], gain_ap.rearrange("(kc p) -> p kc", p=128), writes=[gb], allow_slow_non_contiguous=True)
        colsum_rstd(C, XT, XTb, KC, TP, rstd, rstdb, xin, xinb, sq, sqb, 1.0 / D, tmp, tmpb)
        for kc in range(KC):
            r = kc % 3
            S.dma(C.q(), xin[r][:, :], XT[kc * 128:(kc + 1) * 128, :], reads=[XTb[kc]], writes=[xinb[r]])
            S.op("dve", lambda e, r=r, kc=kc: e.scalar_tensor_tensor(HT[:, kc, :], xin[r][:, :], g[:, kc:kc + 1], rstd[:, :],
                                                                    ALU.mult, ALU.mult),
                 reads=[xinb[r], gb, rstdb], writes=[HTb])


def postnorm_resid(C, YT, YTb, gain_ap, XT, XTb, tag):
    S, nc = C.S, C.nc
    with contextlib.ExitStack() as st:
        sb = lambda n, s, d: st.enter_context(nc.sbuf_tensor(tag + n, s, d))
        xin = [sb("xin%d" % i, [128, TP], F32) for i in range(3)]
        xinb = [Buf() for _ in range(3)]
        yin = [sb("yin%d" % i, [128, TP], F32) for i in range(3)]
        yinb = [Buf() for _ in range(3)]
        sq = [sb("sq%d" % i, [128, TP], BF16) for i in range(2)]
        sqb = [Buf() for _ in range(2)]
        rstd = sb("rstd", [128, TP], F32)
        rstdb = Buf()
        tmp = sb("tmp", [128, TP], F32)
        tmpb = Buf()
        g = sb("g", [128, KC], F32)
        gb = Buf()
        S.dma("sp", g[:, :], gain_ap.rearrange("(kc p) -> p kc", p=128), writes=[gb], allow_slow_non_contiguous=True)
        colsum_rstd(C, YT, YTb, KC, TP, rstd, rstdb, yin, yinb, sq, sqb, 1.0 / D, tmp, tmpb)
        for kc in range(KC):
            r = kc % 3
            rows = slice(kc * 128, (kc + 1) * 128)
            S.dma("sp", yin[r][:, :], YT[rows, :], reads=[YTb[kc]], writes=[yinb[r]])
            S.dma("act", xin[r][:, :], XT[rows, :], reads=[XTb[kc]], writes=[xinb[r]])
            S.op("dve", lambda e, r=r, kc=kc: e.scalar_tensor_tensor(yin[r][:, :], yin[r][:, :], g[:, kc:kc + 1], rstd[:, :],
                                                                    ALU.mult, ALU.mult),
                 reads=[yinb[r], gb, rstdb], writes=[yinb[r]])
            S.op("dve", lambda e, r=r: e.tensor_tensor(xin[r][:, :], xin[r][:, :], yin[r][:, :], ALU.add),
                 reads=[yinb[r], xinb[r]], writes=[xinb[r]])
            S.dma("sp", XT[rows, :], xin[r][:, :], reads=[xinb[r]], writes=[XTb[kc]])
        S.barrier()


def gemm_to_dram(C, W, N, rhs, rhsb, KCr, T, OUT, OUTb, tok0, func, odt, tag):
    S, nc = C.S, C.nc
    CW = 256 if T == 1024 else 512
    nth = T // 512
    noc = CW // 128
    with contextlib.ExitStack() as st:
        ot = [st.enter_context(nc.sbuf_tensor(tag + "ot%d" % i, [128, 512], odt)) for i in range(4)]
        otb = [Buf() for _ in range(4)]
        oi = 0
        for cg in range(N // CW):
            banks = [(cg % 2) * 4 + i for i in range(4)]
            gemm_cg(C, W, cg * CW, CW, rhs, rhsb, KCr, T, banks)
            for oc in range(noc):
                for th in range(nth):
                    bi = banks[oc * nth + th]
                    o = oi % 4
                    oi += 1
                    if func is None:
                        S.op("dve", lambda e, o=o, bi=bi: e.tensor_copy(ot[o][:, :], C.ps[bi][:, :]),
                             reads=[C.psb[bi]], writes=[otb[o]])
                    else:
                        S.op("act", lambda e, o=o, bi=bi: e.activation(ot[o][:, :], C.ps[bi][:, :], func),
                             reads=[C.psb[bi]], writes=[otb[o]])
                    row = cg * CW + oc * 128
                    S.dma(C.q(), OUT[row:row + 128, tok0 + th * 512: tok0 + (th + 1) * 512], ot[o][:, :],
                          reads=[otb[o]], writes=[OUTb[row // 128]])
        S.barrier()


def load_fm(C, SRC, SRCb, nkc, T, tok0, dst, dstb, per=8):
    v = SRC.rearrange("(kc p) t -> p kc t", p=128)
    for k0 in range(0, nkc, per):
        k1 = min(nkc, k0 + per)
        C.S.dma(C.q(), dst[:, k0:k1, 0:T], v[:, k0:k1, tok0:tok0 + T], reads=[SRCb[k] for k in range(k0, k1)], writes=[dstb])


def ffn_stage(C, XT, XTb, YT, YTb, HID, HIDb, g_pre, g_post, Wg, Wu, Wd, tag):
    S, nc = C.S, C.nc
    with contextlib.ExitStack() as st:
        HT = st.enter_context(nc.sbuf_tensor(tag + "HT", [128, KC, TP], BF16))
        HTb = Buf()
        norm_stage(C, XT, XTb, g_pre, HT, HTb, tag + "n")
        sl_t = [st.enter_context(nc.sbuf_tensor(tag + "sl%d" % i, [128, 512], F32)) for i in range(2)]
        slb = [Buf() for _ in range(2)]
        ot = [st.enter_context(nc.sbuf_tensor(tag + "ho%d" % i, [128, 512], BF16)) for i in range(4)]
        otb = [Buf() for _ in range(4)]
        oi = 0
        for cg in range(DFF // 256):
            bg = [0, 1, 2, 3]
            bu = [4, 5, 6, 7]
            gemm_cg(C, Wg, cg * 256, 256, HT, HTb, KC, TP, bg)
            gemm_cg(C, Wu, cg * 256, 256, HT, HTb, KC, TP, bu)
            for oc in range(2):
                for th in range(2):
                    o = oi % 4
                    s2 = oi % 2
                    oi += 1
                    b1, b2 = bg[oc * 2 + th], bu[oc * 2 + th]
                    S.op("act", lambda e, s2=s2, b1=b1: e.activation(sl_t[s2][:, :], C.ps[b1][:, :], AF.Silu),
                         reads=[C.psb[b1]], writes=[slb[s2]])
                    S.op("dve", lambda e, s2=s2, b2=b2, o=o: e.tensor_tensor(ot[o][:, :], sl_t[s2][:, :], C.ps[b2][:, :], ALU.mult),
                         reads=[slb[s2], C.psb[b2]], writes=[otb[o]])
                    row = cg * 256 + oc * 128
                    S.dma(C.q(), HID[row:row + 128, th * 512:(th + 1) * 512], ot[o][:, :], reads=[otb[o]], writes=[HIDb[row // 128]])
        S.barrier()
    KF = DFF // 128
    with contextlib.ExitStack() as st:
        RH = st.enter_context(nc.sbuf_tensor(tag + "RH", [128, KF, 512], BF16))
        RHb = Buf()
        for th2 in range(2):
            load_fm(C, HID, HIDb, KF, 512, th2 * 512, RH, RHb)
            gemm_to_dram(C, Wd, D, RH, RHb, KF, 512, YT, YTb, th2 * 512, None, F32, tag + "d%d" % th2)
    postnorm_resid(C, YT, YTb, g_post, XT, XTb, tag + "p")


def about_stage(C, XT, XTb, YT, YTb, YC, YCb, g_post, Wo, tag):
    S, nc = C.S, C.nc
    with contextlib.ExitStack() as st:
        R = st.enter_context(nc.sbuf_tensor(tag + "R", [128, KC, TP], BF16))
        Rb = Buf()
        load_fm(C, YC, YCb, KC, TP, 0, R, Rb)
        gemm_to_dram(C, Wo, D, R, Rb, KC, TP, YT, YTb, 0, None, F32, tag + "g")
    postnorm_resid(C, YT, YTb, g_post, XT, XTb, tag + "p")


def sgu_stage(C, XT, XTb, YT, YTb, UT, UTb, VTM, VTMb, g_pre, g_post, Win, ln_g, ln_b, wsT, bs, maskT, Wout, tag):
    S, nc = C.S, C.nc
    with contextlib.ExitStack() as st:
        HT = st.enter_context(nc.sbuf_tensor(tag + "HT", [128, KC, TP], BF16))
        HTb = Buf()
        norm_stage(C, XT, XTb, g_pre, HT, HTb, tag + "n")
        gemm_to_dram(C, Win[:, 0:D], D, HT, HTb, KC, TP, UT, UTb, 0, AF.Gelu, BF16, tag + "u")
        vo = [st.enter_context(nc.sbuf_tensor(tag + "vo%d" % i, [128, 512], F32)) for i in range(3)]
        vob = [Buf() for _ in range(3)]
        Wv = Win.rearrange("(kc p) n -> p kc n", p=128)
        oi = 0
        for cg in range(D // 512):
            slots = []
            for u in range(4):
                wt, wb = C.wslot()
                wv = wt[:, :].rearrange("p (k n) -> p k n", n=512)
                S.dma("pool", wv, Wv[:, u * 8:(u + 1) * 8, D + cg * 512: D + (cg + 1) * 512], writes=[wb])
                slots.append((wv, wb))
            for tb in range(TP // 128):
                bi = oi % 8
                for kc in range(KC):
                    wv, wb = slots[kc // 8]
                    S.op("pe", lambda e, bi=bi, wv=wv, kc=kc, tb=tb: e.matmul(
                        C.ps[bi][:, :], HT[:, kc, tb * 128:(tb + 1) * 128], wv[:, kc % 8, :], start=(kc == 0), stop=(kc == KC - 1)),
                        reads=[wb, HTb], writes=[C.psb[bi]], signal=(kc % 8 == 7))
                o = oi % 3
                oi += 1
                S.op("act", lambda e, o=o, bi=bi: e.activation(vo[o][:, :], C.ps[bi][:, :], AF.Gelu), reads=[C.psb[bi]], writes=[vob[o]])
                S.dma(C.q(), VTM[tb * 128:(tb + 1) * 128, cg * 512:(cg + 1) * 512], vo[o][:, :], reads=[vob[o]], writes=[VTMb[tb]])
        S.barrier()
    with contextlib.ExitStack() as st:
        sb = lambda n, s, d: st.enter_context(nc.sbuf_tensor(tag + n, s, d))
        PT = sb("PT", [128, KC, TP], BF16)
        PTb = Buf()
        load_fm(C, UT, UTb, KC, TP, 0, PT, PTb)
        mk = sb("mk", [128, 128], F32)
        mkb = Buf()
        S.dma("sp", mk[:, :], maskT, writes=[mkb])
        wsbf = sb("wsbf", [128, 16, 128], BF16)
        wsbfb = Buf()
        S.dma("pool", wsbf[:, :, :], wsT, writes=[wsbfb])
        for g in range(16):
            S.op("dve", lambda e, g=g: e.tensor_tensor(wsbf[:, g, :], wsbf[:, g, :], mk[:, :], ALU.mult), reads=[wsbfb, mkb], writes=[wsbfb])
        BS = sb("BS", [128, 16, 128], F32)
        BSb = Buf()
        S.dma("sp", BS[:, :, :], bs, writes=[BSb])
        RS = sb("RS", [128, 16, 128], F32)
        RSb = Buf()
        for q4 in range(4):
            S.op("pe", lambda e, q4=q4: e.matmul(C.ps[q4][:, :], C.ones_bf[:, :], wsbf[:, q4 * 4:(q4 + 1) * 4, :], start=True, stop=True),
                 reads=[wsbfb, C.cb], writes=[C.psb[q4]])
            S.op("dve", lambda e, q4=q4: e.tensor_copy(RS[:, q4 * 4:(q4 + 1) * 4, :], C.ps[q4][:, :]), reads=[C.psb[q4]], writes=[RSb])
        lg, lgb = load_gain(C, sb, "lg", ln_g)
        lb, lbb = load_gain(C, sb, "lb", ln_b)
        T2 = sb("T2", [128, KC, 128], F32)
        T2b = Buf()
        for kc in range(KC):
            S.op("dve", lambda e, kc=kc: e.scalar_tensor_tensor(T2[:, kc, :], RS[:, kc // 2, :], lb[:, kc:kc + 1], BS[:, kc // 2, :],
                                                               ALU.mult, ALU.add), reads=[RSb, BSb, lbb], writes=[T2b])
        vin = [sb("vin0", [128, D], F32)] * 2
        vinb = [Buf()] * 2
        vh = [sb("vh0", [128, D], BF16)] * 2
        vhb = [Buf()] * 2
        junk = vh[0]
        junkb = vhb[0]
        st4 = [sb("st%d" % i, [128, 8], F32) for i in range(2)]
        st4b = [SBuf() for _ in range(2)]
        sv = [sb("sv%d" % i, [128, 128], F32) for i in range(3)]
        svb = [Buf() for _ in range(3)]
        oi = 0
        for tb in range(TP // 128):
            r = tb % 2
            S.dma("sp", vin[r][:, 0:D // 2], VTM[tb * 128:(tb + 1) * 128, 0:D // 2], reads=[VTMb[tb]], writes=[vinb[r]])
            S.dma("act", vin[r][:, D // 2:D], VTM[tb * 128:(tb + 1) * 128, D // 2:D], reads=[VTMb[tb]], writes=[vinb[r]])
            s4 = st4[r]
            S.op("act", lambda e, r=r, s4=s4: e.activation(junk[:, :], vin[r][:, :], AF.Identity, accum_out=s4[:, 0:1]),
                 reads=[vinb[r]], writes=[junkb, st4b[r]])
            S.op("act", lambda e, r=r, s4=s4: e.activation(junk[:, :], vin[r][:, :], AF.Square, accum_out=s4[:, 1:2]),
                 reads=[vinb[r]], writes=[junkb, st4b[r]])
            S.op("dve", lambda e, s4=s4: e.tensor_scalar(s4[:, 2:3], s4[:, 0:1], 1.0 / D, None, ALU.mult), reads=[st4b[r]], writes=[st4b[r]])
            S.op("dve", lambda e, s4=s4: e.tensor_tensor(s4[:, 3:4], s4[:, 2:3], s4[:, 2:3], ALU.mult), reads=[st4b[r]], writes=[st4b[r]])
            S.op("dve", lambda e, s4=s4: e.scalar_tensor_tensor(s4[:, 4:5], s4[:, 1:2], 1.0 / D, s4[:, 3:4], ALU.mult, ALU.subtract),
                 reads=[st4b[r]], writes=[st4b[r]])
            S.op("act", lambda e, s4=s4: e.activation(s4[:, 5:6], s4[:, 4:5], AF.Sqrt, bias=C.eps_t[:, 0:1], scale=1.0),
                 reads=[st4b[r], C.cb], writes=[st4b[r]])
            S.op("dve", lambda e, s4=s4: e.reciprocal(s4[:, 6:7], s4[:, 5:6]), reads=[st4b[r]], writes=[st4b[r]])
            S.op("dve", lambda e, s4=s4: e.scalar_tensor_tensor(s4[:, 7:8], s4[:, 2:3], -1.0, s4[:, 6:7], ALU.mult, ALU.mult),
                 reads=[st4b[r]], writes=[st4b[r]])
            S.op("dve", lambda e, r=r, s4=s4: e.tensor_scalar(vh[r][:, :], vin[r][:, :], s4[:, 6:7], s4[:, 7:8], ALU.mult, ALU.add),
                 reads=[vinb[r], st4b[r]], writes=[vhb[r]])
            for k4 in range(KC // 4):
                bi = oi % 8
                oi += 1
                for j in range(4):
                    kc = k4 * 4 + j
                    S.op("pe", lambda e, bi=bi, j=j, kc=kc, r=r: e.matmul(C.ps[bi][:, j * 128:(j + 1) * 128], vh[r][:, kc * 128:(kc + 1) * 128],
                                                                         wsbf[:, kc // 2, :], start=True, stop=True),
                         reads=[vhb[r], wsbfb], writes=[C.psb[bi]], signal=(j == 3))
                for j in range(4):
                    kc = k4 * 4 + j
                    s3 = (k4 * 4 + j) % 3
                    S.op("dve", lambda e, bi=bi, j=j, kc=kc, s3=s3: e.scalar_tensor_tensor(
                        sv[s3][:, :], C.ps[bi][:, j * 128:(j + 1) * 128], lg[:, kc:kc + 1], T2[:, kc, :], ALU.mult, ALU.add),
                        reads=[C.psb[bi], lgb, T2b], writes=[svb[s3]])
                    S.op("dve", lambda e, kc=kc, s3=s3, tb=tb: e.tensor_tensor(
                        PT[:, kc, tb * 128:(tb + 1) * 128], PT[:, kc, tb * 128:(tb + 1) * 128], sv[s3][:, :], ALU.mult),
                        reads=[svb[s3], PTb], writes=[PTb])
        gemm_to_dram(C, Wout, D, PT, PTb, KC, TP, YT, YTb, 0, None, F32, tag + "o")
    postnorm_resid(C, YT, YTb, g_post, XT, XTb, tag + "p")


def xin_stage(C, x_own, XT, XTb, ident, identb):
    S, nc = C.S, C.nc
    with contextlib.ExitStack() as st:
        xr = [st.enter_context(nc.sbuf_tensor("xi_r%d" % i, [128, D], F32)) for i in range(2)]
        xrb = [Buf() for _ in range(2)]
        xo = [st.enter_context(nc.sbuf_tensor("xi_o%d" % i, [128, 4, 128], F32)) for i in range(3)]
        xob = [Buf() for _ in range(3)]
        inb = Buf()
        oi = 0
        for tb in range(TP // 128):
            r = tb % 2
            S.dma("sp", xr[r][:, 0:D // 2], x_own[tb * 128:(tb + 1) * 128, 0:D // 2], reads=[inb], writes=[xrb[r]])
            S.dma("act", xr[r][:, D // 2:D], x_own[tb * 128:(tb + 1) * 128, D // 2:D], reads=[inb], writes=[xrb[r]])
            for k4 in range(KC // 4):
                bi = oi % 8
                o = oi % 3
                oi += 1
                for j in range(4):
                    kc = k4 * 4 + j
                    S.op("pe", lambda e, bi=bi, j=j, kc=kc, r=r: e.transpose(C.ps[bi][:, j * 128:(j + 1) * 128],
                                                                            xr[r][:, kc * 128:(kc + 1) * 128], ident[:, :]),
                         reads=[xrb[r], identb], writes=[C.psb[bi]], signal=(j == 3))
                S.op("dve", lambda e, bi=bi, o=o: e.tensor_copy(xo[o][:, :, :], C.ps[bi][:, :]), reads=[C.psb[bi]], writes=[xob[o]])
                dst = XT[k4 * 512:(k4 + 1) * 512, tb * 128:(tb + 1) * 128].rearrange("(j p) t -> p j t", p=128)
                S.dma(C.q(), dst, xo[o][:, :, :], reads=[xob[o]], writes=[XTb[k4 * 4 + j] for j in range(4)])
        S.barrier()


def xout_stage(C, XT, XTb, out, outb, ident, identb):
    S, nc = C.S, C.nc
    with contextlib.ExitStack() as st:
        xr = [st.enter_context(nc.sbuf_tensor("xo_r%d" % i, [128, TP], F32)) for i in range(2)]
        xrb = [Buf() for _ in range(2)]
        xo = [st.enter_context(nc.sbuf_tensor("xo_o%d" % i, [128, 4, 128], F32)) for i in range(3)]
        xob = [Buf() for _ in range(3)]
        oi = 0
        for kc in range(KC):
            r = kc % 2
            S.dma(C.q(), xr[r][:, :], XT[kc * 128:(kc + 1) * 128, :], reads=[XTb[kc]], writes=[xrb[r]])
            for t4 in range(TP // 512):
                bi = oi % 8
                o = oi % 3
                oi += 1
                for j in range(4):
                    tb = t4 * 4 + j
                    S.op("pe", lambda e, bi=bi, j=j, tb=tb, r=r: e.transpose(C.ps[bi][:, j * 128:(j + 1) * 128],
                                                                            xr[r][:, tb * 128:(tb + 1) * 128], ident[:, :]),
                         reads=[xrb[r], identb], writes=[C.psb[bi]], signal=(j == 3))
                S.op("dve", lambda e, bi=bi, o=o: e.tensor_copy(xo[o][:, :, :], C.ps[bi][:, :]), reads=[C.psb[bi]], writes=[xob[o]])
                dst = out[t4 * 512:(t4 + 1) * 512, kc * 128:(kc + 1) * 128].rearrange("(j p) f -> p j f", p=128)
                S.dma(C.q(), dst, xo[o][:, :, :], reads=[xob[o]], writes=[outb])
        S.barrier()


def dram_in(nc, name, shape, dt=F32):
    return nc.dram_tensor(name, list(shape), dt, kind="ExternalInput").ap()


def dram_scratch(nc, name, shape, dt=F32):
    return nc.dram_tensor(name, list(shape), dt, kind="Internal").ap()


def phase2_decl(nc):
    A = {}
    A["x_own"] = dram_in(nc, "x_own", [TP, D])
    A["ident_d"] = dram_in(nc, "ident", [128, 128])
    A["nmpost"] = dram_in(nc, "norm_mix_post", [2, D])
    A["nmpre"] = dram_in(nc, "norm_mix_pre", [2, D])
    A["nfpre"] = dram_in(nc, "norm_ffn_pre", [2, D])
    A["nfpost"] = dram_in(nc, "norm_ffn_post", [2, D])
    A["Wabo"] = dram_in(nc, "ab_w_out", [D, D])
    A["Wg"] = dram_in(nc, "ffn_w_gate", [2, D, DFF])
    A["Wu"] = dram_in(nc, "ffn_w_up", [2, D, DFF])
    A["Wd"] = dram_in(nc, "ffn_w_down", [2, DFF, D])
    A["Wsi"] = dram_in(nc, "sgu_w_in", [D, 2 * D])
    A["Wso"] = dram_in(nc, "sgu_w_out", [D, D])
    A["lng"] = dram_in(nc, "sgu_ln_g", [D])
    A["lnb"] = dram_in(nc, "sgu_ln_b", [D])
    A["wsT"] = dram_in(nc, "sgu_wsT", [128, 16, 128])
    A["bsb"] = dram_in(nc, "sgu_bs_b", [128, 16, 128])
    A["maskT"] = dram_in(nc, "sgu_maskT", [128, 128])
    A["out"] = nc.dram_tensor("out", [TP, D], F32, kind="ExternalOutput").ap()
    A["XT"] = dram_scratch(nc, "XT", [D, TP])
    A["YT"] = dram_scratch(nc, "YT", [D, TP])
    A["HID"] = dram_scratch(nc, "HID", [DFF, TP], BF16)
    A["UT"] = dram_scratch(nc, "UT", [D, TP], BF16)
    A["VTM"] = dram_scratch(nc, "VTM", [TP, D])
    return A


def build_phase2(stages=("in", "ab", "ffn0", "sgu", "ffn1", "out")):
    nc = bass.Bass("TRN2", target_bir_lowering=False)
    A = phase2_decl(nc)
    A["YC"] = dram_in(nc, "yc", [D, TP], BF16)
    with nc.cleanup_on_exit():
        S = Sched(nc)
        phase2_body(nc, S, A, stages, None)
        S.finish()
    return nc


def about_stage_sel(C, XT, XTb, YT, YTb, G, Gb, sel_d, g_post, Wo, tag):
    S, nc = C.S, C.nc
    with contextlib.ExitStack() as st:
        R = st.enter_context(nc.sbuf_tensor(tag + "R", [128, KC, TP], BF16))
        Rb = Buf()
        sel = st.enter_context(nc.sbuf_tensor(tag + "sel", [128, 4], F32))
        selb = Buf()
        S.dma("sp", sel[:, :], sel_d, writes=[selb])
        c4 = [st.enter_context(nc.sbuf_tensor(tag + "c4%d" % i, [128, 4, TT], BF16)) for i in range(3)]
        c4b = [Buf() for _ in range(3)]
        Gv = G.rearrange("(j i) r t -> i r j t", i=2)
        n = 0
        for kc in range(KC):
            if kc < 16:
                r, lc = kc // 4, kc % 4
            else:
                r, lc = (kc - 16) // 4, 4 + (kc - 16) % 4
            row = r * 1024 + lc * 128
            for hf in range(2):
                i = n % 3
                n += 1
                ts2 = slice(hf * TT, (hf + 1) * TT)
                S.dma(C.q(), c4[i][:, :, :], Gv[hf, row:row + 128, :, :], reads=list(Gb), writes=[c4b[i]])
                S.op("dve", lambda e, i=i, kc=kc, ts2=ts2: e.tensor_scalar(R[:, kc, ts2], c4[i][:, 0, :], sel[:, 0:1], None, ALU.mult),
                     reads=[c4b[i], selb], writes=[Rb])
                for j in range(1, 4):
                    S.op("dve", lambda e, i=i, kc=kc, j=j, ts2=ts2: e.scalar_tensor_tensor(R[:, kc, ts2], c4[i][:, j, :], sel[:, j:j + 1], R[:, kc, ts2],
                                                                                      ALU.mult, ALU.add), reads=[c4b[i], selb, Rb], writes=[Rb])
        gemm_to_dram(C, Wo, D, R, Rb, KC, TP, YT, YTb, 0, None, F32, tag + "g")
    postnorm_resid(C, YT, YTb, g_post, XT, XTb, tag + "p")


def phase2_body(nc, S, A, stages, gathered):
    x_own, ident_d, nmpost, nmpre, nfpre, nfpost, Wabo, Wg, Wu, Wd, Wsi, Wso, lng, lnb, wsT, bsb, maskT, out, XT, YT, HID, UT, VTM = [A[k] for k in (
        "x_own", "ident_d", "nmpost", "nmpre", "nfpre", "nfpost", "Wabo", "Wg", "Wu", "Wd", "Wsi", "Wso", "lng", "lnb", "wsT", "bsb", "maskT",
        "out", "XT", "YT", "HID", "UT", "VTM")]
    XTb = [Buf() for _ in range(KC)]
    YTb = [Buf() for _ in range(KC)]
    HIDb = [Buf() for _ in range(DFF // 128)]
    UTb = [Buf() for _ in range(KC)]
    VTMb = [Buf() for _ in range(TP // 128)]
    YCb = [Buf() for _ in range(KC)]
    outb = Buf()
    if True:
        with contextlib.ExitStack() as st:
            C = Ctx(nc, S, st)
            ident = C.sb("ident_sb", [128, 128], F32)
            identb = Buf()
            S.dma("sp", ident[:, :], ident_d, writes=[identb])
            C.eps_t = C.sb("eps_t2", [128, 1], F32)
            S.op("dve", lambda e: e.memset(C.eps_t[:, :], EPS), writes=[C.cb])
            if "in" in stages:
                xin_stage(C, x_own, XT, XTb, ident, identb)
            if "ab" in stages:
                if gathered is None:
                    about_stage(C, XT, XTb, YT, YTb, A["YC"], YCb, nmpost[0], Wabo, "ab")
                else:
                    G, Gb, sel_d = gathered
                    about_stage_sel(C, XT, XTb, YT, YTb, G, Gb, sel_d, nmpost[0], Wabo, "ab")
            if "ffn0" in stages:
                ffn_stage(C, XT, XTb, YT, YTb, HID, HIDb, nfpre[0], nfpost[0], Wg[0], Wu[0], Wd[0], "f0")
            if "sgu" in stages:
                sgu_stage(C, XT, XTb, YT, YTb, UT, UTb, VTM, VTMb, nmpre[1], nmpost[1], Wsi, lng, lnb, wsT, bsb, maskT, Wso, "sg")
            if "ffn1" in stages:
                ffn_stage(C, XT, XTb, YT, YTb, HID, HIDb, nfpre[1], nfpost[1], Wg[1], Wu[1], Wd[1], "f1")
            if "out" in stages:
                xout_stage(C, XT, XTb, out, outb, ident, identb)
            S.barrier()


def build_fused():
    nc = bass.Bass("TRN2", target_bir_lowering=False)
    A1 = phase1_decl(nc)
    A2 = phase2_decl(nc)
    sel_d = dram_in(nc, "sel", [128, 4])
    NT = SEQ // TT
    YL = [nc.dram_tensor("YL%d" % t, [1024, TT], BF16) for t in range(NT)]
    GG = nc.dram_tensor("YG", [NT, 4 * 1024, TT], BF16)
    A1["YCT"] = lambda t: YL[t].ap()
    with nc.cleanup_on_exit():
        S = Sched(nc)
        ylb = [Buf() for _ in range(NT)]
        Gb = [Buf() for _ in range(NT)]
        A1["after_tile"] = lambda t: S.collective("AllGather", YL[t].ap().opt(), GG.ap()[t].opt(), [[0, 1, 2, 3], [4, 5, 6, 7]],
                                                  reads=[ylb[t]], writes=[Gb[t]])
        phase1_body(nc, S, A1, ylb)
        phase2_body(nc, S, A2, ("in", "ab", "ffn0", "sgu", "ffn1", "out"), (GG.ap(), Gb, sel_d))
        S.finish()
    return nc


def phase2_consts(inputs):
    ws = np.asarray(inputs["sgu_w_s"][0], np.float32)
    wsT = np.ascontiguousarray(ws.transpose(2, 0, 1))
    pos = np.arange(128)
    maskT = ((pos[:, None] // 64) <= (pos[None, :] // 64)).astype(np.float32)
    bs = np.asarray(inputs["sgu_b_s"][0], np.float32)
    bsb = np.ascontiguousarray(np.broadcast_to(bs[None], (128, 16, 128)))
    return {
        "ident": np.eye(128, dtype=np.float32),
        "norm_mix_post": np.asarray(inputs["norm_mix_post"], np.float32),
        "norm_mix_pre": np.asarray(inputs["norm_mix_pre"], np.float32),
        "norm_ffn_pre": np.asarray(inputs["norm_ffn_pre"], np.float32),
        "norm_ffn_post": np.asarray(inputs["norm_ffn_post"], np.float32),
        "ab_w_out": np.asarray(inputs["ab_w_out"][0], np.float32),
        "ffn_w_gate": np.asarray(inputs["ffn_w_gate"], np.float32),
        "ffn_w_up": np.asarray(inputs["ffn_w_up"], np.float32),
        "ffn_w_down": np.asarray(inputs["ffn_w_down"], np.float32),
        "sgu_w_in": np.asarray(inputs["sgu_w_in"][0], np.float32),
        "sgu_w_out": np.asarray(inputs["sgu_w_out"][0], np.float32),
        "sgu_ln_g": np.asarray(inputs["sgu_ln_g"][0], np.float32),
        "sgu_ln_b": np.asarray(inputs["sgu_ln_b"][0], np.float32),
        "sgu_wsT": wsT, "sgu_bs_b": bsb, "sgu_maskT": maskT,
    }


NH = 4
TT = 512
NCOL1 = 2568


def phase1_decl(nc):
    A = {}
    A["xb"] = dram_in(nc, "xb", [SEQ, D])
    A["gpre"] = dram_in(nc, "g_pre", [D])
    A["Wc"] = dram_in(nc, "w_in_c", [D, NCOL1])
    A["cw_d"] = dram_in(nc, "conv_w", [128, 12, 4])
    A["nega_d"] = dram_in(nc, "neg_a", [128, 4])
    A["dtb_d"] = dram_in(nc, "dt_b", [128, 4])
    A["gnw_d"] = dram_in(nc, "gn_w", [128, 128])
    A["pw_d"] = dram_in(nc, "pool_wg", [512, 512])
    A["psc_d"] = dram_in(nc, "pool_sc", [512])
    A["band_d"] = dram_in(nc, "bandm", [3, 128, 128])
    A["mask_d"] = dram_in(nc, "masks", [5, 128, 128])
    return A


def build_phase1(ntiles=SEQ // TT, stop_after=None, dbgk=99):
    nc = bass.Bass("TRN2", target_bir_lowering=False)
    A = phase1_decl(nc)
    yct = nc.dram_tensor("yct", [1024, SEQ], BF16, kind="ExternalOutput").ap()
    A["YCT"] = lambda t: yct[:, t * TT:(t + 1) * TT]
    with nc.cleanup_on_exit():
        S = Sched(nc)
        phase1_body(nc, S, A, [Buf() for _ in range(SEQ // TT)], ntiles, stop_after, dbgk)
        S.finish()
    return nc


def phase1_body(nc, S, A, outb, ntiles=SEQ // TT, stop_after=None, dbgk=99):
    xb, gpre, Wc, cw_d, nega_d, dtb_d, gnw_d, pw_d, psc_d, band_d, mask_d, YCT = [A[k] for k in (
        "xb", "gpre", "Wc", "cw_d", "nega_d", "dtb_d", "gnw_d", "pw_d", "psc_d", "band_d", "mask_d", "YCT")]
    if True:
        with contextlib.ExitStack() as st:
            C = Ctx(nc, S, st, NW=4, pfx="p1_")
            sb = C.sb
            C.eps_t = sb("eps_t", [128, 1], F32)
            S.op("dve", lambda e: e.memset(C.eps_t[:, :], EPS), writes=[C.cb])
            C.one_t = sb("one_t", [128, 1], F32)
            S.op("dve", lambda e: e.memset(C.one_t[:, :], 1.0), writes=[C.cb])
            mk = sb("mk", [128, 5, 128], F32)
            mkb = Buf()
            S.dma("sp", mk[:, :, :], mask_d.rearrange("m p f -> p m f"), writes=[mkb])
            triA, strictU, MS, MU, ident = [mk[:, i, :] for i in range(5)]
            band = sb("band", [128, 3, 128], F32)
            bandb = Buf()
            S.dma("act", band[:, :, :], band_d.rearrange("m p f -> p m f"), writes=[bandb])
            g = sb("g", [128, KC], F32)
            gb = Buf()
            S.dma("sp", g[:, :], gpre.rearrange("(kc p) -> p kc", p=128), writes=[gb], allow_slow_non_contiguous=True)
            cw = sb("cw", [128, 12, 4], F32)
            cwb = Buf()
            S.dma("sp", cw[:, :, :], cw_d, writes=[cwb])
            nega = sb("nega", [128, 4], F32)
            dtb = sb("dtb", [128, 4], F32)
            gnw = sb("gnw", [128, 128], F32)
            psc = sb("psc", [128, 4], F32)
            smb = SBuf()
            S.dma("sp", nega[:, :], nega_d, writes=[smb])
            S.dma("sp", dtb[:, :], dtb_d, writes=[smb])
            S.dma("sp", gnw[:, :], gnw_d, writes=[smb])
            S.dma("sp", psc[:, :], psc_d.rearrange("(c p) -> p c", p=128), writes=[smb], allow_slow_non_contiguous=True)
            S.op("act", lambda e: e.activation(nega[:, :], nega[:, :], AF.Exp), reads=[smb], writes=[smb])
            S.op("dve", lambda e: e.tensor_scalar(nega[:, :], nega[:, :], -1.0, None, ALU.mult), reads=[smb], writes=[smb])
            pw = sb("pw", [128, 4, 512], BF16)
            pwb = Buf()
            S.dma("pool", pw[:, :, :], pw_d.rearrange("(c p) n -> p c n", p=128), writes=[pwb])
            wbd = sb("wbd", [128, KC, 8], BF16)
            wbdb = Buf()
            S.dma("pool", wbd[:, :, :], Wc.rearrange("(kc p) n -> p kc n", p=128)[:, :, 2560:2568], writes=[wbdb],
                  allow_slow_non_contiguous=True)
            xr = sb("xr", [128, D], F32)
            xrb = Buf()
            HT = sb("HT", [128, KC, TT], BF16)
            HTb = Buf()
            st8 = sb("st8", [128, 12], F32)
            st8b = SBuf()
            halo = sb("halo", [128, 12, 4], F32)
            halob = Buf()
            S.op("dve", lambda e: e.memset(halo[:, :, :], 0.0), writes=[halob])
            Z = [sb("Z%d" % i, [128, 3 + TT], F32) for i in range(4)]
            Zb = [Buf() for _ in range(4)]
            junkA = sb("junkA", [128, TT], BF16)
            junkAb = Buf()
            QT = sb("QT", [128, NH, TT], BF16)
            KT = sb("KT", [128, NH, TT], BF16)
            KTf = sb("KTf", [128, NH, TT], F32)
            VT = sb("VT", [128, NH, TT], F32)
            QTb, KTb, KTfb, VTb = [[Buf() for _ in range(NH)] for _ in range(4)]
            sg2 = [sb("sg%d" % i, [128, 4, 512], F32) for i in range(2)]
            sgb2 = [[Buf() for _ in range(4)] for _ in range(2)]
            xa = sb("xa", [128, 5, 512], F32)
            xab = [Buf() for _ in range(5)]
            lg82 = [sb("lg8%d" % i, [128, 4, 8], F32) for i in range(2)]
            lg8b2 = [[SBuf() for _ in range(4)] for _ in range(2)]
            OT = sb("OT", [128, 8, TT], BF16)
            OTb = Buf()
            PLT = sb("PLT", [128, 4, TT], BF16)
            PLTb = Buf()
            Sst = sb("Sst", [128, NH, 128], F32)
            Sbf = sb("Sbf", [128, NH, 128], BF16)
            Sstb = [Buf() for _ in range(NH)]
            Sbfb = [Buf() for _ in range(NH)]
            for h in range(NH):
                S.op("dve", lambda e, h=h: e.memset(Sst[:, h, :], 0.0), writes=[Sstb[h]])
                S.op("dve", lambda e, h=h: e.memset(Sbf[:, h, :], 0.0), writes=[Sbfb[h]])

            def ht(name, n=128, dt=F32, strict=False):
                t = sb(name, [128, NH, n], dt)
                return t, [Buf(name, strict) for _ in range(NH)]
            b4, b4b = ht("b4", 8, strict=True)
            G2, G2b = ht("G2")
            gB, gBb = ht("gB")
            EX, EXb = ht("EX", 392, strict=True)
            Es, Esb = ht("Es")
            ETc, ETcb = ht("ETc")
            ngb, ngbb = ht("ngb", 2, strict=True)
            Lm, Lb = ht("L")
            AT, ATb = ht("AT")
            kd, kdb = ht("kd")
            vb, vbb = ht("vb")
            Xa, Xab_ = ht("Xa")
            Xat, Xatb = ht("Xat")
            Xb, Xbb = ht("Xb")
            Xbt, Xbtb = ht("Xbt")
            Pm, Pmb = ht("Pm")
            A2T, A2Tb = ht("A2T", 128, BF16)
            K2, K2b = ht("K2", 128, BF16)
            qdT, qdTb = ht("qdT", 128, BF16)
            Rbf, Rbfb = ht("Rbf", 128, BF16)
            gsg, gsgb = gB, gBb
            yo, yob = G2, G2b
            flat = lambda t: t[:, :, :].rearrange("p h n -> p (h n)")
            accv = [flat(t) for t in (G2, gB, Es, ETc)]
            accB = [G2b, gBb, Esb, ETcb]
            slv = [flat(t) for t in (Lm, AT, kd, vb)]
            slB = [Lb, ATb, kdb, vbb]
            ps, psb = C.ps, C.psb
            Wv_ = Wc.rearrange("(kc p) n -> p kc n", p=128)

            def emit_A(tile):
                t0 = tile * TT
                for tb in range(4):
                    r0 = t0 + tb * 128
                    S.dma("sp", xr[:, 0:D // 2], xb[r0:r0 + 128, 0:D // 2], writes=[xrb])
                    S.dma("act", xr[:, D // 2:D], xb[r0:r0 + 128, D // 2:D], writes=[xrb])
                    for q8 in range(8):
                        S.op("act", lambda e, q8=q8: e.activation(junkA[:, :], xr[:, q8 * 512:(q8 + 1) * 512], AF.Square,
                                                                  accum_out=st8[:, q8:q8 + 1]), reads=[xrb], writes=[junkAb, st8b])
                    S.op("dve", lambda e: e.tensor_reduce(st8[:, 8:9], st8[:, 0:8], AX.X, ALU.add), reads=[st8b], writes=[st8b])
                    S.op("act", lambda e: e.activation(st8[:, 9:10], st8[:, 8:9], AF.Sqrt, bias=C.eps_t[:, 0:1], scale=1.0 / D),
                         reads=[st8b, C.cb], writes=[st8b])
                    S.op("dve", lambda e: e.reciprocal(st8[:, 10:11], st8[:, 9:10]), reads=[st8b], writes=[st8b])
                    S.op("act", lambda e: e.activation(xr[:, :], xr[:, :], AF.Copy, scale=st8[:, 10:11]), reads=[xrb, st8b], writes=[xrb])
                    for k4 in range(KC // 4):
                        bi = 4 + k4 % 4
                        for j in range(4):
                            kc = k4 * 4 + j
                            S.op("pe", lambda e, bi=bi, j=j, kc=kc: e.transpose(ps[bi][:, j * 128:(j + 1) * 128],
                                                                               xr[:, kc * 128:(kc + 1) * 128], ident),
                                 reads=[xrb, mkb], writes=[psb[bi]], signal=(j == 3))
                        for j in range(4):
                            kc = k4 * 4 + j
                            S.op("dve" if j % 2 == 0 else "act",
                                 (lambda e, bi=bi, j=j, kc=kc, tb=tb: e.tensor_scalar(HT[:, kc, tb * 128:(tb + 1) * 128], ps[bi][:, j * 128:(j + 1) * 128],
                                                                                     g[:, kc:kc + 1], None, ALU.mult)) if j % 2 == 0 else
                                 (lambda e, bi=bi, j=j, kc=kc, tb=tb: e.activation(HT[:, kc, tb * 128:(tb + 1) * 128], ps[bi][:, j * 128:(j + 1) * 128],
                                                                                  AF.Copy, scale=g[:, kc:kc + 1])),
                                 reads=[psb[bi], gb], writes=[HTb])
            def emit_B1(tile):
                t0 = tile * TT
                sg, sgb, lg8, lg8b = sg2[tile % 2], sgb2[tile % 2], lg82[tile % 2], lg8b2[tile % 2]
                for cgi in range(2):
                    slots = []
                    for u in range(4):
                        wt, wb = C.wslot()
                        wv = wt[:, :].rearrange("p (k n) -> p k n", n=512)
                        S.dma("pool", wv, Wv_[:, u * 8:(u + 1) * 8, cgi * 512:(cgi + 1) * 512], writes=[wb])
                        slots.append((wv, wb))
                    for tb in range(4):
                        bi = 4 + tb
                        for kc in range(KC):
                            wv, wb = slots[kc // 8]
                            S.op("pe", lambda e, bi=bi, wv=wv, kc=kc, tb=tb: e.matmul(
                                ps[bi][:, :], HT[:, kc, tb * 128:(tb + 1) * 128], wv[:, kc % 8, :], start=(kc == 0), stop=(kc == KC - 1)),
                                reads=[wb, HTb], writes=[psb[bi]], signal=(kc % 8 == 7))
                        if cgi == 0:
                            S.op("act", lambda e, bi=bi, tb=tb: e.activation(xa[:, tb + 1, :], ps[bi][:, :], AF.Copy),
                                 reads=[psb[bi]], writes=[xab[tb + 1]])
                        else:
                            S.op("act", lambda e, sg=sg, lg8=lg8, bi=bi, tb=tb: e.activation(sg[:, tb, :], ps[bi][:, :], AF.Silu),
                                 reads=[psb[bi]], writes=[sgb[tb]])
                for tb in range(4):
                    bi = 4 + tb
                    for kc in range(KC):
                        S.op("pe", lambda e, bi=bi, kc=kc, tb=tb: e.matmul(ps[bi][:, 0:8], HT[:, kc, tb * 128:(tb + 1) * 128], wbd[:, kc, :],
                                                                          start=(kc == 0), stop=(kc == KC - 1)),
                             reads=[wbdb, HTb], writes=[psb[bi]], signal=(kc == KC - 1))
                    S.op("dve", lambda e, sg=sg, lg8=lg8, bi=bi, tb=tb: e.tensor_copy(lg8[:, tb, :], ps[bi][:, 0:8]), reads=[psb[bi]], writes=[lg8b[tb]])
            def emit_B2E(tile):
                t0 = tile * TT
                bank_of = lambda grp: [4, 5, 6, 7] if grp % 2 == 0 else [0, 1, 2, 3]
                gemm_cg(C, Wc, 1024, 512, HT, HTb, KC, TT, bank_of(0))
                for grp in range(3):
                    banks = bank_of(grp)
                    if grp + 1 < 3:
                        gemm_cg(C, Wc, 1024 + (grp + 1) * 512, 512, HT, HTb, KC, TT, bank_of(grp + 1))
                    lists = []
                    for h in range(NH):
                        S.record()
                        c = grp * 4 + h
                        zi = h
                        bi = banks[h]
                        S.op("dve", lambda e, zi=zi, c=c: e.tensor_copy(Z[zi][:, 0:3], halo[:, c, 0:3]), reads=[halob], writes=[Zb[zi]])
                        S.op("act", lambda e, zi=zi, bi=bi: e.activation(Z[zi][:, 3:3 + TT], ps[bi][:, :], AF.Copy),
                             reads=[psb[bi]], writes=[Zb[zi]])
                        S.op("dve", lambda e, zi=zi, c=c: e.tensor_copy(halo[:, c, 0:3], Z[zi][:, TT:TT + 3]), reads=[Zb[zi]], writes=[halob])
                        S.op("dve", lambda e, zi=zi, c=c: e.tensor_scalar(accv[zi], Z[zi][:, 3:3 + TT], cw[:, c, 3:4], None, ALU.mult),
                             reads=[Zb[zi], cwb], writes=[*accB[zi]])
                        for j in range(3):
                            S.op("dve", lambda e, zi=zi, c=c, j=j: e.scalar_tensor_tensor(accv[zi], Z[zi][:, j:j + TT], cw[:, c, j:j + 1],
                                                                                         accv[zi], ALU.mult, ALU.add),
                                 reads=[Zb[zi], cwb, *accB[zi]], writes=[*accB[zi]])
                        if grp == 2:
                            S.op("act", lambda e, zi=zi, h=h: e.activation(VT[:, h, :], accv[zi], AF.Silu), reads=[*accB[zi]], writes=[VTb[h]])
                            lists.append(S.stop())
                            continue
                        S.op("act", lambda e, zi=zi: e.activation(slv[zi], accv[zi], AF.Silu), reads=[*accB[zi]], writes=[*slB[zi]])
                        S.op("act", lambda e, zi=zi: e.activation(accv[zi], slv[zi], AF.Square),
                             reads=[*slB[zi]], writes=[*accB[zi]])
                        pb = banks[h]
                        S.op("pe", lambda e, pb=pb, zi=zi: e.matmul(ps[pb][:, :], C.ones_f[:, :], accv[zi], start=True, stop=True),
                             reads=[*accB[zi], C.cb], writes=[psb[pb]])
                        S.op("act", lambda e, pb=pb, zi=zi: e.activation(accv[zi], ps[pb][:, :], AF.Sqrt, bias=C.eps_t[:, 0:1], scale=1.0),
                             reads=[psb[pb], C.cb], writes=[*accB[zi]])
                        S.op("dve", lambda e, zi=zi: e.reciprocal(accv[zi], accv[zi]), reads=[*accB[zi]], writes=[*accB[zi]])
                        if grp == 0:
                            S.op("dve", lambda e, zi=zi, h=h: e.scalar_tensor_tensor(QT[:, h, :], slv[zi], 128.0 ** -0.5, accv[zi],
                                                                                    ALU.mult, ALU.mult), reads=[*slB[zi], *accB[zi]], writes=[QTb[h]])
                        else:
                            S.op("dve", lambda e, zi=zi, h=h: e.tensor_tensor(KTf[:, h, :], slv[zi], accv[zi], ALU.mult),
                                 reads=[*slB[zi], *accB[zi]], writes=[KTfb[h]])
                            S.op("pool", lambda e, h=h: e.tensor_copy(KT[:, h, :], KTf[:, h, :]), reads=[KTfb[h]], writes=[KTb[h]])
                        lists.append(S.stop())
                    S.replay(lists)
                for tb in range(4):
                    first = (tile == 0 and tb == 0)
                    for c in range(4):
                        bi = 4 + c
                        if first:
                            S.op("pe", lambda e, bi=bi, c=c, tb=tb: e.matmul(ps[bi][:, 0:128], xa[:, tb + 1, c * 128:(c + 1) * 128], band[:, 0, :],
                                                                            start=True, stop=True), reads=[xab[tb + 1], bandb], writes=[psb[bi]])
                        else:
                            S.op("pe", lambda e, bi=bi, c=c, tb=tb: e.matmul(ps[bi][:, 0:128], xa[:, tb, c * 128:(c + 1) * 128], band[:, 2, :],
                                                                            start=True, stop=False), reads=[xab[tb], bandb], writes=[psb[bi]], signal=False)
                            S.op("pe", lambda e, bi=bi, c=c, tb=tb: e.matmul(ps[bi][:, 0:128], xa[:, tb + 1, c * 128:(c + 1) * 128], band[:, 1, :],
                                                                            start=False, stop=True), reads=[xab[tb + 1], bandb], writes=[psb[bi]])
                        S.op("act", lambda e, bi=bi, c=c, tb=tb: e.activation(PLT[:, c, tb * 128:(tb + 1) * 128], ps[bi][:, 0:128], AF.Copy),
                             reads=[psb[bi]], writes=[PLTb])
                S.op("pool", lambda e: e.tensor_copy(xa[:, 0, :], xa[:, 4, :]), reads=[xab[4]], writes=[xab[0]])
                for dc in range(4):
                    bi = dc
                    for c in range(4):
                        S.op("pe", lambda e, bi=bi, c=c, dc=dc: e.matmul(ps[bi][:, :], pw[:, c, dc * 128:(dc + 1) * 128], PLT[:, c, :],
                                                                        start=(c == 0), stop=(c == 3)), reads=[pwb, PLTb], writes=[psb[bi]], signal=(c == 3))
                    S.op("act", lambda e, bi=bi, dc=dc: e.activation(OT[:, dc, :], ps[bi][:, :], AF.Copy, scale=psc[:, dc:dc + 1]),
                         reads=[psb[bi], smb], writes=[OTb])
            def emit_D(tile):
                t0 = tile * TT
                sg, sgb, lg8, lg8b = sg2[tile % 2], sgb2[tile % 2], lg82[tile % 2], lg8b2[tile % 2]
                H = range(NH)
                for tb in range(4):
                    ts_ = slice(tb * 128, (tb + 1) * 128)
                    S.op("act", lambda e, sg=sg, lg8=lg8, tb=tb: e.activation(b4[:, 0, 0:4], lg8[:, tb, 0:4], AF.Exp, scale=-1.0), reads=[lg8b[tb]], writes=[b4b[0]])
                    S.op("dve", lambda e: e.tensor_scalar(b4[:, 0, 0:4], b4[:, 0, 0:4], 1.0, None, ALU.add), reads=[b4b[0]], writes=[b4b[0]])
                    S.op("dve", lambda e: e.reciprocal(b4[:, 0, 0:4], b4[:, 0, 0:4]), reads=[b4b[0]], writes=[b4b[0]])
                    S.op("dve", lambda e, sg=sg, lg8=lg8, tb=tb: e.tensor_tensor(b4[:, 1, 0:4], lg8[:, tb, 4:8], dtb[:, :], ALU.add), reads=[lg8b[tb], smb], writes=[b4b[0]])
                    S.op("act", lambda e: e.activation(b4[:, 1, 0:4], b4[:, 1, 0:4], AF.Exp), reads=[b4b[0]], writes=[b4b[0]])
                    S.op("act", lambda e: e.activation(b4[:, 1, 0:4], b4[:, 1, 0:4], AF.Ln, bias=C.one_t[:, 0:1], scale=1.0), reads=[b4b[0], C.cb], writes=[b4b[0]])
                    S.op("dve", lambda e: e.tensor_tensor(b4[:, 0, 4:8], b4[:, 1, 0:4], nega[:, :], ALU.mult), reads=[b4b[0], smb], writes=[b4b[0]])
                    beta = lambda h: b4[:, 0, h:h + 1]
                    gcol = lambda h: b4[:, 0, 4 + h:5 + h]
                    gcol2 = lambda h: b4[:, 0, 4 + h:6 + h] if h < 3 else b4[:, 0, 6:8]
                    bb = b4b[0]
                    for h in H:
                        S.op("dve", lambda e, h=h: e.tensor_scalar(G2[:, h, :], strictU, gcol(h), None, ALU.mult), reads=[bb, mkb], writes=[G2b[h]])
                        S.op("dve", lambda e, h=h: e.tensor_scalar(gB[:, h, :], C.ones_f[:, :], gcol(h), None, ALU.mult), reads=[bb, C.cb], writes=[gBb[h]])
                    for h in H:
                        bi = h
                        S.op("pe", lambda e, h=h, bi=bi: e.matmul(ps[bi][:, 0:128], triA, G2[:, h, :], start=True, stop=True),
                             reads=[G2b[h], mkb], writes=[psb[bi]], signal=False)
                        S.op("pe", lambda e, h=h, bi=bi: e.matmul(ps[bi][:, 128:256], G2[:, h, :], triA, start=True, stop=True),
                             reads=[G2b[h], mkb], writes=[psb[bi]], signal=False)
                        S.op("pe", lambda e, h=h, bi=bi: e.matmul(ps[bi][:, 256:384], gB[:, h, :], triA, start=True, stop=True),
                             reads=[gBb[h], mkb], writes=[psb[bi]], signal=False)
                        gsrc = (lambda h: b4[:, 0, 4 + h:6 + h]) if True else None
                        hh = min(h, 2)
                        off = h - hh
                        S.op("pe", lambda e, hh=hh, bi=bi: e.matmul(ps[bi][:, 384:386], triA, b4[:, 0, 4 + hh:6 + hh], start=True, stop=True),
                             reads=[bb, mkb], writes=[psb[bi]], signal=False)
                        S.op("pe", lambda e, hh=hh, bi=bi: e.matmul(ps[bi][:, 386:388], strictU, b4[:, 0, 4 + hh:6 + hh], start=True, stop=True),
                             reads=[bb, mkb], writes=[psb[bi]], signal=False)
                        S.op("pe", lambda e, hh=hh, bi=bi: e.matmul(ps[bi][:, 388:390], C.ones_f[:, :], b4[:, 0, 4 + hh:6 + hh], start=True, stop=True),
                             reads=[bb, C.cb], writes=[psb[bi]])
                        S.op("act", lambda e, h=h, bi=bi: e.activation(EX[:, h, 0:390], ps[bi][:, 0:390], AF.Exp), reads=[psb[bi]], writes=[EXb[h]])
                    gam = lambda h: EX[:, h, 384 + (h - min(h, 2)):385 + (h - min(h, 2))]
                    kds = lambda h: EX[:, h, 386 + (h - min(h, 2)):387 + (h - min(h, 2))]
                    gl_ = lambda h: EX[:, h, 388 + (h - min(h, 2)):389 + (h - min(h, 2))]
                    for h in H:
                        S.op("dve", lambda e, h=h: e.tensor_tensor(Es[:, h, :], EX[:, h, 0:128], MS, ALU.mult), reads=[EXb[h], mkb], writes=[Esb[h]])
                        S.op("dve", lambda e, h=h: e.tensor_tensor(ETc[:, h, :], EX[:, h, 128:256], MU, ALU.mult), reads=[EXb[h], mkb], writes=[ETcb[h]])
                        S.op("dve", lambda e, h=h: e.scalar_tensor_tensor(ngb[:, h, 0:1], gam(h), -1.0, beta(h), ALU.mult, ALU.mult),
                             reads=[EXb[h], bb], writes=[ngbb[h]])
                    for h in H:
                        bi = h
                        S.op("pe", lambda e, ts_=ts_, h=h, bi=bi: e.matmul(ps[bi][:, 0:128], KT[:, h, ts_], KT[:, h, ts_], start=True, stop=True),
                             reads=[KTb[h]], writes=[psb[bi]], signal=False)
                        S.op("pe", lambda e, ts_=ts_, h=h, bi=bi: e.matmul(ps[bi][:, 128:256], KT[:, h, ts_], QT[:, h, ts_], start=True, stop=True),
                             reads=[KTb[h], QTb[h]], writes=[psb[bi]], signal=False)
                        S.op("pe", lambda e, ts_=ts_, h=h, bi=bi: e.transpose(ps[bi][:, 256:384], KTf[:, h, ts_], ident), reads=[KTfb[h], mkb], writes=[psb[bi]], signal=False)
                        S.op("pe", lambda e, ts_=ts_, h=h, bi=bi: e.transpose(ps[bi][:, 384:512], VT[:, h, ts_], ident), reads=[VTb[h], mkb], writes=[psb[bi]])
                    for h in H:
                        bi = h
                        if dbgk > 0:
                            S.op("dve", lambda e, h=h, bi=bi: e.scalar_tensor_tensor(Lm[:, h, :], ps[bi][:, 0:128], beta(h), Es[:, h, :], ALU.mult, ALU.mult),
                                 reads=[psb[bi], bb, Esb[h]], writes=[Lb[h]])
                        if dbgk > 1:
                            S.op("dve", lambda e, h=h, bi=bi: e.tensor_tensor(AT[:, h, :], ps[bi][:, 128:256], ETc[:, h, :], ALU.mult),
                                 reads=[psb[bi], ETcb[h]], writes=[ATb[h]])
                        if dbgk > 2:
                            S.op("dve", lambda e, h=h, bi=bi: e.tensor_scalar(kd[:, h, :], ps[bi][:, 256:384], kds(h), None, ALU.mult),
                                 reads=[psb[bi], EXb[h]], writes=[kdb[h]])
                        if dbgk > 3:
                            S.op("dve", lambda e, h=h, bi=bi: e.tensor_scalar(vb[:, h, :], ps[bi][:, 384:512], beta(h), None, ALU.mult),
                                 reads=[psb[bi], bb], writes=[vbb[h]])
                        if dbgk > 4:
                            S.op("dve", lambda e, ts_=ts_, h=h: e.tensor_tensor(qdT[:, h, :], QT[:, h, ts_], EX[:, h, 256:384], ALU.mult),
                                 reads=[QTb[h], EXb[h]], writes=[qdTb[h]])
                        if dbgk > 5:
                            S.op("dve", lambda e, sg=sg, lg8=lg8, h=h, tb=tb: e.tensor_tensor(gsg[:, h, :], sg[:, tb, h * 128:(h + 1) * 128], gnw[:, :], ALU.mult),
                                 reads=[sgb[tb], smb], writes=[gsgb[h]])
                    for h in H:
                        bi = h
                        S.op("pe", lambda e, h=h, bi=bi: e.transpose(ps[bi][:, 0:128], Lm[:, h, :], ident), reads=[Lb[h], mkb], writes=[psb[bi]])
                        S.op("dve", lambda e, h=h, bi=bi: e.tensor_copy(Xat[:, h, :], ps[bi][:, 0:128]), reads=[psb[bi]], writes=[Xatb[h]])
                        S.op("dve", lambda e, h=h: e.scalar_tensor_tensor(Pm[:, h, :], Lm[:, h, :], -1.0, ident, ALU.mult, ALU.add),
                             reads=[Lb[h], mkb], writes=[Pmb[h]])
                    cur = (Lm, Lb, Xat, Xatb)
                    nxt = [(Xb, Xbb, Xbt, Xbtb), (Xa, Xab_, Xat, Xatb)]
                    for s_ in range(7):
                        X, Xbuf, XT_, XTbuf = cur
                        N_, Nb, NT, NTb = nxt[s_ % 2]
                        for h in H:
                            bi = h
                            if s_ < 5:
                                S.op("pe", lambda e, h=h, bi=bi, X=X, XT_=XT_: e.matmul(ps[bi][:, 0:128], XT_[:, h, :], X[:, h, :], start=True, stop=True),
                                     reads=[Xbuf[h], XTbuf[h]], writes=[psb[bi]], signal=False)
                            if s_ <= 5:
                                S.op("pe", lambda e, h=h, bi=bi, X=X, XT_=XT_: e.matmul(ps[bi][:, 128:256], X[:, h, :], XT_[:, h, :], start=True, stop=True),
                                     reads=[Xbuf[h], XTbuf[h]], writes=[psb[bi]], signal=(s_ == 0))
                            if s_ >= 1:
                                S.op("pe", lambda e, h=h, bi=bi, XT_=XT_: e.matmul(ps[bi][:, 256:384], XT_[:, h, :], Pm[:, h, :], start=True, stop=True),
                                     reads=[XTbuf[h], Pmb[h]], writes=[psb[bi]])
                        for h in H:
                            bi = h
                            if s_ < 5:
                                S.op("dve", lambda e, h=h, bi=bi, N_=N_: e.tensor_copy(N_[:, h, :], ps[bi][:, 0:128]), reads=[psb[bi]], writes=[Nb[h]])
                            if s_ <= 5:
                                S.op("dve", lambda e, h=h, bi=bi, NT=NT: e.tensor_copy(NT[:, h, :], ps[bi][:, 128:256]), reads=[psb[bi]], writes=[NTb[h]])
                            if s_ >= 1:
                                S.op("dve", lambda e, h=h, bi=bi: e.tensor_tensor(Pm[:, h, :], Pm[:, h, :], ps[bi][:, 256:384], ALU.add),
                                     reads=[psb[bi], Pmb[h]], writes=[Pmb[h]])
                        cur = (N_, Nb, NT, NTb)
                    for h in H:
                        bi = h
                        S.op("pe", lambda e, h=h, bi=bi: e.matmul(ps[bi][:, 0:128], Pm[:, h, :], AT[:, h, :], start=True, stop=True),
                             reads=[Pmb[h], ATb[h]], writes=[psb[bi]], signal=False)
                        S.op("pe", lambda e, h=h, bi=bi: e.matmul(ps[bi][:, 128:256], Pm[:, h, :], kd[:, h, :], start=True, stop=True),
                             reads=[Pmb[h], kdb[h]], writes=[psb[bi]])
                    for h in H:
                        bi = h
                        S.op("dve", lambda e, h=h, bi=bi: e.tensor_copy(A2T[:, h, :], ps[bi][:, 0:128]), reads=[psb[bi]], writes=[A2Tb[h]])
                        S.op("dve", lambda e, h=h, bi=bi: e.tensor_copy(K2[:, h, :], ps[bi][:, 128:256]), reads=[psb[bi]], writes=[K2b[h]])
                    for h in H:
                        bi = h
                        S.op("pe", lambda e, ts_=ts_, h=h, bi=bi: e.matmul(ps[bi][:, 0:128], KT[:, h, ts_], Sbf[:, h, :], start=True, stop=True),
                             reads=[KTb[h], Sbfb[h]], writes=[psb[bi]])
                    for h in H:
                        bi = h
                        S.op("dve", lambda e, h=h, bi=bi: e.scalar_tensor_tensor(Rbf[:, h, :], ps[bi][:, 0:128], ngb[:, h, 0:1], vb[:, h, :], ALU.mult, ALU.add),
                             reads=[psb[bi], ngbb[h], vbb[h]], writes=[Rbfb[h]])
                    for h in H:
                        bi = h
                        S.op("pe", lambda e, h=h, bi=bi: e.matmul(ps[bi][:, 128:256], qdT[:, h, :], Sbf[:, h, :], start=True, stop=False),
                             reads=[qdTb[h], Sbfb[h]], writes=[psb[bi]], signal=False)
                        S.op("pe", lambda e, h=h, bi=bi: e.matmul(ps[bi][:, 128:256], A2T[:, h, :], Rbf[:, h, :], start=False, stop=True),
                             reads=[A2Tb[h], Rbfb[h]], writes=[psb[bi]], signal=False)
                        S.op("pe", lambda e, h=h, bi=bi: e.matmul(ps[bi][:, 256:384], K2[:, h, :], Rbf[:, h, :], start=True, stop=True),
                             reads=[K2b[h], Rbfb[h]], writes=[psb[bi]])
                    for h in H:
                        bi = h
                        S.op("dve", lambda e, h=h, bi=bi: e.scalar_tensor_tensor(Sst[:, h, :], Sst[:, h, :], gl_(h), ps[bi][:, 256:384], ALU.mult, ALU.add),
                             reads=[psb[bi], EXb[h], Sstb[h]], writes=[Sstb[h]])
                        S.op("dve", lambda e, h=h: e.tensor_copy(Sbf[:, h, :], Sst[:, h, :]), reads=[Sstb[h]], writes=[Sbfb[h]])
                        S.op("dve", lambda e, h=h, bi=bi: e.tensor_copy(Xa[:, h, :], ps[bi][:, 128:256]), reads=[psb[bi]], writes=[Xab_[h]])
                        S.op("act", lambda e, h=h: e.activation(yo[:, h, :], Xa[:, h, :], AF.Square, accum_out=ngb[:, h, 1:2]),
                             reads=[Xab_[h]], writes=[yob[h], ngbb[h]])
                        S.op("act", lambda e, h=h: e.activation(ngb[:, h, 1:2], ngb[:, h, 1:2], AF.Sqrt, bias=C.eps_t[:, 0:1], scale=1.0 / 128),
                             reads=[ngbb[h], C.cb], writes=[ngbb[h]])
                        S.op("dve", lambda e, h=h: e.reciprocal(ngb[:, h, 1:2], ngb[:, h, 1:2]), reads=[ngbb[h]], writes=[ngbb[h]])
                        S.op("dve", lambda e, sg=sg, lg8=lg8, h=h, bi=bi: e.scalar_tensor_tensor(yo[:, h, :], Xa[:, h, :], ngb[:, h, 1:2], gsg[:, h, :], ALU.mult, ALU.mult),
                             reads=[Xab_[h], ngbb[h], gsgb[h]], writes=[yob[h]])
                    for h in H:
                        bi = h
                        S.op("pe", lambda e, h=h, bi=bi: e.transpose(ps[bi][:, 0:128], yo[:, h, :], ident), reads=[yob[h], mkb], writes=[psb[bi]])
                        S.op("dve", lambda e, ts_=ts_, h=h, bi=bi: e.tensor_copy(OT[:, 4 + h, ts_], ps[bi][:, 0:128]), reads=[psb[bi]], writes=[OTb])
                S.dma("sp", YCT(tile).rearrange("(c p) t -> p c t", p=128), OT[:, :, :], reads=[OTb], writes=[outb[tile]])
                if A.get("after_tile") is not None:
                    A["after_tile"](tile)

            emit_A(0)
            emit_B1(0)
            for tile in range(ntiles):
                emit_B2E(tile)
                if tile + 1 < ntiles:
                    S.record()
                    emit_D(tile)
                    lD = S.stop()
                    S.record()
                    emit_A(tile + 1)
                    emit_B1(tile + 1)
                    lA = S.stop()
                    S.replay([lD, lA])
                else:
                    emit_D(tile)
            print('phase1 sbuf', nc.sbuf_base, nc.sbuf_top)
            S.barrier()


POOL_WINDOWS = (2, 4, 8, 16)


def phase1_inputs(inputs, b, hg):
    w = np.asarray(inputs["ab_w_in"][0], np.float32)
    PW, GW = 2048, 2048
    cols = np.concatenate([
        np.arange(hg * 512, (hg + 1) * 512),
        PW + 3 * GW + np.arange(hg * 512, (hg + 1) * 512),
        PW + np.arange(hg * 512, (hg + 1) * 512),
        PW + GW + np.arange(hg * 512, (hg + 1) * 512),
        PW + 2 * GW + np.arange(hg * 512, (hg + 1) * 512),
        PW + 4 * GW + np.arange(hg * 4, (hg + 1) * 4),
        PW + 4 * GW + 16 + np.arange(hg * 4, (hg + 1) * 4),
    ])
    wc = np.ascontiguousarray(w[:, cols])
    conv = np.asarray(inputs["gdn_conv"][0], np.float32)
    cwl = np.stack([conv[:, s * GW + hg * 512: s * GW + (hg + 1) * 512] for s in range(3)], 0)
    cwl = cwl.reshape(3, 4, 4, 128).transpose(3, 0, 2, 1).reshape(128, 12, 4)
    bc = lambda v: np.ascontiguousarray(np.broadcast_to(np.asarray(v, np.float32)[None, :], (128, len(v))))
    win = POOL_WINDOWS[hg]
    pos = np.arange(256)
    def band_full(first):
        B = np.zeros((256, 128), np.float32)
        for t in range(128):
            cnt = min(t + 1, win) if first else win
            for s in range(max(0, 128 + t - win + 1) if not first else 128 + max(0, t - win + 1), 128 + t + 1):
                B[s, t] = 1.0 / cnt
            B[128 + t, t] -= 1.0
        return B
    Bn = band_full(False)
    B0 = band_full(True)
    band = np.stack([B0[128:], Bn[128:], Bn[:128]], 0)
    k = np.arange(128)
    triA = (k[:, None] <= k[None, :]).astype(np.float32)
    strictU = (k[:, None] > k[None, :]).astype(np.float32)
    MS = (k[:, None] > k[None, :]).astype(np.float32)
    MU = (k[:, None] <= k[None, :]).astype(np.float32)
    masks = np.stack([triA, strictU, MS, MU, np.eye(128, dtype=np.float32)], 0)
    return {
        "xb": np.ascontiguousarray(np.asarray(inputs["x"][b], np.float32)),
        "g_pre": np.asarray(inputs["norm_mix_pre"][0], np.float32),
        "w_in_c": wc,
        "conv_w": np.ascontiguousarray(cwl),
        "neg_a": bc(inputs["gdn_a_log"][0][hg * 4:(hg + 1) * 4]),
        "dt_b": bc(inputs["gdn_dt_bias"][0][hg * 4:(hg + 1) * 4]),
        "gn_w": bc(inputs["gdn_norm"][0]),
        "pool_wg": np.ascontiguousarray(np.asarray(inputs["pool_w"][0][hg], np.float32)),
        "pool_sc": np.ascontiguousarray(np.asarray(inputs["pool_scale"][0][hg * 512:(hg + 1) * 512], np.float32)),
        "bandm": np.ascontiguousarray(band), "masks": np.ascontiguousarray(masks),
    }


def kernel(**inputs):
    n = 8
    nc = build_fused()
    consts = phase2_consts(inputs)
    x = np.asarray(inputs["x"], np.float32)
    maps = []
    for c in range(n):
        b, j = c // 4, c % 4
        m = dict(consts)
        m.update(phase1_inputs(inputs, b, j))
        m["x_own"] = np.ascontiguousarray(x[b, j * TP:(j + 1) * TP, :])
        sel = np.zeros((128, 4), np.float32)
        sel[:, j] = 1.0
        m["sel"] = sel
        maps.append(m)
    res = run_bass_kernel_spmd(nc, maps, core_ids=list(range(n)))
    out = np.empty((2, SEQ, D), np.float32)
    for c in range(n):
        b, j = c // 4, c % 4
        out[b, j * TP:(j + 1) * TP, :] = np.asarray(res.results[c]["out"], np.float32)
    return out
```

```python
import contextlib
import numpy as np
import ml_dtypes
import concourse.bass as bass
import concourse.mybir as mybir
from concourse.bass_utils import run_bass_kernel_spmd

F32 = mybir.dt.float32
BF16 = mybir.dt.bfloat16
AF = mybir.ActivationFunctionType
ALU = mybir.AluOpType
AX = mybir.AxisListType

D = 4096
DFF = 11008
KC = D // 128
TP = 1024
SEQ = 4096
EPS = 1e-6


class Buf:
    __slots__ = ("name", "w", "r", "strict")

    def __init__(self, name="", strict=False):
        self.name = name
        self.w = None
        self.r = []
        self.strict = strict


def SBuf(name=""):
    return Buf(name, True)


class Sched:
    ENGS = ("pe", "act", "dve", "pool", "sp")

    def __init__(self, nc, n_dma_sems=48):
        self.nc = nc
        self.prog = {e: [] for e in self.ENGS}
        self.sems = {e: nc.alloc_semaphore(name="s_" + e) for e in self.ENGS}
        self.cnt = {e: 0 for e in self.ENGS}
        self.waited = {e: {} for e in self.ENGS}
        self.dsems = [nc.alloc_semaphore(name="d%d" % i) for i in range(n_dma_sems)]
        self.dval = [0] * n_dma_sems
        self.dnext = 0
        self.n_ins = 0
        self._rec = None
        self.nosame = 1
        self.sems["cc"] = nc.alloc_semaphore(name="s_cc")
        self.ccval = 0

    def _sem(self, key):
        return self.sems[key] if isinstance(key, str) else self.dsems[key]

    def _collect(self, eng, reads, writes):
        need = {}

        relax = self.nosame and eng in ("dve", "act")

        def add(tok, strict):
            if tok is None:
                return
            k, v = tok
            if k == eng and (eng == "pe" or (relax and not strict)):
                return
            if need.get(k, 0) < v:
                need[k] = v
        for b in reads:
            add(b.w, b.strict)
        for b in writes:
            add(b.w, b.strict)
            for t in b.r:
                add(t, b.strict)
        waits = []
        wd = self.waited[eng]
        for k, v in need.items():
            if wd.get(k, 0) >= v:
                continue
            wd[k] = v
            waits.append((self._sem(k), v))
        return waits

    def _commit(self, tok, reads, writes):
        for b in reads:
            b.r.append(tok)
        for b in writes:
            b.w = tok
            b.r = []

    def record(self):
        self._rec = []

    def stop(self):
        r, self._rec = self._rec, None
        return r

    def replay(self, lists):
        pos = [0] * len(lists)
        tot = max(len(l) for l in lists)
        for step in range(1, tot + 1):
            for i, l in enumerate(lists):
                upto = (step * len(l)) // tot
                while pos[i] < upto:
                    kind, a, kw = l[pos[i]]
                    pos[i] += 1
                    {"op": self.op, "dma": self.dma, "cc": self.collective}[kind](*a, **kw)

    def op(self, eng, fn, reads=(), writes=(), signal=True):
        if self._rec is not None:
            self._rec.append(("op", (eng, fn), dict(reads=list(reads), writes=list(writes), signal=signal)))
            return
        waits = self._collect(eng, reads, writes)
        if signal:
            self.cnt[eng] += 1
            tok = (eng, self.cnt[eng])
        else:
            tok = (eng, self.cnt[eng] + 1)
        sem = self.sems[eng]

        def run(e, waits=waits, fn=fn, sem=sem, signal=signal):
            for s, v in waits:
                e.wait_ge(s, v)
            ins = fn(e)
            if signal:
                ins.then_inc(sem, 1)
        self.prog[eng].append(run)
        self._commit(tok, reads, writes)
        self.n_ins += 1

    def dma(self, eng, out_ap, in_ap, reads=(), writes=(), **kw):
        if self._rec is not None:
            self._rec.append(("dma", (eng, out_ap, in_ap), dict(reads=list(reads), writes=list(writes), **kw)))
            return
        i = self.dnext
        self.dnext = (self.dnext + 1) % len(self.dsems)
        waits = self._collect(eng, reads, writes)
        wd = self.waited[eng]
        if self.dval[i] > 0 and wd.get(i, 0) < self.dval[i]:
            wd[i] = self.dval[i]
            waits.append((self.dsems[i], self.dval[i]))
        self.dval[i] += 16
        tok = (i, self.dval[i])
        sem = self.dsems[i]

        def run(e, waits=waits, sem=sem):
            for s, v in waits:
                e.wait_ge(s, v)
            e.dma_start(out=out_ap, in_=in_ap, **kw).then_inc(sem, 16)
        self.prog[eng].append(run)
        self._commit(tok, reads, writes)
        self.n_ins += 1

    def collective(self, kind, in_ap, out_ap, groups, reads=(), writes=()):
        if self._rec is not None:
            self._rec.append(("cc", (kind, in_ap, out_ap, groups), dict(reads=list(reads), writes=list(writes))))
            return
        waits = self._collect("pool", reads, writes)
        self.ccval += 1
        tok = ("cc", self.ccval)
        sem = self.sems["cc"]

        def run(e, waits=waits, sem=sem):
            for s_, v in waits:
                e.wait_ge(s_, v)
            e.collective_compute(kind, ALU.bypass, replica_groups=groups, ins=[in_ap], outs=[out_ap]).then_inc(sem, 1)
        self.prog["pool"].append(run)
        self._commit(tok, reads, writes)

    def barrier(self):
        for e in self.ENGS:
            waits = []
            wd = self.waited[e]
            for e2 in self.ENGS:
                if e2 != e and self.cnt[e2] > wd.get(e2, 0):
                    wd[e2] = self.cnt[e2]
                    waits.append((self.sems[e2], self.cnt[e2]))
            for i, v in enumerate(self.dval):
                if v > wd.get(i, 0):
                    wd[i] = v
                    waits.append((self.dsems[i], v))
            if self.ccval > wd.get("cc", 0):
                wd["cc"] = self.ccval
                waits.append((self.sems["cc"], self.ccval))

            def run(en, waits=waits):
                for s, v in waits:
                    en.wait_ge(s, v)
            self.prog[e].append(run)

    def finish(self):
        self.barrier()
        nc = self.nc
        with nc.Block() as block:
            @block.tensor
            def _(e):
                for f in self.prog["pe"]:
                    f(e)

            @block.scalar
            def _(e):
                for f in self.prog["act"]:
                    f(e)

            @block.vector
            def _(e):
                for f in self.prog["dve"]:
                    f(e)

            @block.gpsimd
            def _(e):
                for f in self.prog["pool"]:
                    f(e)

            @block.sync
            def _(e):
                for f in self.prog["sp"]:
                    f(e)


class Ctx:
    def __init__(self, nc, S, st, NW=8, pfx=""):
        self.nc, self.S, self.st, self.pfx = nc, S, st, pfx
        self.ps = [st.enter_context(nc.psum_tensor(pfx + "ps%d" % i, [128, 512], F32)) for i in range(8)]
        self.psb = [Buf("ps%d" % i) for i in range(8)]
        self.ones_bf = self.sb("ones_bf", [128, 128], BF16)
        self.ones_f = self.sb("ones_f", [128, 128], F32)
        self.cb = SBuf("consts")
        S.op("dve", lambda e: e.memset(self.ones_bf[:], 1.0), writes=[self.cb])
        S.op("dve", lambda e: e.memset(self.ones_f[:], 1.0), writes=[self.cb])
        self.NW = NW
        self.wt = [self.sb("wt%d" % i, [128, 4096], BF16) for i in range(self.NW)]
        self.wtb = [Buf("wt%d" % i) for i in range(self.NW)]
        self.wnext = 0
        self.dmaq = 0

    def sb(self, name, shape, dt):
        return self.st.enter_context(self.nc.sbuf_tensor(self.pfx + name, shape, dt))

    def wslot(self):
        i = self.wnext
        self.wnext = (i + 1) % self.NW
        return self.wt[i], self.wtb[i]

    def q(self):
        self.dmaq ^= 1
        return "sp" if self.dmaq else "act"


def gemm_cg(C, W, c0, CW, rhs, rhsb, KCr, T, banks, tokmajor=False):
    S = C.S
    nth = T // 512
    noc = CW // 128
    ukc = 4096 // CW
    nu = (KCr + ukc - 1) // ukc
    Wv = W.rearrange("(kc p) n -> p kc n", p=128)
    for u in range(nu):
        k0 = u * ukc
        nk = min(ukc, KCr - k0)
        wt, wb = C.wslot()
        wv = wt[:, 0:nk * CW].rearrange("p (k n) -> p k n", n=CW)
        S.dma("pool", wv, Wv[:, k0:k0 + nk, c0:c0 + CW], writes=[wb])
        for oc in range(noc):
            for th in range(nth):
                bi = banks[oc * nth + th]
                for j in range(nk):
                    kc = k0 + j
                    S.op("pe", lambda e, bi=bi, wv=wv, j=j, oc=oc, kc=kc, th=th: e.matmul(
                        C.ps[bi][:, :], wv[:, j, oc * 128:(oc + 1) * 128], rhs[:, kc, th * 512:(th + 1) * 512],
                        start=(kc == 0), stop=(kc == KCr - 1)),
                        reads=[wb, rhsb], writes=[C.psb[bi]], signal=(j == nk - 1))


def load_gain(C, sb, name, g_ap, ncol=KC):
    t = sb(name, [128, ncol], F32)
    b = Buf(name)
    C.S.dma("sp", t[:, :], g_ap.rearrange("(kc p) -> p kc", p=128), writes=[b], allow_slow_non_contiguous=True)
    return t, b


def colsum_rstd(C, src_dram, srcb, nkc, T, rstd, rstdb, xin, xinb, sq, sqb, scale, tmp, tmpb):
    S = C.S
    nth = T // 512
    for kc in range(nkc):
        r = kc % len(xin)
        S.dma(C.q(), xin[r][:, :], src_dram[kc * 128:(kc + 1) * 128, :], reads=[srcb[kc]], writes=[xinb[r]])
        r2 = kc % len(sq)
        S.op("act", lambda e, r=r, r2=r2: e.activation(sq[r2][:, :], xin[r][:, :], AF.Square), reads=[xinb[r]], writes=[sqb[r2]])
        for th in range(nth):
            S.op("pe", lambda e, th=th, r2=r2, kc=kc: e.matmul(C.ps[th][:, :], C.ones_bf[:, :], sq[r2][:, th * 512:(th + 1) * 512],
                                                             start=(kc == 0), stop=(kc == nkc - 1)),
                 reads=[sqb[r2], C.cb], writes=[C.psb[th]])
    for th in range(nth):
        sl = slice(th * 512, (th + 1) * 512)
        S.op("act", lambda e, th=th, sl=sl: e.activation(tmp[:, sl], C.ps[th][:, :], AF.Sqrt, bias=C.eps_t[:, 0:1], scale=scale),
             reads=[C.psb[th], C.cb], writes=[tmpb])
        S.op("dve", lambda e, sl=sl: e.reciprocal(rstd[:, sl], tmp[:, sl]), reads=[tmpb], writes=[rstdb])


def rstd_from_acc(C, acc, accb, T, rstd, rstdb, tmp, tmpb, scale):
    S = C.S
    for th in range(T // 512):
        sl = slice(th * 512, (th + 1) * 512)
        S.op("pe", lambda e, th=th, sl=sl: e.matmul(C.ps[th][:, :], C.ones_f[:, :], acc[:, sl], start=True, stop=True),
             reads=[accb, C.cb], writes=[C.psb[th]])
        S.op("act", lambda e, th=th, sl=sl: e.activation(tmp[:, sl], C.ps[th][:, :], AF.Sqrt, bias=C.eps_t[:, 0:1], scale=scale),
             reads=[C.psb[th], C.cb], writes=[tmpb])
        S.op("dve", lambda e, sl=sl: e.reciprocal(rstd[:, sl], tmp[:, sl]), reads=[tmpb], writes=[rstdb])


def norm_stage(C, XT, XTb, gain_ap, HT, HTb, tag, xsq=None):
    S, nc = C.S, C.nc
    with contextlib.ExitStack() as st:
        sb = lambda n, s, d: st.enter_context(nc.sbuf_tensor(tag + n, s, d))
        xin = [sb("xin%d" % i, [128, TP], F32) for i in range(3)]
        xinb = [Buf() for _ in range(3)]
        sq = [sb("sq%d" % i, [128, TP], BF16) for i in range(2)]
        sqb = [Buf() for _ in range(2)]
        rstd = sb("rstd", [128, TP], F32)
        rstdb = Buf()
        tmp = sb("tmp", [128, TP], F32)
        tmpb = Buf()
        g = sb("g", [128, KC], F32)
        gb = Buf()
        S.dma("sp", g[:, :], gain_ap.rearrange("(kc p) -> p kc", p=128), writes=[gb], allow_slow_non_contiguous=True)
        if xsq is None:
            colsum_rstd(C, XT, XTb, KC, TP, rstd, rstdb, xin, xinb, sq, sqb, 1.0 / D, tmp, tmpb)
        else:
            rstd_from_acc(C, xsq[0], xsq[1], TP, rstd, rstdb, tmp, tmpb, 1.0 / D)
        for kc in range(KC):
            r = kc % 3
            S.dma(C.q(), xin[r][:, :], XT[kc * 128:(kc + 1) * 128, :], reads=[XTb[kc]], writes=[xinb[r]])
            S.op("dve", lambda e, r=r, kc=kc: e.scalar_tensor_tensor(HT[:, kc, :], xin[r][:, :], g[:, kc:kc + 1], rstd[:, :],
                                                                    ALU.mult, ALU.mult),
                 reads=[xinb[r], gb, rstdb], writes=[HTb])


def postnorm_resid(C, YT, YTb, gain_ap, XT, XTb, tag, ysq=None, xsq=None):
    S, nc = C.S, C.nc
    with contextlib.ExitStack() as st:
        sb = lambda n, s, d: st.enter_context(nc.sbuf_tensor(tag + n, s, d))
        xin = [sb("xin%d" % i, [128, TP], F32) for i in range(3)]
        xinb = [Buf() for _ in range(3)]
        yin = [sb("yin%d" % i, [128, TP], F32) for i in range(3)]
        yinb = [Buf() for _ in range(3)]
        sq = [sb("sq%d" % i, [128, TP], BF16) for i in range(2)]
        sqb = [Buf() for _ in range(2)]
        rstd = sb("rstd", [128, TP], F32)
        rstdb = Buf()
        tmp = sb("tmp", [128, TP], F32)
        tmpb = Buf()
        g = sb("g", [128, KC], F32)
        gb = Buf()
        S.dma("sp", g[:, :], gain_ap.rearrange("(kc p) -> p kc", p=128), writes=[gb], allow_slow_non_contiguous=True)
        if ysq is None:
            colsum_rstd(C, YT, YTb, KC, TP, rstd, rstdb, yin, yinb, sq, sqb, 1.0 / D, tmp, tmpb)
        else:
            rstd_from_acc(C, ysq[0], ysq[1], TP, rstd, rstdb, tmp, tmpb, 1.0 / D)
        if xsq is not None:
            S.op("dve", lambda e: e.memset(xsq[0][:, :], 0.0), writes=[xsq[1]])
        for kc in range(KC):
            r = kc % 3
            rows = slice(kc * 128, (kc + 1) * 128)
            S.dma("sp", yin[r][:, :], YT[rows, :], reads=[YTb[kc]], writes=[yinb[r]])
            S.dma("act", xin[r][:, :], XT[rows, :], reads=[XTb[kc]], writes=[xinb[r]])
            S.op("dve", lambda e, r=r, kc=kc: e.scalar_tensor_tensor(yin[r][:, :], yin[r][:, :], g[:, kc:kc + 1], rstd[:, :],
                                                                    ALU.mult, ALU.mult),
                 reads=[yinb[r], gb, rstdb], writes=[yinb[r]])
            S.op("dve", lambda e, r=r: e.tensor_tensor(xin[r][:, :], xin[r][:, :], yin[r][:, :], ALU.add),
                 reads=[yinb[r], xinb[r]], writes=[xinb[r]])
            S.dma("sp", XT[rows, :], xin[r][:, :], reads=[xinb[r]], writes=[XTb[kc]])
            if xsq is not None:
                S.op("act", lambda e, r=r: e.activation(yin[r][:, :], xin[r][:, :], AF.Square), reads=[xinb[r]], writes=[yinb[r]])
                S.op("dve", lambda e, r=r: e.tensor_tensor(xsq[0][:, :], xsq[0][:, :], yin[r][:, :], ALU.add),
                     reads=[yinb[r], xsq[1]], writes=[xsq[1]])
        S.barrier()


def gemm_to_dram(C, W, N, rhs, rhsb, KCr, T, OUT, OUTb, tok0, func, odt, tag, ysq=None):
    S, nc = C.S, C.nc
    CW = 256 if T == 1024 else 512
    nth = T // 512
    noc = CW // 128
    with contextlib.ExitStack() as st:
        ot = [st.enter_context(nc.sbuf_tensor(tag + "ot%d" % i, [128, 512], odt)) for i in range(4)]
        otb = [Buf() for _ in range(4)]
        if ysq is not None:
            sqt = [st.enter_context(nc.sbuf_tensor(tag + "sqt%d" % i, [128, 512], F32)) for i in range(2)]
            sqtb = [Buf() for _ in range(2)]
            for th in range(nth):
                S.op("dve", lambda e, th=th: e.memset(ysq[0][:, tok0 + th * 512: tok0 + (th + 1) * 512], 0.0), writes=[ysq[1]])
        oi = 0
        for cg in range(N // CW):
            banks = [(cg % 2) * 4 + i for i in range(4)]
            gemm_cg(C, W, cg * CW, CW, rhs, rhsb, KCr, T, banks)
            for oc in range(noc):
                for th in range(nth):
                    bi = banks[oc * nth + th]
                    o = oi % 4
                    oi += 1
                    if func is None:
                        S.op("dve", lambda e, o=o, bi=bi: e.tensor_copy(ot[o][:, :], C.ps[bi][:, :]),
                             reads=[C.psb[bi]], writes=[otb[o]])
                    else:
                        S.op("act", lambda e, o=o, bi=bi: e.activation(ot[o][:, :], C.ps[bi][:, :], func),
                             reads=[C.psb[bi]], writes=[otb[o]])
                    row = cg * CW + oc * 128
                    S.dma(C.q(), OUT[row:row + 128, tok0 + th * 512: tok0 + (th + 1) * 512], ot[o][:, :],
                          reads=[otb[o]], writes=[OUTb[row // 128]])
                    if ysq is not None:
                        q2 = oi % 2
                        cs = slice(tok0 + th * 512, tok0 + (th + 1) * 512)
                        S.op("act", lambda e, o=o, q2=q2: e.activation(sqt[q2][:, :], ot[o][:, :], AF.Square), reads=[otb[o]], writes=[sqtb[q2]])
                        S.op("dve", lambda e, q2=q2, cs=cs: e.tensor_tensor(ysq[0][:, cs], ysq[0][:, cs], sqt[q2][:, :], ALU.add),
                             reads=[sqtb[q2], ysq[1]], writes=[ysq[1]])
        S.barrier()


def load_fm(C, SRC, SRCb, nkc, T, tok0, dst, dstb, per=8):
    v = SRC.rearrange("(kc p) t -> p kc t", p=128)
    for k0 in range(0, nkc, per):
        k1 = min(nkc, k0 + per)
        C.S.dma(C.q(), dst[:, k0:k1, 0:T], v[:, k0:k1, tok0:tok0 + T], reads=[SRCb[k] for k in range(k0, k1)], writes=[dstb])


def ffn_stage(C, XT, XTb, YT, YTb, HID, HIDb, g_pre, g_post, Wg, Wu, Wd, tag, last=False):
    S, nc = C.S, C.nc
    with contextlib.ExitStack() as st:
        HT = st.enter_context(nc.sbuf_tensor(tag + "HT", [128, KC, TP], BF16))
        HTb = Buf()
        norm_stage(C, XT, XTb, g_pre, HT, HTb, tag + "n", xsq=C.xsq)
        sl_t = [st.enter_context(nc.sbuf_tensor(tag + "sl%d" % i, [128, 512], F32)) for i in range(2)]
        slb = [Buf() for _ in range(2)]
        ot = [st.enter_context(nc.sbuf_tensor(tag + "ho%d" % i, [128, 512], BF16)) for i in range(4)]
        otb = [Buf() for _ in range(4)]
        oi = 0
        for cg in range(DFF // 256):
            bg = [0, 1, 2, 3]
            bu = [4, 5, 6, 7]
            gemm_cg(C, Wg, cg * 256, 256, HT, HTb, KC, TP, bg)
            gemm_cg(C, Wu, cg * 256, 256, HT, HTb, KC, TP, bu)
            for oc in range(2):
                for th in range(2):
                    o = oi % 4
                    s2 = oi % 2
                    oi += 1
                    b1, b2 = bg[oc * 2 + th], bu[oc * 2 + th]
                    S.op("act", lambda e, s2=s2, b1=b1: e.activation(sl_t[s2][:, :], C.ps[b1][:, :], AF.Silu),
                         reads=[C.psb[b1]], writes=[slb[s2]])
                    S.op("dve", lambda e, s2=s2, b2=b2, o=o: e.tensor_tensor(ot[o][:, :], sl_t[s2][:, :], C.ps[b2][:, :], ALU.mult),
                         reads=[slb[s2], C.psb[b2]], writes=[otb[o]])
                    row = cg * 256 + oc * 128
                    S.dma(C.q(), HID[row:row + 128, th * 512:(th + 1) * 512], ot[o][:, :], reads=[otb[o]], writes=[HIDb[row // 128]])
        S.barrier()
    KF = DFF // 128
    with contextlib.ExitStack() as st:
        RH = st.enter_context(nc.sbuf_tensor(tag + "RH", [128, KF, 512], BF16))
        RHb = Buf()
        for th2 in range(2):
            load_fm(C, HID, HIDb, KF, 512, th2 * 512, RH, RHb)
            gemm_to_dram(C, Wd, D, RH, RHb, KF, 512, YT, YTb, th2 * 512, None, F32, tag + "d%d" % th2, ysq=C.ysq)
    postnorm_resid(C, YT, YTb, g_post, XT, XTb, tag + "p", ysq=C.ysq, xsq=None if last else C.xsq)


def about_stage(C, XT, XTb, YT, YTb, YC, YCb, g_post, Wo, tag):
    S, nc = C.S, C.nc
    with contextlib.ExitStack() as st:
        R = st.enter_context(nc.sbuf_tensor(tag + "R", [128, KC, TP], BF16))
        Rb = Buf()
        load_fm(C, YC, YCb, KC, TP, 0, R, Rb)
        gemm_to_dram(C, Wo, D, R, Rb, KC, TP, YT, YTb, 0, None, F32, tag + "g")
    postnorm_resid(C, YT, YTb, g_post, XT, XTb, tag + "p")


def sgu_stage(C, XT, XTb, YT, YTb, UT, UTb, VTM, VTMb, g_pre, g_post, Win, ln_g, ln_b, wsT, bs, maskT, Wout, tag):
    S, nc = C.S, C.nc
    with contextlib.ExitStack() as st:
        HT = st.enter_context(nc.sbuf_tensor(tag + "HT", [128, KC, TP], BF16))
        HTb = Buf()
        norm_stage(C, XT, XTb, g_pre, HT, HTb, tag + "n", xsq=C.xsq)
        gemm_to_dram(C, Win[:, 0:D], D, HT, HTb, KC, TP, UT, UTb, 0, AF.Gelu, BF16, tag + "u")
        vo = [st.enter_context(nc.sbuf_tensor(tag + "vo%d" % i, [128, 512], F32)) for i in range(3)]
        vob = [Buf() for _ in range(3)]
        Wv = Win.rearrange("(kc p) n -> p kc n", p=128)
        oi = 0
        for cg in range(D // 512):
            slots = []
            for u in range(4):
                wt, wb = C.wslot()
                wv = wt[:, :].rearrange("p (k n) -> p k n", n=512)
                S.dma("pool", wv, Wv[:, u * 8:(u + 1) * 8, D + cg * 512: D + (cg + 1) * 512], writes=[wb])
                slots.append((wv, wb))
            for tb in range(TP // 128):
                bi = oi % 8
                for kc in range(KC):
                    wv, wb = slots[kc // 8]
                    S.op("pe", lambda e, bi=bi, wv=wv, kc=kc, tb=tb: e.matmul(
                        C.ps[bi][:, :], HT[:, kc, tb * 128:(tb + 1) * 128], wv[:, kc % 8, :], start=(kc == 0), stop=(kc == KC - 1)),
                        reads=[wb, HTb], writes=[C.psb[bi]], signal=(kc % 8 == 7))
                o = oi % 3
                oi += 1
                S.op("act", lambda e, o=o, bi=bi: e.activation(vo[o][:, :], C.ps[bi][:, :], AF.Gelu), reads=[C.psb[bi]], writes=[vob[o]])
                S.dma(C.q(), VTM[tb * 128:(tb + 1) * 128, cg * 512:(cg + 1) * 512], vo[o][:, :], reads=[vob[o]], writes=[VTMb[tb]])
        S.barrier()
    with contextlib.ExitStack() as st:
        sb = lambda n, s, d: st.enter_context(nc.sbuf_tensor(tag + n, s, d))
        PT = sb("PT", [128, KC, TP], BF16)
        PTb = Buf()
        load_fm(C, UT, UTb, KC, TP, 0, PT, PTb)
        mk = sb("mk", [128, 128], F32)
        mkb = Buf()
        S.dma("sp", mk[:, :], maskT, writes=[mkb])
        wsbf = sb("wsbf", [128, 16, 128], BF16)
        wsbfb = Buf()
        S.dma("pool", wsbf[:, :, :], wsT, writes=[wsbfb])
        for g in range(16):
            S.op("dve", lambda e, g=g: e.tensor_tensor(wsbf[:, g, :], wsbf[:, g, :], mk[:, :], ALU.mult), reads=[wsbfb, mkb], writes=[wsbfb])
        BS = sb("BS", [128, 16, 128], F32)
        BSb = Buf()
        S.dma("sp", BS[:, :, :], bs, writes=[BSb])
        RS = sb("RS", [128, 16, 128], F32)
        RSb = Buf()
        for q4 in range(4):
            S.op("pe", lambda e, q4=q4: e.matmul(C.ps[q4][:, :], C.ones_bf[:, :], wsbf[:, q4 * 4:(q4 + 1) * 4, :], start=True, stop=True),
                 reads=[wsbfb, C.cb], writes=[C.psb[q4]])
            S.op("dve", lambda e, q4=q4: e.tensor_copy(RS[:, q4 * 4:(q4 + 1) * 4, :], C.ps[q4][:, :]), reads=[C.psb[q4]], writes=[RSb])
        lg, lgb = load_gain(C, sb, "lg", ln_g)
        lb, lbb = load_gain(C, sb, "lb", ln_b)
        T2 = sb("T2", [128, KC, 128], F32)
        T2b = Buf()
        for kc in range(KC):
            S.op("dve", lambda e, kc=kc: e.scalar_tensor_tensor(T2[:, kc, :], RS[:, kc // 2, :], lb[:, kc:kc + 1], BS[:, kc // 2, :],
                                                               ALU.mult, ALU.add), reads=[RSb, BSb, lbb], writes=[T2b])
        vin = [sb("vin0", [128, D], F32)] * 2
        vinb = [Buf()] * 2
        vh = [sb("vh0", [128, D], BF16)] * 2
        vhb = [Buf()] * 2
        junk = vh[0]
        junkb = vhb[0]
        st4 = [sb("st%d" % i, [128, 8], F32) for i in range(2)]
        st4b = [SBuf() for _ in range(2)]
        sv = [sb("sv%d" % i, [128, 128], F32) for i in range(3)]
        svb = [Buf() for _ in range(3)]
        oi = 0
        for tb in range(TP // 128):
            r = tb % 2
            S.dma("sp", vin[r][:, 0:D // 2], VTM[tb * 128:(tb + 1) * 128, 0:D // 2], reads=[VTMb[tb]], writes=[vinb[r]])
            S.dma("act", vin[r][:, D // 2:D], VTM[tb * 128:(tb + 1) * 128, D // 2:D], reads=[VTMb[tb]], writes=[vinb[r]])
            s4 = st4[r]
            S.op("act", lambda e, r=r, s4=s4: e.activation(junk[:, :], vin[r][:, :], AF.Identity, accum_out=s4[:, 0:1]),
                 reads=[vinb[r]], writes=[junkb, st4b[r]])
            S.op("act", lambda e, r=r, s4=s4: e.activation(junk[:, :], vin[r][:, :], AF.Square, accum_out=s4[:, 1:2]),
                 reads=[vinb[r]], writes=[junkb, st4b[r]])
            S.op("dve", lambda e, s4=s4: e.tensor_scalar(s4[:, 2:3], s4[:, 0:1], 1.0 / D, None, ALU.mult), reads=[st4b[r]], writes=[st4b[r]])
            S.op("dve", lambda e, s4=s4: e.tensor_tensor(s4[:, 3:4], s4[:, 2:3], s4[:, 2:3], ALU.mult), reads=[st4b[r]], writes=[st4b[r]])
            S.op("dve", lambda e, s4=s4: e.scalar_tensor_tensor(s4[:, 4:5], s4[:, 1:2], 1.0 / D, s4[:, 3:4], ALU.mult, ALU.subtract),
                 reads=[st4b[r]], writes=[st4b[r]])
            S.op("act", lambda e, s4=s4: e.activation(s4[:, 5:6], s4[:, 4:5], AF.Sqrt, bias=C.eps_t[:, 0:1], scale=1.0),
                 reads=[st4b[r], C.cb], writes=[st4b[r]])
            S.op("dve", lambda e, s4=s4: e.reciprocal(s4[:, 6:7], s4[:, 5:6]), reads=[st4b[r]], writes=[st4b[r]])
            S.op("dve", lambda e, s4=s4: e.scalar_tensor_tensor(s4[:, 7:8], s4[:, 2:3], -1.0, s4[:, 6:7], ALU.mult, ALU.mult),
                 reads=[st4b[r]], writes=[st4b[r]])
            S.op("dve", lambda e, r=r, s4=s4: e.tensor_scalar(vh[r][:, :], vin[r][:, :], s4[:, 6:7], s4[:, 7:8], ALU.mult, ALU.add),
                 reads=[vinb[r], st4b[r]], writes=[vhb[r]])
            for k4 in range(KC // 4):
                bi = oi % 8
                oi += 1
                for j in range(4):
                    kc = k4 * 4 + j
                    S.op("pe", lambda e, bi=bi, j=j, kc=kc, r=r: e.matmul(C.ps[bi][:, j * 128:(j + 1) * 128], vh[r][:, kc * 128:(kc + 1) * 128],
                                                                         wsbf[:, kc // 2, :], start=True, stop=True),
                         reads=[vhb[r], wsbfb], writes=[C.psb[bi]], signal=(j == 3))
                for j in range(4):
                    kc = k4 * 4 + j
                    s3 = (k4 * 4 + j) % 3
                    S.op("dve", lambda e, bi=bi, j=j, kc=kc, s3=s3: e.scalar_tensor_tensor(
                        sv[s3][:, :], C.ps[bi][:, j * 128:(j + 1) * 128], lg[:, kc:kc + 1], T2[:, kc, :], ALU.mult, ALU.add),
                        reads=[C.psb[bi], lgb, T2b], writes=[svb[s3]])
                    S.op("dve", lambda e, kc=kc, s3=s3, tb=tb: e.tensor_tensor(
                        PT[:, kc, tb * 128:(tb + 1) * 128], PT[:, kc, tb * 128:(tb + 1) * 128], sv[s3][:, :], ALU.mult),
                        reads=[svb[s3], PTb], writes=[PTb])
        gemm_to_dram(C, Wout, D, PT, PTb, KC, TP, YT, YTb, 0, None, F32, tag + "o")
    postnorm_resid(C, YT, YTb, g_post, XT, XTb, tag + "p", xsq=C.xsq)


def xin_stage(C, x_own, XT, XTb, ident, identb):
    S, nc = C.S, C.nc
    with contextlib.ExitStack() as st:
        xr = [st.enter_context(nc.sbuf_tensor("xi_r%d" % i, [128, D], F32)) for i in range(2)]
        xrb = [Buf() for _ in range(2)]
        xo = [st.enter_context(nc.sbuf_tensor("xi_o%d" % i, [128, 4, 128], F32)) for i in range(3)]
        xob = [Buf() for _ in range(3)]
        inb = Buf()
        oi = 0
        for tb in range(TP // 128):
            r = tb % 2
            S.dma("sp", xr[r][:, 0:D // 2], x_own[tb * 128:(tb + 1) * 128, 0:D // 2], reads=[inb], writes=[xrb[r]])
            S.dma("act", xr[r][:, D // 2:D], x_own[tb * 128:(tb + 1) * 128, D // 2:D], reads=[inb], writes=[xrb[r]])
            for k4 in range(KC // 4):
                bi = oi % 8
                o = oi % 3
                oi += 1
                for j in range(4):
                    kc = k4 * 4 + j
                    S.op("pe", lambda e, bi=bi, j=j, kc=kc, r=r: e.transpose(C.ps[bi][:, j * 128:(j + 1) * 128],
                                                                            xr[r][:, kc * 128:(kc + 1) * 128], ident[:, :]),
                         reads=[xrb[r], identb], writes=[C.psb[bi]], signal=(j == 3))
                S.op("dve", lambda e, bi=bi, o=o: e.tensor_copy(xo[o][:, :, :], C.ps[bi][:, :]), reads=[C.psb[bi]], writes=[xob[o]])
                dst = XT[k4 * 512:(k4 + 1) * 512, tb * 128:(tb + 1) * 128].rearrange("(j p) t -> p j t", p=128)
                S.dma(C.q(), dst, xo[o][:, :, :], reads=[xob[o]], writes=[XTb[k4 * 4 + j] for j in range(4)])
        S.barrier()


def xout_stage(C, XT, XTb, out, outb, ident, identb):
    S, nc = C.S, C.nc
    with contextlib.ExitStack() as st:
        xr = [st.enter_context(nc.sbuf_tensor("xo_r%d" % i, [128, TP], F32)) for i in range(2)]
        xrb = [Buf() for _ in range(2)]
        xo = [st.enter_context(nc.sbuf_tensor("xo_o%d" % i, [128, 4, 128], F32)) for i in range(3)]
        xob = [Buf() for _ in range(3)]
        oi = 0
        for kc in range(KC):
            r = kc % 2
            S.dma(C.q(), xr[r][:, :], XT[kc * 128:(kc + 1) * 128, :], reads=[XTb[kc]], writes=[xrb[r]])
            for t4 in range(TP // 512):
                bi = oi % 8
                o = oi % 3
                oi += 1
                for j in range(4):
                    tb = t4 * 4 + j
                    S.op("pe", lambda e, bi=bi, j=j, tb=tb, r=r: e.transpose(C.ps[bi][:, j * 128:(j + 1) * 128],
                                                                            xr[r][:, tb * 128:(tb + 1) * 128], ident[:, :]),
                         reads=[xrb[r], identb], writes=[C.psb[bi]], signal=(j == 3))
                S.op("dve", lambda e, bi=bi, o=o: e.tensor_copy(xo[o][:, :, :], C.ps[bi][:, :]), reads=[C.psb[bi]], writes=[xob[o]])
                dst = out[t4 * 512:(t4 + 1) * 512, kc * 128:(kc + 1) * 128].rearrange("(j p) f -> p j f", p=128)
                S.dma(C.q(), dst, xo[o][:, :, :], reads=[xob[o]], writes=[outb])
        S.barrier()


def dram_in(nc, name, shape, dt=F32):
    return nc.dram_tensor(name, list(shape), dt, kind="ExternalInput").ap()


def dram_scratch(nc, name, shape, dt=F32):
    return nc.dram_tensor(name, list(shape), dt, kind="Internal").ap()


def phase2_decl(nc):
    A = {}
    A["x_own"] = dram_in(nc, "x_own", [TP, D])
    A["ident_d"] = dram_in(nc, "ident", [128, 128])
    A["nmpost"] = dram_in(nc, "norm_mix_post", [2, D])
    A["nmpre"] = dram_in(nc, "norm_mix_pre", [2, D])
    A["nfpre"] = dram_in(nc, "norm_ffn_pre", [2, D])
    A["nfpost"] = dram_in(nc, "norm_ffn_post", [2, D])
    A["Wabo"] = dram_in(nc, "ab_w_out", [D, D])
    A["Wg"] = dram_in(nc, "ffn_w_gate", [2, D, DFF])
    A["Wu"] = dram_in(nc, "ffn_w_up", [2, D, DFF])
    A["Wd"] = dram_in(nc, "ffn_w_down", [2, DFF, D])
    A["Wsi"] = dram_in(nc, "sgu_w_in", [D, 2 * D])
    A["Wso"] = dram_in(nc, "sgu_w_out", [D, D])
    A["lng"] = dram_in(nc, "sgu_ln_g", [D])
    A["lnb"] = dram_in(nc, "sgu_ln_b", [D])
    A["wsT"] = dram_in(nc, "sgu_wsT", [128, 16, 128])
    A["bsb"] = dram_in(nc, "sgu_bs_b", [128, 16, 128])
    A["maskT"] = dram_in(nc, "sgu_maskT", [128, 128])
    A["out"] = nc.dram_tensor("out", [TP, D], F32, kind="ExternalOutput").ap()
    A["XT"] = dram_scratch(nc, "XT", [D, TP])
    A["YT"] = dram_scratch(nc, "YT", [D, TP])
    A["HID"] = dram_scratch(nc, "HID", [DFF, TP], BF16)
    A["UT"] = dram_scratch(nc, "UT", [D, TP], BF16)
    A["VTM"] = dram_scratch(nc, "VTM", [TP, D])
    return A


def build_phase2(stages=("in", "ab", "ffn0", "sgu", "ffn1", "out")):
    nc = bass.Bass("TRN2", target_bir_lowering=False)
    A = phase2_decl(nc)
    A["YC"] = dram_in(nc, "yc", [D, TP], BF16)
    with nc.cleanup_on_exit():
        S = Sched(nc)
        phase2_body(nc, S, A, stages, None)
        S.finish()
    return nc


def about_stage_sel(C, XT, XTb, YT, YTb, G, Gb, sel_d, g_post, Wo, tag):
    S, nc = C.S, C.nc
    with contextlib.ExitStack() as st:
        R = st.enter_context(nc.sbuf_tensor(tag + "R", [128, KC, TP], BF16))
        Rb = Buf()
        sel = st.enter_context(nc.sbuf_tensor(tag + "sel", [128, 4], F32))
        selb = Buf()
        S.dma("sp", sel[:, :], sel_d, writes=[selb])
        c4 = [st.enter_context(nc.sbuf_tensor(tag + "c4%d" % i, [128, 4, TT], BF16)) for i in range(3)]
        c4b = [Buf() for _ in range(3)]
        Gv = G.rearrange("(j i) r t -> i r j t", i=2)
        n = 0
        for kc in range(KC):
            if kc < 16:
                r, lc = kc // 4, kc % 4
            else:
                r, lc = (kc - 16) // 4, 4 + (kc - 16) % 4
            row = r * 1024 + lc * 128
            for hf in range(2):
                i = n % 3
                n += 1
                ts2 = slice(hf * TT, (hf + 1) * TT)
                S.dma(C.q(), c4[i][:, :, :], Gv[hf, row:row + 128, :, :], reads=list(Gb), writes=[c4b[i]])
                S.op("dve", lambda e, i=i, kc=kc, ts2=ts2: e.tensor_scalar(R[:, kc, ts2], c4[i][:, 0, :], sel[:, 0:1], None, ALU.mult),
                     reads=[c4b[i], selb], writes=[Rb])
                for j in range(1, 4):
                    S.op("dve", lambda e, i=i, kc=kc, j=j, ts2=ts2: e.scalar_tensor_tensor(R[:, kc, ts2], c4[i][:, j, :], sel[:, j:j + 1], R[:, kc, ts2],
                                                                                      ALU.mult, ALU.add), reads=[c4b[i], selb, Rb], writes=[Rb])
        gemm_to_dram(C, Wo, D, R, Rb, KC, TP, YT, YTb, 0, None, F32, tag + "g", ysq=C.ysq)
    postnorm_resid(C, YT, YTb, g_post, XT, XTb, tag + "p", ysq=C.ysq, xsq=C.xsq)


def phase2_body(nc, S, A, stages, gathered):
    x_own, ident_d, nmpost, nmpre, nfpre, nfpost, Wabo, Wg, Wu, Wd, Wsi, Wso, lng, lnb, wsT, bsb, maskT, out, XT, YT, HID, UT, VTM = [A[k] for k in (
        "x_own", "ident_d", "nmpost", "nmpre", "nfpre", "nfpost", "Wabo", "Wg", "Wu", "Wd", "Wsi", "Wso", "lng", "lnb", "wsT", "bsb", "maskT",
        "out", "XT", "YT", "HID", "UT", "VTM")]
    XTb = [Buf() for _ in range(KC)]
    YTb = [Buf() for _ in range(KC)]
    HIDb = [Buf() for _ in range(DFF // 128)]
    UTb = [Buf() for _ in range(KC)]
    VTMb = [Buf() for _ in range(TP // 128)]
    YCb = [Buf() for _ in range(KC)]
    outb = Buf()
    if True:
        with contextlib.ExitStack() as st:
            C = Ctx(nc, S, st)
            ident = C.sb("ident_sb", [128, 128], F32)
            identb = Buf()
            S.dma("sp", ident[:, :], ident_d, writes=[identb])
            C.eps_t = C.sb("eps_t2", [128, 1], F32)
            S.op("dve", lambda e: e.memset(C.eps_t[:, :], EPS), writes=[C.cb])
            C.ysq = (C.sb("ysq", [128, TP], F32), Buf())
            C.xsq = (C.sb("xsq", [128, TP], F32), Buf())
            C.xsq = None
            if "in" in stages:
                xin_stage(C, x_own, XT, XTb, ident, identb)
            if "ab" in stages:
                if gathered is None:
                    about_stage(C, XT, XTb, YT, YTb, A["YC"], YCb, nmpost[0], Wabo, "ab")
                else:
                    G, Gb, sel_d = gathered
                    about_stage_sel(C, XT, XTb, YT, YTb, G, Gb, sel_d, nmpost[0], Wabo, "ab")
            if "ffn0" in stages:
                ffn_stage(C, XT, XTb, YT, YTb, HID, HIDb, nfpre[0], nfpost[0], Wg[0], Wu[0], Wd[0], "f0")
            if "sgu" in stages:
                sgu_stage(C, XT, XTb, YT, YTb, UT, UTb, VTM, VTMb, nmpre[1], nmpost[1], Wsi, lng, lnb, wsT, bsb, maskT, Wso, "sg")
            if "ffn1" in stages:
                ffn_stage(C, XT, XTb, YT, YTb, HID, HIDb, nfpre[1], nfpost[1], Wg[1], Wu[1], Wd[1], "f1", last=True)
            if "out" in stages:
                xout_stage(C, XT, XTb, out, outb, ident, identb)
            S.barrier()


def build_fused():
    nc = bass.Bass("TRN2", target_bir_lowering=False)
    A1 = phase1_decl(nc)
    A2 = phase2_decl(nc)
    sel_d = dram_in(nc, "sel", [128, 4])
    NT = SEQ // TT
    YL = [nc.dram_tensor("YL%d" % t, [1024, TT], BF16) for t in range(NT)]
    GG = nc.dram_tensor("YG", [NT, 4 * 1024, TT], BF16)
    A1["YCT"] = lambda t: YL[t].ap()
    with nc.cleanup_on_exit():
        S = Sched(nc)
        ylb = [Buf() for _ in range(NT)]
        Gb = [Buf() for _ in range(NT)]
        A1["after_tile"] = lambda t: S.collective("AllGather", YL[t].ap().opt(), GG.ap()[t].opt(), [[0, 1, 2, 3], [4, 5, 6, 7]],
                                                  reads=[ylb[t]], writes=[Gb[t]])
        phase1_body(nc, S, A1, ylb)
        phase2_body(nc, S, A2, ("in", "ab", "ffn0", "sgu", "ffn1", "out"), (GG.ap(), Gb, sel_d))
        S.finish()
    return nc


def phase2_consts(inputs):
    ws = np.asarray(inputs["sgu_w_s"][0], np.float32)
    wsT = np.ascontiguousarray(ws.transpose(2, 0, 1))
    pos = np.arange(128)
    maskT = ((pos[:, None] // 64) <= (pos[None, :] // 64)).astype(np.float32)
    bs = np.asarray(inputs["sgu_b_s"][0], np.float32)
    bsb = np.ascontiguousarray(np.broadcast_to(bs[None], (128, 16, 128)))
    return {
        "ident": np.eye(128, dtype=np.float32),
        "norm_mix_post": np.asarray(inputs["norm_mix_post"], np.float32),
        "norm_mix_pre": np.asarray(inputs["norm_mix_pre"], np.float32),
        "norm_ffn_pre": np.asarray(inputs["norm_ffn_pre"], np.float32),
        "norm_ffn_post": np.asarray(inputs["norm_ffn_post"], np.float32),
        "ab_w_out": np.asarray(inputs["ab_w_out"][0], np.float32),
        "ffn_w_gate": np.asarray(inputs["ffn_w_gate"], np.float32),
        "ffn_w_up": np.asarray(inputs["ffn_w_up"], np.float32),
        "ffn_w_down": np.asarray(inputs["ffn_w_down"], np.float32),
        "sgu_w_in": np.asarray(inputs["sgu_w_in"][0], np.float32),
        "sgu_w_out": np.asarray(inputs["sgu_w_out"][0], np.float32),
        "sgu_ln_g": np.asarray(inputs["sgu_ln_g"][0], np.float32),
        "sgu_ln_b": np.asarray(inputs["sgu_ln_b"][0], np.float32),
        "sgu_wsT": wsT, "sgu_bs_b": bsb, "sgu_maskT": maskT,
    }


NH = 4
TT = 512
NCOL1 = 2568


def phase1_decl(nc):
    A = {}
    A["xb"] = dram_in(nc, "xb", [SEQ, D])
    A["gpre"] = dram_in(nc, "g_pre", [D])
    A["Wc"] = dram_in(nc, "w_in_c", [D, NCOL1])
    A["cw_d"] = dram_in(nc, "conv_w", [128, 12, 4])
    A["nega_d"] = dram_in(nc, "neg_a", [128, 4])
    A["dtb_d"] = dram_in(nc, "dt_b", [128, 4])
    A["gnw_d"] = dram_in(nc, "gn_w", [128, 128])
    A["pw_d"] = dram_in(nc, "pool_wg", [512, 512])
    A["psc_d"] = dram_in(nc, "pool_sc", [512])
    A["band_d"] = dram_in(nc, "bandm", [3, 128, 128])
    A["mask_d"] = dram_in(nc, "masks", [5, 128, 128])
    return A


def build_phase1(ntiles=SEQ // TT, stop_after=None, dbgk=99):
    nc = bass.Bass("TRN2", target_bir_lowering=False)
    A = phase1_decl(nc)
    yct = nc.dram_tensor("yct", [1024, SEQ], BF16, kind="ExternalOutput").ap()
    A["YCT"] = lambda t: yct[:, t * TT:(t + 1) * TT]
    with nc.cleanup_on_exit():
        S = Sched(nc)
        phase1_body(nc, S, A, [Buf() for _ in range(SEQ // TT)], ntiles, stop_after, dbgk)
        S.finish()
    return nc


def phase1_body(nc, S, A, outb, ntiles=SEQ // TT, stop_after=None, dbgk=99):
    xb, gpre, Wc, cw_d, nega_d, dtb_d, gnw_d, pw_d, psc_d, band_d, mask_d, YCT = [A[k] for k in (
        "xb", "gpre", "Wc", "cw_d", "nega_d", "dtb_d", "gnw_d", "pw_d", "psc_d", "band_d", "mask_d", "YCT")]
    if True:
        with contextlib.ExitStack() as st:
            C = Ctx(nc, S, st, NW=4, pfx="p1_")
            sb = C.sb
            C.eps_t = sb("eps_t", [128, 1], F32)
            S.op("dve", lambda e: e.memset(C.eps_t[:, :], EPS), writes=[C.cb])
            C.one_t = sb("one_t", [128, 1], F32)
            S.op("dve", lambda e: e.memset(C.one_t[:, :], 1.0), writes=[C.cb])
            mk = sb("mk", [128, 5, 128], F32)
            mkb = Buf()
            S.dma("sp", mk[:, :, :], mask_d.rearrange("m p f -> p m f"), writes=[mkb])
            triA, strictU, MS, MU, ident = [mk[:, i, :] for i in range(5)]
            band = sb("band", [128, 3, 128], F32)
            bandb = Buf()
            S.dma("act", band[:, :, :], band_d.rearrange("m p f -> p m f"), writes=[bandb])
            g = sb("g", [128, KC], F32)
            gb = Buf()
            S.dma("sp", g[:, :], gpre.rearrange("(kc p) -> p kc", p=128), writes=[gb], allow_slow_non_contiguous=True)
            cw = sb("cw", [128, 12, 4], F32)
            cwb = Buf()
            S.dma("sp", cw[:, :, :], cw_d, writes=[cwb])
            nega = sb("nega", [128, 4], F32)
            dtb = sb("dtb", [128, 4], F32)
            gnw = sb("gnw", [128, 128], F32)
            psc = sb("psc", [128, 4], F32)
            smb = SBuf()
            S.dma("sp", nega[:, :], nega_d, writes=[smb])
            S.dma("sp", dtb[:, :], dtb_d, writes=[smb])
            S.dma("sp", gnw[:, :], gnw_d, writes=[smb])
            S.dma("sp", psc[:, :], psc_d.rearrange("(c p) -> p c", p=128), writes=[smb], allow_slow_non_contiguous=True)
            S.op("act", lambda e: e.activation(nega[:, :], nega[:, :], AF.Exp), reads=[smb], writes=[smb])
            S.op("dve", lambda e: e.tensor_scalar(nega[:, :], nega[:, :], -1.0, None, ALU.mult), reads=[smb], writes=[smb])
            pw = sb("pw", [128, 4, 512], BF16)
            pwb = Buf()
            S.dma("pool", pw[:, :, :], pw_d.rearrange("(c p) n -> p c n", p=128), writes=[pwb])
            wbd = sb("wbd", [128, KC, 8], BF16)
            wbdb = Buf()
            S.dma("pool", wbd[:, :, :], Wc.rearrange("(kc p) n -> p kc n", p=128)[:, :, 2560:2568], writes=[wbdb],
                  allow_slow_non_contiguous=True)
            xr = sb("xr", [128, D], F32)
            xrb = Buf()
            HT = sb("HT", [128, KC, TT], BF16)
            HTb = Buf()
            st8 = sb("st8", [128, 12], F32)
            st8b = SBuf()
            halo = sb("halo", [128, 12, 4], F32)
            halob = Buf()
            S.op("dve", lambda e: e.memset(halo[:, :, :], 0.0), writes=[halob])
            Z = [sb("Z%d" % i, [128, 3 + TT], F32) for i in range(4)]
            Zb = [Buf() for _ in range(4)]
            junkA = sb("junkA", [128, TT], BF16)
            junkAb = Buf()
            QT = sb("QT", [128, NH, TT], BF16)
            KT = sb("KT", [128, NH, TT], BF16)
            KTf = sb("KTf", [128, NH, TT], F32)
            VT = sb("VT", [128, NH, TT], F32)
            QTb, KTb, KTfb, VTb = [[Buf() for _ in range(NH)] for _ in range(4)]
            sg2 = [sb("sg%d" % i, [128, 4, 512], F32) for i in range(2)]
            sgb2 = [[Buf() for _ in range(4)] for _ in range(2)]
            xa = sb("xa", [128, 5, 512], F32)
            xab = [Buf() for _ in range(5)]
            lg82 = [sb("lg8%d" % i, [128, 4, 8], F32) for i in range(2)]
            lg8b2 = [[SBuf() for _ in range(4)] for _ in range(2)]
            OT = sb("OT", [128, 8, TT], BF16)
            OTb = Buf()
            PLT = sb("PLT", [128, 4, TT], BF16)
            PLTb = Buf()
            Sst = sb("Sst", [128, NH, 128], F32)
            Sbf = sb("Sbf", [128, NH, 128], BF16)
            Sstb = [Buf() for _ in range(NH)]
            Sbfb = [Buf() for _ in range(NH)]
            for h in range(NH):
                S.op("dve", lambda e, h=h: e.memset(Sst[:, h, :], 0.0), writes=[Sstb[h]])
                S.op("dve", lambda e, h=h: e.memset(Sbf[:, h, :], 0.0), writes=[Sbfb[h]])

            def ht(name, n=128, dt=F32, strict=False):
                t = sb(name, [128, NH, n], dt)
                return t, [Buf(name, strict) for _ in range(NH)]
            b4, b4b = ht("b4", 8, strict=True)
            G2, G2b = ht("G2")
            gB, gBb = ht("gB")
            EX, EXb = ht("EX", 392, strict=True)
            Es, Esb = ht("Es")
            ETc, ETcb = ht("ETc")
            ngb, ngbb = ht("ngb", 2, strict=True)
            Lm, Lb = ht("L")
            AT, ATb = ht("AT")
            kd, kdb = ht("kd")
            vb, vbb = ht("vb")
            Xa, Xab_ = ht("Xa")
            Xat, Xatb = ht("Xat")
            Xb, Xbb = ht("Xb")
            Xbt, Xbtb = ht("Xbt")
            Pm, Pmb = ht("Pm")
            A2T, A2Tb = ht("A2T", 128, BF16)
            K2, K2b = ht("K2", 128, BF16)
            qdT, qdTb = ht("qdT", 128, BF16)
            Rbf, Rbfb = ht("Rbf", 128, BF16)
            gsg, gsgb = gB, gBb
            yo, yob = G2, G2b
            flat = lambda t: t[:, :, :].rearrange("p h n -> p (h n)")
            accv = [flat(t) for t in (G2, gB, Es, ETc)]
            accB = [G2b, gBb, Esb, ETcb]
            slv = [flat(t) for t in (Lm, AT, kd, vb)]
            slB = [Lb, ATb, kdb, vbb]
            ps, psb = C.ps, C.psb
            Wv_ = Wc.rearrange("(kc p) n -> p kc n", p=128)

            def emit_A(tile):
                t0 = tile * TT
                for tb in range(4):
                    r0 = t0 + tb * 128
                    S.dma("sp", xr[:, 0:D // 2], xb[r0:r0 + 128, 0:D // 2], writes=[xrb])
                    S.dma("act", xr[:, D // 2:D], xb[r0:r0 + 128, D // 2:D], writes=[xrb])
                    for q8 in range(8):
                        S.op("act", lambda e, q8=q8: e.activation(junkA[:, :], xr[:, q8 * 512:(q8 + 1) * 512], AF.Square,
                                                                  accum_out=st8[:, q8:q8 + 1]), reads=[xrb], writes=[junkAb, st8b])
                    S.op("dve", lambda e: e.tensor_reduce(st8[:, 8:9], st8[:, 0:8], AX.X, ALU.add), reads=[st8b], writes=[st8b])
                    S.op("act", lambda e: e.activation(st8[:, 9:10], st8[:, 8:9], AF.Sqrt, bias=C.eps_t[:, 0:1], scale=1.0 / D),
                         reads=[st8b, C.cb], writes=[st8b])
                    S.op("dve", lambda e: e.reciprocal(st8[:, 10:11], st8[:, 9:10]), reads=[st8b], writes=[st8b])
                    S.op("act", lambda e: e.activation(xr[:, :], xr[:, :], AF.Copy, scale=st8[:, 10:11]), reads=[xrb, st8b], writes=[xrb])
                    for k4 in range(KC // 4):
                        bi = 4 + k4 % 4
                        for j in range(4):
                            kc = k4 * 4 + j
                            S.op("pe", lambda e, bi=bi, j=j, kc=kc: e.transpose(ps[bi][:, j * 128:(j + 1) * 128],
                                                                               xr[:, kc * 128:(kc + 1) * 128], ident),
                                 reads=[xrb, mkb], writes=[psb[bi]], signal=(j == 3))
                        for j in range(4):
                            kc = k4 * 4 + j
                            S.op("dve" if j % 2 == 0 else "act",
                                 (lambda e, bi=bi, j=j, kc=kc, tb=tb: e.tensor_scalar(HT[:, kc, tb * 128:(tb + 1) * 128], ps[bi][:, j * 128:(j + 1) * 128],
                                                                                     g[:, kc:kc + 1], None, ALU.mult)) if j % 2 == 0 else
                                 (lambda e, bi=bi, j=j, kc=kc, tb=tb: e.activation(HT[:, kc, tb * 128:(tb + 1) * 128], ps[bi][:, j * 128:(j + 1) * 128],
                                                                                  AF.Copy, scale=g[:, kc:kc + 1])),
                                 reads=[psb[bi], gb], writes=[HTb])
            def emit_B1(tile):
                t0 = tile * TT
                sg, sgb, lg8, lg8b = sg2[tile % 2], sgb2[tile % 2], lg82[tile % 2], lg8b2[tile % 2]
                for cgi in range(2):
                    slots = []
                    for u in range(4):
                        wt, wb = C.wslot()
                        wv = wt[:, :].rearrange("p (k n) -> p k n", n=512)
                        S.dma("pool", wv, Wv_[:, u * 8:(u + 1) * 8, cgi * 512:(cgi + 1) * 512], writes=[wb])
                        slots.append((wv, wb))
                    for tb in range(4):
                        bi = 4 + tb
                        for kc in range(KC):
                            wv, wb = slots[kc // 8]
                            S.op("pe", lambda e, bi=bi, wv=wv, kc=kc, tb=tb: e.matmul(
                                ps[bi][:, :], HT[:, kc, tb * 128:(tb + 1) * 128], wv[:, kc % 8, :], start=(kc == 0), stop=(kc == KC - 1)),
                                reads=[wb, HTb], writes=[psb[bi]], signal=(kc % 8 == 7))
                        if cgi == 0:
                            S.op("act", lambda e, bi=bi, tb=tb: e.activation(xa[:, tb + 1, :], ps[bi][:, :], AF.Copy),
                                 reads=[psb[bi]], writes=[xab[tb + 1]])
                        else:
                            S.op("act", lambda e, sg=sg, lg8=lg8, bi=bi, tb=tb: e.activation(sg[:, tb, :], ps[bi][:, :], AF.Silu),
                                 reads=[psb[bi]], writes=[sgb[tb]])
                for tb in range(4):
                    bi = 4 + tb
                    for kc in range(KC):
                        S.op("pe", lambda e, bi=bi, kc=kc, tb=tb: e.matmul(ps[bi][:, 0:8], HT[:, kc, tb * 128:(tb + 1) * 128], wbd[:, kc, :],
                                                                          start=(kc == 0), stop=(kc == KC - 1)),
                             reads=[wbdb, HTb], writes=[psb[bi]], signal=(kc == KC - 1))
                    S.op("dve", lambda e, sg=sg, lg8=lg8, bi=bi, tb=tb: e.tensor_copy(lg8[:, tb, :], ps[bi][:, 0:8]), reads=[psb[bi]], writes=[lg8b[tb]])
            def emit_B2E(tile):
                t0 = tile * TT
                bank_of = lambda grp: [4, 5, 6, 7] if grp % 2 == 0 else [0, 1, 2, 3]
                gemm_cg(C, Wc, 1024, 512, HT, HTb, KC, TT, bank_of(0))
                for grp in range(3):
                    banks = bank_of(grp)
                    if grp + 1 < 3:
                        gemm_cg(C, Wc, 1024 + (grp + 1) * 512, 512, HT, HTb, KC, TT, bank_of(grp + 1))
                    lists = []
                    for h in range(NH):
                        S.record()
                        c = grp * 4 + h
                        zi = h
                        bi = banks[h]
                        S.op("dve", lambda e, zi=zi, c=c: e.tensor_copy(Z[zi][:, 0:3], halo[:, c, 0:3]), reads=[halob], writes=[Zb[zi]])
                        S.op("act", lambda e, zi=zi, bi=bi: e.activation(Z[zi][:, 3:3 + TT], ps[bi][:, :], AF.Copy),
                             reads=[psb[bi]], writes=[Zb[zi]])
                        S.op("dve", lambda e, zi=zi, c=c: e.tensor_copy(halo[:, c, 0:3], Z[zi][:, TT:TT + 3]), reads=[Zb[zi]], writes=[halob])
                        S.op("dve", lambda e, zi=zi, c=c: e.tensor_scalar(accv[zi], Z[zi][:, 3:3 + TT], cw[:, c, 3:4], None, ALU.mult),
                             reads=[Zb[zi], cwb], writes=[*accB[zi]])
                        for j in range(3):
                            S.op("dve", lambda e, zi=zi, c=c, j=j: e.scalar_tensor_tensor(accv[zi], Z[zi][:, j:j + TT], cw[:, c, j:j + 1],
                                                                                         accv[zi], ALU.mult, ALU.add),
                                 reads=[Zb[zi], cwb, *accB[zi]], writes=[*accB[zi]])
                        if grp == 2:
                            S.op("act", lambda e, zi=zi, h=h: e.activation(VT[:, h, :], accv[zi], AF.Silu), reads=[*accB[zi]], writes=[VTb[h]])
                            lists.append(S.stop())
                            continue
                        S.op("act", lambda e, zi=zi: e.activation(slv[zi], accv[zi], AF.Silu), reads=[*accB[zi]], writes=[*slB[zi]])
                        S.op("act", lambda e, zi=zi: e.activation(accv[zi], slv[zi], AF.Square),
                             reads=[*slB[zi]], writes=[*accB[zi]])
                        pb = banks[h]
                        S.op("pe", lambda e, pb=pb, zi=zi: e.matmul(ps[pb][:, :], C.ones_f[:, :], accv[zi], start=True, stop=True),
                             reads=[*accB[zi], C.cb], writes=[psb[pb]])
                        S.op("act", lambda e, pb=pb, zi=zi: e.activation(accv[zi], ps[pb][:, :], AF.Sqrt, bias=C.eps_t[:, 0:1], scale=1.0),
                             reads=[psb[pb], C.cb], writes=[*accB[zi]])
                        S.op("dve", lambda e, zi=zi: e.reciprocal(accv[zi], accv[zi]), reads=[*accB[zi]], writes=[*accB[zi]])
                        if grp == 0:
                            S.op("dve", lambda e, zi=zi, h=h: e.scalar_tensor_tensor(QT[:, h, :], slv[zi], 128.0 ** -0.5, accv[zi],
                                                                                    ALU.mult, ALU.mult), reads=[*slB[zi], *accB[zi]], writes=[QTb[h]])
                        else:
                            S.op("dve", lambda e, zi=zi, h=h: e.tensor_tensor(KTf[:, h, :], slv[zi], accv[zi], ALU.mult),
                                 reads=[*slB[zi], *accB[zi]], writes=[KTfb[h]])
                            S.op("pool", lambda e, h=h: e.tensor_copy(KT[:, h, :], KTf[:, h, :]), reads=[KTfb[h]], writes=[KTb[h]])
                        lists.append(S.stop())
                    S.replay(lists)
                for tb in range(4):
                    first = (tile == 0 and tb == 0)
                    for c in range(4):
                        bi = 4 + c
                        if first:
                            S.op("pe", lambda e, bi=bi, c=c, tb=tb: e.matmul(ps[bi][:, 0:128], xa[:, tb + 1, c * 128:(c + 1) * 128], band[:, 0, :],
                                                                            start=True, stop=True), reads=[xab[tb + 1], bandb], writes=[psb[bi]])
                        else:
                            S.op("pe", lambda e, bi=bi, c=c, tb=tb: e.matmul(ps[bi][:, 0:128], xa[:, tb, c * 128:(c + 1) * 128], band[:, 2, :],
                                                                            start=True, stop=False), reads=[xab[tb], bandb], writes=[psb[bi]], signal=False)
                            S.op("pe", lambda e, bi=bi, c=c, tb=tb: e.matmul(ps[bi][:, 0:128], xa[:, tb + 1, c * 128:(c + 1) * 128], band[:, 1, :],
                                                                            start=False, stop=True), reads=[xab[tb + 1], bandb], writes=[psb[bi]])
                        S.op("act", lambda e, bi=bi, c=c, tb=tb: e.activation(PLT[:, c, tb * 128:(tb + 1) * 128], ps[bi][:, 0:128], AF.Copy),
                             reads=[psb[bi]], writes=[PLTb])
                S.op("pool", lambda e: e.tensor_copy(xa[:, 0, :], xa[:, 4, :]), reads=[xab[4]], writes=[xab[0]])
                for dc in range(4):
                    bi = dc
                    for c in range(4):
                        S.op("pe", lambda e, bi=bi, c=c, dc=dc: e.matmul(ps[bi][:, :], pw[:, c, dc * 128:(dc + 1) * 128], PLT[:, c, :],
                                                                        start=(c == 0), stop=(c == 3)), reads=[pwb, PLTb], writes=[psb[bi]], signal=(c == 3))
                    S.op("act", lambda e, bi=bi, dc=dc: e.activation(OT[:, dc, :], ps[bi][:, :], AF.Copy, scale=psc[:, dc:dc + 1]),
                         reads=[psb[bi], smb], writes=[OTb])
            def emit_D(tile):
                t0 = tile * TT
                sg, sgb, lg8, lg8b = sg2[tile % 2], sgb2[tile % 2], lg82[tile % 2], lg8b2[tile % 2]
                H = range(NH)
                for tb in range(4):
                    ts_ = slice(tb * 128, (tb + 1) * 128)
                    S.op("act", lambda e, sg=sg, lg8=lg8, tb=tb: e.activation(b4[:, 0, 0:4], lg8[:, tb, 0:4], AF.Exp, scale=-1.0), reads=[lg8b[tb]], writes=[b4b[0]])
                    S.op("dve", lambda e: e.tensor_scalar(b4[:, 0, 0:4], b4[:, 0, 0:4], 1.0, None, ALU.add), reads=[b4b[0]], writes=[b4b[0]])
                    S.op("dve", lambda e: e.reciprocal(b4[:, 0, 0:4], b4[:, 0, 0:4]), reads=[b4b[0]], writes=[b4b[0]])
                    S.op("dve", lambda e, sg=sg, lg8=lg8, tb=tb: e.tensor_tensor(b4[:, 1, 0:4], lg8[:, tb, 4:8], dtb[:, :], ALU.add), reads=[lg8b[tb], smb], writes=[b4b[0]])
                    S.op("act", lambda e: e.activation(b4[:, 1, 0:4], b4[:, 1, 0:4], AF.Exp), reads=[b4b[0]], writes=[b4b[0]])
                    S.op("act", lambda e: e.activation(b4[:, 1, 0:4], b4[:, 1, 0:4], AF.Ln, bias=C.one_t[:, 0:1], scale=1.0), reads=[b4b[0], C.cb], writes=[b4b[0]])
                    S.op("dve", lambda e: e.tensor_tensor(b4[:, 0, 4:8], b4[:, 1, 0:4], nega[:, :], ALU.mult), reads=[b4b[0], smb], writes=[b4b[0]])
                    beta = lambda h: b4[:, 0, h:h + 1]
                    gcol = lambda h: b4[:, 0, 4 + h:5 + h]
                    gcol2 = lambda h: b4[:, 0, 4 + h:6 + h] if h < 3 else b4[:, 0, 6:8]
                    bb = b4b[0]
                    for h in H:
                        S.op("dve", lambda e, h=h: e.tensor_scalar(G2[:, h, :], strictU, gcol(h), None, ALU.mult), reads=[bb, mkb], writes=[G2b[h]])
                        S.op("dve", lambda e, h=h: e.tensor_scalar(gB[:, h, :], C.ones_f[:, :], gcol(h), None, ALU.mult), reads=[bb, C.cb], writes=[gBb[h]])
                    for h in H:
                        bi = h
                        S.op("pe", lambda e, h=h, bi=bi: e.matmul(ps[bi][:, 0:128], triA, G2[:, h, :], start=True, stop=True),
                             reads=[G2b[h], mkb], writes=[psb[bi]], signal=False)
                        S.op("pe", lambda e, h=h, bi=bi: e.matmul(ps[bi][:, 128:256], G2[:, h, :], triA, start=True, stop=True),
                             reads=[G2b[h], mkb], writes=[psb[bi]], signal=False)
                        S.op("pe", lambda e, h=h, bi=bi: e.matmul(ps[bi][:, 256:384], gB[:, h, :], triA, start=True, stop=True),
                             reads=[gBb[h], mkb], writes=[psb[bi]], signal=False)
                        gsrc = (lambda h: b4[:, 0, 4 + h:6 + h]) if True else None
                        hh = min(h, 2)
                        off = h - hh
                        S.op("pe", lambda e, hh=hh, bi=bi: e.matmul(ps[bi][:, 384:386], triA, b4[:, 0, 4 + hh:6 + hh], start=True, stop=True),
                             reads=[bb, mkb], writes=[psb[bi]], signal=False)
                        S.op("pe", lambda e, hh=hh, bi=bi: e.matmul(ps[bi][:, 386:388], strictU, b4[:, 0, 4 + hh:6 + hh], start=True, stop=True),
                             reads=[bb, mkb], writes=[psb[bi]], signal=False)
                        S.op("pe", lambda e, hh=hh, bi=bi: e.matmul(ps[bi][:, 388:390], C.ones_f[:, :], b4[:, 0, 4 + hh:6 + hh], start=True, stop=True),
                             reads=[bb, C.cb], writes=[psb[bi]])
                        S.op("act", lambda e, h=h, bi=bi: e.activation(EX[:, h, 0:390], ps[bi][:, 0:390], AF.Exp), reads=[psb[bi]], writes=[EXb[h]])
                    gam = lambda h: EX[:, h, 384 + (h - min(h, 2)):385 + (h - min(h, 2))]
                    kds = lambda h: EX[:, h, 386 + (h - min(h, 2)):387 + (h - min(h, 2))]
                    gl_ = lambda h: EX[:, h, 388 + (h - min(h, 2)):389 + (h - min(h, 2))]
                    for h in H:
                        S.op("dve", lambda e, h=h: e.tensor_tensor(Es[:, h, :], EX[:, h, 0:128], MS, ALU.mult), reads=[EXb[h], mkb], writes=[Esb[h]])
                        S.op("dve", lambda e, h=h: e.tensor_tensor(ETc[:, h, :], EX[:, h, 128:256], MU, ALU.mult), reads=[EXb[h], mkb], writes=[ETcb[h]])
                        S.op("dve", lambda e, h=h: e.scalar_tensor_tensor(ngb[:, h, 0:1], gam(h), -1.0, beta(h), ALU.mult, ALU.mult),
                             reads=[EXb[h], bb], writes=[ngbb[h]])
                    for h in H:
                        bi = h
                        S.op("pe", lambda e, ts_=ts_, h=h, bi=bi: e.matmul(ps[bi][:, 0:128], KT[:, h, ts_], KT[:, h, ts_], start=True, stop=True),
                             reads=[KTb[h]], writes=[psb[bi]], signal=False)
                        S.op("pe", lambda e, ts_=ts_, h=h, bi=bi: e.matmul(ps[bi][:, 128:256], KT[:, h, ts_], QT[:, h, ts_], start=True, stop=True),
                             reads=[KTb[h], QTb[h]], writes=[psb[bi]], signal=False)
                        S.op("pe", lambda e, ts_=ts_, h=h, bi=bi: e.transpose(ps[bi][:, 256:384], KTf[:, h, ts_], ident), reads=[KTfb[h], mkb], writes=[psb[bi]], signal=False)
                        S.op("pe", lambda e, ts_=ts_, h=h, bi=bi: e.transpose(ps[bi][:, 384:512], VT[:, h, ts_], ident), reads=[VTb[h], mkb], writes=[psb[bi]])
                    for h in H:
                        bi = h
                        if dbgk > 0:
                            S.op("dve", lambda e, h=h, bi=bi: e.scalar_tensor_tensor(Lm[:, h, :], ps[bi][:, 0:128], beta(h), Es[:, h, :], ALU.mult, ALU.mult),
                                 reads=[psb[bi], bb, Esb[h]], writes=[Lb[h]])
                        if dbgk > 1:
                            S.op("dve", lambda e, h=h, bi=bi: e.tensor_tensor(AT[:, h, :], ps[bi][:, 128:256], ETc[:, h, :], ALU.mult),
                                 reads=[psb[bi], ETcb[h]], writes=[ATb[h]])
                        if dbgk > 2:
                            S.op("dve", lambda e, h=h, bi=bi: e.tensor_scalar(kd[:, h, :], ps[bi][:, 256:384], kds(h), None, ALU.mult),
                                 reads=[psb[bi], EXb[h]], writes=[kdb[h]])
                        if dbgk > 3:
                            S.op("dve", lambda e, h=h, bi=bi: e.tensor_scalar(vb[:, h, :], ps[bi][:, 384:512], beta(h), None, ALU.mult),
                                 reads=[psb[bi], bb], writes=[vbb[h]])
                        if dbgk > 4:
                            S.op("dve", lambda e, ts_=ts_, h=h: e.tensor_tensor(qdT[:, h, :], QT[:, h, ts_], EX[:, h, 256:384], ALU.mult),
                                 reads=[QTb[h], EXb[h]], writes=[qdTb[h]])
                        if dbgk > 5:
                            S.op("dve", lambda e, sg=sg, lg8=lg8, h=h, tb=tb: e.tensor_tensor(gsg[:, h, :], sg[:, tb, h * 128:(h + 1) * 128], gnw[:, :], ALU.mult),
                                 reads=[sgb[tb], smb], writes=[gsgb[h]])
                    for h in H:
                        bi = h
                        S.op("pe", lambda e, h=h, bi=bi: e.transpose(ps[bi][:, 0:128], Lm[:, h, :], ident), reads=[Lb[h], mkb], writes=[psb[bi]])
                        S.op("dve", lambda e, h=h, bi=bi: e.tensor_copy(Xat[:, h, :], ps[bi][:, 0:128]), reads=[psb[bi]], writes=[Xatb[h]])
                        S.op("dve", lambda e, h=h: e.scalar_tensor_tensor(Pm[:, h, :], Lm[:, h, :], -1.0, ident, ALU.mult, ALU.add),
                             reads=[Lb[h], mkb], writes=[Pmb[h]])
                    cur = (Lm, Lb, Xat, Xatb)
                    nxt = [(Xb, Xbb, Xbt, Xbtb), (Xa, Xab_, Xat, Xatb)]
                    for s_ in range(7):
                        X, Xbuf, XT_, XTbuf = cur
                        N_, Nb, NT, NTb = nxt[s_ % 2]
                        for h in H:
                            bi = h
                            if s_ < 5:
                                S.op("pe", lambda e, h=h, bi=bi, X=X, XT_=XT_: e.matmul(ps[bi][:, 0:128], XT_[:, h, :], X[:, h, :], start=True, stop=True),
                                     reads=[Xbuf[h], XTbuf[h]], writes=[psb[bi]], signal=False)
                            if s_ <= 5:
                                S.op("pe", lambda e, h=h, bi=bi, X=X, XT_=XT_: e.matmul(ps[bi][:, 128:256], X[:, h, :], XT_[:, h, :], start=True, stop=True),
                                     reads=[Xbuf[h], XTbuf[h]], writes=[psb[bi]], signal=(s_ == 0))
                            if s_ >= 1:
                                S.op("pe", lambda e, h=h, bi=bi, XT_=XT_: e.matmul(ps[bi][:, 256:384], XT_[:, h, :], Pm[:, h, :], start=True, stop=True),
                                     reads=[XTbuf[h], Pmb[h]], writes=[psb[bi]])
                        for h in H:
                            bi = h
                            if s_ < 5:
                                S.op("dve", lambda e, h=h, bi=bi, N_=N_: e.tensor_copy(N_[:, h, :], ps[bi][:, 0:128]), reads=[psb[bi]], writes=[Nb[h]])
                            if s_ <= 5:
                                S.op("dve", lambda e, h=h, bi=bi, NT=NT: e.tensor_copy(NT[:, h, :], ps[bi][:, 128:256]), reads=[psb[bi]], writes=[NTb[h]])
                            if s_ >= 1:
                                S.op("dve", lambda e, h=h, bi=bi: e.tensor_tensor(Pm[:, h, :], Pm[:, h, :], ps[bi][:, 256:384], ALU.add),
                                     reads=[psb[bi], Pmb[h]], writes=[Pmb[h]])
                        cur = (N_, Nb, NT, NTb)
                    for h in H:
                        bi = h
                        S.op("pe", lambda e, h=h, bi=bi: e.matmul(ps[bi][:, 0:128], Pm[:, h, :], AT[:, h, :], start=True, stop=True),
                             reads=[Pmb[h], ATb[h]], writes=[psb[bi]], signal=False)
                        S.op("pe", lambda e, h=h, bi=bi: e.matmul(ps[bi][:, 128:256], Pm[:, h, :], kd[:, h, :], start=True, stop=True),
                             reads=[Pmb[h], kdb[h]], writes=[psb[bi]])
                    for h in H:
                        bi = h
                        S.op("dve", lambda e, h=h, bi=bi: e.tensor_copy(A2T[:, h, :], ps[bi][:, 0:128]), reads=[psb[bi]], writes=[A2Tb[h]])
                        S.op("dve", lambda e, h=h, bi=bi: e.tensor_copy(K2[:, h, :], ps[bi][:, 128:256]), reads=[psb[bi]], writes=[K2b[h]])
                    for h in H:
                        bi = h
                        S.op("pe", lambda e, ts_=ts_, h=h, bi=bi: e.matmul(ps[bi][:, 0:128], KT[:, h, ts_], Sbf[:, h, :], start=True, stop=True),
                             reads=[KTb[h], Sbfb[h]], writes=[psb[bi]])
                    for h in H:
                        bi = h
                        S.op("dve", lambda e, h=h, bi=bi: e.scalar_tensor_tensor(Rbf[:, h, :], ps[bi][:, 0:128], ngb[:, h, 0:1], vb[:, h, :], ALU.mult, ALU.add),
                             reads=[psb[bi], ngbb[h], vbb[h]], writes=[Rbfb[h]])
                    for h in H:
                        bi = h
                        S.op("pe", lambda e, h=h, bi=bi: e.matmul(ps[bi][:, 128:256], qdT[:, h, :], Sbf[:, h, :], start=True, stop=False),
                             reads=[qdTb[h], Sbfb[h]], writes=[psb[bi]], signal=False)
                        S.op("pe", lambda e, h=h, bi=bi: e.matmul(ps[bi][:, 128:256], A2T[:, h, :], Rbf[:, h, :], start=False, stop=True),
                             reads=[A2Tb[h], Rbfb[h]], writes=[psb[bi]], signal=False)
                        S.op("pe", lambda e, h=h, bi=bi: e.matmul(ps[bi][:, 256:384], K2[:, h, :], Rbf[:, h, :], start=True, stop=True),
                             reads=[K2b[h], Rbfb[h]], writes=[psb[bi]])
                    for h in H:
                        bi = h
                        S.op("dve", lambda e, h=h, bi=bi: e.scalar_tensor_tensor(Sst[:, h, :], Sst[:, h, :], gl_(h), ps[bi][:, 256:384], ALU.mult, ALU.add),
                             reads=[psb[bi], EXb[h], Sstb[h]], writes=[Sstb[h]])
                        S.op("dve", lambda e, h=h: e.tensor_copy(Sbf[:, h, :], Sst[:, h, :]), reads=[Sstb[h]], writes=[Sbfb[h]])
                        S.op("dve", lambda e, h=h, bi=bi: e.tensor_copy(Xa[:, h, :], ps[bi][:, 128:256]), reads=[psb[bi]], writes=[Xab_[h]])
                        S.op("act", lambda e, h=h: e.activation(yo[:, h, :], Xa[:, h, :], AF.Square, accum_out=ngb[:, h, 1:2]),
                             reads=[Xab_[h]], writes=[yob[h], ngbb[h]])
                        S.op("act", lambda e, h=h: e.activation(ngb[:, h, 1:2], ngb[:, h, 1:2], AF.Sqrt, bias=C.eps_t[:, 0:1], scale=1.0 / 128),
                             reads=[ngbb[h], C.cb], writes=[ngbb[h]])
                        S.op("dve", lambda e, h=h: e.reciprocal(ngb[:, h, 1:2], ngb[:, h, 1:2]), reads=[ngbb[h]], writes=[ngbb[h]])
                        S.op("dve", lambda e, sg=sg, lg8=lg8, h=h, bi=bi: e.scalar_tensor_tensor(yo[:, h, :], Xa[:, h, :], ngb[:, h, 1:2], gsg[:, h, :], ALU.mult, ALU.mult),
                             reads=[Xab_[h], ngbb[h], gsgb[h]], writes=[yob[h]])
                    for h in H:
                        bi = h
                        S.op("pe", lambda e, h=h, bi=bi: e.transpose(ps[bi][:, 0:128], yo[:, h, :], ident), reads=[yob[h], mkb], writes=[psb[bi]])
                        S.op("dve", lambda e, ts_=ts_, h=h, bi=bi: e.tensor_copy(OT[:, 4 + h, ts_], ps[bi][:, 0:128]), reads=[psb[bi]], writes=[OTb])
                S.dma("sp", YCT(tile).rearrange("(c p) t -> p c t", p=128), OT[:, :, :], reads=[OTb], writes=[outb[tile]])
                if A.get("after_tile") is not None:
                    A["after_tile"](tile)

            emit_A(0)
            emit_B1(0)
            for tile in range(ntiles):
                emit_B2E(tile)
                if tile + 1 < ntiles:
                    S.record()
                    emit_D(tile)
                    lD = S.stop()
                    S.record()
                    emit_A(tile + 1)
                    emit_B1(tile + 1)
                    lA = S.stop()
                    S.replay([lD, lA])
                else:
                    emit_D(tile)
            print('phase1 sbuf', nc.sbuf_base, nc.sbuf_top)
            S.barrier()


POOL_WINDOWS = (2, 4, 8, 16)


def phase1_inputs(inputs, b, hg):
    w = np.asarray(inputs["ab_w_in"][0], np.float32)
    PW, GW = 2048, 2048
    cols = np.concatenate([
        np.arange(hg * 512, (hg + 1) * 512),
        PW + 3 * GW + np.arange(hg * 512, (hg + 1) * 512),
        PW + np.arange(hg * 512, (hg + 1) * 512),
        PW + GW + np.arange(hg * 512, (hg + 1) * 512),
        PW + 2 * GW + np.arange(hg * 512, (hg + 1) * 512),
        PW + 4 * GW + np.arange(hg * 4, (hg + 1) * 4),
        PW + 4 * GW + 16 + np.arange(hg * 4, (hg + 1) * 4),
    ])
    wc = np.ascontiguousarray(w[:, cols])
    conv = np.asarray(inputs["gdn_conv"][0], np.float32)
    cwl = np.stack([conv[:, s * GW + hg * 512: s * GW + (hg + 1) * 512] for s in range(3)], 0)
    cwl = cwl.reshape(3, 4, 4, 128).transpose(3, 0, 2, 1).reshape(128, 12, 4)
    bc = lambda v: np.ascontiguousarray(np.broadcast_to(np.asarray(v, np.float32)[None, :], (128, len(v))))
    win = POOL_WINDOWS[hg]
    pos = np.arange(256)
    def band_full(first):
        B = np.zeros((256, 128), np.float32)
        for t in range(128):
            cnt = min(t + 1, win) if first else win
            for s in range(max(0, 128 + t - win + 1) if not first else 128 + max(0, t - win + 1), 128 + t + 1):
                B[s, t] = 1.0 / cnt
            B[128 + t, t] -= 1.0
        return B
    Bn = band_full(False)
    B0 = band_full(True)
    band = np.stack([B0[128:], Bn[128:], Bn[:128]], 0)
    k = np.arange(128)
    triA = (k[:, None] <= k[None, :]).astype(np.float32)
    strictU = (k[:, None] > k[None, :]).astype(np.float32)
    MS = (k[:, None] > k[None, :]).astype(np.float32)
    MU = (k[:, None] <= k[None, :]).astype(np.float32)
    masks = np.stack([triA, strictU, MS, MU, np.eye(128, dtype=np.float32)], 0)
    return {
        "xb": np.ascontiguousarray(np.asarray(inputs["x"][b], np.float32)),
        "g_pre": np.asarray(inputs["norm_mix_pre"][0], np.float32),
        "w_in_c": wc,
        "conv_w": np.ascontiguousarray(cwl),
        "neg_a": bc(inputs["gdn_a_log"][0][hg * 4:(hg + 1) * 4]),
        "dt_b": bc(inputs["gdn_dt_bias"][0][hg * 4:(hg + 1) * 4]),
        "gn_w": bc(inputs["gdn_norm"][0]),
        "pool_wg": np.ascontiguousarray(np.asarray(inputs["pool_w"][0][hg], np.float32)),
        "pool_sc": np.ascontiguousarray(np.asarray(inputs["pool_scale"][0][hg * 512:(hg + 1) * 512], np.float32)),
        "bandm": np.ascontiguousarray(band), "masks": np.ascontiguousarray(masks),
    }


def kernel(**inputs):
    n = 8
    nc = build_fused()
    consts = phase2_consts(inputs)
    x = np.asarray(inputs["x"], np.float32)
    maps = []
    for c in range(n):
        b, j = c // 4, c % 4
        m = dict(consts)
        m.update(phase1_inputs(inputs, b, j))
        m["x_own"] = np.ascontiguousarray(x[b, j * TP:(j + 1) * TP, :])
        sel = np.zeros((128, 4), np.float32)
        sel[:, j] = 1.0
        m["sel"] = sel
        maps.append(m)
    res = run_bass_kernel_spmd(nc, maps, core_ids=list(range(n)))
    out = np.empty((2, SEQ, D), np.float32)
    for c in range(n):
        b, j = c // 4, c % 4
        out[b, j * TP:(j + 1) * TP, :] = np.asarray(res.results[c]["out"], np.float32)
    return out
```

```python
import contextlib
import numpy as np
import ml_dtypes
import concourse.bass as bass
import concourse.mybir as mybir
from concourse.bass_utils import run_bass_kernel_spmd

F32 = mybir.dt.float32
BF16 = mybir.dt.bfloat16
AF = mybir.ActivationFunctionType
ALU = mybir.AluOpType
AX = mybir.AxisListType

D = 4096
DFF = 11008
KC = D // 128
TP = 1024
SEQ = 4096
EPS = 1e-6


class Buf:
    __slots__ = ("name", "w", "r", "strict")

    def __init__(self, name="", strict=False):
        self.name = name
        self.w = None
        self.r = []
        self.strict = strict


def SBuf(name=""):
    return Buf(name, True)


class Sched:
    ENGS = ("pe", "act", "dve", "pool", "sp")

    def __init__(self, nc, n_dma_sems=48):
        self.nc = nc
        self.prog = {e: [] for e in self.ENGS}
        self.sems = {e: nc.alloc_semaphore(name="s_" + e) for e in self.ENGS}
        self.cnt = {e: 0 for e in self.ENGS}
        self.waited = {e: {} for e in self.ENGS}
        self.dsems = [nc.alloc_semaphore(name="d%d" % i) for i in range(n_dma_sems)]
        self.dval = [0] * n_dma_sems
        self.dnext = 0
        self.n_ins = 0
        self._rec = None
        self.nosame = 1
        self.sems["cc"] = nc.alloc_semaphore(name="s_cc")
        self.ccval = 0

    def _sem(self, key):
        return self.sems[key] if isinstance(key, str) else self.dsems[key]

    def _collect(self, eng, reads, writes):
        need = {}

        relax = self.nosame and eng in ("dve", "act")

        def add(tok, strict):
            if tok is None:
                return
            k, v = tok
            if k == eng and (eng == "pe" or (relax and not strict)):
                return
            if need.get(k, 0) < v:
                need[k] = v
        for b in reads:
            add(b.w, b.strict)
        for b in writes:
            add(b.w, b.strict)
            for t in b.r:
                add(t, b.strict)
        waits = []
        wd = self.waited[eng]
        for k, v in need.items():
            if wd.get(k, 0) >= v:
                continue
            wd[k] = v
            waits.append((self._sem(k), v))
        return waits

    def _commit(self, tok, reads, writes):
        for b in reads:
            b.r.append(tok)
        for b in writes:
            b.w = tok
            b.r = []

    def record(self):
        self._rec = []

    def stop(self):
        r, self._rec = self._rec, None
        return r

    def replay(self, lists):
        pos = [0] * len(lists)
        tot = max(len(l) for l in lists)
        for step in range(1, tot + 1):
            for i, l in enumerate(lists):
                upto = (step * len(l)) // tot
                while pos[i] < upto:
                    kind, a, kw = l[pos[i]]
                    pos[i] += 1
                    {"op": self.op, "dma": self.dma, "cc": self.collective}[kind](*a, **kw)

    def op(self, eng, fn, reads=(), writes=(), signal=True):
        if self._rec is not None:
            self._rec.append(("op", (eng, fn), dict(reads=list(reads), writes=list(writes), signal=signal)))
            return
        waits = self._collect(eng, reads, writes)
        if signal:
            self.cnt[eng] += 1
            tok = (eng, self.cnt[eng])
        else:
            tok = (eng, self.cnt[eng] + 1)
        sem = self.sems[eng]

        def run(e, waits=waits, fn=fn, sem=sem, signal=signal):
            for s, v in waits:
                e.wait_ge(s, v)
            ins = fn(e)
            if signal:
                ins.then_inc(sem, 1)
        self.prog[eng].append(run)
        self._commit(tok, reads, writes)
        self.n_ins += 1

    def dma(self, eng, out_ap, in_ap, reads=(), writes=(), **kw):
        if self._rec is not None:
            self._rec.append(("dma", (eng, out_ap, in_ap), dict(reads=list(reads), writes=list(writes), **kw)))
            return
        i = self.dnext
        self.dnext = (self.dnext + 1) % len(self.dsems)
        waits = self._collect(eng, reads, writes)
        wd = self.waited[eng]
        if self.dval[i] > 0 and wd.get(i, 0) < self.dval[i]:
            wd[i] = self.dval[i]
            waits.append((self.dsems[i], self.dval[i]))
        self.dval[i] += 16
        tok = (i, self.dval[i])
        sem = self.dsems[i]

        def run(e, waits=waits, sem=sem):
            for s, v in waits:
                e.wait_ge(s, v)
            e.dma_start(out=out_ap, in_=in_ap, **kw).then_inc(sem, 16)
        self.prog[eng].append(run)
        self._commit(tok, reads, writes)
        self.n_ins += 1

    def collective(self, kind, in_ap, out_ap, groups, reads=(), writes=()):
        if self._rec is not None:
            self._rec.append(("cc", (kind, in_ap, out_ap, groups), dict(reads=list(reads), writes=list(writes))))
            return
        waits = self._collect("pool", reads, writes)
        self.ccval += 1
        tok = ("cc", self.ccval)
        sem = self.sems["cc"]

        def run(e, waits=waits, sem=sem):
            for s_, v in waits:
                e.wait_ge(s_, v)
            e.collective_compute(kind, ALU.bypass, replica_groups=groups, ins=[in_ap], outs=[out_ap]).then_inc(sem, 1)
        self.prog["pool"].append(run)
        self._commit(tok, reads, writes)

    def barrier(self):
        for e in self.ENGS:
            waits = []
            wd = self.waited[e]
            for e2 in self.ENGS:
                if e2 != e and self.cnt[e2] > wd.get(e2, 0):
                    wd[e2] = self.cnt[e2]
                    waits.append((self.sems[e2], self.cnt[e2]))
            for i, v in enumerate(self.dval):
                if v > wd.get(i, 0):
                    wd[i] = v
                    waits.append((self.dsems[i], v))
            if self.ccval > wd.get("cc", 0):
                wd["cc"] = self.ccval
                waits.append((self.sems["cc"], self.ccval))

            def run(en, waits=waits):
                for s, v in waits:
                    en.wait_ge(s, v)
            self.prog[e].append(run)

    def finish(self):
        self.barrier()
        nc = self.nc
        with nc.Block() as block:
            @block.tensor
            def _(e):
                for f in self.prog["pe"]:
                    f(e)

            @block.scalar
            def _(e):
                for f in self.prog["act"]:
                    f(e)

            @block.vector
            def _(e):
                for f in self.prog["dve"]:
                    f(e)

            @block.gpsimd
            def _(e):
                for f in self.prog["pool"]:
                    f(e)

            @block.sync
            def _(e):
                for f in self.prog["sp"]:
                    f(e)


class Ctx:
    def __init__(self, nc, S, st, NW=8, pfx=""):
        self.nc, self.S, self.st, self.pfx = nc, S, st, pfx
        self.ps = [st.enter_context(nc.psum_tensor(pfx + "ps%d" % i, [128, 512], F32)) for i in range(8)]
        self.psb = [Buf("ps%d" % i) for i in range(8)]
        self.ones_bf = self.sb("ones_bf", [128, 128], BF16)
        self.ones_f = self.sb("ones_f", [128, 128], F32)
        self.cb = SBuf("consts")
        S.op("dve", lambda e: e.memset(self.ones_bf[:], 1.0), writes=[self.cb])
        S.op("dve", lambda e: e.memset(self.ones_f[:], 1.0), writes=[self.cb])
        self.NW = NW
        self.wt = [self.sb("wt%d" % i, [128, 4096], BF16) for i in range(self.NW)]
        self.wtb = [Buf("wt%d" % i) for i in range(self.NW)]
        self.wnext = 0
        self.dmaq = 0

    def sb(self, name, shape, dt):
        return self.st.enter_context(self.nc.sbuf_tensor(self.pfx + name, shape, dt))

    def wslot(self):
        i = self.wnext
        self.wnext = (i + 1) % self.NW
        return self.wt[i], self.wtb[i]

    def q(self):
        self.dmaq ^= 1
        return "sp" if self.dmaq else "act"


def gemm_cg(C, W, c0, CW, rhs, rhsb, KCr, T, banks, tokmajor=False):
    S = C.S
    nth = T // 512
    noc = CW // 128
    ukc = 4096 // CW
    nu = (KCr + ukc - 1) // ukc
    Wv = W.rearrange("(kc p) n -> p kc n", p=128)
    for u in range(nu):
        k0 = u * ukc
        nk = min(ukc, KCr - k0)
        wt, wb = C.wslot()
        wv = wt[:, 0:nk * CW].rearrange("p (k n) -> p k n", n=CW)
        S.dma("pool", wv, Wv[:, k0:k0 + nk, c0:c0 + CW], writes=[wb])
        for oc in range(noc):
            for th in range(nth):
                bi = banks[oc * nth + th]
                for j in range(nk):
                    kc = k0 + j
                    S.op("pe", lambda e, bi=bi, wv=wv, j=j, oc=oc, kc=kc, th=th: e.matmul(
                        C.ps[bi][:, :], wv[:, j, oc * 128:(oc + 1) * 128], rhs[:, kc, th * 512:(th + 1) * 512],
                        start=(kc == 0), stop=(kc == KCr - 1)),
                        reads=[wb, rhsb], writes=[C.psb[bi]], signal=(j == nk - 1))


def load_gain(C, sb, name, g_ap, ncol=KC):
    t = sb(name, [128, ncol], F32)
    b = Buf(name)
    C.S.dma("sp", t[:, :], g_ap.rearrange("(kc p) -> p kc", p=128), writes=[b], allow_slow_non_contiguous=True)
    return t, b


def colsum_rstd(C, src_dram, srcb, nkc, T, rstd, rstdb, xin, xinb, sq, sqb, scale, tmp, tmpb):
    S = C.S
    nth = T // 512
    for kc in range(nkc):
        r = kc % len(xin)
        S.dma(C.q(), xin[r][:, :], src_dram[kc * 128:(kc + 1) * 128, :], reads=[srcb[kc]], writes=[xinb[r]])
        r2 = kc % len(sq)
        S.op("act", lambda e, r=r, r2=r2: e.activation(sq[r2][:, :], xin[r][:, :], AF.Square), reads=[xinb[r]], writes=[sqb[r2]])
        for th in range(nth):
            S.op("pe", lambda e, th=th, r2=r2, kc=kc: e.matmul(C.ps[th][:, :], C.ones_bf[:, :], sq[r2][:, th * 512:(th + 1) * 512],
                                                             start=(kc == 0), stop=(kc == nkc - 1)),
                 reads=[sqb[r2], C.cb], writes=[C.psb[th]])
    for th in range(nth):
        sl = slice(th * 512, (th + 1) * 512)
        S.op("act", lambda e, th=th, sl=sl: e.activation(tmp[:, sl], C.ps[th][:, :], AF.Sqrt, bias=C.eps_t[:, 0:1], scale=scale),
             reads=[C.psb[th], C.cb], writes=[tmpb])
        S.op("dve", lambda e, sl=sl: e.reciprocal(rstd[:, sl], tmp[:, sl]), reads=[tmpb], writes=[rstdb])


def rstd_from_acc(C, acc, accb, T, rstd, rstdb, tmp, tmpb, scale):
    S = C.S
    for th in range(T // 512):
        sl = slice(th * 512, (th + 1) * 512)
        S.op("pe", lambda e, th=th, sl=sl: e.matmul(C.ps[th][:, :], C.ones_f[:, :], acc[:, sl], start=True, stop=True),
             reads=[accb, C.cb], writes=[C.psb[th]])
        S.op("act", lambda e, th=th, sl=sl: e.activation(tmp[:, sl], C.ps[th][:, :], AF.Sqrt, bias=C.eps_t[:, 0:1], scale=scale),
             reads=[C.psb[th], C.cb], writes=[tmpb])
        S.op("dve", lambda e, sl=sl: e.reciprocal(rstd[:, sl], tmp[:, sl]), reads=[tmpb], writes=[rstdb])


def norm_stage(C, XT, XTb, gain_ap, HT, HTb, tag, xsq=None):
    S, nc = C.S, C.nc
    with contextlib.ExitStack() as st:
        sb = lambda n, s, d: st.enter_context(nc.sbuf_tensor(tag + n, s, d))
        xin = [sb("xin%d" % i, [128, TP], F32) for i in range(3)]
        xinb = [Buf() for _ in range(3)]
        sq = [sb("sq%d" % i, [128, TP], BF16) for i in range(2)]
        sqb = [Buf() for _ in range(2)]
        rstd = sb("rstd", [128, TP], F32)
        rstdb = Buf()
        tmp = sb("tmp", [128, TP], F32)
        tmpb = Buf()
        g = sb("g", [128, KC], F32)
        gb = Buf()
        S.dma("sp", g[:, :], gain_ap.rearrange("(kc p) -> p kc", p=128), writes=[gb], allow_slow_non_contiguous=True)
        if xsq is None:
            colsum_rstd(C, XT, XTb, KC, TP, rstd, rstdb, xin, xinb, sq, sqb, 1.0 / D, tmp, tmpb)
        else:
            rstd_from_acc(C, xsq[0], xsq[1], TP, rstd, rstdb, tmp, tmpb, 1.0 / D)
        for kc in range(KC):
            r = kc % 3
            S.dma(C.q(), xin[r][:, :], XT[kc * 128:(kc + 1) * 128, :], reads=[XTb[kc]], writes=[xinb[r]])
            S.op("dve", lambda e, r=r, kc=kc: e.scalar_tensor_tensor(HT[:, kc, :], xin[r][:, :], g[:, kc:kc + 1], rstd[:, :],
                                                                    ALU.mult, ALU.mult),
                 reads=[xinb[r], gb, rstdb], writes=[HTb])


def postnorm_resid(C, YT, YTb, gain_ap, XT, XTb, tag, ysq=None, xsq=None, final=None):
    S, nc = C.S, C.nc
    with contextlib.ExitStack() as st:
        sb = lambda n, s, d: st.enter_context(nc.sbuf_tensor(tag + n, s, d))
        xin = [sb("xin%d" % i, [128, TP], F32) for i in range(3)]
        xinb = [Buf() for _ in range(3)]
        yin = [sb("yin%d" % i, [128, TP], F32) for i in range(3)]
        yinb = [Buf() for _ in range(3)]
        sq = [sb("sq%d" % i, [128, TP], BF16) for i in range(2)]
        sqb = [Buf() for _ in range(2)]
        rstd = sb("rstd", [128, TP], F32)
        rstdb = Buf()
        tmp = sb("tmp", [128, TP], F32)
        tmpb = Buf()
        g = sb("g", [128, KC], F32)
        gb = Buf()
        S.dma("sp", g[:, :], gain_ap.rearrange("(kc p) -> p kc", p=128), writes=[gb], allow_slow_non_contiguous=True)
        if ysq is None:
            colsum_rstd(C, YT, YTb, KC, TP, rstd, rstdb, yin, yinb, sq, sqb, 1.0 / D, tmp, tmpb)
        else:
            rstd_from_acc(C, ysq[0], ysq[1], TP, rstd, rstdb, tmp, tmpb, 1.0 / D)
        if xsq is not None:
            S.op("dve", lambda e: e.memset(xsq[0][:, :], 0.0), writes=[xsq[1]])
        if final is not None:
            xo = [sb("xo%d" % i, [128, 4, 128], F32) for i in range(3)]
            xob = [Buf() for _ in range(3)]
            oi = 0
        for kc in range(KC):
            r = kc % 3
            rows = slice(kc * 128, (kc + 1) * 128)
            S.dma("sp", yin[r][:, :], YT[rows, :], reads=[YTb[kc]], writes=[yinb[r]])
            S.dma("act", xin[r][:, :], XT[rows, :], reads=[XTb[kc]], writes=[xinb[r]])
            S.op("dve", lambda e, r=r, kc=kc: e.scalar_tensor_tensor(yin[r][:, :], yin[r][:, :], g[:, kc:kc + 1], rstd[:, :],
                                                                    ALU.mult, ALU.mult),
                 reads=[yinb[r], gb, rstdb], writes=[yinb[r]])
            S.op("dve", lambda e, r=r: e.tensor_tensor(xin[r][:, :], xin[r][:, :], yin[r][:, :], ALU.add),
                 reads=[yinb[r], xinb[r]], writes=[xinb[r]])
            if final is None:
                S.dma("sp", XT[rows, :], xin[r][:, :], reads=[xinb[r]], writes=[XTb[kc]])
            else:
                out, outb, ident, identb = final
                for t4 in range(TP // 512):
                    bi = 2 + oi % 6
                    o = oi % 3
                    oi += 1
                    for j in range(4):
                        tb = t4 * 4 + j
                        S.op("pe", lambda e, bi=bi, j=j, tb=tb, r=r: e.transpose(C.ps[bi][:, j * 128:(j + 1) * 128],
                                                                                xin[r][:, tb * 128:(tb + 1) * 128], ident[:, :]),
                             reads=[xinb[r], identb], writes=[C.psb[bi]], signal=(j == 3))
                    S.op("dve", lambda e, bi=bi, o=o: e.tensor_copy(xo[o][:, :, :], C.ps[bi][:, :]), reads=[C.psb[bi]], writes=[xob[o]])
                    dst = out[t4 * 512:(t4 + 1) * 512, kc * 128:(kc + 1) * 128].rearrange("(j p) f -> p j f", p=128)
                    S.dma(C.q(), dst, xo[o][:, :, :], reads=[xob[o]], writes=[outb])
            if xsq is not None:
                S.op("act", lambda e, r=r: e.activation(yin[r][:, :], xin[r][:, :], AF.Square), reads=[xinb[r]], writes=[yinb[r]])
                S.op("dve", lambda e, r=r: e.tensor_tensor(xsq[0][:, :], xsq[0][:, :], yin[r][:, :], ALU.add),
                     reads=[yinb[r], xsq[1]], writes=[xsq[1]])
        S.barrier()


def gemm_to_dram(C, W, N, rhs, rhsb, KCr, T, OUT, OUTb, tok0, func, odt, tag, ysq=None):
    S, nc = C.S, C.nc
    CW = 256 if T == 1024 else 512
    nth = T // 512
    noc = CW // 128
    with contextlib.ExitStack() as st:
        ot = [st.enter_context(nc.sbuf_tensor(tag + "ot%d" % i, [128, 512], odt)) for i in range(4)]
        otb = [Buf() for _ in range(4)]
        if ysq is not None:
            sqt = [st.enter_context(nc.sbuf_tensor(tag + "sqt%d" % i, [128, 512], F32)) for i in range(2)]
            sqtb = [Buf() for _ in range(2)]
            for th in range(nth):
                S.op("dve", lambda e, th=th: e.memset(ysq[0][:, tok0 + th * 512: tok0 + (th + 1) * 512], 0.0), writes=[ysq[1]])
        oi = 0
        for cg in range(N // CW):
            banks = [(cg % 2) * 4 + i for i in range(4)]
            gemm_cg(C, W, cg * CW, CW, rhs, rhsb, KCr, T, banks)
            for oc in range(noc):
                for th in range(nth):
                    bi = banks[oc * nth + th]
                    o = oi % 4
                    oi += 1
                    if func is None:
                        S.op("dve", lambda e, o=o, bi=bi: e.tensor_copy(ot[o][:, :], C.ps[bi][:, :]),
                             reads=[C.psb[bi]], writes=[otb[o]])
                    else:
                        S.op("act", lambda e, o=o, bi=bi: e.activation(ot[o][:, :], C.ps[bi][:, :], func),
                             reads=[C.psb[bi]], writes=[otb[o]])
                    row = cg * CW + oc * 128
                    S.dma(C.q(), OUT[row:row + 128, tok0 + th * 512: tok0 + (th + 1) * 512], ot[o][:, :],
                          reads=[otb[o]], writes=[OUTb[row // 128]])
                    if ysq is not None:
                        q2 = oi % 2
                        cs = slice(tok0 + th * 512, tok0 + (th + 1) * 512)
                        S.op("act", lambda e, o=o, q2=q2: e.activation(sqt[q2][:, :], ot[o][:, :], AF.Square), reads=[otb[o]], writes=[sqtb[q2]])
                        S.op("dve", lambda e, q2=q2, cs=cs: e.tensor_tensor(ysq[0][:, cs], ysq[0][:, cs], sqt[q2][:, :], ALU.add),
                             reads=[sqtb[q2], ysq[1]], writes=[ysq[1]])
        S.barrier()


def load_fm(C, SRC, SRCb, nkc, T, tok0, dst, dstb, per=8):
    v = SRC.rearrange("(kc p) t -> p kc t", p=128)
    for k0 in range(0, nkc, per):
        k1 = min(nkc, k0 + per)
        C.S.dma(C.q(), dst[:, k0:k1, 0:T], v[:, k0:k1, tok0:tok0 + T], reads=[SRCb[k] for k in range(k0, k1)], writes=[dstb])


def ffn_stage(C, XT, XTb, YT, YTb, HID, HIDb, g_pre, g_post, Wg, Wu, Wd, tag, last=False, final=None):
    S, nc = C.S, C.nc
    with contextlib.ExitStack() as st:
        HT = st.enter_context(nc.sbuf_tensor(tag + "HT", [128, KC, TP], BF16))
        HTb = Buf()
        norm_stage(C, XT, XTb, g_pre, HT, HTb, tag + "n", xsq=C.xsq)
        sl_t = [st.enter_context(nc.sbuf_tensor(tag + "sl%d" % i, [128, 512], F32)) for i in range(2)]
        slb = [Buf() for _ in range(2)]
        ot = [st.enter_context(nc.sbuf_tensor(tag + "ho%d" % i, [128, 512], BF16)) for i in range(4)]
        otb = [Buf() for _ in range(4)]
        oi = 0
        for cg in range(DFF // 256):
            bg = [0, 1, 2, 3]
            bu = [4, 5, 6, 7]
            gemm_cg(C, Wg, cg * 256, 256, HT, HTb, KC, TP, bg)
            gemm_cg(C, Wu, cg * 256, 256, HT, HTb, KC, TP, bu)
            for oc in range(2):
                for th in range(2):
                    o = oi % 4
                    s2 = oi % 2
                    oi += 1
                    b1, b2 = bg[oc * 2 + th], bu[oc * 2 + th]
                    S.op("act", lambda e, s2=s2, b1=b1: e.activation(sl_t[s2][:, :], C.ps[b1][:, :], AF.Silu),
                         reads=[C.psb[b1]], writes=[slb[s2]])
                    S.op("dve", lambda e, s2=s2, b2=b2, o=o: e.tensor_tensor(ot[o][:, :], sl_t[s2][:, :], C.ps[b2][:, :], ALU.mult),
                         reads=[slb[s2], C.psb[b2]], writes=[otb[o]])
                    row = cg * 256 + oc * 128
                    S.dma(C.q(), HID[row:row + 128, th * 512:(th + 1) * 512], ot[o][:, :], reads=[otb[o]], writes=[HIDb[row // 128]])
        S.barrier()
    KF = DFF // 128
    with contextlib.ExitStack() as st:
        RH = st.enter_context(nc.sbuf_tensor(tag + "RH", [128, KF, 512], BF16))
        RHb = Buf()
        for th2 in range(2):
            load_fm(C, HID, HIDb, KF, 512, th2 * 512, RH, RHb)
            gemm_to_dram(C, Wd, D, RH, RHb, KF, 512, YT, YTb, th2 * 512, None, F32, tag + "d%d" % th2, ysq=C.ysq)
    postnorm_resid(C, YT, YTb, g_post, XT, XTb, tag + "p", ysq=C.ysq, xsq=None if last else C.xsq, final=final)


def about_stage(C, XT, XTb, YT, YTb, YC, YCb, g_post, Wo, tag):
    S, nc = C.S, C.nc
    with contextlib.ExitStack() as st:
        R = st.enter_context(nc.sbuf_tensor(tag + "R", [128, KC, TP], BF16))
        Rb = Buf()
        load_fm(C, YC, YCb, KC, TP, 0, R, Rb)
        gemm_to_dram(C, Wo, D, R, Rb, KC, TP, YT, YTb, 0, None, F32, tag + "g")
    postnorm_resid(C, YT, YTb, g_post, XT, XTb, tag + "p")


def sgu_stage(C, XT, XTb, YT, YTb, UT, UTb, VTM, VTMb, g_pre, g_post, Win, ln_g, ln_b, wsT, bs, maskT, Wout, tag):
    S, nc = C.S, C.nc
    with contextlib.ExitStack() as st:
        HT = st.enter_context(nc.sbuf_tensor(tag + "HT", [128, KC, TP], BF16))
        HTb = Buf()
        norm_stage(C, XT, XTb, g_pre, HT, HTb, tag + "n", xsq=C.xsq)
        gemm_to_dram(C, Win[:, 0:D], D, HT, HTb, KC, TP, UT, UTb, 0, AF.Gelu, BF16, tag + "u")
        vo = [st.enter_context(nc.sbuf_tensor(tag + "vo%d" % i, [128, 512], F32)) for i in range(3)]
        vob = [Buf() for _ in range(3)]
        Wv = Win.rearrange("(kc p) n -> p kc n", p=128)
        oi = 0
        for cg in range(D // 512):
            slots = []
            for u in range(4):
                wt, wb = C.wslot()
                wv = wt[:, :].rearrange("p (k n) -> p k n", n=512)
                S.dma("pool", wv, Wv[:, u * 8:(u + 1) * 8, D + cg * 512: D + (cg + 1) * 512], writes=[wb])
                slots.append((wv, wb))
            for tb in range(TP // 128):
                bi = oi % 8
                for kc in range(KC):
                    wv, wb = slots[kc // 8]
                    S.op("pe", lambda e, bi=bi, wv=wv, kc=kc, tb=tb: e.matmul(
                        C.ps[bi][:, :], HT[:, kc, tb * 128:(tb + 1) * 128], wv[:, kc % 8, :], start=(kc == 0), stop=(kc == KC - 1)),
                        reads=[wb, HTb], writes=[C.psb[bi]], signal=(kc % 8 == 7))
                o = oi % 3
                oi += 1
                S.op("act", lambda e, o=o, bi=bi: e.activation(vo[o][:, :], C.ps[bi][:, :], AF.Gelu), reads=[C.psb[bi]], writes=[vob[o]])
                S.dma(C.q(), VTM[tb * 128:(tb + 1) * 128, cg * 512:(cg + 1) * 512], vo[o][:, :], reads=[vob[o]], writes=[VTMb[tb]])
        S.barrier()
    with contextlib.ExitStack() as st:
        sb = lambda n, s, d: st.enter_context(nc.sbuf_tensor(tag + n, s, d))
        PT = sb("PT", [128, KC, TP], BF16)
        PTb = Buf()
        load_fm(C, UT, UTb, KC, TP, 0, PT, PTb)
        mk = sb("mk", [128, 128], F32)
        mkb = Buf()
        S.dma("sp", mk[:, :], maskT, writes=[mkb])
        wsbf = sb("wsbf", [128, 16, 128], BF16)
        wsbfb = Buf()
        S.dma("pool", wsbf[:, :, :], wsT, writes=[wsbfb])
        for g in range(16):
            S.op("dve", lambda e, g=g: e.tensor_tensor(wsbf[:, g, :], wsbf[:, g, :], mk[:, :], ALU.mult), reads=[wsbfb, mkb], writes=[wsbfb])
        BS = sb("BS", [128, 16, 128], F32)
        BSb = Buf()
        S.dma("sp", BS[:, :, :], bs, writes=[BSb])
        RS = sb("RS", [128, 16, 128], F32)
        RSb = Buf()
        for q4 in range(4):
            S.op("pe", lambda e, q4=q4: e.matmul(C.ps[q4][:, :], C.ones_bf[:, :], wsbf[:, q4 * 4:(q4 + 1) * 4, :], start=True, stop=True),
                 reads=[wsbfb, C.cb], writes=[C.psb[q4]])
            S.op("dve", lambda e, q4=q4: e.tensor_copy(RS[:, q4 * 4:(q4 + 1) * 4, :], C.ps[q4][:, :]), reads=[C.psb[q4]], writes=[RSb])
        lg, lgb = load_gain(C, sb, "lg", ln_g)
        lb, lbb = load_gain(C, sb, "lb", ln_b)
        T2 = sb("T2", [128, KC, 128], F32)
        T2b = Buf()
        for kc in range(KC):
            S.op("dve", lambda e, kc=kc: e.scalar_tensor_tensor(T2[:, kc, :], RS[:, kc // 2, :], lb[:, kc:kc + 1], BS[:, kc // 2, :],
                                                               ALU.mult, ALU.add), reads=[RSb, BSb, lbb], writes=[T2b])
        vin = [sb("vin0", [128, D], F32)] * 2
        vinb = [Buf()] * 2
        vh = [sb("vh0", [128, D], BF16)] * 2
        vhb = [Buf()] * 2
        junk = vh[0]
        junkb = vhb[0]
        st4 = [sb("st%d" % i, [128, 8], F32) for i in range(2)]
        st4b = [SBuf() for _ in range(2)]
        sv = [sb("sv%d" % i, [128, 128], F32) for i in range(3)]
        svb = [Buf() for _ in range(3)]
        oi = 0
        for tb in range(TP // 128):
            r = tb % 2
            S.dma("sp", vin[r][:, 0:D // 2], VTM[tb * 128:(tb + 1) * 128, 0:D // 2], reads=[VTMb[tb]], writes=[vinb[r]])
            S.dma("act", vin[r][:, D // 2:D], VTM[tb * 128:(tb + 1) * 128, D // 2:D], reads=[VTMb[tb]], writes=[vinb[r]])
            s4 = st4[r]
            S.op("act", lambda e, r=r, s4=s4: e.activation(junk[:, :], vin[r][:, :], AF.Identity, accum_out=s4[:, 0:1]),
                 reads=[vinb[r]], writes=[junkb, st4b[r]])
            S.op("act", lambda e, r=r, s4=s4: e.activation(junk[:, :], vin[r][:, :], AF.Square, accum_out=s4[:, 1:2]),
                 reads=[vinb[r]], writes=[junkb, st4b[r]])
            S.op("dve", lambda e, s4=s4: e.tensor_scalar(s4[:, 2:3], s4[:, 0:1], 1.0 / D, None, ALU.mult), reads=[st4b[r]], writes=[st4b[r]])
            S.op("dve", lambda e, s4=s4: e.tensor_tensor(s4[:, 3:4], s4[:, 2:3], s4[:, 2:3], ALU.mult), reads=[st4b[r]], writes=[st4b[r]])
            S.op("dve", lambda e, s4=s4: e.scalar_tensor_tensor(s4[:, 4:5], s4[:, 1:2], 1.0 / D, s4[:, 3:4], ALU.mult, ALU.subtract),
                 reads=[st4b[r]], writes=[st4b[r]])
            S.op("act", lambda e, s4=s4: e.activation(s4[:, 5:6], s4[:, 4:5], AF.Sqrt, bias=C.eps_t[:, 0:1], scale=1.0),
                 reads=[st4b[r], C.cb], writes=[st4b[r]])
            S.op("dve", lambda e, s4=s4: e.reciprocal(s4[:, 6:7], s4[:, 5:6]), reads=[st4b[r]], writes=[st4b[r]])
            S.op("dve", lambda e, s4=s4: e.scalar_tensor_tensor(s4[:, 7:8], s4[:, 2:3], -1.0, s4[:, 6:7], ALU.mult, ALU.mult),
                 reads=[st4b[r]], writes=[st4b[r]])
            S.op("dve", lambda e, r=r, s4=s4: e.tensor_scalar(vh[r][:, :], vin[r][:, :], s4[:, 6:7], s4[:, 7:8], ALU.mult, ALU.add),
                 reads=[vinb[r], st4b[r]], writes=[vhb[r]])
            for k4 in range(KC // 4):
                bi = oi % 8
                oi += 1
                for j in range(4):
                    kc = k4 * 4 + j
                    S.op("pe", lambda e, bi=bi, j=j, kc=kc, r=r: e.matmul(C.ps[bi][:, j * 128:(j + 1) * 128], vh[r][:, kc * 128:(kc + 1) * 128],
                                                                         wsbf[:, kc // 2, :], start=True, stop=True),
                         reads=[vhb[r], wsbfb], writes=[C.psb[bi]], signal=(j == 3))
                for j in range(4):
                    kc = k4 * 4 + j
                    s3 = (k4 * 4 + j) % 3
                    S.op("dve", lambda e, bi=bi, j=j, kc=kc, s3=s3: e.scalar_tensor_tensor(
                        sv[s3][:, :], C.ps[bi][:, j * 128:(j + 1) * 128], lg[:, kc:kc + 1], T2[:, kc, :], ALU.mult, ALU.add),
                        reads=[C.psb[bi], lgb, T2b], writes=[svb[s3]])
                    S.op("dve", lambda e, kc=kc, s3=s3, tb=tb: e.tensor_tensor(
                        PT[:, kc, tb * 128:(tb + 1) * 128], PT[:, kc, tb * 128:(tb + 1) * 128], sv[s3][:, :], ALU.mult),
                        reads=[svb[s3], PTb], writes=[PTb])
        gemm_to_dram(C, Wout, D, PT, PTb, KC, TP, YT, YTb, 0, None, F32, tag + "o")
    postnorm_resid(C, YT, YTb, g_post, XT, XTb, tag + "p", xsq=C.xsq)


def xin_stage(C, x_own, XT, XTb, ident, identb):
    S, nc = C.S, C.nc
    with contextlib.ExitStack() as st:
        xr = [st.enter_context(nc.sbuf_tensor("xi_r%d" % i, [128, D], F32)) for i in range(2)]
        xrb = [Buf() for _ in range(2)]
        xo = [st.enter_context(nc.sbuf_tensor("xi_o%d" % i, [128, 4, 128], F32)) for i in range(3)]
        xob = [Buf() for _ in range(3)]
        inb = Buf()
        oi = 0
        for tb in range(TP // 128):
            r = tb % 2
            S.dma("sp", xr[r][:, 0:D // 2], x_own[tb * 128:(tb + 1) * 128, 0:D // 2], reads=[inb], writes=[xrb[r]])
            S.dma("act", xr[r][:, D // 2:D], x_own[tb * 128:(tb + 1) * 128, D // 2:D], reads=[inb], writes=[xrb[r]])
            for k4 in range(KC // 4):
                bi = oi % 8
                o = oi % 3
                oi += 1
                for j in range(4):
                    kc = k4 * 4 + j
                    S.op("pe", lambda e, bi=bi, j=j, kc=kc, r=r: e.transpose(C.ps[bi][:, j * 128:(j + 1) * 128],
                                                                            xr[r][:, kc * 128:(kc + 1) * 128], ident[:, :]),
                         reads=[xrb[r], identb], writes=[C.psb[bi]], signal=(j == 3))
                S.op("dve", lambda e, bi=bi, o=o: e.tensor_copy(xo[o][:, :, :], C.ps[bi][:, :]), reads=[C.psb[bi]], writes=[xob[o]])
                dst = XT[k4 * 512:(k4 + 1) * 512, tb * 128:(tb + 1) * 128].rearrange("(j p) t -> p j t", p=128)
                S.dma(C.q(), dst, xo[o][:, :, :], reads=[xob[o]], writes=[XTb[k4 * 4 + j] for j in range(4)])
        S.barrier()


def xout_stage(C, XT, XTb, out, outb, ident, identb):
    S, nc = C.S, C.nc
    with contextlib.ExitStack() as st:
        xr = [st.enter_context(nc.sbuf_tensor("xo_r%d" % i, [128, TP], F32)) for i in range(2)]
        xrb = [Buf() for _ in range(2)]
        xo = [st.enter_context(nc.sbuf_tensor("xo_o%d" % i, [128, 4, 128], F32)) for i in range(3)]
        xob = [Buf() for _ in range(3)]
        oi = 0
        for kc in range(KC):
            r = kc % 2
            S.dma(C.q(), xr[r][:, :], XT[kc * 128:(kc + 1) * 128, :], reads=[XTb[kc]], writes=[xrb[r]])
            for t4 in range(TP // 512):
                bi = oi % 8
                o = oi % 3
                oi += 1
                for j in range(4):
                    tb = t4 * 4 + j
                    S.op("pe", lambda e, bi=bi, j=j, tb=tb, r=r: e.transpose(C.ps[bi][:, j * 128:(j + 1) * 128],
                                                                            xr[r][:, tb * 128:(tb + 1) * 128], ident[:, :]),
                         reads=[xrb[r], identb], writes=[C.psb[bi]], signal=(j == 3))
                S.op("dve", lambda e, bi=bi, o=o: e.tensor_copy(xo[o][:, :, :], C.ps[bi][:, :]), reads=[C.psb[bi]], writes=[xob[o]])
                dst = out[t4 * 512:(t4 + 1) * 512, kc * 128:(kc + 1) * 128].rearrange("(j p) f -> p j f", p=128)
                S.dma(C.q(), dst, xo[o][:, :, :], reads=[xob[o]], writes=[outb])
        S.barrier()


def dram_in(nc, name, shape, dt=F32):
    return nc.dram_tensor(name, list(shape), dt, kind="ExternalInput").ap()


def dram_scratch(nc, name, shape, dt=F32):
    return nc.dram_tensor(name, list(shape), dt, kind="Internal").ap()


def phase2_decl(nc):
    A = {}
    A["x_own"] = dram_in(nc, "x_own", [TP, D])
    A["ident_d"] = dram_in(nc, "ident", [128, 128])
    A["nmpost"] = dram_in(nc, "norm_mix_post", [2, D])
    A["nmpre"] = dram_in(nc, "norm_mix_pre", [2, D])
    A["nfpre"] = dram_in(nc, "norm_ffn_pre", [2, D])
    A["nfpost"] = dram_in(nc, "norm_ffn_post", [2, D])
    A["Wabo"] = dram_in(nc, "ab_w_out", [D, D])
    A["Wg"] = dram_in(nc, "ffn_w_gate", [2, D, DFF])
    A["Wu"] = dram_in(nc, "ffn_w_up", [2, D, DFF])
    A["Wd"] = dram_in(nc, "ffn_w_down", [2, DFF, D])
    A["Wsi"] = dram_in(nc, "sgu_w_in", [D, 2 * D])
    A["Wso"] = dram_in(nc, "sgu_w_out", [D, D])
    A["lng"] = dram_in(nc, "sgu_ln_g", [D])
    A["lnb"] = dram_in(nc, "sgu_ln_b", [D])
    A["wsT"] = dram_in(nc, "sgu_wsT", [128, 16, 128])
    A["bsb"] = dram_in(nc, "sgu_bs_b", [128, 16, 128])
    A["maskT"] = dram_in(nc, "sgu_maskT", [128, 128])
    A["out"] = nc.dram_tensor("out", [TP, D], F32, kind="ExternalOutput").ap()
    A["XT"] = dram_scratch(nc, "XT", [D, TP])
    A["YT"] = dram_scratch(nc, "YT", [D, TP])
    A["HID"] = dram_scratch(nc, "HID", [DFF, TP], BF16)
    A["UT"] = dram_scratch(nc, "UT", [D, TP], BF16)
    A["VTM"] = dram_scratch(nc, "VTM", [TP, D])
    return A


def build_phase2(stages=("in", "ab", "ffn0", "sgu", "ffn1", "out")):
    nc = bass.Bass("TRN2", target_bir_lowering=False)
    A = phase2_decl(nc)
    A["YC"] = dram_in(nc, "yc", [D, TP], BF16)
    with nc.cleanup_on_exit():
        S = Sched(nc)
        phase2_body(nc, S, A, stages, None)
        S.finish()
    return nc


def about_stage_sel(C, XT, XTb, YT, YTb, G, Gb, sel_d, g_post, Wo, tag):
    S, nc = C.S, C.nc
    with contextlib.ExitStack() as st:
        R = st.enter_context(nc.sbuf_tensor(tag + "R", [128, KC, TP], BF16))
        Rb = Buf()
        sel = st.enter_context(nc.sbuf_tensor(tag + "sel", [128, 4], F32))
        selb = Buf()
        S.dma("sp", sel[:, :], sel_d, writes=[selb])
        c4 = [st.enter_context(nc.sbuf_tensor(tag + "c4%d" % i, [128, 4, TT], BF16)) for i in range(3)]
        c4b = [Buf() for _ in range(3)]
        Gv = G.rearrange("(j i) r t -> i r j t", i=2)
        n = 0
        for kc in range(KC):
            if kc < 16:
                r, lc = kc // 4, kc % 4
            else:
                r, lc = (kc - 16) // 4, 4 + (kc - 16) % 4
            row = r * 1024 + lc * 128
            for hf in range(2):
                i = n % 3
                n += 1
                ts2 = slice(hf * TT, (hf + 1) * TT)
                S.dma(C.q(), c4[i][:, :, :], Gv[hf, row:row + 128, :, :], reads=list(Gb), writes=[c4b[i]])
                S.op("dve", lambda e, i=i, kc=kc, ts2=ts2: e.tensor_scalar(R[:, kc, ts2], c4[i][:, 0, :], sel[:, 0:1], None, ALU.mult),
                     reads=[c4b[i], selb], writes=[Rb])
                for j in range(1, 4):
                    S.op("dve", lambda e, i=i, kc=kc, j=j, ts2=ts2: e.scalar_tensor_tensor(R[:, kc, ts2], c4[i][:, j, :], sel[:, j:j + 1], R[:, kc, ts2],
                                                                                      ALU.mult, ALU.add), reads=[c4b[i], selb, Rb], writes=[Rb])
        gemm_to_dram(C, Wo, D, R, Rb, KC, TP, YT, YTb, 0, None, F32, tag + "g", ysq=C.ysq)
    postnorm_resid(C, YT, YTb, g_post, XT, XTb, tag + "p", ysq=C.ysq, xsq=C.xsq)


def phase2_body(nc, S, A, stages, gathered):
    x_own, ident_d, nmpost, nmpre, nfpre, nfpost, Wabo, Wg, Wu, Wd, Wsi, Wso, lng, lnb, wsT, bsb, maskT, out, XT, YT, HID, UT, VTM = [A[k] for k in (
        "x_own", "ident_d", "nmpost", "nmpre", "nfpre", "nfpost", "Wabo", "Wg", "Wu", "Wd", "Wsi", "Wso", "lng", "lnb", "wsT", "bsb", "maskT",
        "out", "XT", "YT", "HID", "UT", "VTM")]
    XTb = [Buf() for _ in range(KC)]
    YTb = [Buf() for _ in range(KC)]
    HIDb = [Buf() for _ in range(DFF // 128)]
    UTb = [Buf() for _ in range(KC)]
    VTMb = [Buf() for _ in range(TP // 128)]
    YCb = [Buf() for _ in range(KC)]
    outb = Buf()
    if True:
        with contextlib.ExitStack() as st:
            C = Ctx(nc, S, st)
            ident = C.sb("ident_sb", [128, 128], F32)
            identb = Buf()
            S.dma("sp", ident[:, :], ident_d, writes=[identb])
            C.eps_t = C.sb("eps_t2", [128, 1], F32)
            S.op("dve", lambda e: e.memset(C.eps_t[:, :], EPS), writes=[C.cb])
            C.ysq = (C.sb("ysq", [128, TP], F32), Buf())
            C.xsq = (C.sb("xsq", [128, TP], F32), Buf())
            C.xsq = None
            if "in" in stages:
                xin_stage(C, x_own, XT, XTb, ident, identb)
            if "ab" in stages:
                if gathered is None:
                    about_stage(C, XT, XTb, YT, YTb, A["YC"], YCb, nmpost[0], Wabo, "ab")
                else:
                    G, Gb, sel_d = gathered
                    about_stage_sel(C, XT, XTb, YT, YTb, G, Gb, sel_d, nmpost[0], Wabo, "ab")
            if "ffn0" in stages:
                ffn_stage(C, XT, XTb, YT, YTb, HID, HIDb, nfpre[0], nfpost[0], Wg[0], Wu[0], Wd[0], "f0")
            if "sgu" in stages:
                sgu_stage(C, XT, XTb, YT, YTb, UT, UTb, VTM, VTMb, nmpre[1], nmpost[1], Wsi, lng, lnb, wsT, bsb, maskT, Wso, "sg")
            if "ffn1" in stages:
                fuse_out = "out" in stages
                ffn_stage(C, XT, XTb, YT, YTb, HID, HIDb, nfpre[1], nfpost[1], Wg[1], Wu[1], Wd[1], "f1", last=True,
                          final=(out, outb, ident, identb) if fuse_out else None)
            if "out" in stages and "ffn1" not in stages:
                xout_stage(C, XT, XTb, out, outb, ident, identb)
            S.barrier()


def build_fused():
    nc = bass.Bass("TRN2", target_bir_lowering=False)
    A1 = phase1_decl(nc)
    A2 = phase2_decl(nc)
    sel_d = dram_in(nc, "sel", [128, 4])
    NT = SEQ // TT
    YL = [nc.dram_tensor("YL%d" % t, [1024, TT], BF16) for t in range(NT)]
    GG = nc.dram_tensor("YG", [NT, 4 * 1024, TT], BF16)
    A1["YCT"] = lambda t: YL[t].ap()
    with nc.cleanup_on_exit():
        S = Sched(nc)
        ylb = [Buf() for _ in range(NT)]
        Gb = [Buf() for _ in range(NT)]
        A1["after_tile"] = lambda t: S.collective("AllGather", YL[t].ap().opt(), GG.ap()[t].opt(), [[0, 1, 2, 3], [4, 5, 6, 7]],
                                                  reads=[ylb[t]], writes=[Gb[t]])
        phase1_body(nc, S, A1, ylb)
        phase2_body(nc, S, A2, ("in", "ab", "ffn0", "sgu", "ffn1", "out"), (GG.ap(), Gb, sel_d))
        S.finish()
    return nc


def phase2_consts(inputs):
    ws = np.asarray(inputs["sgu_w_s"][0], np.float32)
    wsT = np.ascontiguousarray(ws.transpose(2, 0, 1))
    pos = np.arange(128)
    maskT = ((pos[:, None] // 64) <= (pos[None, :] // 64)).astype(np.float32)
    bs = np.asarray(inputs["sgu_b_s"][0], np.float32)
    bsb = np.ascontiguousarray(np.broadcast_to(bs[None], (128, 16, 128)))
    return {
        "ident": np.eye(128, dtype=np.float32),
        "norm_mix_post": np.asarray(inputs["norm_mix_post"], np.float32),
        "norm_mix_pre": np.asarray(inputs["norm_mix_pre"], np.float32),
        "norm_ffn_pre": np.asarray(inputs["norm_ffn_pre"], np.float32),
        "norm_ffn_post": np.asarray(inputs["norm_ffn_post"], np.float32),
        "ab_w_out": np.asarray(inputs["ab_w_out"][0], np.float32),
        "ffn_w_gate": np.asarray(inputs["ffn_w_gate"], np.float32),
        "ffn_w_up": np.asarray(inputs["ffn_w_up"], np.float32),
        "ffn_w_down": np.asarray(inputs["ffn_w_down"], np.float32),
        "sgu_w_in": np.asarray(inputs["sgu_w_in"][0], np.float32),
        "sgu_w_out": np.asarray(inputs["sgu_w_out"][0], np.float32),
        "sgu_ln_g": np.asarray(inputs["sgu_ln_g"][0], np.float32),
        "sgu_ln_b": np.asarray(inputs["sgu_ln_b"][0], np.float32),
        "sgu_wsT": wsT, "sgu_bs_b": bsb, "sgu_maskT": maskT,
    }


NH = 4
TT = 512
NCOL1 = 2568


def phase1_decl(nc):
    A = {}
    A["xb"] = dram_in(nc, "xb", [SEQ, D])
    A["gpre"] = dram_in(nc, "g_pre", [D])
    A["Wc"] = dram_in(nc, "w_in_c", [D, NCOL1])
    A["cw_d"] = dram_in(nc, "conv_w", [128, 12, 4])
    A["nega_d"] = dram_in(nc, "neg_a", [128, 4])
    A["dtb_d"] = dram_in(nc, "dt_b", [128, 4])
    A["gnw_d"] = dram_in(nc, "gn_w", [128, 128])
    A["pw_d"] = dram_in(nc, "pool_wg", [512, 512])
    A["psc_d"] = dram_in(nc, "pool_sc", [512])
    A["band_d"] = dram_in(nc, "bandm", [3, 128, 128])
    A["mask_d"] = dram_in(nc, "masks", [5, 128, 128])
    return A


def build_phase1(ntiles=SEQ // TT, stop_after=None, dbgk=99):
    nc = bass.Bass("TRN2", target_bir_lowering=False)
    A = phase1_decl(nc)
    yct = nc.dram_tensor("yct", [1024, SEQ], BF16, kind="ExternalOutput").ap()
    A["YCT"] = lambda t: yct[:, t * TT:(t + 1) * TT]
    with nc.cleanup_on_exit():
        S = Sched(nc)
        phase1_body(nc, S, A, [Buf() for _ in range(SEQ // TT)], ntiles, stop_after, dbgk)
        S.finish()
    return nc


def phase1_body(nc, S, A, outb, ntiles=SEQ // TT, stop_after=None, dbgk=99):
    xb, gpre, Wc, cw_d, nega_d, dtb_d, gnw_d, pw_d, psc_d, band_d, mask_d, YCT = [A[k] for k in (
        "xb", "gpre", "Wc", "cw_d", "nega_d", "dtb_d", "gnw_d", "pw_d", "psc_d", "band_d", "mask_d", "YCT")]
    if True:
        with contextlib.ExitStack() as st:
            C = Ctx(nc, S, st, NW=4, pfx="p1_")
            sb = C.sb
            C.eps_t = sb("eps_t", [128, 1], F32)
            S.op("dve", lambda e: e.memset(C.eps_t[:, :], EPS), writes=[C.cb])
            C.one_t = sb("one_t", [128, 1], F32)
            S.op("dve", lambda e: e.memset(C.one_t[:, :], 1.0), writes=[C.cb])
            mk = sb("mk", [128, 5, 128], F32)
            mkb = Buf()
            S.dma("sp", mk[:, :, :], mask_d.rearrange("m p f -> p m f"), writes=[mkb])
            triA, strictU, MS, MU, ident = [mk[:, i, :] for i in range(5)]
            band = sb("band", [128, 3, 128], F32)
            bandb = Buf()
            S.dma("act", band[:, :, :], band_d.rearrange("m p f -> p m f"), writes=[bandb])
            g = sb("g", [128, KC], F32)
            gb = Buf()
            S.dma("sp", g[:, :], gpre.rearrange("(kc p) -> p kc", p=128), writes=[gb], allow_slow_non_contiguous=True)
            cw = sb("cw", [128, 12, 4], F32)
            cwb = Buf()
            S.dma("sp", cw[:, :, :], cw_d, writes=[cwb])
            nega = sb("nega", [128, 4], F32)
            dtb = sb("dtb", [128, 4], F32)
            gnw = sb("gnw", [128, 128], F32)
            psc = sb("psc", [128, 4], F32)
            smb = SBuf()
            S.dma("sp", nega[:, :], nega_d, writes=[smb])
            S.dma("sp", dtb[:, :], dtb_d, writes=[smb])
            S.dma("sp", gnw[:, :], gnw_d, writes=[smb])
            S.dma("sp", psc[:, :], psc_d.rearrange("(c p) -> p c", p=128), writes=[smb], allow_slow_non_contiguous=True)
            S.op("act", lambda e: e.activation(nega[:, :], nega[:, :], AF.Exp), reads=[smb], writes=[smb])
            S.op("dve", lambda e: e.tensor_scalar(nega[:, :], nega[:, :], -1.0, None, ALU.mult), reads=[smb], writes=[smb])
            pw = sb("pw", [128, 4, 512], BF16)
            pwb = Buf()
            S.dma("pool", pw[:, :, :], pw_d.rearrange("(c p) n -> p c n", p=128), writes=[pwb])
            wbd = sb("wbd", [128, KC, 8], BF16)
            wbdb = Buf()
            S.dma("pool", wbd[:, :, :], Wc.rearrange("(kc p) n -> p kc n", p=128)[:, :, 2560:2568], writes=[wbdb],
                  allow_slow_non_contiguous=True)
            xr = sb("xr", [128, D], F32)
            xrb = Buf()
            HT = sb("HT", [128, KC, TT], BF16)
            HTb = Buf()
            st8 = sb("st8", [128, 12], F32)
            st8b = SBuf()
            halo = sb("halo", [128, 12, 4], F32)
            halob = Buf()
            S.op("dve", lambda e: e.memset(halo[:, :, :], 0.0), writes=[halob])
            Z = [sb("Z%d" % i, [128, 3 + TT], F32) for i in range(4)]
            Zb = [Buf() for _ in range(4)]
            junkA = sb("junkA", [128, TT], BF16)
            junkAb = Buf()
            QT = sb("QT", [128, NH, TT], BF16)
            KT = sb("KT", [128, NH, TT], BF16)
            KTf = sb("KTf", [128, NH, TT], F32)
            VT = sb("VT", [128, NH, TT], F32)
            QTb, KTb, KTfb, VTb = [[Buf() for _ in range(NH)] for _ in range(4)]
            sg2 = [sb("sg%d" % i, [128, 4, 512], F32) for i in range(2)]
            sgb2 = [[Buf() for _ in range(4)] for _ in range(2)]
            xa = sb("xa", [128, 5, 512], F32)
            xab = [Buf() for _ in range(5)]
            lg82 = [sb("lg8%d" % i, [128, 4, 8], F32) for i in range(2)]
            lg8b2 = [[SBuf() for _ in range(4)] for _ in range(2)]
            OT = sb("OT", [128, 8, TT], BF16)
            OTb = Buf()
            PLT = sb("PLT", [128, 4, TT], BF16)
            PLTb = Buf()
            Sst = sb("Sst", [128, NH, 128], F32)
            Sbf = sb("Sbf", [128, NH, 128], BF16)
            Sstb = [Buf() for _ in range(NH)]
            Sbfb = [Buf() for _ in range(NH)]
            for h in range(NH):
                S.op("dve", lambda e, h=h: e.memset(Sst[:, h, :], 0.0), writes=[Sstb[h]])
                S.op("dve", lambda e, h=h: e.memset(Sbf[:, h, :], 0.0), writes=[Sbfb[h]])

            def ht(name, n=128, dt=F32, strict=False):
                t = sb(name, [128, NH, n], dt)
                return t, [Buf(name, strict) for _ in range(NH)]
            b4, b4b = ht("b4", 8, strict=True)
            G2, G2b = ht("G2")
            gB, gBb = ht("gB")
            EX, EXb = ht("EX", 392, strict=True)
            Es, Esb = ht("Es")
            ETc, ETcb = ht("ETc")
            ngb, ngbb = ht("ngb", 2, strict=True)
            Lm, Lb = ht("L")
            AT, ATb = ht("AT")
            kd, kdb = ht("kd")
            vb, vbb = ht("vb")
            Xa, Xab_ = ht("Xa")
            Xat, Xatb = ht("Xat")
            Xb, Xbb = ht("Xb")
            Xbt, Xbtb = ht("Xbt")
            Pm, Pmb = ht("Pm")
            A2T, A2Tb = ht("A2T", 128, BF16)
            K2, K2b = ht("K2", 128, BF16)
            qdT, qdTb = ht("qdT", 128, BF16)
            Rbf, Rbfb = ht("Rbf", 128, BF16)
            gsg, gsgb = gB, gBb
            yo, yob = G2, G2b
            flat = lambda t: t[:, :, :].rearrange("p h n -> p (h n)")
            accv = [flat(t) for t in (G2, gB, Es, ETc)]
            accB = [G2b, gBb, Esb, ETcb]
            slv = [flat(t) for t in (Lm, AT, kd, vb)]
            slB = [Lb, ATb, kdb, vbb]
            ps, psb = C.ps, C.psb
            Wv_ = Wc.rearrange("(kc p) n -> p kc n", p=128)

            def emit_A(tile):
                t0 = tile * TT
                for tb in range(4):
                    r0 = t0 + tb * 128
                    S.dma("sp", xr[:, 0:D // 2], xb[r0:r0 + 128, 0:D // 2], writes=[xrb])
                    S.dma("act", xr[:, D // 2:D], xb[r0:r0 + 128, D // 2:D], writes=[xrb])
                    for q8 in range(8):
                        S.op("act", lambda e, q8=q8: e.activation(junkA[:, :], xr[:, q8 * 512:(q8 + 1) * 512], AF.Square,
                                                                  accum_out=st8[:, q8:q8 + 1]), reads=[xrb], writes=[junkAb, st8b])
                    S.op("dve", lambda e: e.tensor_reduce(st8[:, 8:9], st8[:, 0:8], AX.X, ALU.add), reads=[st8b], writes=[st8b])
                    S.op("act", lambda e: e.activation(st8[:, 9:10], st8[:, 8:9], AF.Sqrt, bias=C.eps_t[:, 0:1], scale=1.0 / D),
                         reads=[st8b, C.cb], writes=[st8b])
                    S.op("dve", lambda e: e.reciprocal(st8[:, 10:11], st8[:, 9:10]), reads=[st8b], writes=[st8b])
                    S.op("act", lambda e: e.activation(xr[:, :], xr[:, :], AF.Copy, scale=st8[:, 10:11]), reads=[xrb, st8b], writes=[xrb])
                    for k4 in range(KC // 4):
                        bi = 4 + k4 % 4
                        for j in range(4):
                            kc = k4 * 4 + j
                            S.op("pe", lambda e, bi=bi, j=j, kc=kc: e.transpose(ps[bi][:, j * 128:(j + 1) * 128],
                                                                               xr[:, kc * 128:(kc + 1) * 128], ident),
                                 reads=[xrb, mkb], writes=[psb[bi]], signal=(j == 3))
                        for j in range(4):
                            kc = k4 * 4 + j
                            S.op("dve" if j % 2 == 0 else "act",
                                 (lambda e, bi=bi, j=j, kc=kc, tb=tb: e.tensor_scalar(HT[:, kc, tb * 128:(tb + 1) * 128], ps[bi][:, j * 128:(j + 1) * 128],
                                                                                     g[:, kc:kc + 1], None, ALU.mult)) if j % 2 == 0 else
                                 (lambda e, bi=bi, j=j, kc=kc, tb=tb: e.activation(HT[:, kc, tb * 128:(tb + 1) * 128], ps[bi][:, j * 128:(j + 1) * 128],
                                                                                  AF.Copy, scale=g[:, kc:kc + 1])),
                                 reads=[psb[bi], gb], writes=[HTb])
            def emit_B1(tile):
                t0 = tile * TT
                sg, sgb, lg8, lg8b = sg2[tile % 2], sgb2[tile % 2], lg82[tile % 2], lg8b2[tile % 2]
                for cgi in range(2):
                    slots = []
                    for u in range(4):
                        wt, wb = C.wslot()
                        wv = wt[:, :].rearrange("p (k n) -> p k n", n=512)
                        S.dma("pool", wv, Wv_[:, u * 8:(u + 1) * 8, cgi * 512:(cgi + 1) * 512], writes=[wb])
                        slots.append((wv, wb))
                    for tb in range(4):
                        bi = 4 + tb
                        for kc in range(KC):
                            wv, wb = slots[kc // 8]
                            S.op("pe", lambda e, bi=bi, wv=wv, kc=kc, tb=tb: e.matmul(
                                ps[bi][:, :], HT[:, kc, tb * 128:(tb + 1) * 128], wv[:, kc % 8, :], start=(kc == 0), stop=(kc == KC - 1)),
                                reads=[wb, HTb], writes=[psb[bi]], signal=(kc % 8 == 7))
                        if cgi == 0:
                            S.op("act", lambda e, bi=bi, tb=tb: e.activation(xa[:, tb + 1, :], ps[bi][:, :], AF.Copy),
                                 reads=[psb[bi]], writes=[xab[tb + 1]])
                        else:
                            S.op("act", lambda e, sg=sg, lg8=lg8, bi=bi, tb=tb: e.activation(sg[:, tb, :], ps[bi][:, :], AF.Silu),
                                 reads=[psb[bi]], writes=[sgb[tb]])
                for tb in range(4):
                    bi = 4 + tb
                    for kc in range(KC):
                        S.op("pe", lambda e, bi=bi, kc=kc, tb=tb: e.matmul(ps[bi][:, 0:8], HT[:, kc, tb * 128:(tb + 1) * 128], wbd[:, kc, :],
                                                                          start=(kc == 0), stop=(kc == KC - 1)),
                             reads=[wbdb, HTb], writes=[psb[bi]], signal=(kc == KC - 1))
                    S.op("dve", lambda e, sg=sg, lg8=lg8, bi=bi, tb=tb: e.tensor_copy(lg8[:, tb, :], ps[bi][:, 0:8]), reads=[psb[bi]], writes=[lg8b[tb]])
            def emit_B2E(tile):
                t0 = tile * TT
                bank_of = lambda grp: [4, 5, 6, 7] if grp % 2 == 0 else [0, 1, 2, 3]
                gemm_cg(C, Wc, 1024, 512, HT, HTb, KC, TT, bank_of(0))
                for grp in range(3):
                    banks = bank_of(grp)
                    if grp + 1 < 3:
                        gemm_cg(C, Wc, 1024 + (grp + 1) * 512, 512, HT, HTb, KC, TT, bank_of(grp + 1))
                    lists = []
                    for h in range(NH):
                        S.record()
                        c = grp * 4 + h
                        zi = h
                        bi = banks[h]
                        S.op("dve", lambda e, zi=zi, c=c: e.tensor_copy(Z[zi][:, 0:3], halo[:, c, 0:3]), reads=[halob], writes=[Zb[zi]])
                        S.op("act", lambda e, zi=zi, bi=bi: e.activation(Z[zi][:, 3:3 + TT], ps[bi][:, :], AF.Copy),
                             reads=[psb[bi]], writes=[Zb[zi]])
                        S.op("dve", lambda e, zi=zi, c=c: e.tensor_copy(halo[:, c, 0:3], Z[zi][:, TT:TT + 3]), reads=[Zb[zi]], writes=[halob])
                        S.op("dve", lambda e, zi=zi, c=c: e.tensor_scalar(accv[zi], Z[zi][:, 3:3 + TT], cw[:, c, 3:4], None, ALU.mult),
                             reads=[Zb[zi], cwb], writes=[*accB[zi]])
                        for j in range(3):
                            S.op("dve", lambda e, zi=zi, c=c, j=j: e.scalar_tensor_tensor(accv[zi], Z[zi][:, j:j + TT], cw[:, c, j:j + 1],
                                                                                         accv[zi], ALU.mult, ALU.add),
                                 reads=[Zb[zi], cwb, *accB[zi]], writes=[*accB[zi]])
                        if grp == 2:
                            S.op("act", lambda e, zi=zi, h=h: e.activation(VT[:, h, :], accv[zi], AF.Silu), reads=[*accB[zi]], writes=[VTb[h]])
                            lists.append(S.stop())
                            continue
                        S.op("act", lambda e, zi=zi: e.activation(slv[zi], accv[zi], AF.Silu), reads=[*accB[zi]], writes=[*slB[zi]])
                        S.op("act", lambda e, zi=zi: e.activation(accv[zi], slv[zi], AF.Square),
                             reads=[*slB[zi]], writes=[*accB[zi]])
                        pb = banks[h]
                        S.op("pe", lambda e, pb=pb, zi=zi: e.matmul(ps[pb][:, :], C.ones_f[:, :], accv[zi], start=True, stop=True),
                             reads=[*accB[zi], C.cb], writes=[psb[pb]])
                        S.op("act", lambda e, pb=pb, zi=zi: e.activation(accv[zi], ps[pb][:, :], AF.Sqrt, bias=C.eps_t[:, 0:1], scale=1.0),
                             reads=[psb[pb], C.cb], writes=[*accB[zi]])
                        S.op("dve", lambda e, zi=zi: e.reciprocal(accv[zi], accv[zi]), reads=[*accB[zi]], writes=[*accB[zi]])
                        if grp == 0:
                            S.op("dve", lambda e, zi=zi, h=h: e.scalar_tensor_tensor(QT[:, h, :], slv[zi], 128.0 ** -0.5, accv[zi],
                                                                                    ALU.mult, ALU.mult), reads=[*slB[zi], *accB[zi]], writes=[QTb[h]])
                        else:
                            S.op("dve", lambda e, zi=zi, h=h: e.tensor_tensor(KTf[:, h, :], slv[zi], accv[zi], ALU.mult),
                                 reads=[*slB[zi], *accB[zi]], writes=[KTfb[h]])
                            S.op("pool", lambda e, h=h: e.tensor_copy(KT[:, h, :], KTf[:, h, :]), reads=[KTfb[h]], writes=[KTb[h]])
                        lists.append(S.stop())
                    S.replay(lists)
                for tb in range(4):
                    first = (tile == 0 and tb == 0)
                    for c in range(4):
                        bi = 4 + c
                        if first:
                            S.op("pe", lambda e, bi=bi, c=c, tb=tb: e.matmul(ps[bi][:, 0:128], xa[:, tb + 1, c * 128:(c + 1) * 128], band[:, 0, :],
                                                                            start=True, stop=True), reads=[xab[tb + 1], bandb], writes=[psb[bi]])
                        else:
                            S.op("pe", lambda e, bi=bi, c=c, tb=tb: e.matmul(ps[bi][:, 0:128], xa[:, tb, c * 128:(c + 1) * 128], band[:, 2, :],
                                                                            start=True, stop=False), reads=[xab[tb], bandb], writes=[psb[bi]], signal=False)
                            S.op("pe", lambda e, bi=bi, c=c, tb=tb: e.matmul(ps[bi][:, 0:128], xa[:, tb + 1, c * 128:(c + 1) * 128], band[:, 1, :],
                                                                            start=False, stop=True), reads=[xab[tb + 1], bandb], writes=[psb[bi]])
                        S.op("act", lambda e, bi=bi, c=c, tb=tb: e.activation(PLT[:, c, tb * 128:(tb + 1) * 128], ps[bi][:, 0:128], AF.Copy),
                             reads=[psb[bi]], writes=[PLTb])
                S.op("pool", lambda e: e.tensor_copy(xa[:, 0, :], xa[:, 4, :]), reads=[xab[4]], writes=[xab[0]])
                for dc in range(4):
                    bi = dc
                    for c in range(4):
                        S.op("pe", lambda e, bi=bi, c=c, dc=dc: e.matmul(ps[bi][:, :], pw[:, c, dc * 128:(dc + 1) * 128], PLT[:, c, :],
                                                                        start=(c == 0), stop=(c == 3)), reads=[pwb, PLTb], writes=[psb[bi]], signal=(c == 3))
                    S.op("act", lambda e, bi=bi, dc=dc: e.activation(OT[:, dc, :], ps[bi][:, :], AF.Copy, scale=psc[:, dc:dc + 1]),
                         reads=[psb[bi], smb], writes=[OTb])
            def emit_D(tile):
                t0 = tile * TT
                sg, sgb, lg8, lg8b = sg2[tile % 2], sgb2[tile % 2], lg82[tile % 2], lg8b2[tile % 2]
                H = range(NH)
                for tb in range(4):
                    ts_ = slice(tb * 128, (tb + 1) * 128)
                    S.op("act", lambda e, sg=sg, lg8=lg8, tb=tb: e.activation(b4[:, 0, 0:4], lg8[:, tb, 0:4], AF.Exp, scale=-1.0), reads=[lg8b[tb]], writes=[b4b[0]])
                    S.op("dve", lambda e: e.tensor_scalar(b4[:, 0, 0:4], b4[:, 0, 0:4], 1.0, None, ALU.add), reads=[b4b[0]], writes=[b4b[0]])
                    S.op("dve", lambda e: e.reciprocal(b4[:, 0, 0:4], b4[:, 0, 0:4]), reads=[b4b[0]], writes=[b4b[0]])
                    S.op("dve", lambda e, sg=sg, lg8=lg8, tb=tb: e.tensor_tensor(b4[:, 1, 0:4], lg8[:, tb, 4:8], dtb[:, :], ALU.add), reads=[lg8b[tb], smb], writes=[b4b[0]])
                    S.op("act", lambda e: e.activation(b4[:, 1, 0:4], b4[:, 1, 0:4], AF.Exp), reads=[b4b[0]], writes=[b4b[0]])
                    S.op("act", lambda e: e.activation(b4[:, 1, 0:4], b4[:, 1, 0:4], AF.Ln, bias=C.one_t[:, 0:1], scale=1.0), reads=[b4b[0], C.cb], writes=[b4b[0]])
                    S.op("dve", lambda e: e.tensor_tensor(b4[:, 0, 4:8], b4[:, 1, 0:4], nega[:, :], ALU.mult), reads=[b4b[0], smb], writes=[b4b[0]])
                    beta = lambda h: b4[:, 0, h:h + 1]
                    gcol = lambda h: b4[:, 0, 4 + h:5 + h]
                    gcol2 = lambda h: b4[:, 0, 4 + h:6 + h] if h < 3 else b4[:, 0, 6:8]
                    bb = b4b[0]
                    for h in H:
                        S.op("dve", lambda e, h=h: e.tensor_scalar(G2[:, h, :], strictU, gcol(h), None, ALU.mult), reads=[bb, mkb], writes=[G2b[h]])
                        S.op("dve", lambda e, h=h: e.tensor_scalar(gB[:, h, :], C.ones_f[:, :], gcol(h), None, ALU.mult), reads=[bb, C.cb], writes=[gBb[h]])
                    for h in H:
                        bi = h
                        S.op("pe", lambda e, h=h, bi=bi: e.matmul(ps[bi][:, 0:128], triA, G2[:, h, :], start=True, stop=True),
                             reads=[G2b[h], mkb], writes=[psb[bi]], signal=False)
                        S.op("pe", lambda e, h=h, bi=bi: e.matmul(ps[bi][:, 128:256], G2[:, h, :], triA, start=True, stop=True),
                             reads=[G2b[h], mkb], writes=[psb[bi]], signal=False)
                        S.op("pe", lambda e, h=h, bi=bi: e.matmul(ps[bi][:, 256:384], gB[:, h, :], triA, start=True, stop=True),
                             reads=[gBb[h], mkb], writes=[psb[bi]], signal=False)
                        gsrc = (lambda h: b4[:, 0, 4 + h:6 + h]) if True else None
                        hh = min(h, 2)
                        off = h - hh
                        S.op("pe", lambda e, hh=hh, bi=bi: e.matmul(ps[bi][:, 384:386], triA, b4[:, 0, 4 + hh:6 + hh], start=True, stop=True),
                             reads=[bb, mkb], writes=[psb[bi]], signal=False)
                        S.op("pe", lambda e, hh=hh, bi=bi: e.matmul(ps[bi][:, 386:388], strictU, b4[:, 0, 4 + hh:6 + hh], start=True, stop=True),
                             reads=[bb, mkb], writes=[psb[bi]], signal=False)
                        S.op("pe", lambda e, hh=hh, bi=bi: e.matmul(ps[bi][:, 388:390], C.ones_f[:, :], b4[:, 0, 4 + hh:6 + hh], start=True, stop=True),
                             reads=[bb, C.cb], writes=[psb[bi]])
                        S.op("act", lambda e, h=h, bi=bi: e.activation(EX[:, h, 0:390], ps[bi][:, 0:390], AF.Exp), reads=[psb[bi]], writes=[EXb[h]])
                    gam = lambda h: EX[:, h, 384 + (h - min(h, 2)):385 + (h - min(h, 2))]
                    kds = lambda h: EX[:, h, 386 + (h - min(h, 2)):387 + (h - min(h, 2))]
                    gl_ = lambda h: EX[:, h, 388 + (h - min(h, 2)):389 + (h - min(h, 2))]
                    for h in H:
                        S.op("dve", lambda e, h=h: e.tensor_tensor(Es[:, h, :], EX[:, h, 0:128], MS, ALU.mult), reads=[EXb[h], mkb], writes=[Esb[h]])
                        S.op("dve", lambda e, h=h: e.tensor_tensor(ETc[:, h, :], EX[:, h, 128:256], MU, ALU.mult), reads=[EXb[h], mkb], writes=[ETcb[h]])
                        S.op("dve", lambda e, h=h: e.scalar_tensor_tensor(ngb[:, h, 0:1], gam(h), -1.0, beta(h), ALU.mult, ALU.mult),
                             reads=[EXb[h], bb], writes=[ngbb[h]])
                    for h in H:
                        bi = h
                        S.op("pe", lambda e, ts_=ts_, h=h, bi=bi: e.matmul(ps[bi][:, 0:128], KT[:, h, ts_], KT[:, h, ts_], start=True, stop=True),
                             reads=[KTb[h]], writes=[psb[bi]], signal=False)
                        S.op("pe", lambda e, ts_=ts_, h=h, bi=bi: e.matmul(ps[bi][:, 128:256], KT[:, h, ts_], QT[:, h, ts_], start=True, stop=True),
                             reads=[KTb[h], QTb[h]], writes=[psb[bi]], signal=False)
                        S.op("pe", lambda e, ts_=ts_, h=h, bi=bi: e.transpose(ps[bi][:, 256:384], KTf[:, h, ts_], ident), reads=[KTfb[h], mkb], writes=[psb[bi]], signal=False)
                        S.op("pe", lambda e, ts_=ts_, h=h, bi=bi: e.transpose(ps[bi][:, 384:512], VT[:, h, ts_], ident), reads=[VTb[h], mkb], writes=[psb[bi]])
                    for h in H:
                        bi = h
                        if dbgk > 0:
                            S.op("dve", lambda e, h=h, bi=bi: e.scalar_tensor_tensor(Lm[:, h, :], ps[bi][:, 0:128], beta(h), Es[:, h, :], ALU.mult, ALU.mult),
                                 reads=[psb[bi], bb, Esb[h]], writes=[Lb[h]])
                        if dbgk > 1:
                            S.op("dve", lambda e, h=h, bi=bi: e.tensor_tensor(AT[:, h, :], ps[bi][:, 128:256], ETc[:, h, :], ALU.mult),
                                 reads=[psb[bi], ETcb[h]], writes=[ATb[h]])
                        if dbgk > 2:
                            S.op("dve", lambda e, h=h, bi=bi: e.tensor_scalar(kd[:, h, :], ps[bi][:, 256:384], kds(h), None, ALU.mult),
                                 reads=[psb[bi], EXb[h]], writes=[kdb[h]])
                        if dbgk > 3:
                            S.op("dve", lambda e, h=h, bi=bi: e.tensor_scalar(vb[:, h, :], ps[bi][:, 384:512], beta(h), None, ALU.mult),
                                 reads=[psb[bi], bb], writes=[vbb[h]])
                        if dbgk > 4:
                            S.op("dve", lambda e, ts_=ts_, h=h: e.tensor_tensor(qdT[:, h, :], QT[:, h, ts_], EX[:, h, 256:384], ALU.mult),
                                 reads=[QTb[h], EXb[h]], writes=[qdTb[h]])
                        if dbgk > 5:
                            S.op("dve", lambda e, sg=sg, lg8=lg8, h=h, tb=tb: e.tensor_tensor(gsg[:, h, :], sg[:, tb, h * 128:(h + 1) * 128], gnw[:, :], ALU.mult),
                                 reads=[sgb[tb], smb], writes=[gsgb[h]])
                    for h in H:
                        bi = h
                        S.op("pe", lambda e, h=h, bi=bi: e.transpose(ps[bi][:, 0:128], Lm[:, h, :], ident), reads=[Lb[h], mkb], writes=[psb[bi]])
                        S.op("dve", lambda e, h=h, bi=bi: e.tensor_copy(Xat[:, h, :], ps[bi][:, 0:128]), reads=[psb[bi]], writes=[Xatb[h]])
                        S.op("dve", lambda e, h=h: e.scalar_tensor_tensor(Pm[:, h, :], Lm[:, h, :], -1.0, ident, ALU.mult, ALU.add),
                             reads=[Lb[h], mkb], writes=[Pmb[h]])
                    cur = (Lm, Lb, Xat, Xatb)
                    nxt = [(Xb, Xbb, Xbt, Xbtb), (Xa, Xab_, Xat, Xatb)]
                    for s_ in range(7):
                        X, Xbuf, XT_, XTbuf = cur
                        N_, Nb, NT, NTb = nxt[s_ % 2]
                        for h in H:
                            bi = h
                            if s_ < 5:
                                S.op("pe", lambda e, h=h, bi=bi, X=X, XT_=XT_: e.matmul(ps[bi][:, 0:128], XT_[:, h, :], X[:, h, :], start=True, stop=True),
                                     reads=[Xbuf[h], XTbuf[h]], writes=[psb[bi]], signal=False)
                            if s_ <= 5:
                                S.op("pe", lambda e, h=h, bi=bi, X=X, XT_=XT_: e.matmul(ps[bi][:, 128:256], X[:, h, :], XT_[:, h, :], start=True, stop=True),
                                     reads=[Xbuf[h], XTbuf[h]], writes=[psb[bi]], signal=(s_ == 0))
                            if s_ >= 1:
                                S.op("pe", lambda e, h=h, bi=bi, XT_=XT_: e.matmul(ps[bi][:, 256:384], XT_[:, h, :], Pm[:, h, :], start=True, stop=True),
                                     reads=[XTbuf[h], Pmb[h]], writes=[psb[bi]])
                        for h in H:
                            bi = h
                            if s_ < 5:
                                S.op("dve", lambda e, h=h, bi=bi, N_=N_: e.tensor_copy(N_[:, h, :], ps[bi][:, 0:128]), reads=[psb[bi]], writes=[Nb[h]])
                            if s_ <= 5:
                                S.op("dve", lambda e, h=h, bi=bi, NT=NT: e.tensor_copy(NT[:, h, :], ps[bi][:, 128:256]), reads=[psb[bi]], writes=[NTb[h]])
                            if s_ >= 1:
                                S.op("dve", lambda e, h=h, bi=bi: e.tensor_tensor(Pm[:, h, :], Pm[:, h, :], ps[bi][:, 256:384], ALU.add),
                                     reads=[psb[bi], Pmb[h]], writes=[Pmb[h]])
                        cur = (N_, Nb, NT, NTb)
                    for h in H:
                        bi = h
                        S.op("pe", lambda e, h=h, bi=bi: e.matmul(ps[bi][:, 0:128], Pm[:, h, :], AT[:, h, :], start=True, stop=True),
                             reads=[Pmb[h], ATb[h]], writes=[psb[bi]], signal=False)
                        S.op("pe", lambda e, h=h, bi=bi: e.matmul(ps[bi][:, 128:256], Pm[:, h, :], kd[:, h, :], start=True, stop=True),
                             reads=[Pmb[h], kdb[h]], writes=[psb[bi]])
                    for h in H:
                        bi = h
                        S.op("dve", lambda e, h=h, bi=bi: e.tensor_copy(A2T[:, h, :], ps[bi][:, 0:128]), reads=[psb[bi]], writes=[A2Tb[h]])
                        S.op("dve", lambda e, h=h, bi=bi: e.tensor_copy(K2[:, h, :], ps[bi][:, 128:256]), reads=[psb[bi]], writes=[K2b[h]])
                    for h in H:
                        bi = h
                        S.op("pe", lambda e, ts_=ts_, h=h, bi=bi: e.matmul(ps[bi][:, 0:128], KT[:, h, ts_], Sbf[:, h, :], start=True, stop=True),
                             reads=[KTb[h], Sbfb[h]], writes=[psb[bi]])
                    for h in H:
                        bi = h
                        S.op("dve", lambda e, h=h, bi=bi: e.scalar_tensor_tensor(Rbf[:, h, :], ps[bi][:, 0:128], ngb[:, h, 0:1], vb[:, h, :], ALU.mult, ALU.add),
                             reads=[psb[bi], ngbb[h], vbb[h]], writes=[Rbfb[h]])
                    for h in H:
                        bi = h
                        S.op("pe", lambda e, h=h, bi=bi: e.matmul(ps[bi][:, 128:256], qdT[:, h, :], Sbf[:, h, :], start=True, stop=False),
                             reads=[qdTb[h], Sbfb[h]], writes=[psb[bi]], signal=False)
                        S.op("pe", lambda e, h=h, bi=bi: e.matmul(ps[bi][:, 128:256], A2T[:, h, :], Rbf[:, h, :], start=False, stop=True),
                             reads=[A2Tb[h], Rbfb[h]], writes=[psb[bi]], signal=False)
                        S.op("pe", lambda e, h=h, bi=bi: e.matmul(ps[bi][:, 256:384], K2[:, h, :], Rbf[:, h, :], start=True, stop=True),
                             reads=[K2b[h], Rbfb[h]], writes=[psb[bi]])
                    for h in H:
                        bi = h
                        S.op("dve", lambda e, h=h, bi=bi: e.scalar_tensor_tensor(Sst[:, h, :], Sst[:, h, :], gl_(h), ps[bi][:, 256:384], ALU.mult, ALU.add),
                             reads=[psb[bi], EXb[h], Sstb[h]], writes=[Sstb[h]])
                        S.op("dve", lambda e, h=h: e.tensor_copy(Sbf[:, h, :], Sst[:, h, :]), reads=[Sstb[h]], writes=[Sbfb[h]])
                        S.op("dve", lambda e, h=h, bi=bi: e.tensor_copy(Xa[:, h, :], ps[bi][:, 128:256]), reads=[psb[bi]], writes=[Xab_[h]])
                        S.op("act", lambda e, h=h: e.activation(yo[:, h, :], Xa[:, h, :], AF.Square, accum_out=ngb[:, h, 1:2]),
                             reads=[Xab_[h]], writes=[yob[h], ngbb[h]])
                        S.op("act", lambda e, h=h: e.activation(ngb[:, h, 1:2], ngb[:, h, 1:2], AF.Sqrt, bias=C.eps_t[:, 0:1], scale=1.0 / 128),
                             reads=[ngbb[h], C.cb], writes=[ngbb[h]])
                        S.op("dve", lambda e, h=h: e.reciprocal(ngb[:, h, 1:2], ngb[:, h, 1:2]), reads=[ngbb[h]], writes=[ngbb[h]])
                        S.op("dve", lambda e, sg=sg, lg8=lg8, h=h, bi=bi: e.scalar_tensor_tensor(yo[:, h, :], Xa[:, h, :], ngb[:, h, 1:2], gsg[:, h, :], ALU.mult, ALU.mult),
                             reads=[Xab_[h], ngbb[h], gsgb[h]], writes=[yob[h]])
                    for h in H:
                        bi = h
                        S.op("pe", lambda e, h=h, bi=bi: e.transpose(ps[bi][:, 0:128], yo[:, h, :], ident), reads=[yob[h], mkb], writes=[psb[bi]])
                        S.op("dve", lambda e, ts_=ts_, h=h, bi=bi: e.tensor_copy(OT[:, 4 + h, ts_], ps[bi][:, 0:128]), reads=[psb[bi]], writes=[OTb])
                S.dma("sp", YCT(tile).rearrange("(c p) t -> p c t", p=128), OT[:, :, :], reads=[OTb], writes=[outb[tile]])
                if A.get("after_tile") is not None:
                    A["after_tile"](tile)

            emit_A(0)
            emit_B1(0)
            for tile in range(ntiles):
                emit_B2E(tile)
                if tile + 1 < ntiles:
                    S.record()
                    emit_D(tile)
                    lD = S.stop()
                    S.record()
                    emit_A(tile + 1)
                    emit_B1(tile + 1)
                    lA = S.stop()
                    S.replay([lD, lA])
                else:
                    emit_D(tile)
            print('phase1 sbuf', nc.sbuf_base, nc.sbuf_top)
            S.barrier()


POOL_WINDOWS = (2, 4, 8, 16)


def phase1_inputs(inputs, b, hg):
    w = np.asarray(inputs["ab_w_in"][0], np.float32)
    PW, GW = 2048, 2048
    cols = np.concatenate([
        np.arange(hg * 512, (hg + 1) * 512),
        PW + 3 * GW + np.arange(hg * 512, (hg + 1) * 512),
        PW + np.arange(hg * 512, (hg + 1) * 512),
        PW + GW + np.arange(hg * 512, (hg + 1) * 512),
        PW + 2 * GW + np.arange(hg * 512, (hg + 1) * 512),
        PW + 4 * GW + np.arange(hg * 4, (hg + 1) * 4),
        PW + 4 * GW + 16 + np.arange(hg * 4, (hg + 1) * 4),
    ])
    wc = np.ascontiguousarray(w[:, cols])
    conv = np.asarray(inputs["gdn_conv"][0], np.float32)
    cwl = np.stack([conv[:, s * GW + hg * 512: s * GW + (hg + 1) * 512] for s in range(3)], 0)
    cwl = cwl.reshape(3, 4, 4, 128).transpose(3, 0, 2, 1).reshape(128, 12, 4)
    bc = lambda v: np.ascontiguousarray(np.broadcast_to(np.asarray(v, np.float32)[None, :], (128, len(v))))
    win = POOL_WINDOWS[hg]
    pos = np.arange(256)
    def band_full(first):
        B = np.zeros((256, 128), np.float32)
        for t in range(128):
            cnt = min(t + 1, win) if first else win
            for s in range(max(0, 128 + t - win + 1) if not first else 128 + max(0, t - win + 1), 128 + t + 1):
                B[s, t] = 1.0 / cnt
            B[128 + t, t] -= 1.0
        return B
    Bn = band_full(False)
    B0 = band_full(True)
    band = np.stack([B0[128:], Bn[128:], Bn[:128]], 0)
    k = np.arange(128)
    triA = (k[:, None] <= k[None, :]).astype(np.float32)
    strictU = (k[:, None] > k[None, :]).astype(np.float32)
    MS = (k[:, None] > k[None, :]).astype(np.float32)
    MU = (k[:, None] <= k[None, :]).astype(np.float32)
    masks = np.stack([triA, strictU, MS, MU, np.eye(128, dtype=np.float32)], 0)
    return {
        "xb": np.ascontiguousarray(np.asarray(inputs["x"][b], np.float32)),
        "g_pre": np.asarray(inputs["norm_mix_pre"][0], np.float32),
        "w_in_c": wc,
        "conv_w": np.ascontiguousarray(cwl),
        "neg_a": bc(inputs["gdn_a_log"][0][hg * 4:(hg + 1) * 4]),
        "dt_b": bc(inputs["gdn_dt_bias"][0][hg * 4:(hg + 1) * 4]),
        "gn_w": bc(inputs["gdn_norm"][0]),
        "pool_wg": np.ascontiguousarray(np.asarray(inputs["pool_w"][0][hg], np.float32)),
        "pool_sc": np.ascontiguousarray(np.asarray(inputs["pool_scale"][0][hg * 512:(hg + 1) * 512], np.float32)),
        "bandm": np.ascontiguousarray(band), "masks": np.ascontiguousarray(masks),
    }


def kernel(**inputs):
    n = 8
    nc = build_fused()
    consts = phase2_consts(inputs)
    x = np.asarray(inputs["x"], np.float32)
    maps = []
    for c in range(n):
        b, j = c // 4, c % 4
        m = dict(consts)
        m.update(phase1_inputs(inputs, b, j))
        m["x_own"] = np.ascontiguousarray(x[b, j * TP:(j + 1) * TP, :])
        sel = np.zeros((128, 4), np.float32)
        sel[:, j] = 1.0
        m["sel"] = sel
        maps.append(m)
    res = run_bass_kernel_spmd(nc, maps, core_ids=list(range(n)))
    out = np.empty((2, SEQ, D), np.float32)
    for c in range(n):
        b, j = c // 4, c % 4
        out[b, j * TP:(j + 1) * TP, :] = np.asarray(res.results[c]["out"], np.float32)
    return out
```

```python
import contextlib
import numpy as np
import ml_dtypes
import concourse.bass as bass
import concourse.mybir as mybir
from concourse.bass_utils import run_bass_kernel_spmd

F32 = mybir.dt.float32
BF16 = mybir.dt.bfloat16
AF = mybir.ActivationFunctionType
ALU = mybir.AluOpType
AX = mybir.AxisListType

D = 4096
DFF = 11008
KC = D // 128
TP = 1024
SEQ = 4096
EPS = 1e-6


class Buf:
    __slots__ = ("name", "w", "r", "strict")

    def __init__(self, name="", strict=False):
        self.name = name
        self.w = None
        self.r = []
        self.strict = strict


def SBuf(name=""):
    return Buf(name, True)


class Sched:
    ENGS = ("pe", "act", "dve", "pool", "sp")

    def __init__(self, nc, n_dma_sems=48):
        self.nc = nc
        self.prog = {e: [] for e in self.ENGS}
        self.sems = {e: nc.alloc_semaphore(name="s_" + e) for e in self.ENGS}
        self.cnt = {e: 0 for e in self.ENGS}
        self.waited = {e: {} for e in self.ENGS}
        self.dsems = [nc.alloc_semaphore(name="d%d" % i) for i in range(n_dma_sems)]
        self.dval = [0] * n_dma_sems
        self.dnext = 0
        self.n_ins = 0
        self._rec = None
        self.nosame = 1
        self.sems["cc"] = nc.alloc_semaphore(name="s_cc")
        self.ccval = 0

    def _sem(self, key):
        return self.sems[key] if isinstance(key, str) else self.dsems[key]

    def _collect(self, eng, reads, writes):
        need = {}

        relax = self.nosame and eng in ("dve", "act")

        def add(tok, strict):
            if tok is None:
                return
            k, v = tok
            if k == eng and (eng == "pe" or (relax and not strict)):
                return
            if need.get(k, 0) < v:
                need[k] = v
        for b in reads:
            add(b.w, b.strict)
        for b in writes:
            add(b.w, b.strict)
            for t in b.r:
                add(t, b.strict)
        waits = []
        wd = self.waited[eng]
        for k, v in need.items():
            if wd.get(k, 0) >= v:
                continue
            wd[k] = v
            waits.append((self._sem(k), v))
        return waits

    def _commit(self, tok, reads, writes):
        for b in reads:
            b.r.append(tok)
        for b in writes:
            b.w = tok
            b.r = []

    def record(self):
        self._rec = []

    def stop(self):
        r, self._rec = self._rec, None
        return r

    def replay(self, lists):
        pos = [0] * len(lists)
        tot = max(len(l) for l in lists)
        for step in range(1, tot + 1):
            for i, l in enumerate(lists):
                upto = (step * len(l)) // tot
                while pos[i] < upto:
                    kind, a, kw = l[pos[i]]
                    pos[i] += 1
                    {"op": self.op, "dma": self.dma, "cc": self.collective}[kind](*a, **kw)

    def op(self, eng, fn, reads=(), writes=(), signal=True):
        if self._rec is not None:
            self._rec.append(("op", (eng, fn), dict(reads=list(reads), writes=list(writes), signal=signal)))
            return
        waits = self._collect(eng, reads, writes)
        if signal:
            self.cnt[eng] += 1
            tok = (eng, self.cnt[eng])
        else:
            tok = (eng, self.cnt[eng] + 1)
        sem = self.sems[eng]

        def run(e, waits=waits, fn=fn, sem=sem, signal=signal):
            for s, v in waits:
                e.wait_ge(s, v)
            ins = fn(e)
            if signal:
                ins.then_inc(sem, 1)
        self.prog[eng].append(run)
        self._commit(tok, reads, writes)
        self.n_ins += 1

    def dma(self, eng, out_ap, in_ap, reads=(), writes=(), **kw):
        if self._rec is not None:
            self._rec.append(("dma", (eng, out_ap, in_ap), dict(reads=list(reads), writes=list(writes), **kw)))
            return
        i = self.dnext
        self.dnext = (self.dnext + 1) % len(self.dsems)
        waits = self._collect(eng, reads, writes)
        wd = self.waited[eng]
        if self.dval[i] > 0 and wd.get(i, 0) < self.dval[i]:
            wd[i] = self.dval[i]
            waits.append((self.dsems[i], self.dval[i]))
        self.dval[i] += 16
        tok = (i, self.dval[i])
        sem = self.dsems[i]

        def run(e, waits=waits, sem=sem):
            for s, v in waits:
                e.wait_ge(s, v)
            e.dma_start(out=out_ap, in_=in_ap, **kw).then_inc(sem, 16)
        self.prog[eng].append(run)
        self._commit(tok, reads, writes)
        self.n_ins += 1

    def collective(self, kind, in_ap, out_ap, groups, reads=(), writes=()):
        if self._rec is not None:
            self._rec.append(("cc", (kind, in_ap, out_ap, groups), dict(reads=list(reads), writes=list(writes))))
            return
        waits = self._collect("pool", reads, writes)
        self.ccval += 1
        tok = ("cc", self.ccval)
        sem = self.sems["cc"]

        def run(e, waits=waits, sem=sem):
            for s_, v in waits:
                e.wait_ge(s_, v)
            e.collective_compute(kind, ALU.bypass, replica_groups=groups, ins=[in_ap], outs=[out_ap]).then_inc(sem, 1)
        self.prog["pool"].append(run)
        self._commit(tok, reads, writes)

    def barrier(self):
        for e in self.ENGS:
            waits = []
            wd = self.waited[e]
            for e2 in self.ENGS:
                if e2 != e and self.cnt[e2] > wd.get(e2, 0):
                    wd[e2] = self.cnt[e2]
                    waits.append((self.sems[e2], self.cnt[e2]))
            for i, v in enumerate(self.dval):
                if v > wd.get(i, 0):
                    wd[i] = v
                    waits.append((self.dsems[i], v))
            if self.ccval > wd.get("cc", 0):
                wd["cc"] = self.ccval
                waits.append((self.sems["cc"], self.ccval))

            def run(en, waits=waits):
                for s, v in waits:
                    en.wait_ge(s, v)
            self.prog[e].append(run)

    def finish(self):
        self.barrier()
        nc = self.nc
        with nc.Block() as block:
            @block.tensor
            def _(e):
                for f in self.prog["pe"]:
                    f(e)

            @block.scalar
            def _(e):
                for f in self.prog["act"]:
                    f(e)

            @block.vector
            def _(e):
                for f in self.prog["dve"]:
                    f(e)

            @block.gpsimd
            def _(e):
                for f in self.prog["pool"]:
                    f(e)

            @block.sync
            def _(e):
                for f in self.prog["sp"]:
                    f(e)


class Ctx:
    def __init__(self, nc, S, st, NW=8, pfx=""):
        self.nc, self.S, self.st, self.pfx = nc, S, st, pfx
        self.ps = [st.enter_context(nc.psum_tensor(pfx + "ps%d" % i, [128, 512], F32)) for i in range(8)]
        self.psb = [Buf("ps%d" % i) for i in range(8)]
        self.ones_bf = self.sb("ones_bf", [128, 128], BF16)
        self.ones_f = self.sb("ones_f", [128, 128], F32)
        self.cb = SBuf("consts")
        S.op("dve", lambda e: e.memset(self.ones_bf[:], 1.0), writes=[self.cb])
        S.op("dve", lambda e: e.memset(self.ones_f[:], 1.0), writes=[self.cb])
        self.NW = NW
        self.wt = [self.sb("wt%d" % i, [128, 4096], BF16) for i in range(self.NW)]
        self.wtb = [Buf("wt%d" % i) for i in range(self.NW)]
        self.wnext = 0
        self.dmaq = 0

    def sb(self, name, shape, dt):
        return self.st.enter_context(self.nc.sbuf_tensor(self.pfx + name, shape, dt))

    def wslot(self):
        i = self.wnext
        self.wnext = (i + 1) % self.NW
        return self.wt[i], self.wtb[i]

    def q(self):
        self.dmaq ^= 1
        return "sp" if self.dmaq else "act"


def gemm_cg(C, W, c0, CW, rhs, rhsb, KCr, T, banks, tokmajor=False):
    S = C.S
    nth = T // 512
    noc = CW // 128
    ukc = 4096 // CW
    nu = (KCr + ukc - 1) // ukc
    Wv = W.rearrange("(kc p) n -> p kc n", p=128)
    for u in range(nu):
        k0 = u * ukc
        nk = min(ukc, KCr - k0)
        wt, wb = C.wslot()
        wv = wt[:, 0:nk * CW].rearrange("p (k n) -> p k n", n=CW)
        S.dma("pool", wv, Wv[:, k0:k0 + nk, c0:c0 + CW], writes=[wb])
        for oc in range(noc):
            for th in range(nth):
                bi = banks[oc * nth + th]
                for j in range(nk):
                    kc = k0 + j
                    S.op("pe", lambda e, bi=bi, wv=wv, j=j, oc=oc, kc=kc, th=th: e.matmul(
                        C.ps[bi][:, :], wv[:, j, oc * 128:(oc + 1) * 128], rhs[:, kc, th * 512:(th + 1) * 512],
                        start=(kc == 0), stop=(kc == KCr - 1)),
                        reads=[wb, rhsb], writes=[C.psb[bi]], signal=(j == nk - 1))


def load_gain(C, sb, name, g_ap, ncol=KC):
    t = sb(name, [128, ncol], F32)
    b = Buf(name)
    C.S.dma("sp", t[:, :], g_ap.rearrange("(kc p) -> p kc", p=128), writes=[b], allow_slow_non_contiguous=True)
    return t, b


def colsum_rstd(C, src_dram, srcb, nkc, T, rstd, rstdb, xin, xinb, sq, sqb, scale, tmp, tmpb):
    S = C.S
    nth = T // 512
    for kc in range(nkc):
        r = kc % len(xin)
        S.dma(C.q(), xin[r][:, :], src_dram[kc * 128:(kc + 1) * 128, :], reads=[srcb[kc]], writes=[xinb[r]])
        r2 = kc % len(sq)
        S.op("act", lambda e, r=r, r2=r2: e.activation(sq[r2][:, :], xin[r][:, :], AF.Square), reads=[xinb[r]], writes=[sqb[r2]])
        for th in range(nth):
            S.op("pe", lambda e, th=th, r2=r2, kc=kc: e.matmul(C.ps[th][:, :], C.ones_bf[:, :], sq[r2][:, th * 512:(th + 1) * 512],
                                                             start=(kc == 0), stop=(kc == nkc - 1)),
                 reads=[sqb[r2], C.cb], writes=[C.psb[th]])
    for th in range(nth):
        sl = slice(th * 512, (th + 1) * 512)
        S.op("act", lambda e, th=th, sl=sl: e.activation(tmp[:, sl], C.ps[th][:, :], AF.Sqrt, bias=C.eps_t[:, 0:1], scale=scale),
             reads=[C.psb[th], C.cb], writes=[tmpb])
        S.op("dve", lambda e, sl=sl: e.reciprocal(rstd[:, sl], tmp[:, sl]), reads=[tmpb], writes=[rstdb])


def rstd_from_acc(C, acc, accb, T, rstd, rstdb, tmp, tmpb, scale):
    S = C.S
    for th in range(T // 512):
        sl = slice(th * 512, (th + 1) * 512)
        S.op("pe", lambda e, th=th, sl=sl: e.matmul(C.ps[th][:, :], C.ones_f[:, :], acc[:, sl], start=True, stop=True),
             reads=[accb, C.cb], writes=[C.psb[th]])
        S.op("act", lambda e, th=th, sl=sl: e.activation(tmp[:, sl], C.ps[th][:, :], AF.Sqrt, bias=C.eps_t[:, 0:1], scale=scale),
             reads=[C.psb[th], C.cb], writes=[tmpb])
        S.op("dve", lambda e, sl=sl: e.reciprocal(rstd[:, sl], tmp[:, sl]), reads=[tmpb], writes=[rstdb])


def norm_stage(C, XT, XTb, gain_ap, HT, HTb, tag, xsq=None):
    S, nc = C.S, C.nc
    with contextlib.ExitStack() as st:
        sb = lambda n, s, d: st.enter_context(nc.sbuf_tensor(tag + n, s, d))
        xin = [sb("xin%d" % i, [128, TP], F32) for i in range(3)]
        xinb = [Buf() for _ in range(3)]
        sq = [sb("sq%d" % i, [128, TP], BF16) for i in range(2)]
        sqb = [Buf() for _ in range(2)]
        rstd = sb("rstd", [128, TP], F32)
        rstdb = Buf()
        tmp = sb("tmp", [128, TP], F32)
        tmpb = Buf()
        g = sb("g", [128, KC], F32)
        gb = Buf()
        S.dma("sp", g[:, :], gain_ap.rearrange("(kc p) -> p kc", p=128), writes=[gb], allow_slow_non_contiguous=True)
        if xsq is None:
            colsum_rstd(C, XT, XTb, KC, TP, rstd, rstdb, xin, xinb, sq, sqb, 1.0 / D, tmp, tmpb)
        else:
            rstd_from_acc(C, xsq[0], xsq[1], TP, rstd, rstdb, tmp, tmpb, 1.0 / D)
        for kc in range(KC):
            r = kc % 3
            S.dma(C.q(), xin[r][:, :], XT[kc * 128:(kc + 1) * 128, :], reads=[XTb[kc]], writes=[xinb[r]])
            S.op("dve", lambda e, r=r, kc=kc: e.scalar_tensor_tensor(HT[:, kc, :], xin[r][:, :], g[:, kc:kc + 1], rstd[:, :],
                                                                    ALU.mult, ALU.mult),
                 reads=[xinb[r], gb, rstdb], writes=[HTb])


def postnorm_resid(C, YT, YTb, gain_ap, XT, XTb, tag, ysq=None, xsq=None, final=None):
    S, nc = C.S, C.nc
    with contextlib.ExitStack() as st:
        sb = lambda n, s, d: st.enter_context(nc.sbuf_tensor(tag + n, s, d))
        xin = [sb("xin%d" % i, [128, TP], F32) for i in range(3)]
        xinb = [Buf() for _ in range(3)]
        yin = [sb("yin%d" % i, [128, TP], F32) for i in range(3)]
        yinb = [Buf() for _ in range(3)]
        sq = [sb("sq%d" % i, [128, TP], BF16) for i in range(2)]
        sqb = [Buf() for _ in range(2)]
        rstd = sb("rstd", [128, TP], F32)
        rstdb = Buf()
        tmp = sb("tmp", [128, TP], F32)
        tmpb = Buf()
        g = sb("g", [128, KC], F32)
        gb = Buf()
        S.dma("sp", g[:, :], gain_ap.rearrange("(kc p) -> p kc", p=128), writes=[gb], allow_slow_non_contiguous=True)
        if ysq is None:
            colsum_rstd(C, YT, YTb, KC, TP, rstd, rstdb, yin, yinb, sq, sqb, 1.0 / D, tmp, tmpb)
        else:
            rstd_from_acc(C, ysq[0], ysq[1], TP, rstd, rstdb, tmp, tmpb, 1.0 / D)
        if xsq is not None:
            S.op("dve", lambda e: e.memset(xsq[0][:, :], 0.0), writes=[xsq[1]])
        if final is not None:
            xo = [sb("xo%d" % i, [128, 4, 128], F32) for i in range(3)]
            xob = [Buf() for _ in range(3)]
            oi = 0
        for kc in range(KC):
            r = kc % 3
            rows = slice(kc * 128, (kc + 1) * 128)
            S.dma("sp", yin[r][:, :], YT[rows, :], reads=[YTb[kc]], writes=[yinb[r]])
            S.dma("act", xin[r][:, :], XT[rows, :], reads=[XTb[kc]], writes=[xinb[r]])
            S.op("dve", lambda e, r=r, kc=kc: e.scalar_tensor_tensor(yin[r][:, :], yin[r][:, :], g[:, kc:kc + 1], rstd[:, :],
                                                                    ALU.mult, ALU.mult),
                 reads=[yinb[r], gb, rstdb], writes=[yinb[r]])
            S.op("dve", lambda e, r=r: e.tensor_tensor(xin[r][:, :], xin[r][:, :], yin[r][:, :], ALU.add),
                 reads=[yinb[r], xinb[r]], writes=[xinb[r]])
            if final is None:
                S.dma("sp", XT[rows, :], xin[r][:, :], reads=[xinb[r]], writes=[XTb[kc]])
            else:
                out, outb, ident, identb = final
                for t4 in range(TP // 512):
                    bi = 2 + oi % 6
                    o = oi % 3
                    oi += 1
                    for j in range(4):
                        tb = t4 * 4 + j
                        S.op("pe", lambda e, bi=bi, j=j, tb=tb, r=r: e.transpose(C.ps[bi][:, j * 128:(j + 1) * 128],
                                                                                xin[r][:, tb * 128:(tb + 1) * 128], ident[:, :]),
                             reads=[xinb[r], identb], writes=[C.psb[bi]], signal=(j == 3))
                    S.op("dve", lambda e, bi=bi, o=o: e.tensor_copy(xo[o][:, :, :], C.ps[bi][:, :]), reads=[C.psb[bi]], writes=[xob[o]])
                    dst = out[t4 * 512:(t4 + 1) * 512, kc * 128:(kc + 1) * 128].rearrange("(j p) f -> p j f", p=128)
                    S.dma(C.q(), dst, xo[o][:, :, :], reads=[xob[o]], writes=[outb])
            if xsq is not None:
                S.op("act", lambda e, r=r: e.activation(yin[r][:, :], xin[r][:, :], AF.Square), reads=[xinb[r]], writes=[yinb[r]])
                S.op("dve", lambda e, r=r: e.tensor_tensor(xsq[0][:, :], xsq[0][:, :], yin[r][:, :], ALU.add),
                     reads=[yinb[r], xsq[1]], writes=[xsq[1]])
        S.barrier()


def gemm_to_dram(C, W, N, rhs, rhsb, KCr, T, OUT, OUTb, tok0, func, odt, tag, ysq=None):
    S, nc = C.S, C.nc
    CW = 256 if T == 1024 else 512
    nth = T // 512
    noc = CW // 128
    with contextlib.ExitStack() as st:
        ot = [st.enter_context(nc.sbuf_tensor(tag + "ot%d" % i, [128, 512], odt)) for i in range(4)]
        otb = [Buf() for _ in range(4)]
        if ysq is not None:
            sqt = [st.enter_context(nc.sbuf_tensor(tag + "sqt%d" % i, [128, 512], F32)) for i in range(2)]
            sqtb = [Buf() for _ in range(2)]
            for th in range(nth):
                S.op("dve", lambda e, th=th: e.memset(ysq[0][:, tok0 + th * 512: tok0 + (th + 1) * 512], 0.0), writes=[ysq[1]])
        oi = 0
        for cg in range(N // CW):
            banks = [(cg % 2) * 4 + i for i in range(4)]
            gemm_cg(C, W, cg * CW, CW, rhs, rhsb, KCr, T, banks)
            for oc in range(noc):
                for th in range(nth):
                    bi = banks[oc * nth + th]
                    o = oi % 4
                    oi += 1
                    if func is None:
                        S.op("dve", lambda e, o=o, bi=bi: e.tensor_copy(ot[o][:, :], C.ps[bi][:, :]),
                             reads=[C.psb[bi]], writes=[otb[o]])
                    else:
                        S.op("act", lambda e, o=o, bi=bi: e.activation(ot[o][:, :], C.ps[bi][:, :], func),
                             reads=[C.psb[bi]], writes=[otb[o]])
                    row = cg * CW + oc * 128
                    S.dma(C.q(), OUT[row:row + 128, tok0 + th * 512: tok0 + (th + 1) * 512], ot[o][:, :],
                          reads=[otb[o]], writes=[OUTb[row // 128]])
                    if ysq is not None:
                        q2 = oi % 2
                        cs = slice(tok0 + th * 512, tok0 + (th + 1) * 512)
                        S.op("act", lambda e, o=o, q2=q2: e.activation(sqt[q2][:, :], ot[o][:, :], AF.Square), reads=[otb[o]], writes=[sqtb[q2]])
                        S.op("dve", lambda e, q2=q2, cs=cs: e.tensor_tensor(ysq[0][:, cs], ysq[0][:, cs], sqt[q2][:, :], ALU.add),
                             reads=[sqtb[q2], ysq[1]], writes=[ysq[1]])
        S.barrier()


def load_fm(C, SRC, SRCb, nkc, T, tok0, dst, dstb, per=8):
    v = SRC.rearrange("(kc p) t -> p kc t", p=128)
    for k0 in range(0, nkc, per):
        k1 = min(nkc, k0 + per)
        C.S.dma(C.q(), dst[:, k0:k1, 0:T], v[:, k0:k1, tok0:tok0 + T], reads=[SRCb[k] for k in range(k0, k1)], writes=[dstb])


def ffn_stage(C, XT, XTb, YT, YTb, HID, HIDb, g_pre, g_post, Wg, Wu, Wd, tag, last=False, final=None):
    S, nc = C.S, C.nc
    with contextlib.ExitStack() as st:
        HT = st.enter_context(nc.sbuf_tensor(tag + "HT", [128, KC, TP], BF16))
        HTb = Buf()
        norm_stage(C, XT, XTb, g_pre, HT, HTb, tag + "n", xsq=C.xsq)
        sl_t = [st.enter_context(nc.sbuf_tensor(tag + "sl%d" % i, [128, 512], F32)) for i in range(2)]
        slb = [Buf() for _ in range(2)]
        ot = [st.enter_context(nc.sbuf_tensor(tag + "ho%d" % i, [128, 512], BF16)) for i in range(4)]
        otb = [Buf() for _ in range(4)]
        oi = 0
        for cg in range(DFF // 256):
            bg = [0, 1, 2, 3]
            bu = [4, 5, 6, 7]
            gemm_cg(C, Wg, cg * 256, 256, HT, HTb, KC, TP, bg)
            gemm_cg(C, Wu, cg * 256, 256, HT, HTb, KC, TP, bu)
            for oc in range(2):
                for th in range(2):
                    o = oi % 4
                    s2 = oi % 2
                    oi += 1
                    b1, b2 = bg[oc * 2 + th], bu[oc * 2 + th]
                    S.op("act", lambda e, s2=s2, b1=b1: e.activation(sl_t[s2][:, :], C.ps[b1][:, :], AF.Silu),
                         reads=[C.psb[b1]], writes=[slb[s2]])
                    S.op("dve", lambda e, s2=s2, b2=b2, o=o: e.tensor_tensor(ot[o][:, :], sl_t[s2][:, :], C.ps[b2][:, :], ALU.mult),
                         reads=[slb[s2], C.psb[b2]], writes=[otb[o]])
                    row = cg * 256 + oc * 128
                    S.dma(C.q(), HID[row:row + 128, th * 512:(th + 1) * 512], ot[o][:, :], reads=[otb[o]], writes=[HIDb[row // 128]])
        S.barrier()
    KF = DFF // 128
    with contextlib.ExitStack() as st:
        RH = st.enter_context(nc.sbuf_tensor(tag + "RH", [128, KF, 512], BF16))
        RHb = Buf()
        for th2 in range(2):
            load_fm(C, HID, HIDb, KF, 512, th2 * 512, RH, RHb)
            gemm_to_dram(C, Wd, D, RH, RHb, KF, 512, YT, YTb, th2 * 512, None, F32, tag + "d%d" % th2, ysq=C.ysq)
    postnorm_resid(C, YT, YTb, g_post, XT, XTb, tag + "p", ysq=C.ysq, xsq=None if last else C.xsq, final=final)


def about_stage(C, XT, XTb, YT, YTb, YC, YCb, g_post, Wo, tag):
    S, nc = C.S, C.nc
    with contextlib.ExitStack() as st:
        R = st.enter_context(nc.sbuf_tensor(tag + "R", [128, KC, TP], BF16))
        Rb = Buf()
        load_fm(C, YC, YCb, KC, TP, 0, R, Rb)
        gemm_to_dram(C, Wo, D, R, Rb, KC, TP, YT, YTb, 0, None, F32, tag + "g")
    postnorm_resid(C, YT, YTb, g_post, XT, XTb, tag + "p")


def sgu_stage(C, XT, XTb, YT, YTb, UT, UTb, VTM, VTMb, g_pre, g_post, Win, ln_g, ln_b, wsT, bs, maskT, Wout, tag):
    S, nc = C.S, C.nc
    with contextlib.ExitStack() as st:
        HT = st.enter_context(nc.sbuf_tensor(tag + "HT", [128, KC, TP], BF16))
        HTb = Buf()
        norm_stage(C, XT, XTb, g_pre, HT, HTb, tag + "n", xsq=C.xsq)
        gemm_to_dram(C, Win[:, 0:D], D, HT, HTb, KC, TP, UT, UTb, 0, AF.Gelu, BF16, tag + "u")
        vo = [st.enter_context(nc.sbuf_tensor(tag + "vo%d" % i, [128, 512], F32)) for i in range(3)]
        vob = [Buf() for _ in range(3)]
        Wv = Win.rearrange("(kc p) n -> p kc n", p=128)
        oi = 0
        for cg in range(D // 512):
            slots = []
            for u in range(4):
                wt, wb = C.wslot()
                wv = wt[:, :].rearrange("p (k n) -> p k n", n=512)
                S.dma("pool", wv, Wv[:, u * 8:(u + 1) * 8, D + cg * 512: D + (cg + 1) * 512], writes=[wb])
                slots.append((wv, wb))
            for tb in range(TP // 128):
                bi = oi % 8
                for kc in range(KC):
                    wv, wb = slots[kc // 8]
                    S.op("pe", lambda e, bi=bi, wv=wv, kc=kc, tb=tb: e.matmul(
                        C.ps[bi][:, :], HT[:, kc, tb * 128:(tb + 1) * 128], wv[:, kc % 8, :], start=(kc == 0), stop=(kc == KC - 1)),
                        reads=[wb, HTb], writes=[C.psb[bi]], signal=(kc % 8 == 7))
                o = oi % 3
                oi += 1
                S.op("act", lambda e, o=o, bi=bi: e.activation(vo[o][:, :], C.ps[bi][:, :], AF.Gelu), reads=[C.psb[bi]], writes=[vob[o]])
                S.dma(C.q(), VTM[tb * 128:(tb + 1) * 128, cg * 512:(cg + 1) * 512], vo[o][:, :], reads=[vob[o]], writes=[VTMb[tb]])
        S.barrier()
    with contextlib.ExitStack() as st:
        sb = lambda n, s, d: st.enter_context(nc.sbuf_tensor(tag + n, s, d))
        PT = sb("PT", [128, KC, TP], BF16)
        PTb = Buf()
        load_fm(C, UT, UTb, KC, TP, 0, PT, PTb)
        mk = sb("mk", [128, 128], F32)
        mkb = Buf()
        S.dma("sp", mk[:, :], maskT, writes=[mkb])
        wsbf = sb("wsbf", [128, 16, 128], BF16)
        wsbfb = Buf()
        S.dma("pool", wsbf[:, :, :], wsT, writes=[wsbfb])
        for g in range(16):
            S.op("dve", lambda e, g=g: e.tensor_tensor(wsbf[:, g, :], wsbf[:, g, :], mk[:, :], ALU.mult), reads=[wsbfb, mkb], writes=[wsbfb])
        BS = sb("BS", [128, 16, 128], F32)
        BSb = Buf()
        S.dma("sp", BS[:, :, :], bs, writes=[BSb])
        RS = sb("RS", [128, 16, 128], F32)
        RSb = Buf()
        for q4 in range(4):
            S.op("pe", lambda e, q4=q4: e.matmul(C.ps[q4][:, :], C.ones_bf[:, :], wsbf[:, q4 * 4:(q4 + 1) * 4, :], start=True, stop=True),
                 reads=[wsbfb, C.cb], writes=[C.psb[q4]])
            S.op("dve", lambda e, q4=q4: e.tensor_copy(RS[:, q4 * 4:(q4 + 1) * 4, :], C.ps[q4][:, :]), reads=[C.psb[q4]], writes=[RSb])
        lg, lgb = load_gain(C, sb, "lg", ln_g)
        lb, lbb = load_gain(C, sb, "lb", ln_b)
        T2 = sb("T2", [128, KC, 128], F32)
        T2b = Buf()
        for kc in range(KC):
            S.op("dve", lambda e, kc=kc: e.scalar_tensor_tensor(T2[:, kc, :], RS[:, kc // 2, :], lb[:, kc:kc + 1], BS[:, kc // 2, :],
                                                               ALU.mult, ALU.add), reads=[RSb, BSb, lbb], writes=[T2b])
        vin = [sb("vin0", [128, D], F32)] * 2
        vinb = [Buf()] * 2
        vh = [sb("vh0", [128, D], BF16)] * 2
        vhb = [Buf()] * 2
        junk = vh[0]
        junkb = vhb[0]
        st4 = [sb("st%d" % i, [128, 8], F32) for i in range(2)]
        st4b = [SBuf() for _ in range(2)]
        sv = [sb("sv%d" % i, [128, 128], F32) for i in range(3)]
        svb = [Buf() for _ in range(3)]
        oi = 0
        for tb in range(TP // 128):
            r = tb % 2
            S.dma("sp", vin[r][:, 0:D // 2], VTM[tb * 128:(tb + 1) * 128, 0:D // 2], reads=[VTMb[tb]], writes=[vinb[r]])
            S.dma("act", vin[r][:, D // 2:D], VTM[tb * 128:(tb + 1) * 128, D // 2:D], reads=[VTMb[tb]], writes=[vinb[r]])
            s4 = st4[r]
            S.op("act", lambda e, r=r, s4=s4: e.activation(junk[:, :], vin[r][:, :], AF.Identity, accum_out=s4[:, 0:1]),
                 reads=[vinb[r]], writes=[junkb, st4b[r]])
            S.op("act", lambda e, r=r, s4=s4: e.activation(junk[:, :], vin[r][:, :], AF.Square, accum_out=s4[:, 1:2]),
                 reads=[vinb[r]], writes=[junkb, st4b[r]])
            S.op("dve", lambda e, s4=s4: e.tensor_scalar(s4[:, 2:3], s4[:, 0:1], 1.0 / D, None, ALU.mult), reads=[st4b[r]], writes=[st4b[r]])
            S.op("dve", lambda e, s4=s4: e.tensor_tensor(s4[:, 3:4], s4[:, 2:3], s4[:, 2:3], ALU.mult), reads=[st4b[r]], writes=[st4b[r]])
            S.op("dve", lambda e, s4=s4: e.scalar_tensor_tensor(s4[:, 4:5], s4[:, 1:2], 1.0 / D, s4[:, 3:4], ALU.mult, ALU.subtract),
                 reads=[st4b[r]], writes=[st4b[r]])
            S.op("act", lambda e, s4=s4: e.activation(s4[:, 5:6], s4[:, 4:5], AF.Sqrt, bias=C.eps_t[:, 0:1], scale=1.0),
                 reads=[st4b[r], C.cb], writes=[st4b[r]])
            S.op("dve", lambda e, s4=s4: e.reciprocal(s4[:, 6:7], s4[:, 5:6]), reads=[st4b[r]], writes=[st4b[r]])
            S.op("dve", lambda e, s4=s4: e.scalar_tensor_tensor(s4[:, 7:8], s4[:, 2:3], -1.0, s4[:, 6:7], ALU.mult, ALU.mult),
                 reads=[st4b[r]], writes=[st4b[r]])
            S.op("dve", lambda e, r=r, s4=s4: e.tensor_scalar(vh[r][:, :], vin[r][:, :], s4[:, 6:7], s4[:, 7:8], ALU.mult, ALU.add),
                 reads=[vinb[r], st4b[r]], writes=[vhb[r]])
            for k4 in range(KC // 4):
                bi = oi % 8
                oi += 1
                for j in range(4):
                    kc = k4 * 4 + j
                    S.op("pe", lambda e, bi=bi, j=j, kc=kc, r=r: e.matmul(C.ps[bi][:, j * 128:(j + 1) * 128], vh[r][:, kc * 128:(kc + 1) * 128],
                                                                         wsbf[:, kc // 2, :], start=True, stop=True),
                         reads=[vhb[r], wsbfb], writes=[C.psb[bi]], signal=(j == 3))
                for j in range(4):
                    kc = k4 * 4 + j
                    s3 = (k4 * 4 + j) % 3
                    S.op("dve", lambda e, bi=bi, j=j, kc=kc, s3=s3: e.scalar_tensor_tensor(
                        sv[s3][:, :], C.ps[bi][:, j * 128:(j + 1) * 128], lg[:, kc:kc + 1], T2[:, kc, :], ALU.mult, ALU.add),
                        reads=[C.psb[bi], lgb, T2b], writes=[svb[s3]])
                    S.op("dve", lambda e, kc=kc, s3=s3, tb=tb: e.tensor_tensor(
                        PT[:, kc, tb * 128:(tb + 1) * 128], PT[:, kc, tb * 128:(tb + 1) * 128], sv[s3][:, :], ALU.mult),
                        reads=[svb[s3], PTb], writes=[PTb])
        gemm_to_dram(C, Wout, D, PT, PTb, KC, TP, YT, YTb, 0, None, F32, tag + "o")
    postnorm_resid(C, YT, YTb, g_post, XT, XTb, tag + "p", xsq=C.xsq)


def xin_stage(C, x_own, XT, XTb, ident, identb, ext=None, banks=(0, 1, 2, 3, 4, 5, 6, 7)):
    S, nc = C.S, C.nc
    with contextlib.ExitStack() as st0:
        st = ext if ext is not None else st0
        xr = [st.enter_context(nc.sbuf_tensor("xi_r%d" % i, [128, D], F32)) for i in range(2)]
        xrb = [Buf() for _ in range(2)]
        xo = [st.enter_context(nc.sbuf_tensor("xi_o%d" % i, [128, 4, 128], F32)) for i in range(3)]
        xob = [Buf() for _ in range(3)]
        inb = Buf()
        oi = 0
        for tb in range(TP // 128):
            r = tb % 2
            S.dma("sp", xr[r][:, 0:D // 2], x_own[tb * 128:(tb + 1) * 128, 0:D // 2], reads=[inb], writes=[xrb[r]])
            S.dma("act", xr[r][:, D // 2:D], x_own[tb * 128:(tb + 1) * 128, D // 2:D], reads=[inb], writes=[xrb[r]])
            for k4 in range(KC // 4):
                bi = banks[oi % len(banks)]
                o = oi % 3
                oi += 1
                for j in range(4):
                    kc = k4 * 4 + j
                    S.op("pe", lambda e, bi=bi, j=j, kc=kc, r=r: e.transpose(C.ps[bi][:, j * 128:(j + 1) * 128],
                                                                            xr[r][:, kc * 128:(kc + 1) * 128], ident[:, :]),
                         reads=[xrb[r], identb], writes=[C.psb[bi]], signal=(j == 3))
                S.op("dve", lambda e, bi=bi, o=o: e.tensor_copy(xo[o][:, :, :], C.ps[bi][:, :]), reads=[C.psb[bi]], writes=[xob[o]])
                dst = XT[k4 * 512:(k4 + 1) * 512, tb * 128:(tb + 1) * 128].rearrange("(j p) t -> p j t", p=128)
                S.dma(C.q(), dst, xo[o][:, :, :], reads=[xob[o]], writes=[XTb[k4 * 4 + j] for j in range(4)])
        if ext is None:
            S.barrier()


def xout_stage(C, XT, XTb, out, outb, ident, identb):
    S, nc = C.S, C.nc
    with contextlib.ExitStack() as st:
        xr = [st.enter_context(nc.sbuf_tensor("xo_r%d" % i, [128, TP], F32)) for i in range(2)]
        xrb = [Buf() for _ in range(2)]
        xo = [st.enter_context(nc.sbuf_tensor("xo_o%d" % i, [128, 4, 128], F32)) for i in range(3)]
        xob = [Buf() for _ in range(3)]
        oi = 0
        for kc in range(KC):
            r = kc % 2
            S.dma(C.q(), xr[r][:, :], XT[kc * 128:(kc + 1) * 128, :], reads=[XTb[kc]], writes=[xrb[r]])
            for t4 in range(TP // 512):
                bi = oi % 8
                o = oi % 3
                oi += 1
                for j in range(4):
                    tb = t4 * 4 + j
                    S.op("pe", lambda e, bi=bi, j=j, tb=tb, r=r: e.transpose(C.ps[bi][:, j * 128:(j + 1) * 128],
                                                                            xr[r][:, tb * 128:(tb + 1) * 128], ident[:, :]),
                         reads=[xrb[r], identb], writes=[C.psb[bi]], signal=(j == 3))
                S.op("dve", lambda e, bi=bi, o=o: e.tensor_copy(xo[o][:, :, :], C.ps[bi][:, :]), reads=[C.psb[bi]], writes=[xob[o]])
                dst = out[t4 * 512:(t4 + 1) * 512, kc * 128:(kc + 1) * 128].rearrange("(j p) f -> p j f", p=128)
                S.dma(C.q(), dst, xo[o][:, :, :], reads=[xob[o]], writes=[outb])
        S.barrier()


def dram_in(nc, name, shape, dt=F32):
    return nc.dram_tensor(name, list(shape), dt, kind="ExternalInput").ap()


def dram_scratch(nc, name, shape, dt=F32):
    return nc.dram_tensor(name, list(shape), dt, kind="Internal").ap()


def phase2_decl(nc):
    A = {}
    A["x_own"] = dram_in(nc, "x_own", [TP, D])
    A["ident_d"] = dram_in(nc, "ident", [128, 128])
    A["nmpost"] = dram_in(nc, "norm_mix_post", [2, D])
    A["nmpre"] = dram_in(nc, "norm_mix_pre", [2, D])
    A["nfpre"] = dram_in(nc, "norm_ffn_pre", [2, D])
    A["nfpost"] = dram_in(nc, "norm_ffn_post", [2, D])
    A["Wabo"] = dram_in(nc, "ab_w_out", [D, D])
    A["Wg"] = dram_in(nc, "ffn_w_gate", [2, D, DFF])
    A["Wu"] = dram_in(nc, "ffn_w_up", [2, D, DFF])
    A["Wd"] = dram_in(nc, "ffn_w_down", [2, DFF, D])
    A["Wsi"] = dram_in(nc, "sgu_w_in", [D, 2 * D])
    A["Wso"] = dram_in(nc, "sgu_w_out", [D, D])
    A["lng"] = dram_in(nc, "sgu_ln_g", [D])
    A["lnb"] = dram_in(nc, "sgu_ln_b", [D])
    A["wsT"] = dram_in(nc, "sgu_wsT", [128, 16, 128])
    A["bsb"] = dram_in(nc, "sgu_bs_b", [128, 16, 128])
    A["maskT"] = dram_in(nc, "sgu_maskT", [128, 128])
    A["out"] = nc.dram_tensor("out", [TP, D], F32, kind="ExternalOutput").ap()
    A["XT"] = dram_scratch(nc, "XT", [D, TP])
    A["YT"] = dram_scratch(nc, "YT", [D, TP])
    A["HID"] = dram_scratch(nc, "HID", [DFF, TP], BF16)
    A["UT"] = dram_scratch(nc, "UT", [D, TP], BF16)
    A["VTM"] = dram_scratch(nc, "VTM", [TP, D])
    return A


def build_phase2(stages=("in", "ab", "ffn0", "sgu", "ffn1", "out")):
    nc = bass.Bass("TRN2", target_bir_lowering=False)
    A = phase2_decl(nc)
    A["YC"] = dram_in(nc, "yc", [D, TP], BF16)
    with nc.cleanup_on_exit():
        S = Sched(nc)
        phase2_body(nc, S, A, stages, None)
        S.finish()
    return nc


def about_stage_sel(C, XT, XTb, YT, YTb, G, Gb, sel_d, g_post, Wo, tag, xin_args=None):
    S, nc = C.S, C.nc
    with contextlib.ExitStack() as st:
        R = st.enter_context(nc.sbuf_tensor(tag + "R", [128, KC, TP], BF16))
        Rb = Buf()
        sel = st.enter_context(nc.sbuf_tensor(tag + "sel", [128, 4], F32))
        selb = Buf()
        S.dma("sp", sel[:, :], sel_d, writes=[selb])
        c4 = [st.enter_context(nc.sbuf_tensor(tag + "c4%d" % i, [128, 4, TT], BF16)) for i in range(3)]
        c4b = [Buf() for _ in range(3)]
        Gv = G.rearrange("(j i) r t -> i r j t", i=2)
        n = 0
        if xin_args is not None:
            S.record()
        for kc in range(KC):
            if kc < 16:
                r, lc = kc // 4, kc % 4
            else:
                r, lc = (kc - 16) // 4, 4 + (kc - 16) % 4
            row = r * 1024 + lc * 128
            for hf in range(2):
                i = n % 3
                n += 1
                ts2 = slice(hf * TT, (hf + 1) * TT)
                S.dma(C.q(), c4[i][:, :, :], Gv[hf, row:row + 128, :, :], reads=list(Gb), writes=[c4b[i]])
                S.op("dve", lambda e, i=i, kc=kc, ts2=ts2: e.tensor_scalar(R[:, kc, ts2], c4[i][:, 0, :], sel[:, 0:1], None, ALU.mult),
                     reads=[c4b[i], selb], writes=[Rb])
                for j in range(1, 4):
                    S.op("dve", lambda e, i=i, kc=kc, j=j, ts2=ts2: e.scalar_tensor_tensor(R[:, kc, ts2], c4[i][:, j, :], sel[:, j:j + 1], R[:, kc, ts2],
                                                                                      ALU.mult, ALU.add), reads=[c4b[i], selb, Rb], writes=[Rb])
        if xin_args is not None:
            l_sel = S.stop()
            S.record()
            xin_stage(C, *xin_args, ext=st)
            l_xin = S.stop()
            S.replay([l_sel, l_xin])
        gemm_to_dram(C, Wo, D, R, Rb, KC, TP, YT, YTb, 0, None, F32, tag + "g", ysq=C.ysq)
    postnorm_resid(C, YT, YTb, g_post, XT, XTb, tag + "p", ysq=C.ysq, xsq=C.xsq)


def phase2_body(nc, S, A, stages, gathered):
    x_own, ident_d, nmpost, nmpre, nfpre, nfpost, Wabo, Wg, Wu, Wd, Wsi, Wso, lng, lnb, wsT, bsb, maskT, out, XT, YT, HID, UT, VTM = [A[k] for k in (
        "x_own", "ident_d", "nmpost", "nmpre", "nfpre", "nfpost", "Wabo", "Wg", "Wu", "Wd", "Wsi", "Wso", "lng", "lnb", "wsT", "bsb", "maskT",
        "out", "XT", "YT", "HID", "UT", "VTM")]
    XTb = [Buf() for _ in range(KC)]
    YTb = [Buf() for _ in range(KC)]
    HIDb = [Buf() for _ in range(DFF // 128)]
    UTb = [Buf() for _ in range(KC)]
    VTMb = [Buf() for _ in range(TP // 128)]
    YCb = [Buf() for _ in range(KC)]
    outb = Buf()
    if True:
        with contextlib.ExitStack() as st:
            C = Ctx(nc, S, st)
            ident = C.sb("ident_sb", [128, 128], F32)
            identb = Buf()
            S.dma("sp", ident[:, :], ident_d, writes=[identb])
            C.eps_t = C.sb("eps_t2", [128, 1], F32)
            S.op("dve", lambda e: e.memset(C.eps_t[:, :], EPS), writes=[C.cb])
            C.ysq = (C.sb("ysq", [128, TP], F32), Buf())
            C.xsq = (C.sb("xsq", [128, TP], F32), Buf())
            C.xsq = None
            if "in" in stages and not ("ab" in stages and gathered is not None):
                xin_stage(C, x_own, XT, XTb, ident, identb)
            if "ab" in stages:
                if gathered is None:
                    about_stage(C, XT, XTb, YT, YTb, A["YC"], YCb, nmpost[0], Wabo, "ab")
                else:
                    G, Gb, sel_d = gathered
                    about_stage_sel(C, XT, XTb, YT, YTb, G, Gb, sel_d, nmpost[0], Wabo, "ab",
                                    xin_args=(x_own, XT, XTb, ident, identb) if "in" in stages else None)
            if "ffn0" in stages:
                ffn_stage(C, XT, XTb, YT, YTb, HID, HIDb, nfpre[0], nfpost[0], Wg[0], Wu[0], Wd[0], "f0")
            if "sgu" in stages:
                sgu_stage(C, XT, XTb, YT, YTb, UT, UTb, VTM, VTMb, nmpre[1], nmpost[1], Wsi, lng, lnb, wsT, bsb, maskT, Wso, "sg")
            if "ffn1" in stages:
                fuse_out = "out" in stages
                ffn_stage(C, XT, XTb, YT, YTb, HID, HIDb, nfpre[1], nfpost[1], Wg[1], Wu[1], Wd[1], "f1", last=True,
                          final=(out, outb, ident, identb) if fuse_out else None)
            if "out" in stages and "ffn1" not in stages:
                xout_stage(C, XT, XTb, out, outb, ident, identb)
            S.barrier()


def build_fused():
    nc = bass.Bass("TRN2", target_bir_lowering=False)
    A1 = phase1_decl(nc)
    A2 = phase2_decl(nc)
    sel_d = dram_in(nc, "sel", [128, 4])
    NT = SEQ // TT
    YL = [nc.dram_tensor("YL%d" % t, [1024, TT], BF16) for t in range(NT)]
    GG = nc.dram_tensor("YG", [NT, 4 * 1024, TT], BF16)
    A1["YCT"] = lambda t: YL[t].ap()
    with nc.cleanup_on_exit():
        S = Sched(nc)
        ylb = [Buf() for _ in range(NT)]
        Gb = [Buf() for _ in range(NT)]
        A1["after_tile"] = lambda t: S.collective("AllGather", YL[t].ap().opt(), GG.ap()[t].opt(), [[0, 1, 2, 3], [4, 5, 6, 7]],
                                                  reads=[ylb[t]], writes=[Gb[t]])
        phase1_body(nc, S, A1, ylb)
        phase2_body(nc, S, A2, ("in", "ab", "ffn0", "sgu", "ffn1", "out"), (GG.ap(), Gb, sel_d))
        S.finish()
    return nc


def phase2_consts(inputs):
    ws = np.asarray(inputs["sgu_w_s"][0], np.float32)
    wsT = np.ascontiguousarray(ws.transpose(2, 0, 1))
    pos = np.arange(128)
    maskT = ((pos[:, None] // 64) <= (pos[None, :] // 64)).astype(np.float32)
    bs = np.asarray(inputs["sgu_b_s"][0], np.float32)
    bsb = np.ascontiguousarray(np.broadcast_to(bs[None], (128, 16, 128)))
    return {
        "ident": np.eye(128, dtype=np.float32),
        "norm_mix_post": np.asarray(inputs["norm_mix_post"], np.float32),
        "norm_mix_pre": np.asarray(inputs["norm_mix_pre"], np.float32),
        "norm_ffn_pre": np.asarray(inputs["norm_ffn_pre"], np.float32),
        "norm_ffn_post": np.asarray(inputs["norm_ffn_post"], np.float32),
        "ab_w_out": np.asarray(inputs["ab_w_out"][0], np.float32),
        "ffn_w_gate": np.asarray(inputs["ffn_w_gate"], np.float32),
        "ffn_w_up": np.asarray(inputs["ffn_w_up"], np.float32),
        "ffn_w_down": np.asarray(inputs["ffn_w_down"], np.float32),
        "sgu_w_in": np.asarray(inputs["sgu_w_in"][0], np.float32),
        "sgu_w_out": np.asarray(inputs["sgu_w_out"][0], np.float32),
        "sgu_ln_g": np.asarray(inputs["sgu_ln_g"][0], np.float32),
        "sgu_ln_b": np.asarray(inputs["sgu_ln_b"][0], np.float32),
        "sgu_wsT": wsT, "sgu_bs_b": bsb, "sgu_maskT": maskT,
    }


NH = 4
TT = 512
NCOL1 = 2568


def phase1_decl(nc):
    A = {}
    A["xb"] = dram_in(nc, "xb", [SEQ, D])
    A["gpre"] = dram_in(nc, "g_pre", [D])
    A["Wc"] = dram_in(nc, "w_in_c", [D, NCOL1])
    A["cw_d"] = dram_in(nc, "conv_w", [128, 12, 4])
    A["nega_d"] = dram_in(nc, "neg_a", [128, 4])
    A["dtb_d"] = dram_in(nc, "dt_b", [128, 4])
    A["gnw_d"] = dram_in(nc, "gn_w", [128, 128])
    A["pw_d"] = dram_in(nc, "pool_wg", [512, 512])
    A["psc_d"] = dram_in(nc, "pool_sc", [512])
    A["band_d"] = dram_in(nc, "bandm", [3, 128, 128])
    A["mask_d"] = dram_in(nc, "masks", [5, 128, 128])
    return A


def build_phase1(ntiles=SEQ // TT, stop_after=None, dbgk=99):
    nc = bass.Bass("TRN2", target_bir_lowering=False)
    A = phase1_decl(nc)
    yct = nc.dram_tensor("yct", [1024, SEQ], BF16, kind="ExternalOutput").ap()
    A["YCT"] = lambda t: yct[:, t * TT:(t + 1) * TT]
    with nc.cleanup_on_exit():
        S = Sched(nc)
        phase1_body(nc, S, A, [Buf() for _ in range(SEQ // TT)], ntiles, stop_after, dbgk)
        S.finish()
    return nc


def phase1_body(nc, S, A, outb, ntiles=SEQ // TT, stop_after=None, dbgk=99):
    xb, gpre, Wc, cw_d, nega_d, dtb_d, gnw_d, pw_d, psc_d, band_d, mask_d, YCT = [A[k] for k in (
        "xb", "gpre", "Wc", "cw_d", "nega_d", "dtb_d", "gnw_d", "pw_d", "psc_d", "band_d", "mask_d", "YCT")]
    if True:
        with contextlib.ExitStack() as st:
            C = Ctx(nc, S, st, NW=4, pfx="p1_")
            sb = C.sb
            C.eps_t = sb("eps_t", [128, 1], F32)
            S.op("dve", lambda e: e.memset(C.eps_t[:, :], EPS), writes=[C.cb])
            C.one_t = sb("one_t", [128, 1], F32)
            S.op("dve", lambda e: e.memset(C.one_t[:, :], 1.0), writes=[C.cb])
            mk = sb("mk", [128, 5, 128], F32)
            mkb = Buf()
            S.dma("sp", mk[:, :, :], mask_d.rearrange("m p f -> p m f"), writes=[mkb])
            triA, strictU, MS, MU, ident = [mk[:, i, :] for i in range(5)]
            band = sb("band", [128, 3, 128], F32)
            bandb = Buf()
            S.dma("act", band[:, :, :], band_d.rearrange("m p f -> p m f"), writes=[bandb])
            g = sb("g", [128, KC], F32)
            gb = Buf()
            S.dma("sp", g[:, :], gpre.rearrange("(kc p) -> p kc", p=128), writes=[gb], allow_slow_non_contiguous=True)
            cw = sb("cw", [128, 12, 4], F32)
            cwb = Buf()
            S.dma("sp", cw[:, :, :], cw_d, writes=[cwb])
            nega = sb("nega", [128, 4], F32)
            dtb = sb("dtb", [128, 4], F32)
            gnw = sb("gnw", [128, 128], F32)
            psc = sb("psc", [128, 4], F32)
            smb = SBuf()
            S.dma("sp", nega[:, :], nega_d, writes=[smb])
            S.dma("sp", dtb[:, :], dtb_d, writes=[smb])
            S.dma("sp", gnw[:, :], gnw_d, writes=[smb])
            S.dma("sp", psc[:, :], psc_d.rearrange("(c p) -> p c", p=128), writes=[smb], allow_slow_non_contiguous=True)
            S.op("act", lambda e: e.activation(nega[:, :], nega[:, :], AF.Exp), reads=[smb], writes=[smb])
            S.op("dve", lambda e: e.tensor_scalar(nega[:, :], nega[:, :], -1.0, None, ALU.mult), reads=[smb], writes=[smb])
            pw = sb("pw", [128, 4, 512], BF16)
            pwb = Buf()
            S.dma("pool", pw[:, :, :], pw_d.rearrange("(c p) n -> p c n", p=128), writes=[pwb])
            wbd = sb("wbd", [128, KC, 8], BF16)
            wbdb = Buf()
            S.dma("pool", wbd[:, :, :], Wc.rearrange("(kc p) n -> p kc n", p=128)[:, :, 2560:2568], writes=[wbdb],
                  allow_slow_non_contiguous=True)
            xr = sb("xr", [128, D], F32)
            xrb = Buf()
            HT = sb("HT", [128, KC, TT], BF16)
            HTb = Buf()
            st8 = sb("st8", [128, 12], F32)
            st8b = SBuf()
            halo = sb("halo", [128, 12, 4], F32)
            halob = Buf()
            S.op("dve", lambda e: e.memset(halo[:, :, :], 0.0), writes=[halob])
            Z = [sb("Z%d" % i, [128, 3 + TT], F32) for i in range(4)]
            Zb = [Buf() for _ in range(4)]
            junkA = sb("junkA", [128, TT], BF16)
            junkAb = Buf()
            QT = sb("QT", [128, NH, TT], BF16)
            KT = sb("KT", [128, NH, TT], BF16)
            KTf = sb("KTf", [128, NH, TT], F32)
            VT = sb("VT", [128, NH, TT], F32)
            QTb, KTb, KTfb, VTb = [[Buf() for _ in range(NH)] for _ in range(4)]
            sg2 = [sb("sg%d" % i, [128, 4, 512], F32) for i in range(2)]
            sgb2 = [[Buf() for _ in range(4)] for _ in range(2)]
            xa = sb("xa", [128, 5, 512], F32)
            xab = [Buf() for _ in range(5)]
            lg82 = [sb("lg8%d" % i, [128, 4, 8], F32) for i in range(2)]
            lg8b2 = [[SBuf() for _ in range(4)] for _ in range(2)]
            OT = sb("OT", [128, 8, TT], BF16)
            OTb = Buf()
            PLT = sb("PLT", [128, 4, TT], BF16)
            PLTb = Buf()
            Sst = sb("Sst", [128, NH, 128], F32)
            Sbf = sb("Sbf", [128, NH, 128], BF16)
            Sstb = [Buf() for _ in range(NH)]
            Sbfb = [Buf() for _ in range(NH)]
            for h in range(NH):
                S.op("dve", lambda e, h=h: e.memset(Sst[:, h, :], 0.0), writes=[Sstb[h]])
                S.op("dve", lambda e, h=h: e.memset(Sbf[:, h, :], 0.0), writes=[Sbfb[h]])

            def ht(name, n=128, dt=F32, strict=False):
                t = sb(name, [128, NH, n], dt)
                return t, [Buf(name, strict) for _ in range(NH)]
            b4, b4b = ht("b4", 8, strict=True)
            G2, G2b = ht("G2")
            gB, gBb = ht("gB")
            EX, EXb = ht("EX", 392, strict=True)
            Es, Esb = ht("Es")
            ETc, ETcb = ht("ETc")
            ngb, ngbb = ht("ngb", 2, strict=True)
            Lm, Lb = ht("L")
            AT, ATb = ht("AT")
            kd, kdb = ht("kd")
            vb, vbb = ht("vb")
            Xa, Xab_ = ht("Xa")
            Xat, Xatb = ht("Xat")
            Xb, Xbb = ht("Xb")
            Xbt, Xbtb = ht("Xbt")
            Pm, Pmb = ht("Pm")
            A2T, A2Tb = ht("A2T", 128, BF16)
            K2, K2b = ht("K2", 128, BF16)
            qdT, qdTb = ht("qdT", 128, BF16)
            Rbf, Rbfb = ht("Rbf", 128, BF16)
            gsg, gsgb = gB, gBb
            yo, yob = G2, G2b
            flat = lambda t: t[:, :, :].rearrange("p h n -> p (h n)")
            accv = [flat(t) for t in (G2, gB, Es, ETc)]
            accB = [G2b, gBb, Esb, ETcb]
            slv = [flat(t) for t in (Lm, AT, kd, vb)]
            slB = [Lb, ATb, kdb, vbb]
            ps, psb = C.ps, C.psb
            Wv_ = Wc.rearrange("(kc p) n -> p kc n", p=128)

            def emit_A(tile):
                t0 = tile * TT
                for tb in range(4):
                    r0 = t0 + tb * 128
                    S.dma("sp", xr[:, 0:D // 2], xb[r0:r0 + 128, 0:D // 2], writes=[xrb])
                    S.dma("act", xr[:, D // 2:D], xb[r0:r0 + 128, D // 2:D], writes=[xrb])
                    for q8 in range(8):
                        S.op("act", lambda e, q8=q8: e.activation(junkA[:, :], xr[:, q8 * 512:(q8 + 1) * 512], AF.Square,
                                                                  accum_out=st8[:, q8:q8 + 1]), reads=[xrb], writes=[junkAb, st8b])
                    S.op("dve", lambda e: e.tensor_reduce(st8[:, 8:9], st8[:, 0:8], AX.X, ALU.add), reads=[st8b], writes=[st8b])
                    S.op("act", lambda e: e.activation(st8[:, 9:10], st8[:, 8:9], AF.Sqrt, bias=C.eps_t[:, 0:1], scale=1.0 / D),
                         reads=[st8b, C.cb], writes=[st8b])
                    S.op("dve", lambda e: e.reciprocal(st8[:, 10:11], st8[:, 9:10]), reads=[st8b], writes=[st8b])
                    S.op("act", lambda e: e.activation(xr[:, :], xr[:, :], AF.Copy, scale=st8[:, 10:11]), reads=[xrb, st8b], writes=[xrb])
                    for k4 in range(KC // 4):
                        bi = 4 + k4 % 4
                        for j in range(4):
                            kc = k4 * 4 + j
                            S.op("pe", lambda e, bi=bi, j=j, kc=kc: e.transpose(ps[bi][:, j * 128:(j + 1) * 128],
                                                                               xr[:, kc * 128:(kc + 1) * 128], ident),
                                 reads=[xrb, mkb], writes=[psb[bi]], signal=(j == 3))
                        for j in range(4):
                            kc = k4 * 4 + j
                            S.op("dve" if j % 2 == 0 else "act",
                                 (lambda e, bi=bi, j=j, kc=kc, tb=tb: e.tensor_scalar(HT[:, kc, tb * 128:(tb + 1) * 128], ps[bi][:, j * 128:(j + 1) * 128],
                                                                                     g[:, kc:kc + 1], None, ALU.mult)) if j % 2 == 0 else
                                 (lambda e, bi=bi, j=j, kc=kc, tb=tb: e.activation(HT[:, kc, tb * 128:(tb + 1) * 128], ps[bi][:, j * 128:(j + 1) * 128],
                                                                                  AF.Copy, scale=g[:, kc:kc + 1])),
                                 reads=[psb[bi], gb], writes=[HTb])
            def emit_B1(tile):
                t0 = tile * TT
                sg, sgb, lg8, lg8b = sg2[tile % 2], sgb2[tile % 2], lg82[tile % 2], lg8b2[tile % 2]
                for cgi in range(2):
                    slots = []
                    for u in range(4):
                        wt, wb = C.wslot()
                        wv = wt[:, :].rearrange("p (k n) -> p k n", n=512)
                        S.dma("pool", wv, Wv_[:, u * 8:(u + 1) * 8, cgi * 512:(cgi + 1) * 512], writes=[wb])
                        slots.append((wv, wb))
                    for tb in range(4):
                        bi = 4 + tb
                        for kc in range(KC):
                            wv, wb = slots[kc // 8]
                            S.op("pe", lambda e, bi=bi, wv=wv, kc=kc, tb=tb: e.matmul(
                                ps[bi][:, :], HT[:, kc, tb * 128:(tb + 1) * 128], wv[:, kc % 8, :], start=(kc == 0), stop=(kc == KC - 1)),
                                reads=[wb, HTb], writes=[psb[bi]], signal=(kc % 8 == 7))
                        if cgi == 0:
                            S.op("act", lambda e, bi=bi, tb=tb: e.activation(xa[:, tb + 1, :], ps[bi][:, :], AF.Copy),
                                 reads=[psb[bi]], writes=[xab[tb + 1]])
                        else:
                            S.op("act", lambda e, sg=sg, lg8=lg8, bi=bi, tb=tb: e.activation(sg[:, tb, :], ps[bi][:, :], AF.Silu),
                                 reads=[psb[bi]], writes=[sgb[tb]])
                for tb in range(4):
                    bi = 4 + tb
                    for kc in range(KC):
                        S.op("pe", lambda e, bi=bi, kc=kc, tb=tb: e.matmul(ps[bi][:, 0:8], HT[:, kc, tb * 128:(tb + 1) * 128], wbd[:, kc, :],
                                                                          start=(kc == 0), stop=(kc == KC - 1)),
                             reads=[wbdb, HTb], writes=[psb[bi]], signal=(kc == KC - 1))
                    S.op("dve", lambda e, sg=sg, lg8=lg8, bi=bi, tb=tb: e.tensor_copy(lg8[:, tb, :], ps[bi][:, 0:8]), reads=[psb[bi]], writes=[lg8b[tb]])
            def emit_B2E(tile):
                t0 = tile * TT
                bank_of = lambda grp: [4, 5, 6, 7] if grp % 2 == 0 else [0, 1, 2, 3]
                gemm_cg(C, Wc, 1024, 512, HT, HTb, KC, TT, bank_of(0))
                for grp in range(3):
                    banks = bank_of(grp)
                    if grp + 1 < 3:
                        gemm_cg(C, Wc, 1024 + (grp + 1) * 512, 512, HT, HTb, KC, TT, bank_of(grp + 1))
                    lists = []
                    for h in range(NH):
                        S.record()
                        c = grp * 4 + h
                        zi = h
                        bi = banks[h]
                        S.op("dve", lambda e, zi=zi, c=c: e.tensor_copy(Z[zi][:, 0:3], halo[:, c, 0:3]), reads=[halob], writes=[Zb[zi]])
                        S.op("act", lambda e, zi=zi, bi=bi: e.activation(Z[zi][:, 3:3 + TT], ps[bi][:, :], AF.Copy),
                             reads=[psb[bi]], writes=[Zb[zi]])
                        S.op("dve", lambda e, zi=zi, c=c: e.tensor_copy(halo[:, c, 0:3], Z[zi][:, TT:TT + 3]), reads=[Zb[zi]], writes=[halob])
                        S.op("dve", lambda e, zi=zi, c=c: e.tensor_scalar(accv[zi], Z[zi][:, 3:3 + TT], cw[:, c, 3:4], None, ALU.mult),
                             reads=[Zb[zi], cwb], writes=[*accB[zi]])
                        for j in range(3):
                            S.op("dve", lambda e, zi=zi, c=c, j=j: e.scalar_tensor_tensor(accv[zi], Z[zi][:, j:j + TT], cw[:, c, j:j + 1],
                                                                                         accv[zi], ALU.mult, ALU.add),
                                 reads=[Zb[zi], cwb, *accB[zi]], writes=[*accB[zi]])
                        if grp == 2:
                            S.op("act", lambda e, zi=zi, h=h: e.activation(VT[:, h, :], accv[zi], AF.Silu), reads=[*accB[zi]], writes=[VTb[h]])
                            lists.append(S.stop())
                            continue
                        S.op("act", lambda e, zi=zi: e.activation(slv[zi], accv[zi], AF.Silu), reads=[*accB[zi]], writes=[*slB[zi]])
                        S.op("act", lambda e, zi=zi: e.activation(accv[zi], slv[zi], AF.Square),
                             reads=[*slB[zi]], writes=[*accB[zi]])
                        pb = banks[h]
                        S.op("pe", lambda e, pb=pb, zi=zi: e.matmul(ps[pb][:, :], C.ones_f[:, :], accv[zi], start=True, stop=True),
                             reads=[*accB[zi], C.cb], writes=[psb[pb]])
                        S.op("act", lambda e, pb=pb, zi=zi: e.activation(accv[zi], ps[pb][:, :], AF.Sqrt, bias=C.eps_t[:, 0:1], scale=1.0),
                             reads=[psb[pb], C.cb], writes=[*accB[zi]])
                        S.op("dve", lambda e, zi=zi: e.reciprocal(accv[zi], accv[zi]), reads=[*accB[zi]], writes=[*accB[zi]])
                        if grp == 0:
                            S.op("dve", lambda e, zi=zi, h=h: e.scalar_tensor_tensor(QT[:, h, :], slv[zi], 128.0 ** -0.5, accv[zi],
                                                                                    ALU.mult, ALU.mult), reads=[*slB[zi], *accB[zi]], writes=[QTb[h]])
                        else:
                            S.op("dve", lambda e, zi=zi, h=h: e.tensor_tensor(KTf[:, h, :], slv[zi], accv[zi], ALU.mult),
                                 reads=[*slB[zi], *accB[zi]], writes=[KTfb[h]])
                            S.op("pool", lambda e, h=h: e.tensor_copy(KT[:, h, :], KTf[:, h, :]), reads=[KTfb[h]], writes=[KTb[h]])
                        lists.append(S.stop())
                    S.replay(lists)
                for tb in range(4):
                    first = (tile == 0 and tb == 0)
                    for c in range(4):
                        bi = 4 + c
                        if first:
                            S.op("pe", lambda e, bi=bi, c=c, tb=tb: e.matmul(ps[bi][:, 0:128], xa[:, tb + 1, c * 128:(c + 1) * 128], band[:, 0, :],
                                                                            start=True, stop=True), reads=[xab[tb + 1], bandb], writes=[psb[bi]])
                        else:
                            S.op("pe", lambda e, bi=bi, c=c, tb=tb: e.matmul(ps[bi][:, 0:128], xa[:, tb, c * 128:(c + 1) * 128], band[:, 2, :],
                                                                            start=True, stop=False), reads=[xab[tb], bandb], writes=[psb[bi]], signal=False)
                            S.op("pe", lambda e, bi=bi, c=c, tb=tb: e.matmul(ps[bi][:, 0:128], xa[:, tb + 1, c * 128:(c + 1) * 128], band[:, 1, :],
                                                                            start=False, stop=True), reads=[xab[tb + 1], bandb], writes=[psb[bi]])
                        S.op("act", lambda e, bi=bi, c=c, tb=tb: e.activation(PLT[:, c, tb * 128:(tb + 1) * 128], ps[bi][:, 0:128], AF.Copy),
                             reads=[psb[bi]], writes=[PLTb])
                S.op("pool", lambda e: e.tensor_copy(xa[:, 0, :], xa[:, 4, :]), reads=[xab[4]], writes=[xab[0]])
                for dc in range(4):
                    bi = dc
                    for c in range(4):
                        S.op("pe", lambda e, bi=bi, c=c, dc=dc: e.matmul(ps[bi][:, :], pw[:, c, dc * 128:(dc + 1) * 128], PLT[:, c, :],
                                                                        start=(c == 0), stop=(c == 3)), reads=[pwb, PLTb], writes=[psb[bi]], signal=(c == 3))
                    S.op("act", lambda e, bi=bi, dc=dc: e.activation(OT[:, dc, :], ps[bi][:, :], AF.Copy, scale=psc[:, dc:dc + 1]),
                         reads=[psb[bi], smb], writes=[OTb])
            def emit_D(tile):
                t0 = tile * TT
                sg, sgb, lg8, lg8b = sg2[tile % 2], sgb2[tile % 2], lg82[tile % 2], lg8b2[tile % 2]
                H = range(NH)
                for tb in range(4):
                    ts_ = slice(tb * 128, (tb + 1) * 128)
                    S.op("act", lambda e, sg=sg, lg8=lg8, tb=tb: e.activation(b4[:, 0, 0:4], lg8[:, tb, 0:4], AF.Exp, scale=-1.0), reads=[lg8b[tb]], writes=[b4b[0]])
                    S.op("dve", lambda e: e.tensor_scalar(b4[:, 0, 0:4], b4[:, 0, 0:4], 1.0, None, ALU.add), reads=[b4b[0]], writes=[b4b[0]])
                    S.op("dve", lambda e: e.reciprocal(b4[:, 0, 0:4], b4[:, 0, 0:4]), reads=[b4b[0]], writes=[b4b[0]])
                    S.op("dve", lambda e, sg=sg, lg8=lg8, tb=tb: e.tensor_tensor(b4[:, 1, 0:4], lg8[:, tb, 4:8], dtb[:, :], ALU.add), reads=[lg8b[tb], smb], writes=[b4b[0]])
                    S.op("act", lambda e: e.activation(b4[:, 1, 0:4], b4[:, 1, 0:4], AF.Exp), reads=[b4b[0]], writes=[b4b[0]])
                    S.op("act", lambda e: e.activation(b4[:, 1, 0:4], b4[:, 1, 0:4], AF.Ln, bias=C.one_t[:, 0:1], scale=1.0), reads=[b4b[0], C.cb], writes=[b4b[0]])
                    S.op("dve", lambda e: e.tensor_tensor(b4[:, 0, 4:8], b4[:, 1, 0:4], nega[:, :], ALU.mult), reads=[b4b[0], smb], writes=[b4b[0]])
                    beta = lambda h: b4[:, 0, h:h + 1]
                    gcol = lambda h: b4[:, 0, 4 + h:5 + h]
                    gcol2 = lambda h: b4[:, 0, 4 + h:6 + h] if h < 3 else b4[:, 0, 6:8]
                    bb = b4b[0]
                    for h in H:
                        S.op("dve", lambda e, h=h: e.tensor_scalar(G2[:, h, :], strictU, gcol(h), None, ALU.mult), reads=[bb, mkb], writes=[G2b[h]])
                        S.op("dve", lambda e, h=h: e.tensor_scalar(gB[:, h, :], C.ones_f[:, :], gcol(h), None, ALU.mult), reads=[bb, C.cb], writes=[gBb[h]])
                    for h in H:
                        bi = h
                        S.op("pe", lambda e, h=h, bi=bi: e.matmul(ps[bi][:, 0:128], triA, G2[:, h, :], start=True, stop=True),
                             reads=[G2b[h], mkb], writes=[psb[bi]], signal=False)
                        S.op("pe", lambda e, h=h, bi=bi: e.matmul(ps[bi][:, 128:256], G2[:, h, :], triA, start=True, stop=True),
                             reads=[G2b[h], mkb], writes=[psb[bi]], signal=False)
                        S.op("pe", lambda e, h=h, bi=bi: e.matmul(ps[bi][:, 256:384], gB[:, h, :], triA, start=True, stop=True),
                             reads=[gBb[h], mkb], writes=[psb[bi]], signal=False)
                        gsrc = (lambda h: b4[:, 0, 4 + h:6 + h]) if True else None
                        hh = min(h, 2)
                        off = h - hh
                        S.op("pe", lambda e, hh=hh, bi=bi: e.matmul(ps[bi][:, 384:386], triA, b4[:, 0, 4 + hh:6 + hh], start=True, stop=True),
                             reads=[bb, mkb], writes=[psb[bi]], signal=False)
                        S.op("pe", lambda e, hh=hh, bi=bi: e.matmul(ps[bi][:, 386:388], strictU, b4[:, 0, 4 + hh:6 + hh], start=True, stop=True),
                             reads=[bb, mkb], writes=[psb[bi]], signal=False)
                        S.op("pe", lambda e, hh=hh, bi=bi: e.matmul(ps[bi][:, 388:390], C.ones_f[:, :], b4[:, 0, 4 + hh:6 + hh], start=True, stop=True),
                             reads=[bb, C.cb], writes=[psb[bi]])
                        S.op("act", lambda e, h=h, bi=bi: e.activation(EX[:, h, 0:390], ps[bi][:, 0:390], AF.Exp), reads=[psb[bi]], writes=[EXb[h]])
                    gam = lambda h: EX[:, h, 384 + (h - min(h, 2)):385 + (h - min(h, 2))]
                    kds = lambda h: EX[:, h, 386 + (h - min(h, 2)):387 + (h - min(h, 2))]
                    gl_ = lambda h: EX[:, h, 388 + (h - min(h, 2)):389 + (h - min(h, 2))]
                    for h in H:
                        S.op("dve", lambda e, h=h: e.tensor_tensor(Es[:, h, :], EX[:, h, 0:128], MS, ALU.mult), reads=[EXb[h], mkb], writes=[Esb[h]])
                        S.op("dve", lambda e, h=h: e.tensor_tensor(ETc[:, h, :], EX[:, h, 128:256], MU, ALU.mult), reads=[EXb[h], mkb], writes=[ETcb[h]])
                        S.op("dve", lambda e, h=h: e.scalar_tensor_tensor(ngb[:, h, 0:1], gam(h), -1.0, beta(h), ALU.mult, ALU.mult),
                             reads=[EXb[h], bb], writes=[ngbb[h]])
                    for h in H:
                        bi = h
                        S.op("pe", lambda e, ts_=ts_, h=h, bi=bi: e.matmul(ps[bi][:, 0:128], KT[:, h, ts_], KT[:, h, ts_], start=True, stop=True),
                             reads=[KTb[h]], writes=[psb[bi]], signal=False)
                        S.op("pe", lambda e, ts_=ts_, h=h, bi=bi: e.matmul(ps[bi][:, 128:256], KT[:, h, ts_], QT[:, h, ts_], start=True, stop=True),
                             reads=[KTb[h], QTb[h]], writes=[psb[bi]], signal=False)
                        S.op("pe", lambda e, ts_=ts_, h=h, bi=bi: e.transpose(ps[bi][:, 256:384], KTf[:, h, ts_], ident), reads=[KTfb[h], mkb], writes=[psb[bi]], signal=False)
                        S.op("pe", lambda e, ts_=ts_, h=h, bi=bi: e.transpose(ps[bi][:, 384:512], VT[:, h, ts_], ident), reads=[VTb[h], mkb], writes=[psb[bi]])
                    for h in H:
                        bi = h
                        if dbgk > 0:
                            S.op("dve", lambda e, h=h, bi=bi: e.scalar_tensor_tensor(Lm[:, h, :], ps[bi][:, 0:128], beta(h), Es[:, h, :], ALU.mult, ALU.mult),
                                 reads=[psb[bi], bb, Esb[h]], writes=[Lb[h]])
                        if dbgk > 1:
                            S.op("dve", lambda e, h=h, bi=bi: e.tensor_tensor(AT[:, h, :], ps[bi][:, 128:256], ETc[:, h, :], ALU.mult),
                                 reads=[psb[bi], ETcb[h]], writes=[ATb[h]])
                        if dbgk > 2:
                            S.op("dve", lambda e, h=h, bi=bi: e.tensor_scalar(kd[:, h, :], ps[bi][:, 256:384], kds(h), None, ALU.mult),
                                 reads=[psb[bi], EXb[h]], writes=[kdb[h]])
                        if dbgk > 3:
                            S.op("dve", lambda e, h=h, bi=bi: e.tensor_scalar(vb[:, h, :], ps[bi][:, 384:512], beta(h), None, ALU.mult),
                                 reads=[psb[bi], bb], writes=[vbb[h]])
                        if dbgk > 4:
                            S.op("dve", lambda e, ts_=ts_, h=h: e.tensor_tensor(qdT[:, h, :], QT[:, h, ts_], EX[:, h, 256:384], ALU.mult),
                                 reads=[QTb[h], EXb[h]], writes=[qdTb[h]])
                        if dbgk > 5:
                            S.op("dve", lambda e, sg=sg, lg8=lg8, h=h, tb=tb: e.tensor_tensor(gsg[:, h, :], sg[:, tb, h * 128:(h + 1) * 128], gnw[:, :], ALU.mult),
                                 reads=[sgb[tb], smb], writes=[gsgb[h]])
                    for h in H:
                        bi = h
                        S.op("pe", lambda e, h=h, bi=bi: e.transpose(ps[bi][:, 0:128], Lm[:, h, :], ident), reads=[Lb[h], mkb], writes=[psb[bi]])
                        S.op("dve", lambda e, h=h, bi=bi: e.tensor_copy(Xat[:, h, :], ps[bi][:, 0:128]), reads=[psb[bi]], writes=[Xatb[h]])
                        S.op("dve", lambda e, h=h: e.scalar_tensor_tensor(Pm[:, h, :], Lm[:, h, :], -1.0, ident, ALU.mult, ALU.add),
                             reads=[Lb[h], mkb], writes=[Pmb[h]])
                    cur = (Lm, Lb, Xat, Xatb)
                    nxt = [(Xb, Xbb, Xbt, Xbtb), (Xa, Xab_, Xat, Xatb)]
                    for s_ in range(7):
                        X, Xbuf, XT_, XTbuf = cur
                        N_, Nb, NT, NTb = nxt[s_ % 2]
                        for h in H:
                            bi = h
                            if s_ < 5:
                                S.op("pe", lambda e, h=h, bi=bi, X=X, XT_=XT_: e.matmul(ps[bi][:, 0:128], XT_[:, h, :], X[:, h, :], start=True, stop=True),
                                     reads=[Xbuf[h], XTbuf[h]], writes=[psb[bi]], signal=False)
                            if s_ <= 5:
                                S.op("pe", lambda e, h=h, bi=bi, X=X, XT_=XT_: e.matmul(ps[bi][:, 128:256], X[:, h, :], XT_[:, h, :], start=True, stop=True),
                                     reads=[Xbuf[h], XTbuf[h]], writes=[psb[bi]], signal=(s_ == 0))
                            if s_ >= 1:
                                S.op("pe", lambda e, h=h, bi=bi, XT_=XT_: e.matmul(ps[bi][:, 256:384], XT_[:, h, :], Pm[:, h, :], start=True, stop=True),
                                     reads=[XTbuf[h], Pmb[h]], writes=[psb[bi]])
                        for h in H:
                            bi = h
                            if s_ < 5:
                                S.op("dve", lambda e, h=h, bi=bi, N_=N_: e.tensor_copy(N_[:, h, :], ps[bi][:, 0:128]), reads=[psb[bi]], writes=[Nb[h]])
                            if s_ <= 5:
                                S.op("dve", lambda e, h=h, bi=bi, NT=NT: e.tensor_copy(NT[:, h, :], ps[bi][:, 128:256]), reads=[psb[bi]], writes=[NTb[h]])
                            if s_ >= 1:
                                S.op("dve", lambda e, h=h, bi=bi: e.tensor_tensor(Pm[:, h, :], Pm[:, h, :], ps[bi][:, 256:384], ALU.add),
                                     reads=[psb[bi], Pmb[h]], writes=[Pmb[h]])
                        cur = (N_, Nb, NT, NTb)
                    for h in H:
                        bi = h
                        S.op("pe", lambda e, h=h, bi=bi: e.matmul(ps[bi][:, 0:128], Pm[:, h, :], AT[:, h, :], start=True, stop=True),
                             reads=[Pmb[h], ATb[h]], writes=[psb[bi]], signal=False)
                        S.op("pe", lambda e, h=h, bi=bi: e.matmul(ps[bi][:, 128:256], Pm[:, h, :], kd[:, h, :], start=True, stop=True),
                             reads=[Pmb[h], kdb[h]], writes=[psb[bi]])
                    for h in H:
                        bi = h
                        S.op("dve", lambda e, h=h, bi=bi: e.tensor_copy(A2T[:, h, :], ps[bi][:, 0:128]), reads=[psb[bi]], writes=[A2Tb[h]])
                        S.op("dve", lambda e, h=h, bi=bi: e.tensor_copy(K2[:, h, :], ps[bi][:, 128:256]), reads=[psb[bi]], writes=[K2b[h]])
                    for h in H:
                        bi = h
                        S.op("pe", lambda e, ts_=ts_, h=h, bi=bi: e.matmul(ps[bi][:, 0:128], KT[:, h, ts_], Sbf[:, h, :], start=True, stop=True),
                             reads=[KTb[h], Sbfb[h]], writes=[psb[bi]])
                    for h in H:
                        bi = h
                        S.op("dve", lambda e, h=h, bi=bi: e.scalar_tensor_tensor(Rbf[:, h, :], ps[bi][:, 0:128], ngb[:, h, 0:1], vb[:, h, :], ALU.mult, ALU.add),
                             reads=[psb[bi], ngbb[h], vbb[h]], writes=[Rbfb[h]])
                    for h in H:
                        bi = h
                        S.op("pe", lambda e, h=h, bi=bi: e.matmul(ps[bi][:, 128:256], qdT[:, h, :], Sbf[:, h, :], start=True, stop=False),
                             reads=[qdTb[h], Sbfb[h]], writes=[psb[bi]], signal=False)
                        S.op("pe", lambda e, h=h, bi=bi: e.matmul(ps[bi][:, 128:256], A2T[:, h, :], Rbf[:, h, :], start=False, stop=True),
                             reads=[A2Tb[h], Rbfb[h]], writes=[psb[bi]], signal=False)
                        S.op("pe", lambda e, h=h, bi=bi: e.matmul(ps[bi][:, 256:384], K2[:, h, :], Rbf[:, h, :], start=True, stop=True),
                             reads=[K2b[h], Rbfb[h]], writes=[psb[bi]])
                    for h in H:
                        bi = h
                        S.op("dve", lambda e, h=h, bi=bi: e.scalar_tensor_tensor(Sst[:, h, :], Sst[:, h, :], gl_(h), ps[bi][:, 256:384], ALU.mult, ALU.add),
                             reads=[psb[bi], EXb[h], Sstb[h]], writes=[Sstb[h]])
                        S.op("dve", lambda e, h=h: e.tensor_copy(Sbf[:, h, :], Sst[:, h, :]), reads=[Sstb[h]], writes=[Sbfb[h]])
                        S.op("dve", lambda e, h=h, bi=bi: e.tensor_copy(Xa[:, h, :], ps[bi][:, 128:256]), reads=[psb[bi]], writes=[Xab_[h]])
                        S.op("act", lambda e, h=h: e.activation(yo[:, h, :], Xa[:, h, :], AF.Square, accum_out=ngb[:, h, 1:2]),
                             reads=[Xab_[h]], writes=[yob[h], ngbb[h]])
                        S.op("act", lambda e, h=h: e.activation(ngb[:, h, 1:2], ngb[:, h, 1:2], AF.Sqrt, bias=C.eps_t[:, 0:1], scale=1.0 / 128),
                             reads=[ngbb[h], C.cb], writes=[ngbb[h]])
                        S.op("dve", lambda e, h=h: e.reciprocal(ngb[:, h, 1:2], ngb[:, h, 1:2]), reads=[ngbb[h]], writes=[ngbb[h]])
                        S.op("dve", lambda e, sg=sg, lg8=lg8, h=h, bi=bi: e.scalar_tensor_tensor(yo[:, h, :], Xa[:, h, :], ngb[:, h, 1:2], gsg[:, h, :], ALU.mult, ALU.mult),
                             reads=[Xab_[h], ngbb[h], gsgb[h]], writes=[yob[h]])
                    for h in H:
                        bi = h
                        S.op("pe", lambda e, h=h, bi=bi: e.transpose(ps[bi][:, 0:128], yo[:, h, :], ident), reads=[yob[h], mkb], writes=[psb[bi]])
                        S.op("dve", lambda e, ts_=ts_, h=h, bi=bi: e.tensor_copy(OT[:, 4 + h, ts_], ps[bi][:, 0:128]), reads=[psb[bi]], writes=[OTb])
                S.dma("sp", YCT(tile).rearrange("(c p) t -> p c t", p=128), OT[:, :, :], reads=[OTb], writes=[outb[tile]])
                if A.get("after_tile") is not None:
                    A["after_tile"](tile)

            emit_A(0)
            emit_B1(0)
            for tile in range(ntiles):
                emit_B2E(tile)
                if tile + 1 < ntiles:
                    S.record()
                    emit_D(tile)
                    lD = S.stop()
                    S.record()
                    emit_A(tile + 1)
                    emit_B1(tile + 1)
                    lA = S.stop()
                    S.replay([lD, lA])
                else:
                    emit_D(tile)
            print('phase1 sbuf', nc.sbuf_base, nc.sbuf_top)
            S.barrier()


POOL_WINDOWS = (2, 4, 8, 16)


def phase1_inputs(inputs, b, hg):
    w = np.asarray(inputs["ab_w_in"][0], np.float32)
    PW, GW = 2048, 2048
    cols = np.concatenate([
        np.arange(hg * 512, (hg + 1) * 512),
        PW + 3 * GW + np.arange(hg * 512, (hg + 1) * 512),
        PW + np.arange(hg * 512, (hg + 1) * 512),
        PW + GW + np.arange(hg * 512, (hg + 1) * 512),
        PW + 2 * GW + np.arange(hg * 512, (hg + 1) * 512),
        PW + 4 * GW + np.arange(hg * 4, (hg + 1) * 4),
        PW + 4 * GW + 16 + np.arange(hg * 4, (hg + 1) * 4),
    ])
    wc = np.ascontiguousarray(w[:, cols])
    conv = np.asarray(inputs["gdn_conv"][0], np.float32)
    cwl = np.stack([conv[:, s * GW + hg * 512: s * GW + (hg + 1) * 512] for s in range(3)], 0)
    cwl = cwl.reshape(3, 4, 4, 128).transpose(3, 0, 2, 1).reshape(128, 12, 4)
    bc = lambda v: np.ascontiguousarray(np.broadcast_to(np.asarray(v, np.float32)[None, :], (128, len(v))))
    win = POOL_WINDOWS[hg]
    pos = np.arange(256)
    def band_full(first):
        B = np.zeros((256, 128), np.float32)
        for t in range(128):
            cnt = min(t + 1, win) if first else win
            for s in range(max(0, 128 + t - win + 1) if not first else 128 + max(0, t - win + 1), 128 + t + 1):
                B[s, t] = 1.0 / cnt
            B[128 + t, t] -= 1.0
        return B
    Bn = band_full(False)
    B0 = band_full(True)
    band = np.stack([B0[128:], Bn[128:], Bn[:128]], 0)
    k = np.arange(128)
    triA = (k[:, None] <= k[None, :]).astype(np.float32)
    strictU = (k[:, None] > k[None, :]).astype(np.float32)
    MS = (k[:, None] > k[None, :]).astype(np.float32)
    MU = (k[:, None] <= k[None, :]).astype(np.float32)
    masks = np.stack([triA, strictU, MS, MU, np.eye(128, dtype=np.float32)], 0)
    return {
        "xb": np.ascontiguousarray(np.asarray(inputs["x"][b], np.float32)),
        "g_pre": np.asarray(inputs["norm_mix_pre"][0], np.float32),
        "w_in_c": wc,
        "conv_w": np.ascontiguousarray(cwl),
        "neg_a": bc(inputs["gdn_a_log"][0][hg * 4:(hg + 1) * 4]),
        "dt_b": bc(inputs["gdn_dt_bias"][0][hg * 4:(hg + 1) * 4]),
        "gn_w": bc(inputs["gdn_norm"][0]),
        "pool_wg": np.ascontiguousarray(np.asarray(inputs["pool_w"][0][hg], np.float32)),
        "pool_sc": np.ascontiguousarray(np.asarray(inputs["pool_scale"][0][hg * 512:(hg + 1) * 512], np.float32)),
        "bandm": np.ascontiguousarray(band), "masks": np.ascontiguousarray(masks),
    }


def kernel(**inputs):
    n = 8
    nc = build_fused()
    consts = phase2_consts(inputs)
    x = np.asarray(inputs["x"], np.float32)
    maps = []
    for c in range(n):
        b, j = c // 4, c % 4
        m = dict(consts)
        m.update(phase1_inputs(inputs, b, j))
        m["x_own"] = np.ascontiguousarray(x[b, j * TP:(j + 1) * TP, :])
        sel = np.zeros((128, 4), np.float32)
        sel[:, j] = 1.0
        m["sel"] = sel
        maps.append(m)
    res = run_bass_kernel_spmd(nc, maps, core_ids=list(range(n)))
    out = np.empty((2, SEQ, D), np.float32)
    for c in range(n):
        b, j = c // 4, c % 4
        out[b, j * TP:(j + 1) * TP, :] = np.asarray(res.results[c]["out"], np.float32)
    return out
```
